# Optimizing a Trainium2 kernel written in Bass

```python
import jax, jax.numpy as jnp
from jax import lax
import numpy as np

D_MODEL = 1024
BATCH = 16
SEQ = 256
DEPTH = 4
DEC_BATCH = 2
DEC_SEQ = 4096
PAST_LEN = 256

GRID_W = 64
NA_HEADS = 8
NA_HEAD_DIM = 64
NA_WIDTH = NA_HEADS * NA_HEAD_DIM
NA_ROWS = 8
NA_COLS = 16
NA_SCALE = NA_HEAD_DIM ** -0.5
MLA_HEADS = 8
MLA_NOPE = 64
MLA_ROPE = 32
MLA_V = 64
MLA_WIDTH = MLA_HEADS * MLA_V
Q_LORA = 256
KV_LORA = 128
MLA_SCALE = (MLA_NOPE + MLA_ROPE) ** -0.5
ROPE_THETA = 10000.0
Q_BLOCK = 128
FN_GROUPS = 4
FN_GROUP_W = 128
FN_WIDTH = FN_GROUPS * FN_GROUP_W
EPS = 1e-6
SPLIT_SIZES = (3 * NA_WIDTH, NA_WIDTH, Q_LORA, KV_LORA, MLA_ROPE, MLA_WIDTH, FN_WIDTH, FN_WIDTH, 3 * D_MODEL)
SPLIT_POINTS = tuple(sum(SPLIT_SIZES[:i + 1]) for i in range(len(SPLIT_SIZES) - 1))
D_IN = sum(SPLIT_SIZES)

kernel_name = 'hybrid_flow_natten_mla_fnet_step'


def rms_norm(x, g):
    xf = x.astype(jnp.float32)
    y = xf * lax.rsqrt(jnp.mean(xf * xf, axis=-1, keepdims=True) + EPS)
    return (y * g.astype(jnp.float32)).astype(x.dtype)


def axial_rope_tables(n, dtype):
    t = jnp.arange(n, dtype=jnp.int32)
    row = (t // GRID_W).astype(jnp.float32)
    col = (t % GRID_W).astype(jnp.float32)
    half = MLA_ROPE // 2
    inv_freq = ROPE_THETA ** (-jnp.arange(0, half, 2, dtype=jnp.float32) / half)
    ar = row[:, None] * inv_freq[None, :]
    ac = col[:, None] * inv_freq[None, :]
    ang = jnp.concatenate([ar, ar, ac, ac], axis=-1)
    return jnp.cos(ang).astype(dtype), jnp.sin(ang).astype(dtype)


def _rotate_half(z):
    z1, z2 = jnp.split(z, 2, axis=-1)
    return jnp.concatenate([-z2, z1], axis=-1)


def apply_axial_rope(x, cos, sin):
    xr, xc = jnp.split(x, 2, axis=-1)
    rot = jnp.concatenate([_rotate_half(xr), _rotate_half(xc)], axis=-1)
    return x * cos + rot * sin


def branch_inputs(x, cond, w_ada, b_ada, norm_g, w_in):
    shift, scale, gate = jnp.split(jax.nn.silu(cond) @ w_ada + b_ada, 3, axis=-1)
    xm = rms_norm(x, norm_g) * (1.0 + scale) + shift
    parts = jnp.split(xm @ w_in, SPLIT_POINTS, axis=-1)
    return gate, parts


def merge_branches(o_na, o_mla, o_fn, gate_na, gate_mla, gate_fn, merge_logits,
                   w_o_na, w_o_mla, w_o_fourier, w_out):
    g_na, g_mla, g_fn = jnp.split(jax.nn.sigmoid(merge_logits), 3, axis=-1)
    merged = (g_na * ((o_na * jax.nn.silu(gate_na)) @ w_o_na)
              + g_mla * ((o_mla * jax.nn.silu(gate_mla)) @ w_o_mla)
              + g_fn * ((o_fn * jax.nn.silu(gate_fn)) @ w_o_fourier))
    return merged @ w_out


def dense_attend(q, k, v):
    s = jnp.einsum('bqhd,bkhd->bhqk', q, k).astype(jnp.float32) * NA_SCALE
    p = jax.nn.softmax(s, axis=-1).astype(v.dtype)
    return jnp.einsum('bhqk,bkhd->bqhd', p, v)


def neighbourhood_attend(q, k, v, k_ctx, v_ctx, bias_table):
    B, N = q.shape[0], q.shape[1]
    rows = N // GRID_W
    kr = min(NA_ROWS, rows)
    r = np.arange(rows)
    row_start = np.clip(r - kr // 2, 0, rows - kr)
    row_idx = row_start[:, None] + np.arange(kr)[None, :]
    dr = row_idx - r[:, None]
    col = np.arange(GRID_W)
    col_start = np.clip(col - NA_COLS // 2, 0, GRID_W - NA_COLS)
    dc = col[None, :] - col[:, None]
    col_in = (col[None, :] >= col_start[:, None]) & (col[None, :] < col_start[:, None] + NA_COLS)
    bias = bias_table[:, (dr + NA_ROWS - 1)[:, None, :, None],
                      np.clip(dc + NA_COLS - 1, 0, 2 * NA_COLS - 2)[None, :, None, :]]
    qg = q.reshape(B, rows, GRID_W, NA_HEADS, NA_HEAD_DIM)
    kg = k.reshape(B, rows, GRID_W, NA_HEADS, NA_HEAD_DIM)[:, row_idx]
    vg = v.reshape(B, rows, GRID_W, NA_HEADS, NA_HEAD_DIM)[:, row_idx]
    s_loc = jnp.einsum('brchd,brijhd->bhrcij', qg, kg).astype(jnp.float32) * NA_SCALE + bias.astype(jnp.float32)[None]
    s_loc = jnp.where(col_in[:, None, :], s_loc, -jnp.inf)
    s_loc = s_loc.reshape(B, NA_HEADS, rows, GRID_W, kr * GRID_W)
    s_ctx = jnp.einsum('brchd,blhd->bhrcl', qg, k_ctx).astype(jnp.float32) * NA_SCALE
    p = jax.nn.softmax(jnp.concatenate([s_loc, s_ctx], axis=-1), axis=-1).astype(v.dtype)
    n_loc = kr * GRID_W
    o = (jnp.einsum('bhrck,brkhd->brchd', p[..., :n_loc], vg.reshape(B, rows, n_loc, NA_HEADS, NA_HEAD_DIM))
         + jnp.einsum('bhrcl,blhd->brchd', p[..., n_loc:], v_ctx))
    return o.reshape(B, N, NA_WIDTH)


def mla_queries(q_lat, q_norm_g, w_uq):
    q = rms_norm(q_lat, q_norm_g) @ w_uq
    q = q.reshape(q.shape[0], q.shape[1], MLA_HEADS, MLA_NOPE + MLA_ROPE)
    return q[..., :MLA_NOPE], q[..., MLA_NOPE:]


def mla_keys_values(ckv, w_ukv):
    kv = (ckv @ w_ukv).reshape(ckv.shape[0], ckv.shape[1], MLA_HEADS, MLA_NOPE + MLA_V)
    return kv[..., :MLA_NOPE], kv[..., MLA_NOPE:]


def mla_attend(q_nope, q_rope, k_nope, k_rope, v):
    s = (jnp.einsum('bqhd,bkhd->bhqk', q_nope, k_nope)
         + jnp.einsum('bqhr,bkr->bhqk', q_rope, k_rope))
    p = jax.nn.softmax(s.astype(jnp.float32) * MLA_SCALE, axis=-1).astype(v.dtype)
    return jnp.einsum('bhqk,bkhd->bqhd', p, v)


def mla_blockwise_attend(q_nope, q_rope, k_nope, k_rope, v):
    B, N = q_nope.shape[0], q_nope.shape[1]
    nb = N // Q_BLOCK
    qn = q_nope.reshape(B, nb, Q_BLOCK, MLA_HEADS, MLA_NOPE).transpose(1, 0, 2, 3, 4)
    qr = q_rope.reshape(B, nb, Q_BLOCK, MLA_HEADS, MLA_ROPE).transpose(1, 0, 2, 3, 4)
    o = lax.map(lambda a: mla_attend(a[0], a[1], k_nope, k_rope, v), (qn, qr))
    return o.transpose(1, 0, 2, 3, 4).reshape(B, N, MLA_WIDTH)


def fourier_mix(u):
    B, N = u.shape[0], u.shape[1]
    ug = u.astype(jnp.float32).reshape(B, N, FN_GROUPS, FN_GROUP_W)
    y = jnp.fft.fft2(ug, axes=(1, 3), norm='ortho').real
    return y.reshape(B, N, FN_WIDTH).astype(u.dtype)


def context_layer(x, c_ctx, w_ada, b_ada, norm_g, w_in, q_norm_g, kv_norm_g, w_uq, w_ukv,
                  w_o_na, w_o_mla, w_o_fourier, w_out):
    B, N = x.shape[0], x.shape[1]
    gate, parts = branch_inputs(x, c_ctx, w_ada, b_ada, norm_g, w_in)
    qkv, gate_na, q_lat, ckv_raw, k_rope, gate_mla, u_fn, gate_fn, merge_logits = parts
    q, k, v = [t.reshape(B, N, NA_HEADS, NA_HEAD_DIM) for t in jnp.split(qkv, 3, axis=-1)]
    o_na = dense_attend(q, k, v).reshape(B, N, NA_WIDTH)
    ckv = rms_norm(ckv_raw, kv_norm_g)
    q_nope, q_rope = mla_queries(q_lat, q_norm_g, w_uq)
    k_nope, v_mla = mla_keys_values(ckv, w_ukv)
    o_mla = mla_attend(q_nope, q_rope, k_nope, k_rope, v_mla).reshape(B, N, MLA_WIDTH)
    o_fn = fourier_mix(u_fn)
    out = merge_branches(o_na, o_mla, o_fn, gate_na, gate_mla, gate_fn, merge_logits,
                         w_o_na, w_o_mla, w_o_fourier, w_out)
    return x + gate * out, k, v, ckv, k_rope


def latent_layer(x, c, na_k_ctx, na_v_ctx, ckv_ctx, krope_ctx, cos, sin,
                 w_ada, b_ada, norm_g, w_in, q_norm_g, kv_norm_g, w_uq, w_ukv, na_bias,
                 w_o_na, w_o_mla, w_o_fourier, w_out):
    B, N = x.shape[0], x.shape[1]
    gate, parts = branch_inputs(x, c[:, None, :], w_ada, b_ada, norm_g, w_in)
    qkv, gate_na, q_lat, ckv_raw, k_rope, gate_mla, u_fn, gate_fn, merge_logits = parts
    q, k, v = [t.reshape(B, N, NA_HEADS, NA_HEAD_DIM) for t in jnp.split(qkv, 3, axis=-1)]
    o_na = neighbourhood_attend(q, k, v, na_k_ctx, na_v_ctx, na_bias)
    ckv = rms_norm(ckv_raw, kv_norm_g)
    q_nope, q_rope = mla_queries(q_lat, q_norm_g, w_uq)
    q_rope = apply_axial_rope(q_rope, cos[:, None, :], sin[:, None, :])
    k_rope = apply_axial_rope(k_rope, cos, sin)
    k_nope_lat, v_lat = mla_keys_values(ckv, w_ukv)
    k_nope_ctx, v_ctx = mla_keys_values(ckv_ctx, w_ukv)
    k_nope_all = jnp.concatenate([k_nope_lat, k_nope_ctx], axis=1)
    k_rope_all = jnp.concatenate([k_rope, krope_ctx], axis=1)
    v_all = jnp.concatenate([v_lat, v_ctx], axis=1)
    o_mla = mla_blockwise_attend(q_nope, q_rope, k_nope_all, k_rope_all, v_all)
    o_fn = fourier_mix(u_fn)
    out = merge_branches(o_na, o_mla, o_fn, gate_na, gate_mla, gate_fn, merge_logits,
                         w_o_na, w_o_mla, w_o_fourier, w_out)
    return x + gate * out


def setup_inputs(seed: int = 0) -> dict:
    key = jax.random.key(seed)
    ks = jax.random.split(key, 24)

    def nrm(k, shape, scale):
        return jax.random.normal(k, shape, jnp.float32) * scale

    return {
        'x_prompt': nrm(ks[0], (BATCH, SEQ, D_MODEL), 1.0),
        'x_sample': nrm(ks[1], (DEC_BATCH, DEC_SEQ, D_MODEL), 1.0),
        'cache_na_k': nrm(ks[2], (DEC_BATCH, DEPTH, PAST_LEN, NA_HEADS, NA_HEAD_DIM), 1.0),
        'cache_na_v': nrm(ks[3], (DEC_BATCH, DEPTH, PAST_LEN, NA_HEADS, NA_HEAD_DIM), 1.0),
        'cache_mla_ckv': nrm(ks[4], (DEC_BATCH, DEPTH, PAST_LEN, KV_LORA), 1.0),
        'cache_mla_krope': nrm(ks[5], (DEC_BATCH, DEPTH, PAST_LEN, MLA_ROPE), 1.0),
        'c': nrm(ks[6], (DEC_BATCH, D_MODEL), 1.0),
        'c_ctx': nrm(ks[7], (D_MODEL,), 1.0),
        'w_ada': nrm(ks[8], (DEPTH, D_MODEL, 3 * D_MODEL), 0.2 * D_MODEL ** -0.5),
        'b_ada': nrm(ks[9], (DEPTH, 3 * D_MODEL), 0.01),
        'norm_g': 1.0 + nrm(ks[10], (DEPTH, D_MODEL), 0.05),
        'w_in': nrm(ks[11], (DEPTH, D_MODEL, D_IN), D_MODEL ** -0.5),
        'q_norm_g': 1.0 + nrm(ks[12], (DEPTH, Q_LORA), 0.05),
        'kv_norm_g': 1.0 + nrm(ks[13], (DEPTH, KV_LORA), 0.05),
        'w_uq': nrm(ks[14], (DEPTH, Q_LORA, MLA_HEADS * (MLA_NOPE + MLA_ROPE)), Q_LORA ** -0.5),
        'w_ukv': nrm(ks[15], (DEPTH, KV_LORA, MLA_HEADS * (MLA_NOPE + MLA_V)), KV_LORA ** -0.5),
        'na_bias': nrm(ks[16], (DEPTH, NA_HEADS, 2 * NA_ROWS - 1, 2 * NA_COLS - 1), 0.1),
        'w_o_na': nrm(ks[17], (DEPTH, NA_WIDTH, D_MODEL), NA_WIDTH ** -0.5),
        'w_o_mla': nrm(ks[18], (DEPTH, MLA_WIDTH, D_MODEL), MLA_WIDTH ** -0.5),
        'w_o_fourier': nrm(ks[19], (DEPTH, FN_WIDTH, D_MODEL), FN_WIDTH ** -0.5),
        'w_out': nrm(ks[20], (DEPTH, D_MODEL, D_MODEL), D_MODEL ** -0.5),
        'final_norm_g': 1.0 + nrm(ks[21], (D_MODEL,), 0.05),
    }


def reference(x_prompt, x_sample, cache_na_k, cache_na_v, cache_mla_ckv, cache_mla_krope, c, c_ctx,
              w_ada, b_ada, norm_g, w_in, q_norm_g, kv_norm_g, w_uq, w_ukv, na_bias,
              w_o_na, w_o_mla, w_o_fourier, w_out, final_norm_g):
    h_ctx = x_prompt
    h_lat = x_sample
    cos, sin = axial_rope_tables(x_sample.shape[1], x_sample.dtype)
    ks_na, vs_na, ckvs, krs = [], [], [], []
    for l in range(DEPTH):
        h_ctx, k_l, v_l, ckv_l, kr_l = context_layer(
            h_ctx, c_ctx, w_ada[l], b_ada[l], norm_g[l], w_in[l], q_norm_g[l], kv_norm_g[l],
            w_uq[l], w_ukv[l], w_o_na[l], w_o_mla[l], w_o_fourier[l], w_out[l])
        ks_na.append(k_l)
        vs_na.append(v_l)
        ckvs.append(ckv_l)
        krs.append(kr_l)
        h_lat = latent_layer(
            h_lat, c, cache_na_k[:, l], cache_na_v[:, l], cache_mla_ckv[:, l], cache_mla_krope[:, l],
            cos, sin, w_ada[l], b_ada[l], norm_g[l], w_in[l], q_norm_g[l], kv_norm_g[l],
            w_uq[l], w_ukv[l], na_bias[l], w_o_na[l], w_o_mla[l], w_o_fourier[l], w_out[l])
    y_prompt = rms_norm(h_ctx, final_norm_g)
    y_sample = rms_norm(h_lat, final_norm_g)
    new_na_k = jnp.stack(ks_na, axis=1)
    new_na_v = jnp.stack(vs_na, axis=1)
    new_mla_ckv = jnp.stack(ckvs, axis=1)
    new_mla_krope = jnp.stack(krs, axis=1)
    return (y_prompt, y_sample, new_na_k, new_na_v, new_mla_ckv, new_mla_krope)
```

```python
import os
import numpy as np
import ml_dtypes
from contextlib import ExitStack
import concourse.bass as bass
import concourse.mybir as mybir
from concourse.bass_utils import run_bass_kernel_spmd

F32 = mybir.dt.float32
BF16 = mybir.dt.bfloat16
AF = mybir.ActivationFunctionType
ALU = mybir.AluOpType

L = 4
D = 1024
D_IN = 7072
NA_SCALE = 64 ** -0.5
MLA_SCALE = 96 ** -0.5
EPS = 1e-6
C_Q, C_K, C_V, C_GNA, C_QLAT, C_CKV, C_KR, C_GMLA, C_UFN, C_GFN, C_MRG = (
    0, 512, 1024, 1536, 2048, 2304, 2432, 2464, 2976, 3488, 4000)
NEG = -30000.0
GROUPS = [[0, 1, 2, 3], [4, 5, 6, 7]]


class Buf:
    __slots__ = ("ap", "w", "r", "pm", "name", "fw", "excl")

    def __init__(self, ap, name=""):
        self.ap = ap
        self.w = {}
        self.r = {}
        self.pm = False
        self.fw = {}
        self.excl = False
        self.name = name


class Sched:
    LIMIT = 12000
    NLANES = 8

    def __init__(self, nc, es):
        self.nc = nc
        self.es = es
        self.eng = dict(pe=nc.tensor, act=nc.scalar, dve=nc.vector, pool=nc.gpsimd, sp=nc.sync)
        self.sems = []
        self.cur = {}
        self.known = {e: {} for e in self.eng}
        self.lanes = {e: [] for e in self.eng}
        self.rr = {e: 0 for e in self.eng}
        self.pe_sems = set()
        self.nins = 0

    def new_sem(self):
        h = self.es.enter_context(self.nc.semaphore("sm%d" % len(self.sems)))
        self.sems.append(h)
        return len(self.sems) - 1

    def _deps(self, reads, writes, pwrites):
        deps = {}

        def add(d):
            for k, v in d.items():
                if deps.get(k, 0) < v:
                    deps[k] = v
        for b in reads:
            add(b.w)
            if b.excl:
                add(b.r)
        for b in writes:
            add(b.w)
            add(b.r)
        for b in pwrites:
            add(b.r)
            add(b.fw)
            if not b.pm:
                add(b.w)
        return deps

    def _wait(self, e, deps):
        kn = self.known[e]
        for sem, val in deps.items():
            if e == "pe" and sem in self.pe_sems:
                continue
            if kn.get(sem, 0) >= val:
                continue
            self.eng[e].wait_ge(self.sems[sem], val)
            kn[sem] = val
            self.nins += 1

    def _mark(self, stamp, reads, writes, pwrites):
        s, v = stamp
        for b in writes:
            b.w = {s: v}
            b.fw = {s: v}
            b.r = {}
            b.pm = False
        for b in pwrites:
            b.w[s] = v
            b.pm = True
        for b in reads:
            b.r[s] = v

    def op(self, e, fn, reads=(), writes=(), pwrites=()):
        self._wait(e, self._deps(reads, writes, pwrites))
        c = self.cur.get(e)
        if c is None or c[1] >= self.LIMIT:
            c = [self.new_sem(), 0]
            self.cur[e] = c
            if e == "pe":
                self.pe_sems.add(c[0])
        c[1] += 1
        ins = fn(self.eng[e])
        ins.then_inc(self.sems[c[0]], 1)
        self.nins += 1
        self._mark((c[0], c[1]), reads, writes, pwrites)

    def dma(self, e, fn, reads=(), writes=(), pwrites=(), inc=16):
        deps = self._deps(reads, writes, pwrites)
        lanes = self.lanes[e]
        if len(lanes) < self.NLANES:
            lanes.append([self.new_sem(), 0])
            lane = lanes[-1]
        else:
            lane = lanes[self.rr[e] % self.NLANES]
            self.rr[e] += 1
        if lane[1] > 0 and deps.get(lane[0], 0) < lane[1]:
            deps[lane[0]] = lane[1]
        self._wait(e, deps)
        lane[1] += inc
        ins = fn(self.eng[e])
        ins.then_inc(self.sems[lane[0]], inc)
        self.nins += 1
        self._mark((lane[0], lane[1]), reads, writes, pwrites)

    def cc(self, fn, reads=(), writes=()):
        deps = self._deps(reads, writes, ())
        if not hasattr(self, "cclane"):
            self.cclane = [self.new_sem(), 0]
        lane = self.cclane
        if lane[1] > 0 and deps.get(lane[0], 0) < lane[1]:
            deps[lane[0]] = lane[1]
        self._wait("pool", deps)
        lane[1] += 1
        ins = fn(self.eng["pool"])
        ins.then_inc(self.sems[lane[0]], 1)
        self.nins += 1
        self._mark((lane[0], lane[1]), reads, writes, ())

    def barrier(self):
        allst = {}
        for e, c in self.cur.items():
            allst[c[0]] = c[1]
        for e, lanes in self.lanes.items():
            for ln in lanes:
                if ln[1] > 0:
                    allst[ln[0]] = ln[1]
        for e in self.eng:
            d = dict(allst)
            c = self.cur.get(e)
            if e == "pe" and c is not None:
                d.pop(c[0], None)
            self._wait(e, d)

    def finish(self):
        allst = {}
        for e, lanes in self.lanes.items():
            for ln in lanes:
                if ln[1] > 0:
                    allst[ln[0]] = ln[1]
        for e, c in self.cur.items():
            allst[c[0]] = c[1]
        if hasattr(self, "cclane") and self.cclane[1] > 0:
            allst[self.cclane[0]] = self.cclane[1]
        d = dict(allst)
        self._wait("sp", d)


class KB:
    def __init__(self, run_ctx=True, run_lat=True, nlayers=L, dbg=False, lw=L, stage=99):
        self.LW = lw
        self.stage = stage
        self.run_ctx = run_ctx
        self.run_lat = run_lat
        self.nlayers = nlayers
        self.nc = bass.Bass("TRN2", target_bir_lowering=False)
        self.es = ExitStack()
        self.s = Sched(self.nc, self.es)
        self.tcount = 0

    def din(self, name, shape, dt=F32):
        return self.nc.dram_tensor(name, list(shape), dt, kind="ExternalInput").ap()

    def dout(self, name, shape, dt=F32):
        return self.nc.dram_tensor(name, list(shape), dt, kind="ExternalOutput").ap()

    def dint(self, name, shape, dt=BF16):
        return self.nc.dram_tensor(name, list(shape), dt).ap()

    def sb(self, st, name, shape, dt):
        self.tcount += 1
        return st.enter_context(self.nc.sbuf_tensor("%s_%d" % (name, self.tcount), list(shape), dt))

    def ps(self, grp):
        idxs = self.psgrp[grp]
        i = idxs[self.psrr[grp] % len(idxs)]
        self.psrr[grp] += 1
        return self.PS[i]

    def build(self):
        nc, s, es = self.nc, self.s, self.es
        NL = self.nlayers
        d = {}
        d["xc"] = self.din("xc", [D, 512])
        d["xl"] = self.din("xl", [D, 1024])
        d["cond"] = self.din("cond", [D, 2])
        d["w_ada"] = self.din("w_ada", [self.LW, D, 3 * D])
        d["b_adaT"] = self.din("b_adaT", [128, L, 24])
        d["norm_gT"] = self.din("norm_gT", [128, L, 8])
        d["fin_gT"] = self.din("fin_gT", [128, 8])
        d["qn_gT"] = self.din("qn_gT", [128, L, 2])
        d["kvn_gT"] = self.din("kvn_gT", [128, L])
        d["kvn_bc"] = self.din("kvn_bc", [128, L, 128])
        d["w_in"] = self.din("w_in", [self.LW, D, D_IN])
        d["w_krp"] = self.din("w_krp", [self.LW, D, 32])
        d["w_uq"] = self.din("w_uq", [self.LW, 256, 768])
        d["w_uqp"] = self.din("w_uqp", [self.LW, 256, 768])
        d["w_ukv"] = self.din("w_ukv", [self.LW, 128, 1024])
        d["w_o_na"] = self.din("w_o_na", [self.LW, 512, D])
        d["w_o_mla"] = self.din("w_o_mla", [self.LW, 512, D])
        d["w_o_fn"] = self.din("w_o_fn", [self.LW, 512, D])
        d["w_out"] = self.din("w_out", [self.LW, D, D])
        d["identb"] = self.din("identb", [128, 128], BF16)
        d["cs128"] = self.din("cs128", [128, 256], BF16)
        d["c256"] = self.din("c256", [256, 256], BF16)
        d["ns256"] = self.din("ns256", [256, 256], BF16)
        d["sel32"] = self.din("sel32", [32, 96], BF16)
        d["ropeT"] = self.din("ropeT", [2, 128, 1024])
        d["dftc"] = self.din("dftc", [4096, 1024], BF16)
        d["dftns"] = self.din("dftns", [4096, 1024], BF16)
        d["nab"] = self.din("nab", [self.LW, 128, 8, 7, 128])
        d["namask"] = self.din("namask", [128, 48, 128], BF16)
        d["selr"] = self.din("selr", [128, 8])
        d["cnakT"] = self.din("cnakT", [L, 512, 256])
        d["cnav"] = self.din("cnav", [L, 256, 512])
        d["cckvT"] = self.din("cckvT", [L, 128, 256])
        d["ckrT"] = self.din("ckrT", [L, 32, 256])
        d["yc"] = self.dout("yc", [D, 512])
        d["yl"] = self.dout("yl", [D, 1024])
        d["onk"] = self.dout("onk", [L, 512, 512])
        d["onv"] = self.dout("onv", [L, 512, 512])
        d["ockv"] = self.dout("ockv", [L, 512, 128])
        d["okr"] = self.dout("okr", [L, 512, 32])
        d["pay_mla"] = [self.dint("pay_mla%d" % l, [160, 1024]) for l in range(L)]
        d["g_mla"] = [self.dint("g_mla%d" % l, [640, 1024]) for l in range(L)]
        d["pay_k"] = [self.dint("pay_k%d" % l, [512, 512]) for l in range(L)]
        d["g_k"] = [self.dint("g_k%d" % l, [2048, 512]) for l in range(L)]
        d["pay_v"] = [self.dint("pay_v%d" % l, [512, 768]) for l in range(L)]
        d["g_v"] = [self.dint("g_v%d" % l, [2048, 768]) for l in range(L)]
        d["pay_ab0"] = [self.dint("pay_ab0_%d" % l, [512, 1024]) for l in range(L)]
        d["pay_ab1"] = [self.dint("pay_ab1_%d" % l, [512, 1024]) for l in range(L)]
        d["g_ab0"] = [self.dint("g_ab0_%d" % l, [2048, 1024]) for l in range(L)]
        d["g_ab1"] = [self.dint("g_ab1_%d" % l, [2048, 1024]) for l in range(L)]
        self.d = d
        self.db = {k: [Buf(a) for a in d[k]] for k in ("pay_mla", "g_mla", "pay_k", "g_k", "pay_v", "g_v", "pay_ab0", "pay_ab1", "g_ab0", "g_ab1")}

        self.PS = [Buf(es.enter_context(nc.psum_tensor("ps%d" % i, [128, 512], F32)), "ps%d" % i) for i in range(8)]
        for b in self.PS:
            b.excl = True
        self.psgrp = {"g": [0, 1, 2, 3], "o": [4, 5], "x": [6, 7], "all": list(range(8))}
        self.psrr = {k: 0 for k in self.psgrp}

        P = es
        self.identb = Buf(self.sb(P, "identb", [128, 128], BF16))
        self.onesb = Buf(self.sb(P, "onesb", [128, 3, 128], BF16))
        self.ones1b = Buf(self.sb(P, "ones1b", [128, 128], BF16))
        self.cs128 = Buf(self.sb(P, "cs128", [128, 256], BF16))
        self.c256 = Buf(self.sb(P, "c256", [128, 2, 256], BF16))
        self.ns256 = Buf(self.sb(P, "ns256", [128, 2, 256], BF16))
        self.sel32 = Buf(self.sb(P, "sel32", [32, 96], BF16))
        self.condt = Buf(self.sb(P, "condt", [128, 8, 2], F32))
        self.condb = Buf(self.sb(P, "condb", [128, 8, 2], BF16))
        self.bada = Buf(self.sb(P, "bada", [128, L, 24], F32))
        self.normg = Buf(self.sb(P, "normg", [128, L, 8], F32))
        self.fing = Buf(self.sb(P, "fing", [128, 8], F32))
        self.qng = Buf(self.sb(P, "qng", [128, L, 2], F32))
        self.kvng = Buf(self.sb(P, "kvng", [128, L], F32))
        self.kvnbc = Buf(self.sb(P, "kvnbc", [128, L, 128], F32))
        self.MOD = Buf(self.sb(P, "mod", [128, L, 24, 2], F32))
        self.GS = Buf(self.sb(P, "gs", [128, L, 8, 2], F32))
        self.WB = [Buf(self.sb(P, "wb%d" % i, [128, 8, 512], BF16)) for i in range(3)]
        self.wbi = 0
        self.SQ = [Buf(self.sb(P, "sq%d" % i, [128, 512], BF16)) for i in range(2)]
        self.TMP = [Buf(self.sb(P, "tmp%d" % i, [128, 512], F32)) for i in range(2)]
        self.RS = Buf(self.sb(P, "rs", [128, 512], F32))
        self.RD = Buf(self.sb(P, "rd", [128, 512], F32))
        self.BCS = Buf(self.sb(P, "bcs", [128, 512], F32))
        self.TMPO = Buf(self.sb(P, "tmpo", [128, 512], F32))
        self.PT = [Buf(self.sb(P, "pt%d" % i, [128, 512], BF16)) for i in range(5)]
        self.pending = None
        self.pti = 0
        self.sqi = 0
        self.tmi = 0

        def ld(buf, src, e="sp"):
            s.dma(e, lambda q: q.dma_start(out=buf.ap[:], in_=src), writes=[buf])
        ld(self.identb, d["identb"][:, :])
        ld(self.cs128, d["cs128"][:, :])
        ld(self.c256, d["c256"].rearrange("(t p) n -> p t n", p=128))
        ld(self.ns256, d["ns256"].rearrange("(t p) n -> p t n", p=128))
        ld(self.sel32, d["sel32"][:, :])
        ld(self.condt, d["cond"].rearrange("(c p) n -> p c n", p=128))
        ld(self.bada, d["b_adaT"][:, :, :])
        ld(self.normg, d["norm_gT"][:, :, :])
        ld(self.fing, d["fin_gT"][:, :])
        ld(self.qng, d["qn_gT"][:, :, :])
        ld(self.kvng, d["kvn_gT"][:, :])
        ld(self.kvnbc, d["kvn_bc"][:, :, :])
        for i, v in enumerate((1.0 / 1024, 1.0 / 256, 1.0 / 128)):
            s.op("dve", lambda q: q.memset(self.onesb.ap[:, i, :], v), pwrites=[self.onesb])
        s.op("dve", lambda q: q.memset(self.ones1b.ap[:], 1.0), writes=[self.ones1b])
        s.op("act", lambda q: q.activation(out=self.condb.ap[:], in_=self.condt.ap[:], func=AF.Silu),
             reads=[self.condt], writes=[self.condb])

        for l in range(NL):
            for jb in range(6):
                wb = self.next_wb()
                s.dma("pool", lambda q: q.dma_start(
                    out=wb.ap[:], in_=d["w_ada"][l, :, jb * 512:(jb + 1) * 512].rearrange("(c p) n -> p c n", p=128)),
                    writes=[wb])
                for jj in range(4):
                    j = jb * 4 + jj
                    ps = self.ps("g")
                    for kc in range(8):
                        s.op("pe", lambda q: q.matmul(ps.ap[:, 0:2], lhsT=wb.ap[:, kc, jj * 128:(jj + 1) * 128],
                                                      rhs=self.condb.ap[:, kc, :], start=(kc == 0), stop=(kc == 7)),
                             reads=[wb, self.condb], pwrites=[ps])
                    s.op("dve", lambda q: q.tensor_scalar(out=self.MOD.ap[:, l, j, :], in0=ps.ap[:, 0:2],
                                                          scalar1=self.bada.ap[:, l, j:j + 1], scalar2=0.0,
                                                          op0=ALU.add, op1=ALU.add),
                         reads=[ps, self.bada], pwrites=[self.MOD])
            for kc in range(8):
                s.op("dve", lambda q: q.tensor_scalar(out=self.GS.ap[:, l, kc, :], in0=self.MOD.ap[:, l, 8 + kc, :],
                                                      scalar1=1.0, scalar2=self.normg.ap[:, l, kc:kc + 1],
                                                      op0=ALU.add, op1=ALU.mult),
                     reads=[self.MOD, self.normg], pwrites=[self.GS])

        if self.run_ctx and self.stage >= 1:
            self.chain("ctx")
        if self.run_lat and self.stage >= 1:
            self.chain("lat")
        s.finish()
        return nc

    def tick_wb_hook(self):
        hooks = getattr(self, "wb_hooks", [])
        self.wb_hooks = []
        for h in hooks:
            h[0] -= 1
            if h[0] <= 0:
                h[1]()
            else:
                self.wb_hooks.append(h)

    def flush_wb_hook(self):
        hooks = getattr(self, "wb_hooks", [])
        self.wb_hooks = []
        for h in hooks:
            h[1]()

    def next_wb(self):
        wb = self.WB[self.wbi % 3]
        self.wbi += 1
        return wb

    def next_pt(self):
        b = self.PT[self.pti % 5]
        self.pti += 1
        return b

    def run_pipeline(self, items, LA=3):
        n = len(items)
        for step in range(n + LA):
            if step < n:
                items[step][0]()
            if step == min(2, n - 1):
                self.flush_pending()
            if step >= LA:
                items[step - LA][1]()

    def run_pipeline2(self, items, LA=2, DL=2):
        n = len(items)
        due = []
        for step in range(n + LA + DL + 1):
            if step < n:
                items[step][0]()
            if LA <= step < n + LA:
                it = items[step - LA]
                it[1]()
                if it[2] is not None:
                    due.append((step + DL, it[2]))
            while due and due[0][0] <= step:
                due.pop(0)[1]()

    def flush_pending(self):
        if self.pending is not None:
            p = self.pending
            self.pending = None
            p()

    def next_sq(self):
        b = self.SQ[self.sqi % 2]
        self.sqi += 1
        return b

    def next_tmp(self):
        b = self.TMP[self.tmi % 2]
        self.tmi += 1
        return b

    def rstd_of(self, chunks, reads, ones_idx, n=512):
        s = self.s
        ps = self.ps("x")
        nk = len(chunks)
        for i, ch in enumerate(chunks):
            sq = self.next_sq()
            s.op("act", lambda q: q.activation(out=sq.ap[:, 0:n], in_=ch, func=AF.Square), reads=reads, writes=[sq])
            s.op("pe", lambda q: q.matmul(ps.ap[:, 0:n], lhsT=self.onesb.ap[:, ones_idx, :], rhs=sq.ap[:, 0:n],
                                          start=(i == 0), stop=(i == nk - 1)),
                 reads=[sq, self.onesb], pwrites=[ps])
        s.op("act", lambda q: q.activation(out=self.RS.ap[:, 0:n], in_=ps.ap[:, 0:n], func=AF.Sqrt, bias=EPS, scale=1.0),
             reads=[ps], writes=[self.RS])
        s.op("dve", lambda q: q.reciprocal(out=self.RS.ap[:, 0:n], in_=self.RS.ap[:, 0:n]),
             reads=[self.RS], writes=[self.RS])
        return self.RS

    def attn_finish(self, po, h, OG, OGb, ts, n=512):
        s = self.s
        j, par = h // 2, h % 2
        base = 64 * par
        dp = 64 if par == 0 else 0
        s.op("dve", lambda q: q.reciprocal(out=self.RD.ap[dp:dp + 1, 0:n], in_=po.ap[dp:dp + 1, 0:n]),
             reads=[po], writes=[self.RD])
        hi, lo = self.SQ[0], self.SQ[1]
        s.op("dve", lambda q: q.tensor_copy(out=hi.ap[dp:dp + 1, 0:n], in_=self.RD.ap[dp:dp + 1, 0:n]),
             reads=[self.RD], writes=[hi])
        s.op("dve", lambda q: q.tensor_tensor(out=self.TMPO.ap[dp:dp + 1, 0:n], in0=self.RD.ap[dp:dp + 1, 0:n],
                                              in1=hi.ap[dp:dp + 1, 0:n], op=ALU.subtract),
             reads=[self.RD, hi], writes=[self.TMPO])
        s.op("dve", lambda q: q.tensor_copy(out=lo.ap[dp:dp + 1, 0:n], in_=self.TMPO.ap[dp:dp + 1, 0:n]),
             reads=[self.TMPO], writes=[lo])
        bc = self.ps("x")
        s.op("pe", lambda q: q.matmul(bc.ap[:, 0:n], lhsT=self.ones1b.ap[dp:dp + 1, :], rhs=hi.ap[dp:dp + 1, 0:n],
                                      start=True, stop=False),
             reads=[hi, self.ones1b], pwrites=[bc])
        s.op("pe", lambda q: q.matmul(bc.ap[:, 0:n], lhsT=self.ones1b.ap[dp:dp + 1, :], rhs=lo.ap[dp:dp + 1, 0:n],
                                      start=False, stop=True),
             reads=[lo, self.ones1b], pwrites=[bc])
        s.op("dve", lambda q: q.tensor_copy(out=self.BCS.ap[base:base + 64, 0:n], in_=bc.ap[base:base + 64, 0:n]),
             reads=[bc], writes=[self.BCS])
        s.op("dve", lambda q: q.tensor_tensor(out=self.TMPO.ap[base:base + 64, 0:n], in0=po.ap[base:base + 64, 0:n],
                                              in1=self.BCS.ap[base:base + 64, 0:n], op=ALU.mult),
             reads=[po, self.BCS], writes=[self.TMPO])
        s.op("dve", lambda q: q.tensor_tensor(out=OG[base:base + 64, j, ts], in0=self.TMPO.ap[base:base + 64, 0:n],
                                              in1=OG[base:base + 64, j, ts], op=ALU.mult),
             reads=[self.TMPO, OGb], pwrites=[OGb])

    def chain(self, mode):
        nc, s, d = self.nc, self.s, self.d
        lat = (mode == "lat")
        T = 1024 if lat else 512
        NTB = T // 512
        NT = T // 128
        col = 1 if lat else 0
        with ExitStack() as C:
            X = self.sb(C, "X", [128, 8, T], F32)
            Xb = [Buf(X) for _ in range(NTB)]
            XM = self.sb(C, "XM", [128, 8, T], BF16)
            XMb = [Buf(XM) for _ in range(NTB)]
            OG = [self.sb(C, "OG%d" % r, [128, 4, T], BF16) for r in range(3)]
            OGb = [[Buf(OG[r]) for _ in range(NTB)] for r in range(3)]
            xin = d["xl"] if lat else d["xc"]
            for tb in range(NTB):
                s.dma("sp", lambda q: q.dma_start(
                    out=X[:, :, tb * 512:(tb + 1) * 512],
                    in_=xin[:, tb * 512:(tb + 1) * 512].rearrange("(c p) n -> p c n", p=128)), writes=[Xb[tb]])
            ctxs = dict(lat=lat, T=T, NTB=NTB, NT=NT, col=col, X=X, Xb=Xb, XM=XM, XMb=XMb, OG=OG, OGb=OGb)
            if lat:
                ROPE = self.sb(C, "rope", [128, 2, 1024], F32)
                ROPEb = Buf(ROPE)
                s.dma("sp", lambda q: q.dma_start(out=ROPE[:], in_=d["ropeT"].rearrange("a p n -> p a n")), writes=[ROPEb])
                SELR = self.sb(C, "selr", [128, 8], F32)
                SELRb = Buf(SELR)
                s.dma("sp", lambda q: q.dma_start(out=SELR[:], in_=d["selr"][:, :]), writes=[SELRb])
                ctxs.update(ROPE=ROPE, ROPEb=ROPEb, SELR=SELR, SELRb=SELRb)
            for l in range(self.nlayers):
                self.layer(l, ctxs)
            yout = d["yl"] if lat else d["yc"]
            for tb in range(NTB):
                ts = slice(tb * 512, (tb + 1) * 512)
                rs = self.rstd_of([X[:, kc, ts] for kc in range(8)], [Xb[tb]], 0)
                for kc in range(8):
                    tmp = self.next_tmp()
                    s.op("dve", lambda q: q.tensor_tensor(out=tmp.ap[:], in0=X[:, kc, ts], in1=rs.ap[:], op=ALU.mult),
                         reads=[Xb[tb], rs], writes=[tmp])
                    s.op("act", lambda q: q.activation(out=tmp.ap[:], in_=tmp.ap[:], func=AF.Identity,
                                                       scale=self.fing.ap[:, kc:kc + 1]),
                         reads=[tmp, self.fing], writes=[tmp])
                    s.dma("sp", lambda q: q.dma_start(out=yout[kc * 128:(kc + 1) * 128, ts], in_=tmp.ap[:]), reads=[tmp])
            s.barrier()

    def layer(self, l, c):
        nc, s, d = self.nc, self.s, self.d
        lat, T, NTB, NT, col = c["lat"], c["T"], c["NTB"], c["NT"], c["col"]
        X, Xb, XM, XMb, OG, OGb = c["X"], c["Xb"], c["XM"], c["XMb"], c["OG"], c["OGb"]
        w_in = d["w_in"]

        def TS(tb):
            return slice(tb * 512, (tb + 1) * 512)

        for tb in range(NTB):
            ts = TS(tb)
            rs = self.rstd_of([X[:, kc, ts] for kc in range(8)], [Xb[tb]], 0)
            for kc in range(8):
                tmp = self.next_tmp()
                s.op("dve", lambda q: q.tensor_tensor(out=tmp.ap[:], in0=X[:, kc, ts], in1=rs.ap[:], op=ALU.mult),
                     reads=[Xb[tb], rs], writes=[tmp])
                s.op("act", lambda q: q.activation(out=XM[:, kc, ts], in_=tmp.ap[:], func=AF.Identity,
                                                   scale=self.GS.ap[:, l, kc, col:col + 1],
                                                   bias=self.MOD.ap[:, l, kc, col:col + 1]),
                     reads=[tmp, self.GS, self.MOD], pwrites=[XMb[tb]])

        if self.stage < 3:
            return
        A0 = ExitStack()
        A1 = ExitStack()
        QL = self.sb(A0, "QL", [128, 2, T], BF16)
        QLb = [Buf(QL) for _ in range(NTB)]
        QT = self.sb(A1, "QT", [128, 4, T], BF16)
        QTb = [Buf(QT) for _ in range(NTB)]
        KT = self.sb(A1, "KT", [128, 4, T], BF16)
        KTb = [Buf(KT) for _ in range(NTB)]
        V = self.sb(A1, "V", [128, NT, 768], BF16)
        Vb = [Buf(V) for _ in range(NT)]
        nap = None
        if lat and self.stage >= 4.1:
            nap = self.lat_na_prefetch(l, A1)
        with ExitStack() as A:
            CK = self.sb(A, "CK", [128, T], BF16)
            CKb = [Buf(CK) for _ in range(NTB)]
            KR = self.sb(A, "KR", [32, T], BF16)
            KRb = [Buf(KR) for _ in range(NTB)]
            UF = self.sb(A, "UF", [128, 4, 512], BF16)
            UFb = Buf(UF)
            QLR = self.sb(A, "QLR", [128, 3, 512], F32)
            QLRb = Buf(QLR)
            WKRP = self.sb(A, "WKRP", [128, 8, 32], BF16)
            WKRPb = Buf(WKRP)
            ABS = [self.sb(A, "ABS%d" % i, [128, 1024], BF16) for i in range(2)]
            ABSb = [Buf(ABS[i]) for i in range(2)]
            if not lat:
                ABC = self.sb(A, "ABC", [128, 4, 1024], BF16)
                ABCb = [Buf(ABC) for _ in range(4)]
                STG = [self.sb(A, "STG%d" % i, [128, 512], F32) for i in range(2)]
                STGb = [Buf(STG[i]) for i in range(2)]
                STK = self.sb(A, "STK", [128, 160], F32)
                STKb = Buf(STK)
                STS = self.sb(A, "STS", [128, 4], F32)
                STSb = Buf(STS)
            s.op("dve", lambda q: q.memset(V[:, :, :], 0.0), writes=Vb)
            for p in range(4):
                s.op("dve", lambda q: q.memset(V[:, :, p * 192 + 64:p * 192 + 65], 1.0), pwrites=Vb)
            if lat:
                s.dma("pool", lambda q: q.dma_start(out=WKRP[:], in_=d["w_krp"][l].rearrange("(c p) n -> p c n", p=128)),
                      writes=[WKRPb])

            def load_w(c0, n):
                self.tick_wb_hook()
                wb = self.next_wb()
                s.dma("pool", lambda q: q.dma_start(
                    out=wb.ap[:, :, 0:n], in_=w_in[l, :, c0:c0 + n].rearrange("(c p) n -> p c n", p=128)), writes=[wb])
                return wb

            def mm_fm(ps, M, wb, wsl, tb, extra_reads=()):
                for kc in range(8):
                    s.op("pe", lambda q: q.matmul(ps.ap[0:M, :], lhsT=wb.ap[:, kc, wsl], rhs=XM[:, kc, TS(tb)],
                                                  start=(kc == 0), stop=(kc == 7)),
                         reads=[wb, XMb[tb]], pwrites=[ps])

            def blk_qk(sel=None):
                for (c0, dst, dstb, scl) in ((C_Q, QT, QTb, NA_SCALE), (C_K, KT, KTb, 1.0)) if self.stage >= 3.1 else ():
                    if sel is not None and c0 != sel:
                        continue
                    wb = load_w(c0, 512)
                    for tb in range(NTB):
                        for j in range(4):
                            ps = self.ps("g")
                            mm_fm(ps, 128, wb, slice(j * 128, (j + 1) * 128), tb)
                            if j % 2 == 0:
                                s.op("dve", lambda q: q.tensor_scalar(out=dst[:, j, TS(tb)], in0=ps.ap[:], scalar1=scl,
                                                                      scalar2=0.0, op0=ALU.mult, op1=ALU.add),
                                     reads=[ps], pwrites=[dstb[tb]])
                            else:
                                s.op("act", lambda q: q.activation(out=dst[:, j, TS(tb)], in_=ps.ap[:], func=AF.Identity,
                                                                   scale=scl),
                                     reads=[ps], pwrites=[dstb[tb]])
                    if (not lat) and c0 == C_K:
                        for t in range(NT):
                            ps = self.ps("g")
                            for kc in range(8):
                                s.op("pe", lambda q: q.matmul(ps.ap[:], lhsT=XM[:, kc, t * 128:(t + 1) * 128],
                                                              rhs=wb.ap[:, kc, :], start=(kc == 0), stop=(kc == 7)),
                                     reads=[wb, XMb[0]], pwrites=[ps])
                            st = STGb[t % 2]
                            s.op("act", lambda q: q.activation(out=st.ap[:], in_=ps.ap[:], func=AF.Identity),
                                 reads=[ps], writes=[st])
                            s.dma("sp", lambda q: q.dma_start(out=d["onk"][l, t * 128:(t + 1) * 128, :], in_=st.ap[:]),
                                  reads=[st])
            def blk_v(sel=None):
                wb = load_w(C_V, 512)
                for t in range(NT) if self.stage >= 3.2 else ():
                    ps = self.ps("g")
                    tb = t // 4
                    for kc in range(8):
                        s.op("pe", lambda q: q.matmul(ps.ap[:], lhsT=XM[:, kc, t * 128:(t + 1) * 128], rhs=wb.ap[:, kc, :],
                                                      start=(kc == 0), stop=(kc == 7)),
                             reads=[wb, XMb[tb]], pwrites=[ps])
                    vv = V[:, t, :].rearrange("p (a b) -> p a b", b=192)
                    pv = ps.ap[:].rearrange("p (a e x) -> p a e x", e=2, x=64)
                    s.op("dve", lambda q: q.tensor_copy(out=vv[:, :, 0:64], in_=pv[:, :, 0, :]), reads=[ps], pwrites=[Vb[t]])
                    s.op("dve", lambda q: q.tensor_copy(out=vv[:, :, 128:192], in_=pv[:, :, 1, :]), reads=[ps], pwrites=[Vb[t]])
                    if not lat and not os.environ.get("KDBG_NOONV"):
                        st = STGb[t % 2]
                        s.op("dve", lambda q: q.tensor_copy(out=st.ap[:], in_=ps.ap[:]), reads=[ps], writes=[st])
                        s.dma("sp", lambda q: q.dma_start(out=d["onv"][l, t * 128:(t + 1) * 128, :], in_=st.ap[:]), reads=[st])
            def blk_gates(sel=None):
                for (c0, r) in ((C_GNA, 0), (C_GMLA, 1), (C_GFN, 2)) if self.stage >= 3.3 else ():
                    wb = load_w(c0, 512)
                    for tb in range(NTB):
                        for j in range(4):
                            ps = self.ps("g")
                            mm_fm(ps, 128, wb, slice(j * 128, (j + 1) * 128), tb)
                            s.op("act", lambda q: q.activation(out=OG[r][:, j, TS(tb)], in_=ps.ap[:], func=AF.Silu),
                                 reads=[ps], pwrites=[OGb[r][tb]])
            def blk_qlat(sel=None):
                wb = load_w(C_QLAT, 416)
                for tb in range(NTB) if self.stage >= 3.4 else ():
                    ts = TS(tb)
                    for j in range(3):
                        ps = self.ps("g")
                        mm_fm(ps, 128, wb, slice(j * 128, (j + 1) * 128), tb)
                        s.op("dve", lambda q: q.tensor_copy(out=QLR[:, j, :], in_=ps.ap[:]), reads=[ps], pwrites=[QLRb])
                    rs = self.rstd_of([QLR[:, 0, :], QLR[:, 1, :]], [QLRb], 1)
                    for j in range(2):
                        tmp = self.next_tmp()
                        s.op("dve", lambda q: q.tensor_tensor(out=tmp.ap[:], in0=QLR[:, j, :], in1=rs.ap[:], op=ALU.mult),
                             reads=[QLRb, rs], writes=[tmp])
                        s.op("act", lambda q: q.activation(out=QL[:, j, ts], in_=tmp.ap[:], func=AF.Identity,
                                                           scale=self.qng.ap[:, l, j:j + 1]),
                             reads=[tmp, self.qng], pwrites=[QLb[tb]])
                    rs = self.rstd_of([QLR[:, 2, :]], [QLRb], 2)
                    tmp = self.next_tmp()
                    s.op("dve", lambda q: q.tensor_tensor(out=tmp.ap[:], in0=QLR[:, 2, :], in1=rs.ap[:], op=ALU.mult),
                         reads=[QLRb, rs], writes=[tmp])
                    s.op("act", lambda q: q.activation(out=CK[:, ts], in_=tmp.ap[:], func=AF.Identity,
                                                       scale=self.kvng.ap[:, l:l + 1]),
                         reads=[tmp, self.kvng], pwrites=[CKb[tb]])
                    ps = self.ps("g")
                    mm_fm(ps, 32, wb, slice(384, 416), tb)
                    if lat:
                        ps2 = self.ps("g")
                        for kc in range(8):
                            s.op("pe", lambda q: q.matmul(ps2.ap[0:32, :], lhsT=WKRP[:, kc, :], rhs=XM[:, kc, ts],
                                                          start=(kc == 0), stop=(kc == 7)),
                                 reads=[WKRPb, XMb[tb]], pwrites=[ps2])
                        t1 = self.next_tmp()
                        t2 = self.next_tmp()
                        s.op("dve", lambda q: q.tensor_tensor(out=t1.ap[0:32, :], in0=ps.ap[0:32, :],
                                                              in1=c["ROPE"][0:32, 0, ts], op=ALU.mult),
                             reads=[ps, c["ROPEb"]], writes=[t1])
                        s.op("dve", lambda q: q.tensor_tensor(out=t2.ap[0:32, :], in0=ps2.ap[0:32, :],
                                                              in1=c["ROPE"][0:32, 1, ts], op=ALU.mult),
                             reads=[ps2, c["ROPEb"]], writes=[t2])
                        s.op("dve", lambda q: q.tensor_tensor(out=KR[0:32, ts], in0=t1.ap[0:32, :], in1=t2.ap[0:32, :],
                                                              op=ALU.add),
                             reads=[t1, t2], pwrites=[KRb[tb]])
                    else:
                        s.op("act", lambda q: q.activation(out=KR[0:32, ts], in_=ps.ap[0:32, :], func=AF.Identity),
                             reads=[ps], pwrites=[KRb[tb]])
                    if not lat:
                        for t in range(4):
                            ps = self.ps("g")
                            for kc in range(8):
                                s.op("pe", lambda q: q.matmul(ps.ap[:, 0:160], lhsT=XM[:, kc, t * 128:(t + 1) * 128],
                                                              rhs=wb.ap[:, kc, 256:416], start=(kc == 0), stop=(kc == 7)),
                                     reads=[wb, XMb[0]], pwrites=[ps])
                            s.op("act", lambda q: q.activation(out=STK[:, 0:128], in_=ps.ap[:, 0:128], func=AF.Square),
                                 reads=[ps], writes=[STKb])
                            s.op("dve", lambda q: q.reduce_sum(out=STS[:, 0:1], in_=STK[:, 0:128], axis=mybir.AxisListType.X),
                                 reads=[STKb], writes=[STSb])
                            s.op("act", lambda q: q.activation(out=STS[:, 1:2], in_=STS[:, 0:1], func=AF.Sqrt, bias=EPS,
                                                               scale=1.0 / 128),
                                 reads=[STSb], writes=[STSb])
                            s.op("dve", lambda q: q.reciprocal(out=STS[:, 2:3], in_=STS[:, 1:2]), reads=[STSb], writes=[STSb])
                            s.op("dve", lambda q: q.tensor_scalar(out=STK[:, 0:128], in0=ps.ap[:, 0:128],
                                                                  scalar1=STS[:, 2:3], scalar2=0.0, op0=ALU.mult, op1=ALU.add),
                                 reads=[ps, STSb], writes=[STKb])
                            s.op("dve", lambda q: q.tensor_tensor(out=STK[:, 0:128], in0=STK[:, 0:128],
                                                                  in1=self.kvnbc.ap[:, l, :], op=ALU.mult),
                                 reads=[STKb, self.kvnbc], writes=[STKb])
                            s.op("act", lambda q: q.activation(out=STK[:, 128:160], in_=ps.ap[:, 128:160], func=AF.Identity),
                                 reads=[ps], pwrites=[STKb])
                            s.dma("sp", lambda q: q.dma_start(out=d["ockv"][l, t * 128:(t + 1) * 128, :], in_=STK[:, 0:128]),
                                  reads=[STKb])
                            s.dma("sp", lambda q: q.dma_start(out=d["okr"][l, t * 128:(t + 1) * 128, :], in_=STK[:, 128:160]),
                                  reads=[STKb])
            def blk_ufn(sel=None):
                wb = load_w(C_UFN, 512)
                for tb in range(NTB) if self.stage >= 3.5 else ():
                    for j in range(4):
                        ps = self.ps("g")
                        mm_fm(ps, 128, wb, slice(j * 128, (j + 1) * 128), tb)
                        s.op("dve", lambda q: q.tensor_copy(out=UF[:, j, :], in_=ps.ap[:]), reads=[ps], pwrites=[UFb])
                    for tt in range(4):
                        t = tb * 4 + tt
                        if lat:
                            ab = ABS[t % 2]
                            abb = ABSb[t % 2]
                        else:
                            ab = ABC[:, t, :]
                            abb = ABCb[t]
                        for half in range(2):
                            ps = self.ps("g")
                            for gg in range(2):
                                g = half * 2 + gg
                                s.op("pe", lambda q: q.matmul(ps.ap[:, gg * 256:(gg + 1) * 256],
                                                              lhsT=UF[:, g, tt * 128:(tt + 1) * 128], rhs=self.cs128.ap[:],
                                                              start=True, stop=True),
                                     reads=[UFb, self.cs128], pwrites=[ps])
                            dst = ab[:, half * 512:(half + 1) * 512]
                            if half == 0:
                                s.op("dve", lambda q: q.tensor_copy(out=dst, in_=ps.ap[:]), reads=[ps], pwrites=[abb])
                            else:
                                s.op("act", lambda q: q.activation(out=dst, in_=ps.ap[:], func=AF.Identity),
                                     reads=[ps], pwrites=[abb])
                        if lat:
                            pn = "pay_ab%d" % (t // 4)
                            s.dma("sp", lambda q: q.dma_start(out=d[pn][l][(t % 4) * 128:(t % 4 + 1) * 128, :], in_=ab[:]),
                                  reads=[abb], pwrites=[self.db[pn][l]])

            def emit_cc(names):
                for nm in names:
                    pay, g = d['pay_' + nm][l], d['g_' + nm][l]
                    s.cc(lambda q: q.collective_compute('AllGather', ALU.bypass, replica_groups=GROUPS,
                                                        ins=[pay[:, :]], outs=[g[:, :]]),
                         reads=[self.db['pay_' + nm][l]], writes=[self.db['g_' + nm][l]])
            if not lat:
                blk_qk()
                blk_v()
                blk_gates()
                blk_qlat()
                blk_ufn()
            else:
                self.wb_hooks = []
                blk_qk(sel=C_K)
                blk_v()
                self.lat_pay_kv(l, c, KT, KTb, V, Vb)
                self.wb_hooks.append([2, lambda: emit_cc(('k',))])
                self.wb_hooks.append([3, lambda: emit_cc(('v',))])
                blk_qlat()
                self.lat_pay_mla(l, c, CK, CKb, KR, KRb)
                blk_ufn()
                blk_qk(sel=C_Q)
                blk_gates()
                self.flush_wb_hook()
                self.cc_late = [(lambda: emit_cc(('mla',))), (lambda: emit_cc(('ab0',))), (lambda: emit_cc(('ab1',)))]
            if self.stage >= 4:
                if not lat:
                    self.ctx_attention(l, c, A, QT, QTb, KT, KTb, V, Vb, QL, QLb, CK, CKb, KR, KRb, ABC, ABCb)
            s.barrier()
        if lat and self.stage >= 4.1:
            self.lat_na(l, c, QT, QTb, KT, KTb, V, Vb, nap)
            s.barrier()
        A1.close()
        if lat and self.stage >= 4.2:
            self.lat_mla(l, c, QL, QLb)
            s.barrier()
        if lat and self.stage >= 4.3:
            self.lat_fourier(l, c)
            s.barrier()
        A0.close()
        if self.stage < 5:
            return

        with ExitStack() as Fz:
            WO = [self.sb(Fz, "WO%d" % r, [128, 4, 1024], BF16) for r in range(3)]
            WOb = [Buf(WO[r]) for r in range(3)]
            MG = self.sb(Fz, "MG", [128, 8, T], BF16)
            MGb = [Buf(MG) for _ in range(NTB)]
            SG = [self.sb(Fz, "SG%d" % r, [128, 512], F32) for r in range(3)]
            SGb = [Buf(SG[r]) for r in range(3)]
            MT = self.sb(Fz, "MT", [128, 512], F32)
            MTb = Buf(MT)
            MT2 = self.sb(Fz, "MT2", [128, 512], F32)
            MT2b = Buf(MT2)
            for r, nm in enumerate(("w_o_na", "w_o_mla", "w_o_fn")):
                s.dma("pool", lambda q: q.dma_start(out=WO[r][:], in_=d[nm][l].rearrange("(c p) n -> p c n", p=128)),
                      writes=[WOb[r]])
            wm = w_in[l, :, C_MRG:D_IN].rearrange("(c p) (r n) -> p c r n", p=128, r=3)
            for cch in range(8):
                wb = self.next_wb()
                wv = wb.ap[:, :, 0:384].rearrange("p c (r n) -> p c r n", r=3)
                for r in range(3):
                    s.dma("pool", lambda q: q.dma_start(out=wv[:, :, r, :], in_=wm[:, :, r, cch * 128:(cch + 1) * 128]),
                          pwrites=[wb])
                for tb in range(NTB):
                    ts = TS(tb)
                    for r in range(3):
                        ps = self.ps("all")
                        for kc in range(8):
                            s.op("pe", lambda q: q.matmul(ps.ap[:], lhsT=wv[:, kc, r, :], rhs=XM[:, kc, ts],
                                                          start=(kc == 0), stop=(kc == 7)),
                                 reads=[wb, XMb[tb]], pwrites=[ps])
                        s.op("act", lambda q: q.activation(out=SG[r][:], in_=ps.ap[:], func=AF.Sigmoid),
                             reads=[ps], writes=[SGb[r]])
                    for r in range(3):
                        ps = self.ps("all")
                        for kc in range(4):
                            s.op("pe", lambda q: q.matmul(ps.ap[:], lhsT=WO[r][:, kc, cch * 128:(cch + 1) * 128],
                                                          rhs=OG[r][:, kc, ts], start=(kc == 0), stop=(kc == 3)),
                                 reads=[WOb[r], OGb[r][tb]], pwrites=[ps])
                        if r == 0:
                            s.op("dve", lambda q: q.tensor_tensor(out=MT[:], in0=ps.ap[:], in1=SG[0][:], op=ALU.mult),
                                 reads=[ps, SGb[0]], writes=[MTb])
                        else:
                            s.op("dve", lambda q: q.tensor_tensor(out=MT2[:], in0=ps.ap[:], in1=SG[r][:], op=ALU.mult),
                                 reads=[ps, SGb[r]], writes=[MT2b])
                            if r == 1:
                                s.op("dve", lambda q: q.tensor_tensor(out=MT[:], in0=MT[:], in1=MT2[:], op=ALU.add),
                                     reads=[MTb, MT2b], writes=[MTb])
                            else:
                                s.op("dve", lambda q: q.tensor_tensor(out=MG[:, cch, ts], in0=MT[:], in1=MT2[:], op=ALU.add),
                                     reads=[MTb, MT2b], pwrites=[MGb[tb]])
            for half in range(2):
                wb = self.next_wb()
                s.dma("pool", lambda q: q.dma_start(
                    out=wb.ap[:], in_=d["w_out"][l, :, half * 512:(half + 1) * 512].rearrange("(c p) n -> p c n", p=128)),
                    writes=[wb])
                for tb in range(NTB):
                    ts = TS(tb)
                    for jj in range(4):
                        cch = half * 4 + jj
                        ps = self.ps("all")
                        for kc in range(8):
                            s.op("pe", lambda q: q.matmul(ps.ap[:], lhsT=wb.ap[:, kc, jj * 128:(jj + 1) * 128],
                                                          rhs=MG[:, kc, ts], start=(kc == 0), stop=(kc == 7)),
                                 reads=[wb, MGb[tb]], pwrites=[ps])
                        s.op("dve", lambda q: q.scalar_tensor_tensor(out=X[:, cch, ts], in0=ps.ap[:],
                                                                     scalar=self.MOD.ap[:, l, 16 + cch, col:col + 1],
                                                                     in1=X[:, cch, ts], op0=ALU.mult, op1=ALU.add),
                             reads=[ps, self.MOD, Xb[tb]], pwrites=[Xb[tb]])
            s.barrier()

    def ctx_attention(self, l, c, A, QT, QTb, KT, KTb, V, Vb, QL, QLb, CK, CKb, KR, KRb, ABC, ABCb):
        nc, s, d = self.nc, self.s, self.d
        OG, OGb = c["OG"], c["OGb"]
        ts = slice(0, 512)
        items = []
        for h in range(8):
            j, par = h // 2, h % 2
            base = 64 * par
            hst = {}
            for bb in range(2):
                st = {}

                def front(h=h, j=j, par=par, base=base, bb=bb, st=st, hst=hst):
                    if bb == 0:
                        hst["po"] = self.ps("o")
                    ps = self.ps("g")
                    for kt in range(2):
                        k0 = bb * 256 + kt * 128
                        s.op("pe", lambda q: q.matmul(ps.ap[:, kt * 256:(kt + 1) * 256], lhsT=KT[base:base + 64, j, k0:k0 + 128],
                                                      rhs=QT[base:base + 64, j, bb * 256:(bb + 1) * 256], start=True, stop=True),
                             reads=[KTb[0], QTb[0]], pwrites=[ps])
                    pt = self.next_pt()
                    s.op("act", lambda q: q.activation(out=pt.ap[:], in_=ps.ap[:], func=AF.Exp), reads=[ps], writes=[pt])
                    st["pt"] = pt

                def back(h=h, j=j, par=par, bb=bb, st=st, hst=hst):
                    po, pt = hst["po"], st["pt"]
                    for kt in range(2):
                        t = bb * 2 + kt
                        if par == 0:
                            out = po.ap[0:65, bb * 256:(bb + 1) * 256]
                            lhsT = V[:, t, j * 192:j * 192 + 65]
                        else:
                            out = po.ap[0:128, bb * 256:(bb + 1) * 256]
                            lhsT = V[:, t, j * 192 + 64:j * 192 + 192]
                        s.op("pe", lambda q: q.matmul(out, lhsT=lhsT, rhs=pt.ap[:, kt * 256:(kt + 1) * 256],
                                                      start=(kt == 0), stop=(kt == 1)),
                             reads=[Vb[t], pt], pwrites=[po])
                fin = None
                if bb == 1:
                    fin = (lambda h=h, hst=hst: self.attn_finish(hst["po"], h, OG[0], OGb[0][0], ts))
                items.append((front, back, fin))
        self.run_pipeline2(items, LA=2, DL=2)

        WUQ = self.sb(A, "WUQ", [128, 2, 768], BF16)
        WUQb = Buf(WUQ)
        WK = self.sb(A, "WK", [128, 8, 96], BF16)
        WKb = Buf(WK)
        WV = self.sb(A, "WV", [128, 512], BF16)
        WVb = Buf(WV)
        VM = self.sb(A, "VM", [128, 4, 192], BF16)
        VMb = Buf(VM)
        KHs = [self.sb(A, "KH%d" % i, [96, 512], BF16) for i in range(2)]
        KHbs = [Buf(KHs[i]) for i in range(2)]
        QHs = [self.sb(A, "QH%d" % i, [96, 512], BF16) for i in range(2)]
        QHbs = [Buf(QHs[i]) for i in range(2)]
        VM2 = self.sb(A, "VM2", [128, 4, 192], BF16)
        VM2b = Buf(VM2)
        s.dma("pool", lambda q: q.dma_start(out=WUQ[:], in_=d["w_uq"][l].rearrange("(c p) n -> p c n", p=128)), writes=[WUQb])
        s.op("dve", lambda q: q.memset(WK[:], 0.0), writes=[WKb])
        wukv = d["w_ukv"][l].rearrange("c (h t x) -> c h t x", t=2, x=64)
        s.dma("pool", lambda q: q.dma_start(out=WK[:, :, 0:64], in_=wukv[:, :, 0, :]), pwrites=[WKb])
        s.dma("pool", lambda q: q.dma_start(out=WV[:].rearrange("p (h x) -> p h x", x=64), in_=wukv[:, :, 1, :]), writes=[WVb])
        VMs, VMbs = [VM, VM2], [VMb, VM2b]
        for i in range(2):
            s.op("dve", lambda q: q.memset(VMs[i][:], 0.0), writes=[VMbs[i]])
            s.op("dve", lambda q: q.memset(VMs[i][:, :, 64:65], 1.0), pwrites=[VMbs[i]])
        items = []
        for h in range(8):
            p, par = h // 2, h % 2
            vm, vmb = VMs[p % 2], VMbs[p % 2]
            KH, KHb, QH, QHb = KHs[h % 2], KHbs[h % 2], QHs[h % 2], QHbs[h % 2]
            hst = {}
            for bb in range(2):
                st = {}

                def front(h=h, p=p, par=par, bb=bb, st=st, hst=hst, vm=vm, vmb=vmb, KH=KH, KHb=KHb, QH=QH, QHb=QHb):
                    if bb == 0 and par == 0:
                        ps = self.ps("g")
                        for t in range(4):
                            s.op("pe", lambda q: q.matmul(ps.ap[:, t * 128:(t + 1) * 128], lhsT=CK[:, t * 128:(t + 1) * 128],
                                                          rhs=WV[:, p * 128:(p + 1) * 128], start=True, stop=True),
                                 reads=[CKb[0], WVb], pwrites=[ps])
                        pv = ps.ap[:].rearrange("p (t e x) -> p t e x", e=2, x=64)
                        s.op("dve", lambda q: q.tensor_copy(out=vm[:, :, 0:64], in_=pv[:, :, 0, :]), reads=[ps], pwrites=[vmb])
                        s.op("dve", lambda q: q.tensor_copy(out=vm[:, :, 128:192], in_=pv[:, :, 1, :]), reads=[ps], pwrites=[vmb])
                    if bb == 0:
                        hst["po"] = self.ps("o")
                        ps = self.ps("g")
                        s.op("pe", lambda q: q.matmul(ps.ap[0:96, :], lhsT=WK[:, h, :], rhs=CK[:, 0:512], start=True, stop=False),
                             reads=[WKb, CKb[0]], pwrites=[ps])
                        s.op("pe", lambda q: q.matmul(ps.ap[0:96, :], lhsT=self.sel32.ap[:, :], rhs=KR[0:32, 0:512],
                                                      start=False, stop=True),
                             reads=[self.sel32, KRb[0]], pwrites=[ps])
                        s.op("dve", lambda q: q.tensor_copy(out=KH[:, :], in_=ps.ap[0:96, :]), reads=[ps], writes=[KHb])
                        ps = self.ps("g")
                        for kc in range(2):
                            s.op("pe", lambda q: q.matmul(ps.ap[0:96, :], lhsT=WUQ[:, kc, h * 96:(h + 1) * 96], rhs=QL[:, kc, 0:512],
                                                          start=(kc == 0), stop=(kc == 1)),
                                 reads=[WUQb, QLb[0]], pwrites=[ps])
                        s.op("dve", lambda q: q.tensor_copy(out=QH[:, :], in_=ps.ap[0:96, :]), reads=[ps], writes=[QHb])
                    ps = self.ps("g")
                    for kt in range(2):
                        k0 = bb * 256 + kt * 128
                        s.op("pe", lambda q: q.matmul(ps.ap[:, kt * 256:(kt + 1) * 256], lhsT=KH[:, k0:k0 + 128],
                                                      rhs=QH[:, bb * 256:(bb + 1) * 256], start=True, stop=True),
                             reads=[KHb, QHb], pwrites=[ps])
                    pt = self.next_pt()
                    s.op("act", lambda q: q.activation(out=pt.ap[:], in_=ps.ap[:], func=AF.Exp, scale=MLA_SCALE),
                         reads=[ps], writes=[pt])
                    st["pt"] = pt

                def back(par=par, bb=bb, st=st, hst=hst, vm=vm, vmb=vmb):
                    po, pt = hst["po"], st["pt"]
                    for kt in range(2):
                        t = bb * 2 + kt
                        if par == 0:
                            out = po.ap[0:65, bb * 256:(bb + 1) * 256]
                            lhsT = vm[:, t, 0:65]
                        else:
                            out = po.ap[0:128, bb * 256:(bb + 1) * 256]
                            lhsT = vm[:, t, 64:192]
                        s.op("pe", lambda q: q.matmul(out, lhsT=lhsT, rhs=pt.ap[:, kt * 256:(kt + 1) * 256],
                                                      start=(kt == 0), stop=(kt == 1)),
                             reads=[vmb, pt], pwrites=[po])
                fin = None
                if bb == 1:
                    fin = (lambda h=h, hst=hst: self.attn_finish(hst["po"], h, OG[1], OGb[1][0], ts))
                items.append((front, back, fin))
        self.run_pipeline2(items, LA=2, DL=2)

        for g in range(4):
            po = self.ps("o")
            for bb in range(2):
                n = 0
                for nt in range(2):
                    t = bb * 2 + nt
                    for (off, mat) in ((0, self.c256), (128, self.ns256)):
                        s.op("pe", lambda q: q.matmul(po.ap[:, bb * 256:(bb + 1) * 256],
                                                      lhsT=ABC[:, t, g * 256 + off:g * 256 + off + 128],
                                                      rhs=mat.ap[:, nt, :], start=(n == 0), stop=(n == 3)),
                             reads=[ABCb[t], mat], pwrites=[po])
                        n += 1
            s.op("dve", lambda q: q.tensor_tensor(out=OG[2][:, g, ts], in0=po.ap[:], in1=OG[2][:, g, ts], op=ALU.mult),
                 reads=[po, OGb[2][0]], pwrites=[OGb[2][0]])


    def lat_pay_kv(self, l, c, KT, KTb, V, Vb):
        s, d, db = self.s, self.d, self.db
        pk = d["pay_k"][l].rearrange("(j p) n -> p j n", p=128)
        s.dma("sp", lambda q: q.dma_start(out=pk[:, :, 0:256], in_=KT[:, :, 0:256]), reads=[KTb[0]], pwrites=[db["pay_k"][l]])
        s.dma("sp", lambda q: q.dma_start(out=pk[:, :, 256:512], in_=KT[:, :, 768:1024]), reads=[KTb[1]], pwrites=[db["pay_k"][l]])
        pv = d["pay_v"][l].rearrange("(t p) n -> p t n", p=128)
        s.dma("sp", lambda q: q.dma_start(out=pv[:, 0:2, :], in_=V[:, 0:2, :]), reads=[Vb[0], Vb[1]], pwrites=[db["pay_v"][l]])
        s.dma("sp", lambda q: q.dma_start(out=pv[:, 2:4, :], in_=V[:, 6:8, :]), reads=[Vb[6], Vb[7]], pwrites=[db["pay_v"][l]])

    def lat_pay_mla(self, l, c, CK, CKb, KR, KRb):
        s, d, db = self.s, self.d, self.db
        s.dma("sp", lambda q: q.dma_start(out=d["pay_mla"][l][0:128, :], in_=CK[:, :]), reads=CKb, pwrites=[db["pay_mla"][l]])
        s.dma("sp", lambda q: q.dma_start(out=d["pay_mla"][l][128:160, :], in_=KR[0:32, :]), reads=KRb, pwrites=[db["pay_mla"][l]])

    def lat_na_prefetch(self, l, st):
        s, d = self.s, self.d
        KCT = self.sb(st, "KCT", [128, 4, 256], BF16)
        KCTb = Buf(KCT)
        VCX = self.sb(st, "VCX", [128, 2, 768], BF16)
        VCXb = Buf(VCX)
        BT = [self.sb(st, "BT%d" % i, [128, 2, 7, 128], BF16) for i in range(2)]
        BTb = [Buf(BT[i]) for i in range(2)]
        s.dma("pool", lambda q: q.dma_start(out=KCT[:], in_=d["cnakT"][l].rearrange("(j p) n -> p j n", p=128)), writes=[KCTb])
        s.op("dve", lambda q: q.memset(VCX[:], 0.0), writes=[VCXb])
        for p in range(4):
            s.op("dve", lambda q: q.memset(VCX[:, :, p * 192 + 64:p * 192 + 65], 1.0), pwrites=[VCXb])
        cv = d["cnav"][l].rearrange("(t p) (a e x) -> p t a e x", p=128, e=2, x=64)
        for t in range(2):
            vx = VCX[:, t, :].rearrange("p (a b) -> p a b", b=192)
            s.dma("pool", lambda q: q.dma_start(out=vx[:, :, 0:64], in_=cv[:, t, :, 0, :]), pwrites=[VCXb])
            s.dma("pool", lambda q: q.dma_start(out=vx[:, :, 128:192], in_=cv[:, t, :, 1, :]), pwrites=[VCXb])
        for p in range(2):
            s.dma("pool", lambda q: q.dma_start(out=BT[p][:], in_=d["nab"][l, :, 2 * p:2 * p + 2, :, :]), writes=[BTb[p]])
            s.op("act", lambda q: q.activation(out=BT[p][:], in_=BT[p][:], func=AF.Exp), reads=[BTb[p]], writes=[BTb[p]])
        return dict(KCT=KCT, KCTb=KCTb, VCX=VCX, VCXb=VCXb, BT=BT, BTb=BTb)

    def lat_na(self, l, c, QT, QTb, KT, KTb, V, Vb, nap):
        s, d, db = self.s, self.d, self.db
        KCT, KCTb, VCX, VCXb, BT, BTb = nap["KCT"], nap["KCTb"], nap["VCX"], nap["VCXb"], nap["BT"], nap["BTb"]
        OG, OGb = c["OG"], c["OGb"]
        SELR, SELRb = c["SELR"], c["SELRb"]
        with ExitStack() as N:
            HK = self.sb(N, "HK", [128, 4, 2, 256], BF16)
            HKb = Buf(HK)
            HV = self.sb(N, "HV", [128, 4, 768], BF16)
            HVb = Buf(HV)
            with ExitStack() as N2:
                KC = [self.sb(N2, "KC%d" % i, [128, 4, 512], BF16) for i in range(2)]
                KCb = [Buf(KC[i]) for i in range(2)]
                VC = [self.sb(N2, "VC%d" % i, [128, 4, 768], BF16) for i in range(2)]
                VCb = [Buf(VC[i]) for i in range(2)]
                gk = d["g_k"][l].rearrange("(c j p) n -> p j c n", c=4, j=4)
                for j in range(4):
                    kc, kcb = KC[j % 2], KCb[j % 2]
                    s.dma("sp", lambda q: q.dma_start(out=kc[:], in_=gk[:, j]), reads=[db["g_k"][l]], writes=[kcb])
                    for side in range(2):
                        cols = slice(256, 512) if side == 0 else slice(0, 256)
                        for cc in range(4):
                            sc = SELR[:, side * 4 + cc:side * 4 + cc + 1]
                            if cc == 0:
                                s.op("dve", lambda q: q.tensor_scalar(out=HK[:, j, side, :], in0=kc[:, cc, cols], scalar1=sc,
                                                                      scalar2=0.0, op0=ALU.mult, op1=ALU.add),
                                     reads=[kcb, SELRb], pwrites=[HKb])
                            else:
                                s.op("dve", lambda q: q.scalar_tensor_tensor(out=HK[:, j, side, :], in0=kc[:, cc, cols], scalar=sc,
                                                                             in1=HK[:, j, side, :], op0=ALU.mult, op1=ALU.add),
                                     reads=[kcb, SELRb, HKb], pwrites=[HKb])
                gv = d["g_v"][l].rearrange("(c t p) n -> p t c n", c=4, t=4)
                for ht in range(4):
                    side = ht // 2
                    src_t = (2 + ht) if side == 0 else (ht - 2)
                    vc, vcb = VC[ht % 2], VCb[ht % 2]
                    s.dma("sp", lambda q: q.dma_start(out=vc[:], in_=gv[:, src_t]), reads=[db["g_v"][l]], writes=[vcb])
                    for cc in range(4):
                        sc = SELR[:, side * 4 + cc:side * 4 + cc + 1]
                        if cc == 0:
                            s.op("dve", lambda q: q.tensor_scalar(out=HV[:, ht, :], in0=vc[:, cc, :], scalar1=sc, scalar2=0.0,
                                                                  op0=ALU.mult, op1=ALU.add),
                                 reads=[vcb, SELRb], pwrites=[HVb])
                        else:
                            s.op("dve", lambda q: q.scalar_tensor_tensor(out=HV[:, ht, :], in0=vc[:, cc, :], scalar=sc,
                                                                         in1=HV[:, ht, :], op0=ALU.mult, op1=ALU.add),
                                 reads=[vcb, SELRb, HVb], pwrites=[HVb])
                s.barrier()
            MK = self.sb(N, "MK", [128, 48, 128], BF16)
            MKb = Buf(MK)
            s.dma("sp", lambda q: q.dma_start(out=MK[:], in_=d["namask"][:, :, :]), writes=[MKb])
            all_items = []
            for p in range(4):
                bt, btb = BT[p % 2], BTb[p % 2]
                need_bt = [p >= 2]
                for h in (2 * p, 2 * p + 1):
                    par = h % 2
                    base = 64 * par
                    for grp in range(2):
                        po = self.ps("o")
                        items = []
                        for bi in range(4):
                            b = grp * 4 + bi
                            qs = slice(b * 128, (b + 1) * 128)
                            tiles = []
                            lts = [(b - 2 + jj, jj) for jj in range(5)]
                            if b == 0:
                                lts.append((3, 5))
                            if b == 7:
                                lts.append((4, 5))
                            for (lt, slot) in lts:
                                if 0 <= lt <= 7:
                                    kap = KT[base:base + 64, p, lt * 128:(lt + 1) * 128]
                                    kb_ = KTb[lt // 4]
                                    vt = V[:, lt, :]
                                    vb_ = Vb[lt]
                                elif lt < 0:
                                    kap = HK[base:base + 64, p, 0, (lt + 2) * 128:(lt + 3) * 128]
                                    kb_ = HKb
                                    vt = HV[:, lt + 2, :]
                                    vb_ = HVb
                                else:
                                    kap = HK[base:base + 64, p, 1, (lt - 8) * 128:(lt - 7) * 128]
                                    kb_ = HKb
                                    vt = HV[:, 2 + lt - 8, :]
                                    vb_ = HVb
                                tiles.append((kap, kb_, vt, vb_, lt - b + 3, b * 6 + slot))
                            for t in range(2):
                                tiles.append((KCT[base:base + 64, p, t * 128:(t + 1) * 128], KCTb, VCX[:, t, :], VCXb, None, None))
                            nt = len(tiles)
                            for g0 in range(0, nt, 4):
                                grpt = tiles[g0:g0 + 4]
                                st = {}

                                def front(grpt=grpt, st=st, qs=qs, grp=grp, base=base, p=p, par=par, bt=bt, btb=btb, need_bt=need_bt):
                                    if need_bt[0]:
                                        need_bt[0] = False
                                        s.dma("pool", lambda q: q.dma_start(out=bt[:], in_=d["nab"][l, :, 2 * p:2 * p + 2, :, :]),
                                              writes=[btb])
                                        s.op("act", lambda q: q.activation(out=bt[:], in_=bt[:], func=AF.Exp), reads=[btb], writes=[btb])
                                    ps = self.ps("g")
                                    for i, (kap, kb_, vt, vb_, dj, ms) in enumerate(grpt):
                                        reg = ps.ap[:, i * 128:(i + 1) * 128]
                                        s.op("pe", lambda q: q.matmul(reg, lhsT=kap, rhs=QT[base:base + 64, p, qs], start=True,
                                                                      stop=True),
                                             reads=[kb_, QTb[grp]], pwrites=[ps])
                                    w = len(grpt) * 128
                                    pt = self.next_pt()
                                    s.op("act", lambda q: q.activation(out=pt.ap[:, 0:w], in_=ps.ap[:, 0:w], func=AF.Exp),
                                         reads=[ps], writes=[pt])
                                    i = 0
                                    while i < len(grpt):
                                        dj, ms = grpt[i][4], grpt[i][5]
                                        if dj is None:
                                            i += 1
                                            continue
                                        n = 1
                                        while (i + n < len(grpt) and grpt[i + n][4] is not None
                                               and grpt[i + n][4] == dj + n and grpt[i + n][5] == ms + n):
                                            n += 1
                                        pv3 = pt.ap[:, i * 128:(i + n) * 128].rearrange("p (a b) -> p a b", b=128)
                                        s.op("dve", lambda q: q.tensor_tensor(out=pv3, in0=pv3, in1=bt[:, par, dj:dj + n, :], op=ALU.mult),
                                             reads=[pt, btb], writes=[pt])
                                        s.op("pool", lambda q: q.tensor_tensor(out=pv3, in0=pv3, in1=MK[:, ms:ms + n, :], op=ALU.mult),
                                             reads=[pt, MKb], writes=[pt])
                                        i += n
                                    st["pt"] = pt

                                def back(grpt=grpt, st=st, g0=g0, nt=nt, bi=bi, po=po, par=par, p=p):
                                    pt = st["pt"]
                                    for i, (kap, kb_, vt, vb_, dj, ms) in enumerate(grpt):
                                        n = g0 + i
                                        if par == 0:
                                            out = po.ap[0:65, bi * 128:(bi + 1) * 128]
                                            lhsT = vt[:, p * 192:p * 192 + 65]
                                        else:
                                            out = po.ap[0:128, bi * 128:(bi + 1) * 128]
                                            lhsT = vt[:, p * 192 + 64:p * 192 + 192]
                                        s.op("pe", lambda q: q.matmul(out, lhsT=lhsT, rhs=pt.ap[:, i * 128:(i + 1) * 128],
                                                                      start=(n == 0), stop=(n == nt - 1)),
                                             reads=[vb_, pt], pwrites=[po])
                                items.append([front, back, None])
                        items[-1][2] = (lambda po=po, h=h, grp=grp: self.attn_finish(po, h, OG[0], OGb[0][grp],
                                                                                      slice(grp * 512, (grp + 1) * 512)))
                        all_items.extend(items)
            late = getattr(self, "cc_late", None) or []
            self.cc_late = None
            npos = len(all_items)
            for i, f in enumerate(late):
                pos = (i * npos) // 4
                fr = all_items[pos][0]
                all_items[pos][0] = (lambda fr=fr, f=f: (f(), fr()))
            self.run_pipeline2(all_items, LA=4, DL=2)

    def lat_mla(self, l, c, QL, QLb):
        s, d, db = self.s, self.d, self.db
        OG, OGb = c["OG"], c["OGb"]
        ROPE, ROPEb = c["ROPE"], c["ROPEb"]
        NK = 4352
        NKT = 34
        with ExitStack() as M:
            CKA = self.sb(M, "CKA", [128, NK], BF16)
            CKAb = Buf(CKA)
            KRA = self.sb(M, "KRA", [32, NK], BF16)
            KRAb = Buf(KRA)
            KH = self.sb(M, "KH", [96, NK], BF16)
            KHb = Buf(KH)
            VM = [self.sb(M, "VM%d" % i, [128, NKT, 192], BF16) for i in range(2)]
            VMb = [Buf(VM[i]) for i in range(2)]
            WUQ = self.sb(M, "WUQ", [128, 2, 768], BF16)
            WUQb = Buf(WUQ)
            WUQP = self.sb(M, "WUQP", [128, 2, 768], BF16)
            WUQPb = Buf(WUQP)
            WK = self.sb(M, "WK", [128, 8, 96], BF16)
            WKb = Buf(WK)
            WV = self.sb(M, "WV", [128, 512], BF16)
            WVb = Buf(WV)
            QH = [self.sb(M, "QH%d" % i, [96, 1024], BF16) for i in range(2)]
            QHb = [Buf(QH[i]) for i in range(2)]
            for cc in range(4):
                s.dma("sp", lambda q: q.dma_start(out=CKA[:, cc * 1024:(cc + 1) * 1024], in_=d["g_mla"][l][cc * 160:cc * 160 + 128, :]),
                      reads=[db["g_mla"][l]], pwrites=[CKAb])
                s.dma("sp", lambda q: q.dma_start(out=KRA[0:32, cc * 1024:(cc + 1) * 1024],
                                                  in_=d["g_mla"][l][cc * 160 + 128:cc * 160 + 160, :]),
                      reads=[db["g_mla"][l]], pwrites=[KRAb])
            s.dma("pool", lambda q: q.dma_start(out=CKA[:, 4096:NK], in_=d["cckvT"][l]), pwrites=[CKAb])
            s.dma("pool", lambda q: q.dma_start(out=KRA[0:32, 4096:NK], in_=d["ckrT"][l]), pwrites=[KRAb])
            s.dma("pool", lambda q: q.dma_start(out=WUQ[:], in_=d["w_uq"][l].rearrange("(c p) n -> p c n", p=128)), writes=[WUQb])
            s.dma("pool", lambda q: q.dma_start(out=WUQP[:], in_=d["w_uqp"][l].rearrange("(c p) n -> p c n", p=128)), writes=[WUQPb])
            s.op("dve", lambda q: q.memset(WK[:], 0.0), writes=[WKb])
            wukv = d["w_ukv"][l].rearrange("c (h t x) -> c h t x", t=2, x=64)
            s.dma("pool", lambda q: q.dma_start(out=WK[:, :, 0:64], in_=wukv[:, :, 0, :]), pwrites=[WKb])
            s.dma("pool", lambda q: q.dma_start(out=WV[:].rearrange("p (h x) -> p h x", x=64), in_=wukv[:, :, 1, :]), writes=[WVb])
            for i in range(2):
                s.op("dve", lambda q: q.memset(VM[i][:], 0.0), writes=[VMb[i]])
                s.op("dve", lambda q: q.memset(VM[i][:, :, 64:65], 1.0), pwrites=[VMb[i]])
            KH2 = self.sb(M, "KH2", [96, NK], BF16)
            KHs, KHbs = [KH, KH2], [KHb, Buf(KH2)]

            def gen_v(p):
                vm, vmb = VM[p % 2], VMb[p % 2]
                for k0 in range(0, NKT, 4):
                    nt = min(4, NKT - k0)
                    ps = self.ps("g")
                    for t in range(nt):
                        kt = k0 + t
                        s.op("pe", lambda q: q.matmul(ps.ap[:, t * 128:(t + 1) * 128], lhsT=CKA[:, kt * 128:(kt + 1) * 128],
                                                      rhs=WV[:, p * 128:(p + 1) * 128], start=True, stop=True),
                             reads=[CKAb, WVb], pwrites=[ps])
                    pv = ps.ap[:, 0:nt * 128].rearrange("p (t e x) -> p t e x", e=2, x=64)
                    s.op("dve", lambda q: q.tensor_copy(out=vm[:, k0:k0 + nt, 0:64], in_=pv[:, :, 0, :]), reads=[ps], pwrites=[vmb])
                    s.op("dve", lambda q: q.tensor_copy(out=vm[:, k0:k0 + nt, 128:192], in_=pv[:, :, 1, :]), reads=[ps], pwrites=[vmb])

            def gen_kq(h):
                kh, khb = KHs[h % 2], KHbs[h % 2]
                qh, qhb = QH[h % 2], QHb[h % 2]
                for k0 in range(0, NK, 512):
                    n = min(512, NK - k0)
                    ps = self.ps("g")
                    s.op("pe", lambda q: q.matmul(ps.ap[0:96, 0:n], lhsT=WK[:, h, :], rhs=CKA[:, k0:k0 + n], start=True, stop=False),
                         reads=[WKb, CKAb], pwrites=[ps])
                    s.op("pe", lambda q: q.matmul(ps.ap[0:96, 0:n], lhsT=self.sel32.ap[:, :], rhs=KRA[0:32, k0:k0 + n],
                                                  start=False, stop=True),
                         reads=[self.sel32, KRAb], pwrites=[ps])
                    s.op("dve", lambda q: q.tensor_copy(out=kh[:, k0:k0 + n], in_=ps.ap[0:96, 0:n]), reads=[ps], pwrites=[khb])
                for tb in range(2):
                    ts = slice(tb * 512, (tb + 1) * 512)
                    ps = self.ps("g")
                    ps2 = self.ps("g")
                    for (pp, ww, wwb) in ((ps, WUQ, WUQb), (ps2, WUQP, WUQPb)):
                        for kc in range(2):
                            s.op("pe", lambda q: q.matmul(pp.ap[0:96, :], lhsT=ww[:, kc, h * 96:(h + 1) * 96], rhs=QL[:, kc, ts],
                                                          start=(kc == 0), stop=(kc == 1)),
                                 reads=[wwb, QLb[tb]], pwrites=[pp])
                    s.op("dve", lambda q: q.tensor_copy(out=qh[0:64, ts], in_=ps.ap[0:64, :]), reads=[ps], pwrites=[qhb])
                    t1 = self.next_tmp()
                    t2 = self.next_tmp()
                    s.op("dve", lambda q: q.tensor_tensor(out=t1.ap[64:96, :], in0=ps.ap[64:96, :], in1=ROPE[64:96, 0, ts], op=ALU.mult),
                         reads=[ps, ROPEb], writes=[t1])
                    s.op("dve", lambda q: q.tensor_tensor(out=t2.ap[64:96, :], in0=ps2.ap[64:96, :], in1=ROPE[64:96, 1, ts], op=ALU.mult),
                         reads=[ps2, ROPEb], writes=[t2])
                    s.op("dve", lambda q: q.tensor_tensor(out=qh[64:96, ts], in0=t1.ap[64:96, :], in1=t2.ap[64:96, :], op=ALU.add),
                         reads=[t1, t2], pwrites=[qhb])

            gen_v(0)
            gen_kq(0)
            items = []
            for h in range(8):
                p, par = h // 2, h % 2
                vm, vmb = VM[p % 2], VMb[p % 2]
                kh, khb = KHs[h % 2], KHbs[h % 2]
                qh, qhb = QH[h % 2], QHb[h % 2]
                for tb in range(2):
                    ts = slice(tb * 512, (tb + 1) * 512)
                    po = self.ps("o")
                    for kt in range(NKT):
                        st = {}
                        pre = None
                        if par == 1 and tb == 0 and kt == 0 and p + 1 < 4:
                            pre = (lambda p=p: gen_v(p + 1))
                        if tb == 1 and kt == NKT - 12 and h + 1 < 8:
                            pre = (lambda h=h: gen_kq(h + 1))

                        def front(kt=kt, st=st, ts=ts, kh=kh, khb=khb, qh=qh, qhb=qhb, pre=pre):
                            if pre is not None:
                                pre()
                            ps = self.ps("g")
                            s.op("pe", lambda q: q.matmul(ps.ap[:], lhsT=kh[:, kt * 128:(kt + 1) * 128], rhs=qh[:, ts],
                                                          start=True, stop=True),
                                 reads=[khb, qhb], writes=[ps])
                            pt = self.next_pt()
                            s.op("act", lambda q: q.activation(out=pt.ap[:], in_=ps.ap[:], func=AF.Exp, scale=MLA_SCALE),
                                 reads=[ps], writes=[pt])
                            st["pt"] = pt

                        def back(kt=kt, st=st, po=po, par=par, vm=vm, vmb=vmb):
                            pt = st["pt"]
                            if par == 0:
                                out = po.ap[0:65, :]
                                lhsT = vm[:, kt, 0:65]
                            else:
                                out = po.ap[0:128, :]
                                lhsT = vm[:, kt, 64:192]
                            s.op("pe", lambda q: q.matmul(out, lhsT=lhsT, rhs=pt.ap[:], start=(kt == 0), stop=(kt == NKT - 1)),
                                 reads=[vmb, pt], pwrites=[po])
                        fin = None
                        if kt == NKT - 1:
                            fin = (lambda po=po, h=h, tb=tb, ts=ts: self.attn_finish(po, h, OG[1], OGb[1][tb], ts))
                        items.append((front, back, fin))
            self.run_pipeline2(items, LA=3, DL=2)


    def lat_fourier(self, l, c):
        s, d, db = self.s, self.d, self.db
        OG, OGb = c["OG"], c["OGb"]
        with ExitStack() as Fs:
            DC = [self.sb(Fs, "DC%d" % i, [128, 4, 512], BF16) for i in range(2)]
            DCb = [Buf(DC[i]) for i in range(2)]
            DS = [self.sb(Fs, "DS%d" % i, [128, 4, 512], BF16) for i in range(2)]
            DSb = [Buf(DS[i]) for i in range(2)]
            AB = [self.sb(Fs, "AB%d" % i, [128, 4, 1024], BF16) for i in range(2)]
            ABb = [Buf(AB[i]) for i in range(2)]
            it = 0
            for tb in range(2):
                ts = slice(tb * 512, (tb + 1) * 512)
                acc = [self.PS[4 + g] for g in range(4)]
                for nb in range(8):
                    dc, dcb, ds_, dsb, ab, abb = DC[it % 2], DCb[it % 2], DS[it % 2], DSb[it % 2], AB[it % 2], ABb[it % 2]
                    it += 1
                    rows = slice(nb * 512, (nb + 1) * 512)
                    s.dma("sp", lambda q: q.dma_start(out=dc[:], in_=d["dftc"][rows, ts].rearrange("(i p) n -> p i n", p=128)), writes=[dcb])
                    s.dma("sp", lambda q: q.dma_start(out=ds_[:], in_=d["dftns"][rows, ts].rearrange("(i p) n -> p i n", p=128)), writes=[dsb])
                    gn = "g_ab%d" % (nb % 2)
                    grow = slice((nb // 2) * 512, (nb // 2 + 1) * 512)
                    s.dma("sp", lambda q: q.dma_start(out=ab[:], in_=d[gn][l][grow, :].rearrange("(i p) n -> p i n", p=128)),
                          reads=[db[gn][l]], writes=[abb])
                    for i in range(4):
                        for g in range(4):
                            first = (nb == 0 and i == 0)
                            last = (nb == 7 and i == 3)
                            s.op("pe", lambda q: q.matmul(acc[g].ap[:], lhsT=ab[:, i, g * 256:g * 256 + 128], rhs=dc[:, i, :],
                                                          start=first, stop=False),
                                 reads=[abb, dcb], pwrites=[acc[g]])
                            s.op("pe", lambda q: q.matmul(acc[g].ap[:], lhsT=ab[:, i, g * 256 + 128:g * 256 + 256], rhs=ds_[:, i, :],
                                                          start=False, stop=last),
                                 reads=[abb, dsb], pwrites=[acc[g]])
                for g in range(4):
                    s.op("dve", lambda q: q.tensor_tensor(out=OG[2][:, g, ts], in0=acc[g].ap[:], in1=OG[2][:, g, ts], op=ALU.mult),
                         reads=[acc[g], OGb[2][tb]], pwrites=[OGb[2][tb]])


def _rope_perm():
    P = np.array([i + 8 if (i % 16) < 8 else i - 8 for i in range(32)])
    sgn = np.array([-1.0 if (i % 16) < 8 else 1.0 for i in range(32)], np.float32)
    return P, sgn


def _consts():
    bf = ml_dtypes.bfloat16
    c = {}
    c["identb"] = np.eye(128, dtype=np.float32).astype(bf)
    n = np.arange(128)
    ang = 2 * np.pi * np.outer(n, n) / 128.0
    c["cs128"] = (np.concatenate([np.cos(ang), np.sin(ang)], axis=1) / np.sqrt(128.0)).astype(np.float32).astype(bf)
    n = np.arange(256)
    ang = 2 * np.pi * np.outer(n, n) / 256.0
    c["c256"] = (np.cos(ang) / 16.0).astype(np.float32).astype(bf)
    c["ns256"] = (-np.sin(ang) / 16.0).astype(np.float32).astype(bf)
    sel = np.zeros((32, 96), np.float32)
    sel[np.arange(32), 64 + np.arange(32)] = 1.0
    c["sel32"] = sel.astype(bf)
    return c


def _lat_consts(q):
    bf = ml_dtypes.bfloat16
    c = {}
    P, sgn = _rope_perm()
    t = np.arange(1024 * q, 1024 * q + 1024)
    row = (t // 64).astype(np.float32)
    colp = (t % 64).astype(np.float32)
    half = 16
    inv = (10000.0 ** (-np.arange(0, half, 2, dtype=np.float32) / half)).astype(np.float32)
    ar = row[:, None] * inv[None, :]
    ac = colp[:, None] * inv[None, :]
    ang = np.concatenate([ar, ar, ac, ac], axis=-1)
    cos = np.cos(ang).astype(np.float32)
    sin = (np.sin(ang).astype(np.float32) * sgn[None, :]).astype(np.float32)
    rope = np.zeros((2, 128, 1024), np.float32)
    rope[0, 0:32] = cos.T
    rope[0, 64:96] = cos.T
    rope[1, 0:32] = sin.T
    rope[1, 64:96] = sin.T
    c["ropeT"] = rope
    n = np.arange(4096, dtype=np.float64)
    k = np.arange(1024 * q, 1024 * q + 1024, dtype=np.float64)
    ang = 2 * np.pi * ((np.outer(n, k)) % 4096) / 4096.0
    c["dftc"] = (np.cos(ang) / 64.0).astype(np.float32).astype(bf)
    c["dftns"] = (-np.sin(ang) / 64.0).astype(np.float32).astype(bf)
    kk = np.arange(128)
    kr, kcol = kk // 64, kk % 64
    mask = np.zeros((128, 48, 128), np.float32)
    for b in range(8):
        for slot in range(6):
            if slot < 5:
                lt = b - 2 + slot
            elif b == 0:
                lt = 3
            elif b == 7:
                lt = 4
            else:
                continue
            krow = 16 * q + 2 * lt + kr
            r = 16 * q + 2 * b + kr
            rs = np.clip(r - 4, 0, 56)
            row_ok = ((krow[:, None] >= 0) & (krow[:, None] < 64) &
                      (krow[:, None] >= rs[None, :]) & (krow[:, None] < rs[None, :] + 8))
            cs = np.clip(kcol - 8, 0, 48)
            col_ok = (kcol[:, None] >= cs[None, :]) & (kcol[:, None] < cs[None, :] + 16)
            mask[:, b * 6 + slot, :] = np.where(row_ok & col_ok, 1.0, 0.0)
    c["namask"] = mask.astype(bf)
    sel = np.zeros((128, 8), np.float32)
    if q - 1 >= 0:
        sel[:, q - 1] = 1.0
    if q + 1 <= 3:
        sel[:, 4 + q + 1] = 1.0
    c["selr"] = sel
    return c


_NC_CACHE = {}


def _get_nc(key=("full",)):
    if key not in _NC_CACHE:
        if key[0] == "full":
            kb = KB(run_ctx=True, run_lat=True)
        else:
            kb = KB(**dict(key[1]))
        _NC_CACHE[key] = kb.build()
    return _NC_CACHE[key]


def make_in_maps(inp):
    f = lambda a: np.ascontiguousarray(np.asarray(a, dtype=np.float32))
    x_prompt, x_sample = f(inp["x_prompt"]), f(inp["x_sample"])
    P, sgn = _rope_perm()
    w_in = f(inp["w_in"])
    w_uq = f(inp["w_uq"])
    shared = {}
    shared["w_ada"] = f(inp["w_ada"])
    shared["b_adaT"] = f(f(inp["b_ada"]).reshape(L, 24, 128).transpose(2, 0, 1))
    shared["norm_gT"] = f(f(inp["norm_g"]).reshape(L, 8, 128).transpose(2, 0, 1))
    shared["fin_gT"] = f(f(inp["final_norm_g"]).reshape(8, 128).T)
    shared["qn_gT"] = f(f(inp["q_norm_g"]).reshape(L, 2, 128).transpose(2, 0, 1))
    shared["kvn_gT"] = f(f(inp["kv_norm_g"]).T)
    shared["kvn_bc"] = f(np.broadcast_to(f(inp["kv_norm_g"])[None, :, :], (128, L, 128)))
    shared["w_in"] = w_in
    shared["w_krp"] = f(w_in[:, :, C_KR:C_KR + 32][:, :, P])
    shared["w_uq"] = w_uq
    wq = w_uq.reshape(L, 256, 8, 96).copy()
    wq[:, :, :, 64:96] = wq[:, :, :, 64:96][:, :, :, P]
    shared["w_uqp"] = f(wq.reshape(L, 256, 768))
    shared["w_ukv"] = f(inp["w_ukv"])
    shared["w_o_na"] = f(inp["w_o_na"])
    shared["w_o_mla"] = f(inp["w_o_mla"])
    shared["w_o_fn"] = f(inp["w_o_fourier"])
    shared["w_out"] = f(inp["w_out"])
    shared.update(_consts())
    nb = f(inp["na_bias"])
    kr = np.arange(128) // 64
    kc = np.arange(128) % 64
    dj = np.arange(7)
    dr = 2 * (dj[None, :, None] - 3) + kr[:, None, None] - kr[None, None, :]
    dc = np.clip(kc[:, None, None] - kc[None, None, :] + 15, 0, 30) + 0 * dr
    drc = np.clip(dr + 7, 0, 14)
    nab = nb[:, :, drc, dc]
    shared["nab"] = f(nab.transpose(0, 2, 1, 3, 4))
    cna_k, cna_v = f(inp["cache_na_k"]), f(inp["cache_na_v"])
    cckv, ckr = f(inp["cache_mla_ckv"]), f(inp["cache_mla_krope"])
    cvec, c_ctx = f(inp["c"]), f(inp["c_ctx"])
    maps = []
    latc = [_lat_consts(q) for q in range(4)]
    for core in range(8):
        b, q = core // 4, core % 4
        m = dict(shared)
        m["xc"] = f(x_prompt[2 * core:2 * core + 2].reshape(512, D).T)
        m["xl"] = f(x_sample[b, 1024 * q:1024 * q + 1024].T)
        m["cond"] = f(np.stack([c_ctx, cvec[b]], axis=1))
        m.update(latc[q])
        m["cnakT"] = f(cna_k[b].reshape(L, 256, 512).transpose(0, 2, 1))
        m["cnav"] = f(cna_v[b].reshape(L, 256, 512))
        m["cckvT"] = f(cckv[b].transpose(0, 2, 1))
        m["ckrT"] = f(ckr[b].transpose(0, 2, 1))
        maps.append(m)
    return maps


def assemble(res):
    r = res.results
    y_prompt = np.stack([r[c]["yc"].T.reshape(2, 256, D) for c in range(8)]).reshape(16, 256, D)
    y_sample = np.stack([np.concatenate([r[4 * b + q]["yl"].T for q in range(4)], axis=0) for b in range(2)])

    def cache(name, w):
        return np.concatenate([r[c][name].reshape(L, 2, 256, w).transpose(1, 0, 2, 3) for c in range(8)], axis=0)
    nk = cache("onk", 512).reshape(16, L, 256, 8, 64)
    nv = cache("onv", 512).reshape(16, L, 256, 8, 64)
    nckv = cache("ockv", 128)
    nkr = cache("okr", 32)
    return tuple(np.ascontiguousarray(a.astype(np.float32)) for a in (y_prompt, y_sample, nk, nv, nckv, nkr))


def kernel(**inputs):
    nc = _get_nc()
    in_maps = make_in_maps(inputs)
    res = run_bass_kernel_spmd(nc, in_maps, core_ids=list(range(8)))
    return assemble(res)
```

```python
import os
import numpy as np
import ml_dtypes
from contextlib import ExitStack
import concourse.bass as bass
import concourse.mybir as mybir
from concourse.bass_utils import run_bass_kernel_spmd

F32 = mybir.dt.float32
BF16 = mybir.dt.bfloat16
AF = mybir.ActivationFunctionType
ALU = mybir.AluOpType

L = 4
D = 1024
D_IN = 7072
NA_SCALE = 64 ** -0.5
MLA_SCALE = 96 ** -0.5
EPS = 1e-6
C_Q, C_K, C_V, C_GNA, C_QLAT, C_CKV, C_KR, C_GMLA, C_UFN, C_GFN, C_MRG = (
    0, 512, 1024, 1536, 2048, 2304, 2432, 2464, 2976, 3488, 4000)
NEG = -30000.0
GROUPS = [[0, 1, 2, 3], [4, 5, 6, 7]]


class Buf:
    __slots__ = ("ap", "w", "r", "pm", "name", "fw", "excl")

    def __init__(self, ap, name=""):
        self.ap = ap
        self.w = {}
        self.r = {}
        self.pm = False
        self.fw = {}
        self.excl = False
        self.name = name


class Sched:
    LIMIT = 12000
    NLANES = 8

    def __init__(self, nc, es):
        self.nc = nc
        self.es = es
        self.eng = dict(pe=nc.tensor, act=nc.scalar, dve=nc.vector, pool=nc.gpsimd, sp=nc.sync)
        self.sems = []
        self.cur = {}
        self.known = {e: {} for e in self.eng}
        self.lanes = {e: [] for e in self.eng}
        self.rr = {e: 0 for e in self.eng}
        self.pe_sems = set()
        self.nins = 0

    def new_sem(self):
        h = self.es.enter_context(self.nc.semaphore("sm%d" % len(self.sems)))
        self.sems.append(h)
        return len(self.sems) - 1

    def _deps(self, reads, writes, pwrites):
        deps = {}

        def add(d):
            for k, v in d.items():
                if deps.get(k, 0) < v:
                    deps[k] = v
        for b in reads:
            add(b.w)
            if b.excl:
                add(b.r)
        for b in writes:
            add(b.w)
            add(b.r)
        for b in pwrites:
            add(b.r)
            add(b.fw)
            if not b.pm:
                add(b.w)
        return deps

    def _wait(self, e, deps):
        kn = self.known[e]
        for sem, val in deps.items():
            if e == "pe" and sem in self.pe_sems:
                continue
            if kn.get(sem, 0) >= val:
                continue
            self.eng[e].wait_ge(self.sems[sem], val)
            kn[sem] = val
            self.nins += 1

    def _mark(self, stamp, reads, writes, pwrites):
        s, v = stamp
        for b in writes:
            b.w = {s: v}
            b.fw = {s: v}
            b.r = {}
            b.pm = False
        for b in pwrites:
            b.w[s] = v
            b.pm = True
        for b in reads:
            b.r[s] = v

    def op(self, e, fn, reads=(), writes=(), pwrites=()):
        self._wait(e, self._deps(reads, writes, pwrites))
        c = self.cur.get(e)
        if c is None or c[1] >= self.LIMIT:
            c = [self.new_sem(), 0]
            self.cur[e] = c
            if e == "pe":
                self.pe_sems.add(c[0])
        c[1] += 1
        ins = fn(self.eng[e])
        ins.then_inc(self.sems[c[0]], 1)
        self.nins += 1
        self._mark((c[0], c[1]), reads, writes, pwrites)

    def dma(self, e, fn, reads=(), writes=(), pwrites=(), inc=16):
        deps = self._deps(reads, writes, pwrites)
        lanes = self.lanes[e]
        if len(lanes) < self.NLANES:
            lanes.append([self.new_sem(), 0])
            lane = lanes[-1]
        else:
            lane = lanes[self.rr[e] % self.NLANES]
            self.rr[e] += 1
        if lane[1] > 0 and deps.get(lane[0], 0) < lane[1]:
            deps[lane[0]] = lane[1]
        self._wait(e, deps)
        lane[1] += inc
        ins = fn(self.eng[e])
        ins.then_inc(self.sems[lane[0]], inc)
        self.nins += 1
        self._mark((lane[0], lane[1]), reads, writes, pwrites)

    def cc(self, fn, reads=(), writes=()):
        deps = self._deps(reads, writes, ())
        if not hasattr(self, "cclane"):
            self.cclane = [self.new_sem(), 0]
        lane = self.cclane
        if lane[1] > 0 and deps.get(lane[0], 0) < lane[1]:
            deps[lane[0]] = lane[1]
        self._wait("pool", deps)
        lane[1] += 1
        ins = fn(self.eng["pool"])
        ins.then_inc(self.sems[lane[0]], 1)
        self.nins += 1
        self._mark((lane[0], lane[1]), reads, writes, ())

    def barrier(self):
        allst = {}
        for e, c in self.cur.items():
            allst[c[0]] = c[1]
        for e, lanes in self.lanes.items():
            for ln in lanes:
                if ln[1] > 0:
                    allst[ln[0]] = ln[1]
        for e in self.eng:
            d = dict(allst)
            c = self.cur.get(e)
            if e == "pe" and c is not None:
                d.pop(c[0], None)
            self._wait(e, d)

    def finish(self):
        allst = {}
        for e, lanes in self.lanes.items():
            for ln in lanes:
                if ln[1] > 0:
                    allst[ln[0]] = ln[1]
        for e, c in self.cur.items():
            allst[c[0]] = c[1]
        if hasattr(self, "cclane") and self.cclane[1] > 0:
            allst[self.cclane[0]] = self.cclane[1]
        d = dict(allst)
        self._wait("sp", d)


class KB:
    def __init__(self, run_ctx=True, run_lat=True, nlayers=L, dbg=False, lw=L, stage=99):
        self.LW = lw
        self.stage = stage
        self.run_ctx = run_ctx
        self.run_lat = run_lat
        self.nlayers = nlayers
        self.nc = bass.Bass("TRN2", target_bir_lowering=False)
        self.es = ExitStack()
        self.s = Sched(self.nc, self.es)
        self.tcount = 0

    def din(self, name, shape, dt=F32):
        return self.nc.dram_tensor(name, list(shape), dt, kind="ExternalInput").ap()

    def dout(self, name, shape, dt=F32):
        return self.nc.dram_tensor(name, list(shape), dt, kind="ExternalOutput").ap()

    def dint(self, name, shape, dt=BF16):
        return self.nc.dram_tensor(name, list(shape), dt).ap()

    def sb(self, st, name, shape, dt):
        self.tcount += 1
        return st.enter_context(self.nc.sbuf_tensor("%s_%d" % (name, self.tcount), list(shape), dt))

    def ps(self, grp):
        idxs = self.psgrp[grp]
        i = idxs[self.psrr[grp] % len(idxs)]
        self.psrr[grp] += 1
        return self.PS[i]

    def build(self):
        nc, s, es = self.nc, self.s, self.es
        NL = self.nlayers
        d = {}
        d["xc"] = self.din("xc", [D, 512])
        d["xl"] = self.din("xl", [D, 1024])
        d["cond"] = self.din("cond", [D, 2])
        d["w_ada"] = self.din("w_ada", [self.LW, D, 3 * D])
        d["b_adaT"] = self.din("b_adaT", [128, L, 24])
        d["norm_gT"] = self.din("norm_gT", [128, L, 8])
        d["fin_gT"] = self.din("fin_gT", [128, 8])
        d["qn_gT"] = self.din("qn_gT", [128, L, 2])
        d["kvn_gT"] = self.din("kvn_gT", [128, L])
        d["kvn_bc"] = self.din("kvn_bc", [128, L, 128])
        d["w_in"] = self.din("w_in", [self.LW, D, D_IN])
        d["w_krp"] = self.din("w_krp", [self.LW, D, 32])
        d["w_uq"] = self.din("w_uq", [self.LW, 256, 768])
        d["w_uqp"] = self.din("w_uqp", [self.LW, 256, 768])
        d["w_ukv"] = self.din("w_ukv", [self.LW, 128, 1024])
        d["w_o_na"] = self.din("w_o_na", [self.LW, 512, D])
        d["w_o_mla"] = self.din("w_o_mla", [self.LW, 512, D])
        d["w_o_fn"] = self.din("w_o_fn", [self.LW, 512, D])
        d["w_out"] = self.din("w_out", [self.LW, D, D])
        d["identb"] = self.din("identb", [128, 128], BF16)
        d["cs128"] = self.din("cs128", [128, 256], BF16)
        d["c256"] = self.din("c256", [256, 256], BF16)
        d["ns256"] = self.din("ns256", [256, 256], BF16)
        d["sel32"] = self.din("sel32", [32, 96], BF16)
        d["ropeT"] = self.din("ropeT", [2, 128, 1024])
        d["dftc"] = self.din("dftc", [4096, 1024], BF16)
        d["dftns"] = self.din("dftns", [4096, 1024], BF16)
        d["nab"] = self.din("nab", [self.LW, 128, 8, 7, 128])
        d["namask"] = self.din("namask", [128, 48, 128], BF16)
        d["selr"] = self.din("selr", [128, 8])
        d["cnakT"] = self.din("cnakT", [L, 512, 256])
        d["cnav"] = self.din("cnav", [L, 256, 512])
        d["cckvT"] = self.din("cckvT", [L, 128, 256])
        d["ckrT"] = self.din("ckrT", [L, 32, 256])
        d["yc"] = self.dout("yc", [D, 512])
        d["yl"] = self.dout("yl", [D, 1024])
        d["onk"] = self.dout("onk", [L, 512, 512])
        d["onv"] = self.dout("onv", [L, 512, 512])
        d["ockv"] = self.dout("ockv", [L, 512, 128])
        d["okr"] = self.dout("okr", [L, 512, 32])
        d["pay_mla"] = [self.dint("pay_mla%d" % l, [160, 1024]) for l in range(L)]
        d["g_mla"] = [self.dint("g_mla%d" % l, [640, 1024]) for l in range(L)]
        d["pay_k"] = [self.dint("pay_k%d" % l, [512, 512]) for l in range(L)]
        d["g_k"] = [self.dint("g_k%d" % l, [2048, 512]) for l in range(L)]
        d["pay_v"] = [self.dint("pay_v%d" % l, [512, 768]) for l in range(L)]
        d["g_v"] = [self.dint("g_v%d" % l, [2048, 768]) for l in range(L)]
        d["pay_ab0"] = [self.dint("pay_ab0_%d" % l, [512, 1024]) for l in range(L)]
        d["pay_ab1"] = [self.dint("pay_ab1_%d" % l, [512, 1024]) for l in range(L)]
        d["g_ab0"] = [self.dint("g_ab0_%d" % l, [2048, 1024]) for l in range(L)]
        d["g_ab1"] = [self.dint("g_ab1_%d" % l, [2048, 1024]) for l in range(L)]
        self.d = d
        self.db = {k: [Buf(a) for a in d[k]] for k in ("pay_mla", "g_mla", "pay_k", "g_k", "pay_v", "g_v", "pay_ab0", "pay_ab1", "g_ab0", "g_ab1")}

        self.PS = [Buf(es.enter_context(nc.psum_tensor("ps%d" % i, [128, 512], F32)), "ps%d" % i) for i in range(8)]
        for b in self.PS:
            b.excl = True
        self.psgrp = {"g": [0, 1, 2, 3], "o": [4, 5], "x": [6, 7], "all": list(range(8))}
        self.psrr = {k: 0 for k in self.psgrp}

        P = es
        self.identb = Buf(self.sb(P, "identb", [128, 128], BF16))
        self.onesb = Buf(self.sb(P, "onesb", [128, 3, 128], BF16))
        self.ones1b = Buf(self.sb(P, "ones1b", [128, 128], BF16))
        self.cs128 = Buf(self.sb(P, "cs128", [128, 256], BF16))
        self.c256 = Buf(self.sb(P, "c256", [128, 2, 256], BF16))
        self.ns256 = Buf(self.sb(P, "ns256", [128, 2, 256], BF16))
        self.sel32 = Buf(self.sb(P, "sel32", [32, 96], BF16))
        self.condt = Buf(self.sb(P, "condt", [128, 8, 2], F32))
        self.condb = Buf(self.sb(P, "condb", [128, 8, 2], BF16))
        self.bada = Buf(self.sb(P, "bada", [128, L, 24], F32))
        self.normg = Buf(self.sb(P, "normg", [128, L, 8], F32))
        self.fing = Buf(self.sb(P, "fing", [128, 8], F32))
        self.qng = Buf(self.sb(P, "qng", [128, L, 2], F32))
        self.kvng = Buf(self.sb(P, "kvng", [128, L], F32))
        self.kvnbc = Buf(self.sb(P, "kvnbc", [128, L, 128], F32))
        self.MOD = Buf(self.sb(P, "mod", [128, L, 24, 2], F32))
        self.GS = Buf(self.sb(P, "gs", [128, L, 8, 2], F32))
        self.WB = [Buf(self.sb(P, "wb%d" % i, [128, 8, 512], BF16)) for i in range(3)]
        self.wbi = 0
        self.SQ = [Buf(self.sb(P, "sq%d" % i, [128, 512], BF16)) for i in range(2)]
        self.TMP = [Buf(self.sb(P, "tmp%d" % i, [128, 512], F32)) for i in range(2)]
        self.RS = Buf(self.sb(P, "rs", [128, 512], F32))
        self.RD = Buf(self.sb(P, "rd", [128, 512], F32))
        self.BCS = Buf(self.sb(P, "bcs", [128, 512], F32))
        self.TMPO = Buf(self.sb(P, "tmpo", [128, 512], F32))
        self.PT = [Buf(self.sb(P, "pt%d" % i, [128, 512], BF16)) for i in range(5)]
        self.pending = None
        self.pti = 0
        self.sqi = 0
        self.tmi = 0

        def ld(buf, src, e="sp"):
            s.dma(e, lambda q: q.dma_start(out=buf.ap[:], in_=src), writes=[buf])
        ld(self.identb, d["identb"][:, :])
        ld(self.cs128, d["cs128"][:, :])
        ld(self.c256, d["c256"].rearrange("(t p) n -> p t n", p=128))
        ld(self.ns256, d["ns256"].rearrange("(t p) n -> p t n", p=128))
        ld(self.sel32, d["sel32"][:, :])
        ld(self.condt, d["cond"].rearrange("(c p) n -> p c n", p=128))
        ld(self.bada, d["b_adaT"][:, :, :])
        ld(self.normg, d["norm_gT"][:, :, :])
        ld(self.fing, d["fin_gT"][:, :])
        ld(self.qng, d["qn_gT"][:, :, :])
        ld(self.kvng, d["kvn_gT"][:, :])
        ld(self.kvnbc, d["kvn_bc"][:, :, :])
        for i, v in enumerate((1.0 / 1024, 1.0 / 256, 1.0 / 128)):
            s.op("dve", lambda q: q.memset(self.onesb.ap[:, i, :], v), pwrites=[self.onesb])
        s.op("dve", lambda q: q.memset(self.ones1b.ap[:], 1.0), writes=[self.ones1b])
        s.op("act", lambda q: q.activation(out=self.condb.ap[:], in_=self.condt.ap[:], func=AF.Silu),
             reads=[self.condt], writes=[self.condb])

        for l in range(NL):
            for jb in range(6):
                wb = self.next_wb()
                s.dma("pool", lambda q: q.dma_start(
                    out=wb.ap[:], in_=d["w_ada"][l, :, jb * 512:(jb + 1) * 512].rearrange("(c p) n -> p c n", p=128)),
                    writes=[wb])
                for jj in range(4):
                    j = jb * 4 + jj
                    ps = self.ps("g")
                    for kc in range(8):
                        s.op("pe", lambda q: q.matmul(ps.ap[:, 0:2], lhsT=wb.ap[:, kc, jj * 128:(jj + 1) * 128],
                                                      rhs=self.condb.ap[:, kc, :], start=(kc == 0), stop=(kc == 7)),
                             reads=[wb, self.condb], pwrites=[ps])
                    s.op("dve", lambda q: q.tensor_scalar(out=self.MOD.ap[:, l, j, :], in0=ps.ap[:, 0:2],
                                                          scalar1=self.bada.ap[:, l, j:j + 1], scalar2=0.0,
                                                          op0=ALU.add, op1=ALU.add),
                         reads=[ps, self.bada], pwrites=[self.MOD])
            for kc in range(8):
                s.op("dve", lambda q: q.tensor_scalar(out=self.GS.ap[:, l, kc, :], in0=self.MOD.ap[:, l, 8 + kc, :],
                                                      scalar1=1.0, scalar2=self.normg.ap[:, l, kc:kc + 1],
                                                      op0=ALU.add, op1=ALU.mult),
                     reads=[self.MOD, self.normg], pwrites=[self.GS])

        if self.run_ctx and self.stage >= 1:
            self.chain("ctx")
        if self.run_lat and self.stage >= 1:
            self.chain("lat")
        s.finish()
        return nc

    def tick_wb_hook(self):
        hooks = getattr(self, "wb_hooks", [])
        self.wb_hooks = []
        for h in hooks:
            h[0] -= 1
            if h[0] <= 0:
                h[1]()
            else:
                self.wb_hooks.append(h)

    def flush_wb_hook(self):
        hooks = getattr(self, "wb_hooks", [])
        self.wb_hooks = []
        for h in hooks:
            h[1]()

    def next_wb(self):
        wb = self.WB[self.wbi % 3]
        self.wbi += 1
        return wb

    def next_pt(self):
        b = self.PT[self.pti % 5]
        self.pti += 1
        return b

    def run_pipeline(self, items, LA=3):
        n = len(items)
        for step in range(n + LA):
            if step < n:
                items[step][0]()
            if step == min(2, n - 1):
                self.flush_pending()
            if step >= LA:
                items[step - LA][1]()

    def run_pipeline2(self, items, LA=2, DL=2):
        n = len(items)
        due = []
        for step in range(n + LA + DL + 1):
            if step < n:
                items[step][0]()
            if LA <= step < n + LA:
                it = items[step - LA]
                it[1]()
                if it[2] is not None:
                    due.append((step + DL, it[2]))
            while due and due[0][0] <= step:
                due.pop(0)[1]()

    def flush_pending(self):
        if self.pending is not None:
            p = self.pending
            self.pending = None
            p()

    def next_sq(self):
        b = self.SQ[self.sqi % 2]
        self.sqi += 1
        return b

    def next_tmp(self):
        b = self.TMP[self.tmi % 2]
        self.tmi += 1
        return b

    def rstd_of(self, chunks, reads, ones_idx, n=512):
        s = self.s
        ps = self.ps("x")
        nk = len(chunks)
        for i, ch in enumerate(chunks):
            sq = self.next_sq()
            s.op("act", lambda q: q.activation(out=sq.ap[:, 0:n], in_=ch, func=AF.Square), reads=reads, writes=[sq])
            s.op("pe", lambda q: q.matmul(ps.ap[:, 0:n], lhsT=self.onesb.ap[:, ones_idx, :], rhs=sq.ap[:, 0:n],
                                          start=(i == 0), stop=(i == nk - 1)),
                 reads=[sq, self.onesb], pwrites=[ps])
        s.op("act", lambda q: q.activation(out=self.RS.ap[:, 0:n], in_=ps.ap[:, 0:n], func=AF.Sqrt, bias=EPS, scale=1.0),
             reads=[ps], writes=[self.RS])
        s.op("dve", lambda q: q.reciprocal(out=self.RS.ap[:, 0:n], in_=self.RS.ap[:, 0:n]),
             reads=[self.RS], writes=[self.RS])
        return self.RS

    def attn_finish(self, po, h, OG, OGb, ts, n=512):
        s = self.s
        j, par = h // 2, h % 2
        base = 64 * par
        dp = 64 if par == 0 else 0
        s.op("dve", lambda q: q.reciprocal(out=self.RD.ap[dp:dp + 1, 0:n], in_=po.ap[dp:dp + 1, 0:n]),
             reads=[po], writes=[self.RD])
        hi, lo = self.SQ[0], self.SQ[1]
        s.op("dve", lambda q: q.tensor_copy(out=hi.ap[dp:dp + 1, 0:n], in_=self.RD.ap[dp:dp + 1, 0:n]),
             reads=[self.RD], writes=[hi])
        s.op("dve", lambda q: q.tensor_tensor(out=self.TMPO.ap[dp:dp + 1, 0:n], in0=self.RD.ap[dp:dp + 1, 0:n],
                                              in1=hi.ap[dp:dp + 1, 0:n], op=ALU.subtract),
             reads=[self.RD, hi], writes=[self.TMPO])
        s.op("dve", lambda q: q.tensor_copy(out=lo.ap[dp:dp + 1, 0:n], in_=self.TMPO.ap[dp:dp + 1, 0:n]),
             reads=[self.TMPO], writes=[lo])
        bc = self.ps("x")
        s.op("pe", lambda q: q.matmul(bc.ap[:, 0:n], lhsT=self.ones1b.ap[dp:dp + 1, :], rhs=hi.ap[dp:dp + 1, 0:n],
                                      start=True, stop=False),
             reads=[hi, self.ones1b], pwrites=[bc])
        s.op("pe", lambda q: q.matmul(bc.ap[:, 0:n], lhsT=self.ones1b.ap[dp:dp + 1, :], rhs=lo.ap[dp:dp + 1, 0:n],
                                      start=False, stop=True),
             reads=[lo, self.ones1b], pwrites=[bc])
        s.op("act", lambda q: q.activation(out=self.BCS.ap[base:base + 64, 0:n], in_=bc.ap[base:base + 64, 0:n],
                                           func=AF.Identity),
             reads=[bc], writes=[self.BCS])
        s.op("dve", lambda q: q.tensor_tensor(out=self.TMPO.ap[base:base + 64, 0:n], in0=po.ap[base:base + 64, 0:n],
                                              in1=self.BCS.ap[base:base + 64, 0:n], op=ALU.mult),
             reads=[po, self.BCS], writes=[self.TMPO])
        s.op("pool", lambda q: q.tensor_tensor(out=OG[base:base + 64, j, ts], in0=self.TMPO.ap[base:base + 64, 0:n],
                                               in1=OG[base:base + 64, j, ts], op=ALU.mult),
             reads=[self.TMPO, OGb], pwrites=[OGb])

    def chain(self, mode):
        nc, s, d = self.nc, self.s, self.d
        lat = (mode == "lat")
        T = 1024 if lat else 512
        NTB = T // 512
        NT = T // 128
        col = 1 if lat else 0
        with ExitStack() as C:
            X = self.sb(C, "X", [128, 8, T], F32)
            Xb = [Buf(X) for _ in range(NTB)]
            XM = self.sb(C, "XM", [128, 8, T], BF16)
            XMb = [Buf(XM) for _ in range(NTB)]
            OG = [self.sb(C, "OG%d" % r, [128, 4, T], BF16) for r in range(3)]
            OGb = [[Buf(OG[r]) for _ in range(NTB)] for r in range(3)]
            xin = d["xl"] if lat else d["xc"]
            for tb in range(NTB):
                s.dma("sp", lambda q: q.dma_start(
                    out=X[:, :, tb * 512:(tb + 1) * 512],
                    in_=xin[:, tb * 512:(tb + 1) * 512].rearrange("(c p) n -> p c n", p=128)), writes=[Xb[tb]])
            ctxs = dict(lat=lat, T=T, NTB=NTB, NT=NT, col=col, X=X, Xb=Xb, XM=XM, XMb=XMb, OG=OG, OGb=OGb)
            if lat:
                ROPE = self.sb(C, "rope", [128, 2, 1024], F32)
                ROPEb = Buf(ROPE)
                s.dma("sp", lambda q: q.dma_start(out=ROPE[:], in_=d["ropeT"].rearrange("a p n -> p a n")), writes=[ROPEb])
                SELR = self.sb(C, "selr", [128, 8], F32)
                SELRb = Buf(SELR)
                s.dma("sp", lambda q: q.dma_start(out=SELR[:], in_=d["selr"][:, :]), writes=[SELRb])
                ctxs.update(ROPE=ROPE, ROPEb=ROPEb, SELR=SELR, SELRb=SELRb)
            for l in range(self.nlayers):
                self.layer(l, ctxs)
            yout = d["yl"] if lat else d["yc"]
            for tb in range(NTB):
                ts = slice(tb * 512, (tb + 1) * 512)
                rs = self.rstd_of([X[:, kc, ts] for kc in range(8)], [Xb[tb]], 0)
                for kc in range(8):
                    tmp = self.next_tmp()
                    s.op("dve", lambda q: q.tensor_tensor(out=tmp.ap[:], in0=X[:, kc, ts], in1=rs.ap[:], op=ALU.mult),
                         reads=[Xb[tb], rs], writes=[tmp])
                    s.op("act", lambda q: q.activation(out=tmp.ap[:], in_=tmp.ap[:], func=AF.Identity,
                                                       scale=self.fing.ap[:, kc:kc + 1]),
                         reads=[tmp, self.fing], writes=[tmp])
                    s.dma("sp", lambda q: q.dma_start(out=yout[kc * 128:(kc + 1) * 128, ts], in_=tmp.ap[:]), reads=[tmp])
            s.barrier()

    def layer(self, l, c):
        nc, s, d = self.nc, self.s, self.d
        lat, T, NTB, NT, col = c["lat"], c["T"], c["NTB"], c["NT"], c["col"]
        X, Xb, XM, XMb, OG, OGb = c["X"], c["Xb"], c["XM"], c["XMb"], c["OG"], c["OGb"]
        w_in = d["w_in"]

        def TS(tb):
            return slice(tb * 512, (tb + 1) * 512)

        for tb in range(NTB):
            ts = TS(tb)
            rs = self.rstd_of([X[:, kc, ts] for kc in range(8)], [Xb[tb]], 0)
            for kc in range(8):
                tmp = self.next_tmp()
                s.op("dve", lambda q: q.tensor_tensor(out=tmp.ap[:], in0=X[:, kc, ts], in1=rs.ap[:], op=ALU.mult),
                     reads=[Xb[tb], rs], writes=[tmp])
                s.op("act", lambda q: q.activation(out=XM[:, kc, ts], in_=tmp.ap[:], func=AF.Identity,
                                                   scale=self.GS.ap[:, l, kc, col:col + 1],
                                                   bias=self.MOD.ap[:, l, kc, col:col + 1]),
                     reads=[tmp, self.GS, self.MOD], pwrites=[XMb[tb]])

        if self.stage < 3:
            return
        A0 = ExitStack()
        A1 = ExitStack()
        QL = self.sb(A0, "QL", [128, 2, T], BF16)
        QLb = [Buf(QL) for _ in range(NTB)]
        QT = self.sb(A1, "QT", [128, 4, T], BF16)
        QTb = [Buf(QT) for _ in range(NTB)]
        KT = self.sb(A1, "KT", [128, 4, T], BF16)
        KTb = [Buf(KT) for _ in range(NTB)]
        V = self.sb(A1, "V", [128, NT, 768], BF16)
        Vb = [Buf(V) for _ in range(NT)]
        nap = None
        if lat and self.stage >= 4.1:
            nap = self.lat_na_prefetch(l, A1)
        with ExitStack() as A:
            CK = self.sb(A, "CK", [128, T], BF16)
            CKb = [Buf(CK) for _ in range(NTB)]
            KR = self.sb(A, "KR", [32, T], BF16)
            KRb = [Buf(KR) for _ in range(NTB)]
            UF = self.sb(A, "UF", [128, 4, 512], BF16)
            UFb = Buf(UF)
            QLR = self.sb(A, "QLR", [128, 3, 512], F32)
            QLRb = Buf(QLR)
            WKRP = self.sb(A, "WKRP", [128, 8, 32], BF16)
            WKRPb = Buf(WKRP)
            ABS = [self.sb(A, "ABS%d" % i, [128, 1024], BF16) for i in range(2)]
            ABSb = [Buf(ABS[i]) for i in range(2)]
            if not lat:
                ABC = self.sb(A, "ABC", [128, 4, 1024], BF16)
                ABCb = [Buf(ABC) for _ in range(4)]
                STG = [self.sb(A, "STG%d" % i, [128, 512], F32) for i in range(2)]
                STGb = [Buf(STG[i]) for i in range(2)]
                STK = self.sb(A, "STK", [128, 160], F32)
                STKb = Buf(STK)
                STS = self.sb(A, "STS", [128, 4], F32)
                STSb = Buf(STS)
            s.op("dve", lambda q: q.memset(V[:, :, :], 0.0), writes=Vb)
            for p in range(4):
                s.op("dve", lambda q: q.memset(V[:, :, p * 192 + 64:p * 192 + 65], 1.0), pwrites=Vb)
            if lat:
                s.dma("pool", lambda q: q.dma_start(out=WKRP[:], in_=d["w_krp"][l].rearrange("(c p) n -> p c n", p=128)),
                      writes=[WKRPb])

            def load_w(c0, n):
                self.tick_wb_hook()
                wb = self.next_wb()
                s.dma("pool", lambda q: q.dma_start(
                    out=wb.ap[:, :, 0:n], in_=w_in[l, :, c0:c0 + n].rearrange("(c p) n -> p c n", p=128)), writes=[wb])
                return wb

            def mm_fm(ps, M, wb, wsl, tb, extra_reads=()):
                for kc in range(8):
                    s.op("pe", lambda q: q.matmul(ps.ap[0:M, :], lhsT=wb.ap[:, kc, wsl], rhs=XM[:, kc, TS(tb)],
                                                  start=(kc == 0), stop=(kc == 7)),
                         reads=[wb, XMb[tb]], pwrites=[ps])

            def blk_qk(sel=None):
                for (c0, dst, dstb, scl) in ((C_Q, QT, QTb, NA_SCALE), (C_K, KT, KTb, 1.0)) if self.stage >= 3.1 else ():
                    if sel is not None and c0 != sel:
                        continue
                    wb = load_w(c0, 512)
                    for tb in range(NTB):
                        for j in range(4):
                            ps = self.ps("g")
                            mm_fm(ps, 128, wb, slice(j * 128, (j + 1) * 128), tb)
                            if j % 2 == 0:
                                s.op("dve", lambda q: q.tensor_scalar(out=dst[:, j, TS(tb)], in0=ps.ap[:], scalar1=scl,
                                                                      scalar2=0.0, op0=ALU.mult, op1=ALU.add),
                                     reads=[ps], pwrites=[dstb[tb]])
                            else:
                                s.op("act", lambda q: q.activation(out=dst[:, j, TS(tb)], in_=ps.ap[:], func=AF.Identity,
                                                                   scale=scl),
                                     reads=[ps], pwrites=[dstb[tb]])
                    if (not lat) and c0 == C_K:
                        for t in range(NT):
                            ps = self.ps("g")
                            for kc in range(8):
                                s.op("pe", lambda q: q.matmul(ps.ap[:], lhsT=XM[:, kc, t * 128:(t + 1) * 128],
                                                              rhs=wb.ap[:, kc, :], start=(kc == 0), stop=(kc == 7)),
                                     reads=[wb, XMb[0]], pwrites=[ps])
                            st = STGb[t % 2]
                            s.op("act", lambda q: q.activation(out=st.ap[:], in_=ps.ap[:], func=AF.Identity),
                                 reads=[ps], writes=[st])
                            s.dma("sp", lambda q: q.dma_start(out=d["onk"][l, t * 128:(t + 1) * 128, :], in_=st.ap[:]),
                                  reads=[st])
            def blk_v(sel=None):
                wb = load_w(C_V, 512)
                for t in range(NT) if self.stage >= 3.2 else ():
                    ps = self.ps("g")
                    tb = t // 4
                    for kc in range(8):
                        s.op("pe", lambda q: q.matmul(ps.ap[:], lhsT=XM[:, kc, t * 128:(t + 1) * 128], rhs=wb.ap[:, kc, :],
                                                      start=(kc == 0), stop=(kc == 7)),
                             reads=[wb, XMb[tb]], pwrites=[ps])
                    vv = V[:, t, :].rearrange("p (a b) -> p a b", b=192)
                    pv = ps.ap[:].rearrange("p (a e x) -> p a e x", e=2, x=64)
                    s.op("dve", lambda q: q.tensor_copy(out=vv[:, :, 0:64], in_=pv[:, :, 0, :]), reads=[ps], pwrites=[Vb[t]])
                    s.op("dve", lambda q: q.tensor_copy(out=vv[:, :, 128:192], in_=pv[:, :, 1, :]), reads=[ps], pwrites=[Vb[t]])
                    if not lat and not os.environ.get("KDBG_NOONV"):
                        st = STGb[t % 2]
                        s.op("dve", lambda q: q.tensor_copy(out=st.ap[:], in_=ps.ap[:]), reads=[ps], writes=[st])
                        s.dma("sp", lambda q: q.dma_start(out=d["onv"][l, t * 128:(t + 1) * 128, :], in_=st.ap[:]), reads=[st])
            def blk_gates(sel=None):
                for (c0, r) in ((C_GNA, 0), (C_GMLA, 1), (C_GFN, 2)) if self.stage >= 3.3 else ():
                    wb = load_w(c0, 512)
                    for tb in range(NTB):
                        for j in range(4):
                            ps = self.ps("g")
                            mm_fm(ps, 128, wb, slice(j * 128, (j + 1) * 128), tb)
                            s.op("act", lambda q: q.activation(out=OG[r][:, j, TS(tb)], in_=ps.ap[:], func=AF.Silu),
                                 reads=[ps], pwrites=[OGb[r][tb]])
            def blk_qlat(sel=None):
                wb = load_w(C_QLAT, 416)
                for tb in range(NTB) if self.stage >= 3.4 else ():
                    ts = TS(tb)
                    for j in range(3):
                        ps = self.ps("g")
                        mm_fm(ps, 128, wb, slice(j * 128, (j + 1) * 128), tb)
                        s.op("dve", lambda q: q.tensor_copy(out=QLR[:, j, :], in_=ps.ap[:]), reads=[ps], pwrites=[QLRb])
                    rs = self.rstd_of([QLR[:, 0, :], QLR[:, 1, :]], [QLRb], 1)
                    for j in range(2):
                        tmp = self.next_tmp()
                        s.op("dve", lambda q: q.tensor_tensor(out=tmp.ap[:], in0=QLR[:, j, :], in1=rs.ap[:], op=ALU.mult),
                             reads=[QLRb, rs], writes=[tmp])
                        s.op("act", lambda q: q.activation(out=QL[:, j, ts], in_=tmp.ap[:], func=AF.Identity,
                                                           scale=self.qng.ap[:, l, j:j + 1]),
                             reads=[tmp, self.qng], pwrites=[QLb[tb]])
                    rs = self.rstd_of([QLR[:, 2, :]], [QLRb], 2)
                    tmp = self.next_tmp()
                    s.op("dve", lambda q: q.tensor_tensor(out=tmp.ap[:], in0=QLR[:, 2, :], in1=rs.ap[:], op=ALU.mult),
                         reads=[QLRb, rs], writes=[tmp])
                    s.op("act", lambda q: q.activation(out=CK[:, ts], in_=tmp.ap[:], func=AF.Identity,
                                                       scale=self.kvng.ap[:, l:l + 1]),
                         reads=[tmp, self.kvng], pwrites=[CKb[tb]])
                    ps = self.ps("g")
                    mm_fm(ps, 32, wb, slice(384, 416), tb)
                    if lat:
                        ps2 = self.ps("g")
                        for kc in range(8):
                            s.op("pe", lambda q: q.matmul(ps2.ap[0:32, :], lhsT=WKRP[:, kc, :], rhs=XM[:, kc, ts],
                                                          start=(kc == 0), stop=(kc == 7)),
                                 reads=[WKRPb, XMb[tb]], pwrites=[ps2])
                        t1 = self.next_tmp()
                        t2 = self.next_tmp()
                        s.op("dve", lambda q: q.tensor_tensor(out=t1.ap[0:32, :], in0=ps.ap[0:32, :],
                                                              in1=c["ROPE"][0:32, 0, ts], op=ALU.mult),
                             reads=[ps, c["ROPEb"]], writes=[t1])
                        s.op("dve", lambda q: q.tensor_tensor(out=t2.ap[0:32, :], in0=ps2.ap[0:32, :],
                                                              in1=c["ROPE"][0:32, 1, ts], op=ALU.mult),
                             reads=[ps2, c["ROPEb"]], writes=[t2])
                        s.op("dve", lambda q: q.tensor_tensor(out=KR[0:32, ts], in0=t1.ap[0:32, :], in1=t2.ap[0:32, :],
                                                              op=ALU.add),
                             reads=[t1, t2], pwrites=[KRb[tb]])
                    else:
                        s.op("act", lambda q: q.activation(out=KR[0:32, ts], in_=ps.ap[0:32, :], func=AF.Identity),
                             reads=[ps], pwrites=[KRb[tb]])
                    if not lat:
                        for t in range(4):
                            ps = self.ps("g")
                            for kc in range(8):
                                s.op("pe", lambda q: q.matmul(ps.ap[:, 0:160], lhsT=XM[:, kc, t * 128:(t + 1) * 128],
                                                              rhs=wb.ap[:, kc, 256:416], start=(kc == 0), stop=(kc == 7)),
                                     reads=[wb, XMb[0]], pwrites=[ps])
                            s.op("act", lambda q: q.activation(out=STK[:, 0:128], in_=ps.ap[:, 0:128], func=AF.Square),
                                 reads=[ps], writes=[STKb])
                            s.op("dve", lambda q: q.reduce_sum(out=STS[:, 0:1], in_=STK[:, 0:128], axis=mybir.AxisListType.X),
                                 reads=[STKb], writes=[STSb])
                            s.op("act", lambda q: q.activation(out=STS[:, 1:2], in_=STS[:, 0:1], func=AF.Sqrt, bias=EPS,
                                                               scale=1.0 / 128),
                                 reads=[STSb], writes=[STSb])
                            s.op("dve", lambda q: q.reciprocal(out=STS[:, 2:3], in_=STS[:, 1:2]), reads=[STSb], writes=[STSb])
                            s.op("dve", lambda q: q.tensor_scalar(out=STK[:, 0:128], in0=ps.ap[:, 0:128],
                                                                  scalar1=STS[:, 2:3], scalar2=0.0, op0=ALU.mult, op1=ALU.add),
                                 reads=[ps, STSb], writes=[STKb])
                            s.op("dve", lambda q: q.tensor_tensor(out=STK[:, 0:128], in0=STK[:, 0:128],
                                                                  in1=self.kvnbc.ap[:, l, :], op=ALU.mult),
                                 reads=[STKb, self.kvnbc], writes=[STKb])
                            s.op("act", lambda q: q.activation(out=STK[:, 128:160], in_=ps.ap[:, 128:160], func=AF.Identity),
                                 reads=[ps], pwrites=[STKb])
                            s.dma("sp", lambda q: q.dma_start(out=d["ockv"][l, t * 128:(t + 1) * 128, :], in_=STK[:, 0:128]),
                                  reads=[STKb])
                            s.dma("sp", lambda q: q.dma_start(out=d["okr"][l, t * 128:(t + 1) * 128, :], in_=STK[:, 128:160]),
                                  reads=[STKb])
            def blk_ufn(sel=None):
                wb = load_w(C_UFN, 512)
                for tb in range(NTB) if self.stage >= 3.5 else ():
                    for j in range(4):
                        ps = self.ps("g")
                        mm_fm(ps, 128, wb, slice(j * 128, (j + 1) * 128), tb)
                        s.op("dve", lambda q: q.tensor_copy(out=UF[:, j, :], in_=ps.ap[:]), reads=[ps], pwrites=[UFb])
                    for tt in range(4):
                        t = tb * 4 + tt
                        if lat:
                            ab = ABS[t % 2]
                            abb = ABSb[t % 2]
                        else:
                            ab = ABC[:, t, :]
                            abb = ABCb[t]
                        for half in range(2):
                            ps = self.ps("g")
                            for gg in range(2):
                                g = half * 2 + gg
                                s.op("pe", lambda q: q.matmul(ps.ap[:, gg * 256:(gg + 1) * 256],
                                                              lhsT=UF[:, g, tt * 128:(tt + 1) * 128], rhs=self.cs128.ap[:],
                                                              start=True, stop=True),
                                     reads=[UFb, self.cs128], pwrites=[ps])
                            dst = ab[:, half * 512:(half + 1) * 512]
                            if half == 0:
                                s.op("dve", lambda q: q.tensor_copy(out=dst, in_=ps.ap[:]), reads=[ps], pwrites=[abb])
                            else:
                                s.op("act", lambda q: q.activation(out=dst, in_=ps.ap[:], func=AF.Identity),
                                     reads=[ps], pwrites=[abb])
                        if lat:
                            pn = "pay_ab%d" % (t // 4)
                            s.dma("sp", lambda q: q.dma_start(out=d[pn][l][(t % 4) * 128:(t % 4 + 1) * 128, :], in_=ab[:]),
                                  reads=[abb], pwrites=[self.db[pn][l]])

            def emit_cc(names):
                for nm in names:
                    pay, g = d['pay_' + nm][l], d['g_' + nm][l]
                    s.cc(lambda q: q.collective_compute('AllGather', ALU.bypass, replica_groups=GROUPS,
                                                        ins=[pay[:, :]], outs=[g[:, :]]),
                         reads=[self.db['pay_' + nm][l]], writes=[self.db['g_' + nm][l]])
            if not lat:
                blk_qk()
                blk_v()
                blk_gates()
                blk_qlat()
                blk_ufn()
            else:
                self.wb_hooks = []
                blk_qk(sel=C_K)
                blk_v()
                self.lat_pay_kv(l, c, KT, KTb, V, Vb)
                self.wb_hooks.append([2, lambda: emit_cc(('k',))])
                self.wb_hooks.append([3, lambda: emit_cc(('v',))])
                blk_qlat()
                self.lat_pay_mla(l, c, CK, CKb, KR, KRb)
                blk_ufn()
                blk_qk(sel=C_Q)
                blk_gates()
                self.flush_wb_hook()
                self.cc_late = [(lambda: emit_cc(('mla',))), (lambda: emit_cc(('ab0',))), (lambda: emit_cc(('ab1',)))]
            if self.stage >= 4:
                if not lat:
                    self.ctx_attention(l, c, A, QT, QTb, KT, KTb, V, Vb, QL, QLb, CK, CKb, KR, KRb, ABC, ABCb)
            s.barrier()
        if lat and self.stage >= 4.1:
            self.lat_na(l, c, QT, QTb, KT, KTb, V, Vb, nap)
            s.barrier()
        A1.close()
        if lat and self.stage >= 4.2:
            self.lat_mla(l, c, QL, QLb)
            s.barrier()
        if lat and self.stage >= 4.3:
            self.lat_fourier(l, c)
            s.barrier()
        A0.close()
        if self.stage < 5:
            return

        with ExitStack() as Fz:
            WO = [self.sb(Fz, "WO%d" % r, [128, 4, 1024], BF16) for r in range(3)]
            WOb = [Buf(WO[r]) for r in range(3)]
            MG = self.sb(Fz, "MG", [128, 8, T], BF16)
            MGb = [Buf(MG) for _ in range(NTB)]
            SG = [self.sb(Fz, "SG%d" % r, [128, 512], F32) for r in range(3)]
            SGb = [Buf(SG[r]) for r in range(3)]
            MT = self.sb(Fz, "MT", [128, 512], F32)
            MTb = Buf(MT)
            MT2 = self.sb(Fz, "MT2", [128, 512], F32)
            MT2b = Buf(MT2)
            for r, nm in enumerate(("w_o_na", "w_o_mla", "w_o_fn")):
                s.dma("pool", lambda q: q.dma_start(out=WO[r][:], in_=d[nm][l].rearrange("(c p) n -> p c n", p=128)),
                      writes=[WOb[r]])
            wm = w_in[l, :, C_MRG:D_IN].rearrange("(c p) (r n) -> p c r n", p=128, r=3)
            for cch in range(8):
                wb = self.next_wb()
                wv = wb.ap[:, :, 0:384].rearrange("p c (r n) -> p c r n", r=3)
                for r in range(3):
                    s.dma("pool", lambda q: q.dma_start(out=wv[:, :, r, :], in_=wm[:, :, r, cch * 128:(cch + 1) * 128]),
                          pwrites=[wb])
                for tb in range(NTB):
                    ts = TS(tb)
                    for r in range(3):
                        ps = self.ps("all")
                        for kc in range(8):
                            s.op("pe", lambda q: q.matmul(ps.ap[:], lhsT=wv[:, kc, r, :], rhs=XM[:, kc, ts],
                                                          start=(kc == 0), stop=(kc == 7)),
                                 reads=[wb, XMb[tb]], pwrites=[ps])
                        s.op("act", lambda q: q.activation(out=SG[r][:], in_=ps.ap[:], func=AF.Sigmoid),
                             reads=[ps], writes=[SGb[r]])
                    for r in range(3):
                        ps = self.ps("all")
                        for kc in range(4):
                            s.op("pe", lambda q: q.matmul(ps.ap[:], lhsT=WO[r][:, kc, cch * 128:(cch + 1) * 128],
                                                          rhs=OG[r][:, kc, ts], start=(kc == 0), stop=(kc == 3)),
                                 reads=[WOb[r], OGb[r][tb]], pwrites=[ps])
                        if r == 0:
                            s.op("dve", lambda q: q.tensor_tensor(out=MT[:], in0=ps.ap[:], in1=SG[0][:], op=ALU.mult),
                                 reads=[ps, SGb[0]], writes=[MTb])
                        else:
                            s.op("dve", lambda q: q.tensor_tensor(out=MT2[:], in0=ps.ap[:], in1=SG[r][:], op=ALU.mult),
                                 reads=[ps, SGb[r]], writes=[MT2b])
                            if r == 1:
                                s.op("dve", lambda q: q.tensor_tensor(out=MT[:], in0=MT[:], in1=MT2[:], op=ALU.add),
                                     reads=[MTb, MT2b], writes=[MTb])
                            else:
                                s.op("dve", lambda q: q.tensor_tensor(out=MG[:, cch, ts], in0=MT[:], in1=MT2[:], op=ALU.add),
                                     reads=[MTb, MT2b], pwrites=[MGb[tb]])
            for half in range(2):
                wb = self.next_wb()
                s.dma("pool", lambda q: q.dma_start(
                    out=wb.ap[:], in_=d["w_out"][l, :, half * 512:(half + 1) * 512].rearrange("(c p) n -> p c n", p=128)),
                    writes=[wb])
                for tb in range(NTB):
                    ts = TS(tb)
                    for jj in range(4):
                        cch = half * 4 + jj
                        ps = self.ps("all")
                        for kc in range(8):
                            s.op("pe", lambda q: q.matmul(ps.ap[:], lhsT=wb.ap[:, kc, jj * 128:(jj + 1) * 128],
                                                          rhs=MG[:, kc, ts], start=(kc == 0), stop=(kc == 7)),
                                 reads=[wb, MGb[tb]], pwrites=[ps])
                        s.op("dve", lambda q: q.scalar_tensor_tensor(out=X[:, cch, ts], in0=ps.ap[:],
                                                                     scalar=self.MOD.ap[:, l, 16 + cch, col:col + 1],
                                                                     in1=X[:, cch, ts], op0=ALU.mult, op1=ALU.add),
                             reads=[ps, self.MOD, Xb[tb]], pwrites=[Xb[tb]])
            s.barrier()

    def ctx_attention(self, l, c, A, QT, QTb, KT, KTb, V, Vb, QL, QLb, CK, CKb, KR, KRb, ABC, ABCb):
        nc, s, d = self.nc, self.s, self.d
        OG, OGb = c["OG"], c["OGb"]
        ts = slice(0, 512)
        items = []
        for h in range(8):
            j, par = h // 2, h % 2
            base = 64 * par
            hst = {}
            for bb in range(2):
                st = {}

                def front(h=h, j=j, par=par, base=base, bb=bb, st=st, hst=hst):
                    if bb == 0:
                        hst["po"] = self.ps("o")
                    ps = self.ps("g")
                    for kt in range(2):
                        k0 = bb * 256 + kt * 128
                        s.op("pe", lambda q: q.matmul(ps.ap[:, kt * 256:(kt + 1) * 256], lhsT=KT[base:base + 64, j, k0:k0 + 128],
                                                      rhs=QT[base:base + 64, j, bb * 256:(bb + 1) * 256], start=True, stop=True),
                             reads=[KTb[0], QTb[0]], pwrites=[ps])
                    pt = self.next_pt()
                    s.op("act", lambda q: q.activation(out=pt.ap[:], in_=ps.ap[:], func=AF.Exp), reads=[ps], writes=[pt])
                    st["pt"] = pt

                def back(h=h, j=j, par=par, bb=bb, st=st, hst=hst):
                    po, pt = hst["po"], st["pt"]
                    for kt in range(2):
                        t = bb * 2 + kt
                        if par == 0:
                            out = po.ap[0:65, bb * 256:(bb + 1) * 256]
                            lhsT = V[:, t, j * 192:j * 192 + 65]
                        else:
                            out = po.ap[0:128, bb * 256:(bb + 1) * 256]
                            lhsT = V[:, t, j * 192 + 64:j * 192 + 192]
                        s.op("pe", lambda q: q.matmul(out, lhsT=lhsT, rhs=pt.ap[:, kt * 256:(kt + 1) * 256],
                                                      start=(kt == 0), stop=(kt == 1)),
                             reads=[Vb[t], pt], pwrites=[po])
                fin = None
                if bb == 1:
                    fin = (lambda h=h, hst=hst: self.attn_finish(hst["po"], h, OG[0], OGb[0][0], ts))
                items.append((front, back, fin))
        self.run_pipeline2(items, LA=2, DL=2)

        WUQ = self.sb(A, "WUQ", [128, 2, 768], BF16)
        WUQb = Buf(WUQ)
        WK = self.sb(A, "WK", [128, 8, 96], BF16)
        WKb = Buf(WK)
        WV = self.sb(A, "WV", [128, 512], BF16)
        WVb = Buf(WV)
        VM = self.sb(A, "VM", [128, 4, 192], BF16)
        VMb = Buf(VM)
        KHs = [self.sb(A, "KH%d" % i, [96, 512], BF16) for i in range(2)]
        KHbs = [Buf(KHs[i]) for i in range(2)]
        QHs = [self.sb(A, "QH%d" % i, [96, 512], BF16) for i in range(2)]
        QHbs = [Buf(QHs[i]) for i in range(2)]
        VM2 = self.sb(A, "VM2", [128, 4, 192], BF16)
        VM2b = Buf(VM2)
        s.dma("pool", lambda q: q.dma_start(out=WUQ[:], in_=d["w_uq"][l].rearrange("(c p) n -> p c n", p=128)), writes=[WUQb])
        s.op("dve", lambda q: q.memset(WK[:], 0.0), writes=[WKb])
        wukv = d["w_ukv"][l].rearrange("c (h t x) -> c h t x", t=2, x=64)
        s.dma("pool", lambda q: q.dma_start(out=WK[:, :, 0:64], in_=wukv[:, :, 0, :]), pwrites=[WKb])
        s.dma("pool", lambda q: q.dma_start(out=WV[:].rearrange("p (h x) -> p h x", x=64), in_=wukv[:, :, 1, :]), writes=[WVb])
        VMs, VMbs = [VM, VM2], [VMb, VM2b]
        for i in range(2):
            s.op("dve", lambda q: q.memset(VMs[i][:], 0.0), writes=[VMbs[i]])
            s.op("dve", lambda q: q.memset(VMs[i][:, :, 64:65], 1.0), pwrites=[VMbs[i]])
        items = []
        for h in range(8):
            p, par = h // 2, h % 2
            vm, vmb = VMs[p % 2], VMbs[p % 2]
            KH, KHb, QH, QHb = KHs[h % 2], KHbs[h % 2], QHs[h % 2], QHbs[h % 2]
            hst = {}
            for bb in range(2):
                st = {}

                def front(h=h, p=p, par=par, bb=bb, st=st, hst=hst, vm=vm, vmb=vmb, KH=KH, KHb=KHb, QH=QH, QHb=QHb):
                    if bb == 0 and par == 0:
                        ps = self.ps("g")
                        for t in range(4):
                            s.op("pe", lambda q: q.matmul(ps.ap[:, t * 128:(t + 1) * 128], lhsT=CK[:, t * 128:(t + 1) * 128],
                                                          rhs=WV[:, p * 128:(p + 1) * 128], start=True, stop=True),
                                 reads=[CKb[0], WVb], pwrites=[ps])
                        pv = ps.ap[:].rearrange("p (t e x) -> p t e x", e=2, x=64)
                        s.op("dve", lambda q: q.tensor_copy(out=vm[:, :, 0:64], in_=pv[:, :, 0, :]), reads=[ps], pwrites=[vmb])
                        s.op("dve", lambda q: q.tensor_copy(out=vm[:, :, 128:192], in_=pv[:, :, 1, :]), reads=[ps], pwrites=[vmb])
                    if bb == 0:
                        hst["po"] = self.ps("o")
                        ps = self.ps("g")
                        s.op("pe", lambda q: q.matmul(ps.ap[0:96, :], lhsT=WK[:, h, :], rhs=CK[:, 0:512], start=True, stop=False),
                             reads=[WKb, CKb[0]], pwrites=[ps])
                        s.op("pe", lambda q: q.matmul(ps.ap[0:96, :], lhsT=self.sel32.ap[:, :], rhs=KR[0:32, 0:512],
                                                      start=False, stop=True),
                             reads=[self.sel32, KRb[0]], pwrites=[ps])
                        s.op("act", lambda q: q.activation(out=KH[:, :], in_=ps.ap[0:96, :], func=AF.Identity),
                             reads=[ps], writes=[KHb])
                        ps = self.ps("g")
                        for kc in range(2):
                            s.op("pe", lambda q: q.matmul(ps.ap[0:96, :], lhsT=WUQ[:, kc, h * 96:(h + 1) * 96], rhs=QL[:, kc, 0:512],
                                                          start=(kc == 0), stop=(kc == 1)),
                                 reads=[WUQb, QLb[0]], pwrites=[ps])
                        s.op("act", lambda q: q.activation(out=QH[:, :], in_=ps.ap[0:96, :], func=AF.Identity),
                             reads=[ps], writes=[QHb])
                    ps = self.ps("g")
                    for kt in range(2):
                        k0 = bb * 256 + kt * 128
                        s.op("pe", lambda q: q.matmul(ps.ap[:, kt * 256:(kt + 1) * 256], lhsT=KH[:, k0:k0 + 128],
                                                      rhs=QH[:, bb * 256:(bb + 1) * 256], start=True, stop=True),
                             reads=[KHb, QHb], pwrites=[ps])
                    pt = self.next_pt()
                    s.op("act", lambda q: q.activation(out=pt.ap[:], in_=ps.ap[:], func=AF.Exp, scale=MLA_SCALE),
                         reads=[ps], writes=[pt])
                    st["pt"] = pt

                def back(par=par, bb=bb, st=st, hst=hst, vm=vm, vmb=vmb):
                    po, pt = hst["po"], st["pt"]
                    for kt in range(2):
                        t = bb * 2 + kt
                        if par == 0:
                            out = po.ap[0:65, bb * 256:(bb + 1) * 256]
                            lhsT = vm[:, t, 0:65]
                        else:
                            out = po.ap[0:128, bb * 256:(bb + 1) * 256]
                            lhsT = vm[:, t, 64:192]
                        s.op("pe", lambda q: q.matmul(out, lhsT=lhsT, rhs=pt.ap[:, kt * 256:(kt + 1) * 256],
                                                      start=(kt == 0), stop=(kt == 1)),
                             reads=[vmb, pt], pwrites=[po])
                fin = None
                if bb == 1:
                    fin = (lambda h=h, hst=hst: self.attn_finish(hst["po"], h, OG[1], OGb[1][0], ts))
                items.append((front, back, fin))
        self.run_pipeline2(items, LA=2, DL=2)

        for g in range(4):
            po = self.ps("o")
            for bb in range(2):
                n = 0
                for nt in range(2):
                    t = bb * 2 + nt
                    for (off, mat) in ((0, self.c256), (128, self.ns256)):
                        s.op("pe", lambda q: q.matmul(po.ap[:, bb * 256:(bb + 1) * 256],
                                                      lhsT=ABC[:, t, g * 256 + off:g * 256 + off + 128],
                                                      rhs=mat.ap[:, nt, :], start=(n == 0), stop=(n == 3)),
                             reads=[ABCb[t], mat], pwrites=[po])
                        n += 1
            s.op("dve", lambda q: q.tensor_tensor(out=OG[2][:, g, ts], in0=po.ap[:], in1=OG[2][:, g, ts], op=ALU.mult),
                 reads=[po, OGb[2][0]], pwrites=[OGb[2][0]])


    def lat_pay_kv(self, l, c, KT, KTb, V, Vb):
        s, d, db = self.s, self.d, self.db
        pk = d["pay_k"][l].rearrange("(j p) n -> p j n", p=128)
        s.dma("sp", lambda q: q.dma_start(out=pk[:, :, 0:256], in_=KT[:, :, 0:256]), reads=[KTb[0]], pwrites=[db["pay_k"][l]])
        s.dma("sp", lambda q: q.dma_start(out=pk[:, :, 256:512], in_=KT[:, :, 768:1024]), reads=[KTb[1]], pwrites=[db["pay_k"][l]])
        pv = d["pay_v"][l].rearrange("(t p) n -> p t n", p=128)
        s.dma("sp", lambda q: q.dma_start(out=pv[:, 0:2, :], in_=V[:, 0:2, :]), reads=[Vb[0], Vb[1]], pwrites=[db["pay_v"][l]])
        s.dma("sp", lambda q: q.dma_start(out=pv[:, 2:4, :], in_=V[:, 6:8, :]), reads=[Vb[6], Vb[7]], pwrites=[db["pay_v"][l]])

    def lat_pay_mla(self, l, c, CK, CKb, KR, KRb):
        s, d, db = self.s, self.d, self.db
        s.dma("sp", lambda q: q.dma_start(out=d["pay_mla"][l][0:128, :], in_=CK[:, :]), reads=CKb, pwrites=[db["pay_mla"][l]])
        s.dma("sp", lambda q: q.dma_start(out=d["pay_mla"][l][128:160, :], in_=KR[0:32, :]), reads=KRb, pwrites=[db["pay_mla"][l]])

    def lat_na_prefetch(self, l, st):
        s, d = self.s, self.d
        KCT = self.sb(st, "KCT", [128, 4, 256], BF16)
        KCTb = Buf(KCT)
        VCX = self.sb(st, "VCX", [128, 2, 768], BF16)
        VCXb = Buf(VCX)
        BT = [self.sb(st, "BT%d" % i, [128, 2, 7, 128], BF16) for i in range(2)]
        BTb = [Buf(BT[i]) for i in range(2)]
        s.dma("pool", lambda q: q.dma_start(out=KCT[:], in_=d["cnakT"][l].rearrange("(j p) n -> p j n", p=128)), writes=[KCTb])
        s.op("dve", lambda q: q.memset(VCX[:], 0.0), writes=[VCXb])
        for p in range(4):
            s.op("dve", lambda q: q.memset(VCX[:, :, p * 192 + 64:p * 192 + 65], 1.0), pwrites=[VCXb])
        cv = d["cnav"][l].rearrange("(t p) (a e x) -> p t a e x", p=128, e=2, x=64)
        for t in range(2):
            vx = VCX[:, t, :].rearrange("p (a b) -> p a b", b=192)
            s.dma("pool", lambda q: q.dma_start(out=vx[:, :, 0:64], in_=cv[:, t, :, 0, :]), pwrites=[VCXb])
            s.dma("pool", lambda q: q.dma_start(out=vx[:, :, 128:192], in_=cv[:, t, :, 1, :]), pwrites=[VCXb])
        for p in range(2):
            s.dma("pool", lambda q: q.dma_start(out=BT[p][:], in_=d["nab"][l, :, 2 * p:2 * p + 2, :, :]), writes=[BTb[p]])
            s.op("act", lambda q: q.activation(out=BT[p][:], in_=BT[p][:], func=AF.Exp), reads=[BTb[p]], writes=[BTb[p]])
        return dict(KCT=KCT, KCTb=KCTb, VCX=VCX, VCXb=VCXb, BT=BT, BTb=BTb)

    def lat_na(self, l, c, QT, QTb, KT, KTb, V, Vb, nap):
        s, d, db = self.s, self.d, self.db
        KCT, KCTb, VCX, VCXb, BT, BTb = nap["KCT"], nap["KCTb"], nap["VCX"], nap["VCXb"], nap["BT"], nap["BTb"]
        OG, OGb = c["OG"], c["OGb"]
        SELR, SELRb = c["SELR"], c["SELRb"]
        with ExitStack() as N:
            HK = self.sb(N, "HK", [128, 4, 2, 256], BF16)
            HKb = Buf(HK)
            HV = self.sb(N, "HV", [128, 4, 768], BF16)
            HVb = Buf(HV)
            with ExitStack() as N2:
                KC = [self.sb(N2, "KC%d" % i, [128, 4, 512], BF16) for i in range(2)]
                KCb = [Buf(KC[i]) for i in range(2)]
                VC = [self.sb(N2, "VC%d" % i, [128, 4, 768], BF16) for i in range(2)]
                VCb = [Buf(VC[i]) for i in range(2)]
                gk = d["g_k"][l].rearrange("(c j p) n -> p j c n", c=4, j=4)
                for j in range(4):
                    kc, kcb = KC[j % 2], KCb[j % 2]
                    s.dma("sp", lambda q: q.dma_start(out=kc[:], in_=gk[:, j]), reads=[db["g_k"][l]], writes=[kcb])
                    for side in range(2):
                        cols = slice(256, 512) if side == 0 else slice(0, 256)
                        for cc in range(4):
                            sc = SELR[:, side * 4 + cc:side * 4 + cc + 1]
                            if cc == 0:
                                s.op("dve", lambda q: q.tensor_scalar(out=HK[:, j, side, :], in0=kc[:, cc, cols], scalar1=sc,
                                                                      scalar2=0.0, op0=ALU.mult, op1=ALU.add),
                                     reads=[kcb, SELRb], pwrites=[HKb])
                            else:
                                s.op("dve", lambda q: q.scalar_tensor_tensor(out=HK[:, j, side, :], in0=kc[:, cc, cols], scalar=sc,
                                                                             in1=HK[:, j, side, :], op0=ALU.mult, op1=ALU.add),
                                     reads=[kcb, SELRb, HKb], pwrites=[HKb])
                gv = d["g_v"][l].rearrange("(c t p) n -> p t c n", c=4, t=4)
                for ht in range(4):
                    side = ht // 2
                    src_t = (2 + ht) if side == 0 else (ht - 2)
                    vc, vcb = VC[ht % 2], VCb[ht % 2]
                    s.dma("sp", lambda q: q.dma_start(out=vc[:], in_=gv[:, src_t]), reads=[db["g_v"][l]], writes=[vcb])
                    for cc in range(4):
                        sc = SELR[:, side * 4 + cc:side * 4 + cc + 1]
                        if cc == 0:
                            s.op("dve", lambda q: q.tensor_scalar(out=HV[:, ht, :], in0=vc[:, cc, :], scalar1=sc, scalar2=0.0,
                                                                  op0=ALU.mult, op1=ALU.add),
                                 reads=[vcb, SELRb], pwrites=[HVb])
                        else:
                            s.op("dve", lambda q: q.scalar_tensor_tensor(out=HV[:, ht, :], in0=vc[:, cc, :], scalar=sc,
                                                                         in1=HV[:, ht, :], op0=ALU.mult, op1=ALU.add),
                                 reads=[vcb, SELRb, HVb], pwrites=[HVb])
                s.barrier()
            MK = self.sb(N, "MK", [128, 48, 128], BF16)
            MKb = Buf(MK)
            s.dma("sp", lambda q: q.dma_start(out=MK[:], in_=d["namask"][:, :, :]), writes=[MKb])
            all_items = []
            for p in range(4):
                bt, btb = BT[p % 2], BTb[p % 2]
                need_bt = [p >= 2]
                for h in (2 * p, 2 * p + 1):
                    par = h % 2
                    base = 64 * par
                    for grp in range(2):
                        po = self.ps("o")
                        items = []
                        for bi in range(4):
                            b = grp * 4 + bi
                            qs = slice(b * 128, (b + 1) * 128)
                            tiles = []
                            lts = [(b - 2 + jj, jj) for jj in range(5)]
                            if b == 0:
                                lts.append((3, 5))
                            if b == 7:
                                lts.append((4, 5))
                            for (lt, slot) in lts:
                                if 0 <= lt <= 7:
                                    kap = KT[base:base + 64, p, lt * 128:(lt + 1) * 128]
                                    kb_ = KTb[lt // 4]
                                    vt = V[:, lt, :]
                                    vb_ = Vb[lt]
                                elif lt < 0:
                                    kap = HK[base:base + 64, p, 0, (lt + 2) * 128:(lt + 3) * 128]
                                    kb_ = HKb
                                    vt = HV[:, lt + 2, :]
                                    vb_ = HVb
                                else:
                                    kap = HK[base:base + 64, p, 1, (lt - 8) * 128:(lt - 7) * 128]
                                    kb_ = HKb
                                    vt = HV[:, 2 + lt - 8, :]
                                    vb_ = HVb
                                tiles.append((kap, kb_, vt, vb_, lt - b + 3, b * 6 + slot))
                            for t in range(2):
                                tiles.append((KCT[base:base + 64, p, t * 128:(t + 1) * 128], KCTb, VCX[:, t, :], VCXb, None, None))
                            nt = len(tiles)
                            for g0 in range(0, nt, 4):
                                grpt = tiles[g0:g0 + 4]
                                st = {}

                                def front(grpt=grpt, st=st, qs=qs, grp=grp, base=base, p=p, par=par, bt=bt, btb=btb, need_bt=need_bt):
                                    if need_bt[0]:
                                        need_bt[0] = False
                                        s.dma("pool", lambda q: q.dma_start(out=bt[:], in_=d["nab"][l, :, 2 * p:2 * p + 2, :, :]),
                                              writes=[btb])
                                        s.op("act", lambda q: q.activation(out=bt[:], in_=bt[:], func=AF.Exp), reads=[btb], writes=[btb])
                                    ps = self.ps("g")
                                    for i, (kap, kb_, vt, vb_, dj, ms) in enumerate(grpt):
                                        reg = ps.ap[:, i * 128:(i + 1) * 128]
                                        s.op("pe", lambda q: q.matmul(reg, lhsT=kap, rhs=QT[base:base + 64, p, qs], start=True,
                                                                      stop=True),
                                             reads=[kb_, QTb[grp]], pwrites=[ps])
                                    w = len(grpt) * 128
                                    pt = self.next_pt()
                                    s.op("act", lambda q: q.activation(out=pt.ap[:, 0:w], in_=ps.ap[:, 0:w], func=AF.Exp),
                                         reads=[ps], writes=[pt])
                                    i = 0
                                    while i < len(grpt):
                                        dj, ms = grpt[i][4], grpt[i][5]
                                        if dj is None:
                                            i += 1
                                            continue
                                        n = 1
                                        while (i + n < len(grpt) and grpt[i + n][4] is not None
                                               and grpt[i + n][4] == dj + n and grpt[i + n][5] == ms + n):
                                            n += 1
                                        pv3 = pt.ap[:, i * 128:(i + n) * 128].rearrange("p (a b) -> p a b", b=128)
                                        s.op("dve", lambda q: q.tensor_tensor(out=pv3, in0=pv3, in1=bt[:, par, dj:dj + n, :], op=ALU.mult),
                                             reads=[pt, btb], writes=[pt])
                                        s.op("pool", lambda q: q.tensor_tensor(out=pv3, in0=pv3, in1=MK[:, ms:ms + n, :], op=ALU.mult),
                                             reads=[pt, MKb], writes=[pt])
                                        i += n
                                    st["pt"] = pt

                                def back(grpt=grpt, st=st, g0=g0, nt=nt, bi=bi, po=po, par=par, p=p):
                                    pt = st["pt"]
                                    for i, (kap, kb_, vt, vb_, dj, ms) in enumerate(grpt):
                                        n = g0 + i
                                        if par == 0:
                                            out = po.ap[0:65, bi * 128:(bi + 1) * 128]
                                            lhsT = vt[:, p * 192:p * 192 + 65]
                                        else:
                                            out = po.ap[0:128, bi * 128:(bi + 1) * 128]
                                            lhsT = vt[:, p * 192 + 64:p * 192 + 192]
                                        s.op("pe", lambda q: q.matmul(out, lhsT=lhsT, rhs=pt.ap[:, i * 128:(i + 1) * 128],
                                                                      start=(n == 0), stop=(n == nt - 1)),
                                             reads=[vb_, pt], pwrites=[po])
                                items.append([front, back, None])
                        items[-1][2] = (lambda po=po, h=h, grp=grp: self.attn_finish(po, h, OG[0], OGb[0][grp],
                                                                                      slice(grp * 512, (grp + 1) * 512)))
                        all_items.extend(items)
            late = getattr(self, "cc_late", None) or []
            self.cc_late = None
            npos = len(all_items)
            for i, f in enumerate(late):
                pos = (i * npos) // 4
                fr = all_items[pos][0]
                all_items[pos][0] = (lambda fr=fr, f=f: (f(), fr()))
            self.run_pipeline2(all_items, LA=4, DL=2)

    def lat_mla(self, l, c, QL, QLb):
        s, d, db = self.s, self.d, self.db
        OG, OGb = c["OG"], c["OGb"]
        ROPE, ROPEb = c["ROPE"], c["ROPEb"]
        NK = 4352
        NKT = 34
        with ExitStack() as M:
            CKA = self.sb(M, "CKA", [128, NK], BF16)
            CKAb = Buf(CKA)
            KRA = self.sb(M, "KRA", [32, NK], BF16)
            KRAb = Buf(KRA)
            KH = self.sb(M, "KH", [96, NK], BF16)
            KHb = Buf(KH)
            VM = [self.sb(M, "VM%d" % i, [128, NKT, 192], BF16) for i in range(2)]
            VMb = [Buf(VM[i]) for i in range(2)]
            WUQ = self.sb(M, "WUQ", [128, 2, 768], BF16)
            WUQb = Buf(WUQ)
            WUQP = self.sb(M, "WUQP", [128, 2, 768], BF16)
            WUQPb = Buf(WUQP)
            WK = self.sb(M, "WK", [128, 8, 96], BF16)
            WKb = Buf(WK)
            WV = self.sb(M, "WV", [128, 512], BF16)
            WVb = Buf(WV)
            QH = [self.sb(M, "QH%d" % i, [96, 1024], BF16) for i in range(2)]
            QHb = [Buf(QH[i]) for i in range(2)]
            for cc in range(4):
                s.dma("sp", lambda q: q.dma_start(out=CKA[:, cc * 1024:(cc + 1) * 1024], in_=d["g_mla"][l][cc * 160:cc * 160 + 128, :]),
                      reads=[db["g_mla"][l]], pwrites=[CKAb])
                s.dma("sp", lambda q: q.dma_start(out=KRA[0:32, cc * 1024:(cc + 1) * 1024],
                                                  in_=d["g_mla"][l][cc * 160 + 128:cc * 160 + 160, :]),
                      reads=[db["g_mla"][l]], pwrites=[KRAb])
            s.dma("pool", lambda q: q.dma_start(out=CKA[:, 4096:NK], in_=d["cckvT"][l]), pwrites=[CKAb])
            s.dma("pool", lambda q: q.dma_start(out=KRA[0:32, 4096:NK], in_=d["ckrT"][l]), pwrites=[KRAb])
            s.dma("pool", lambda q: q.dma_start(out=WUQ[:], in_=d["w_uq"][l].rearrange("(c p) n -> p c n", p=128)), writes=[WUQb])
            s.dma("pool", lambda q: q.dma_start(out=WUQP[:], in_=d["w_uqp"][l].rearrange("(c p) n -> p c n", p=128)), writes=[WUQPb])
            s.op("dve", lambda q: q.memset(WK[:], 0.0), writes=[WKb])
            wukv = d["w_ukv"][l].rearrange("c (h t x) -> c h t x", t=2, x=64)
            s.dma("pool", lambda q: q.dma_start(out=WK[:, :, 0:64], in_=wukv[:, :, 0, :]), pwrites=[WKb])
            s.dma("pool", lambda q: q.dma_start(out=WV[:].rearrange("p (h x) -> p h x", x=64), in_=wukv[:, :, 1, :]), writes=[WVb])
            for i in range(2):
                s.op("dve", lambda q: q.memset(VM[i][:], 0.0), writes=[VMb[i]])
                s.op("dve", lambda q: q.memset(VM[i][:, :, 64:65], 1.0), pwrites=[VMb[i]])
            KH2 = self.sb(M, "KH2", [96, NK], BF16)
            KHs, KHbs = [KH, KH2], [KHb, Buf(KH2)]

            def gen_v(p):
                vm, vmb = VM[p % 2], VMb[p % 2]
                for k0 in range(0, NKT, 4):
                    nt = min(4, NKT - k0)
                    ps = self.ps("g")
                    for t in range(nt):
                        kt = k0 + t
                        s.op("pe", lambda q: q.matmul(ps.ap[:, t * 128:(t + 1) * 128], lhsT=CKA[:, kt * 128:(kt + 1) * 128],
                                                      rhs=WV[:, p * 128:(p + 1) * 128], start=True, stop=True),
                             reads=[CKAb, WVb], pwrites=[ps])
                    pv = ps.ap[:, 0:nt * 128].rearrange("p (t e x) -> p t e x", e=2, x=64)
                    s.op("dve", lambda q: q.tensor_copy(out=vm[:, k0:k0 + nt, 0:64], in_=pv[:, :, 0, :]), reads=[ps], pwrites=[vmb])
                    s.op("dve", lambda q: q.tensor_copy(out=vm[:, k0:k0 + nt, 128:192], in_=pv[:, :, 1, :]), reads=[ps], pwrites=[vmb])

            def gen_kq(h):
                kh, khb = KHs[h % 2], KHbs[h % 2]
                qh, qhb = QH[h % 2], QHb[h % 2]
                for k0 in range(0, NK, 512):
                    n = min(512, NK - k0)
                    ps = self.ps("g")
                    s.op("pe", lambda q: q.matmul(ps.ap[0:96, 0:n], lhsT=WK[:, h, :], rhs=CKA[:, k0:k0 + n], start=True, stop=False),
                         reads=[WKb, CKAb], pwrites=[ps])
                    s.op("pe", lambda q: q.matmul(ps.ap[0:96, 0:n], lhsT=self.sel32.ap[:, :], rhs=KRA[0:32, k0:k0 + n],
                                                  start=False, stop=True),
                         reads=[self.sel32, KRAb], pwrites=[ps])
                    s.op("dve", lambda q: q.tensor_copy(out=kh[:, k0:k0 + n], in_=ps.ap[0:96, 0:n]), reads=[ps], pwrites=[khb])
                for tb in range(2):
                    ts = slice(tb * 512, (tb + 1) * 512)
                    ps = self.ps("g")
                    ps2 = self.ps("g")
                    for (pp, ww, wwb) in ((ps, WUQ, WUQb), (ps2, WUQP, WUQPb)):
                        for kc in range(2):
                            s.op("pe", lambda q: q.matmul(pp.ap[0:96, :], lhsT=ww[:, kc, h * 96:(h + 1) * 96], rhs=QL[:, kc, ts],
                                                          start=(kc == 0), stop=(kc == 1)),
                                 reads=[wwb, QLb[tb]], pwrites=[pp])
                    s.op("dve", lambda q: q.tensor_copy(out=qh[0:64, ts], in_=ps.ap[0:64, :]), reads=[ps], pwrites=[qhb])
                    t1 = self.next_tmp()
                    t2 = self.next_tmp()
                    s.op("dve", lambda q: q.tensor_tensor(out=t1.ap[64:96, :], in0=ps.ap[64:96, :], in1=ROPE[64:96, 0, ts], op=ALU.mult),
                         reads=[ps, ROPEb], writes=[t1])
                    s.op("dve", lambda q: q.tensor_tensor(out=t2.ap[64:96, :], in0=ps2.ap[64:96, :], in1=ROPE[64:96, 1, ts], op=ALU.mult),
                         reads=[ps2, ROPEb], writes=[t2])
                    s.op("dve", lambda q: q.tensor_tensor(out=qh[64:96, ts], in0=t1.ap[64:96, :], in1=t2.ap[64:96, :], op=ALU.add),
                         reads=[t1, t2], pwrites=[qhb])

            gen_v(0)
            gen_kq(0)
            items = []
            for h in range(8):
                p, par = h // 2, h % 2
                vm, vmb = VM[p % 2], VMb[p % 2]
                kh, khb = KHs[h % 2], KHbs[h % 2]
                qh, qhb = QH[h % 2], QHb[h % 2]
                for tb in range(2):
                    ts = slice(tb * 512, (tb + 1) * 512)
                    po = self.ps("o")
                    for kt in range(NKT):
                        st = {}
                        pre = None
                        if par == 1 and tb == 0 and kt == 0 and p + 1 < 4:
                            pre = (lambda p=p: gen_v(p + 1))
                        if tb == 1 and kt == NKT - 12 and h + 1 < 8:
                            pre = (lambda h=h: gen_kq(h + 1))

                        def front(kt=kt, st=st, ts=ts, kh=kh, khb=khb, qh=qh, qhb=qhb, pre=pre):
                            if pre is not None:
                                pre()
                            ps = self.ps("g")
                            s.op("pe", lambda q: q.matmul(ps.ap[:], lhsT=kh[:, kt * 128:(kt + 1) * 128], rhs=qh[:, ts],
                                                          start=True, stop=True),
                                 reads=[khb, qhb], writes=[ps])
                            pt = self.next_pt()
                            s.op("act", lambda q: q.activation(out=pt.ap[:], in_=ps.ap[:], func=AF.Exp, scale=MLA_SCALE),
                                 reads=[ps], writes=[pt])
                            st["pt"] = pt

                        def back(kt=kt, st=st, po=po, par=par, vm=vm, vmb=vmb):
                            pt = st["pt"]
                            if par == 0:
                                out = po.ap[0:65, :]
                                lhsT = vm[:, kt, 0:65]
                            else:
                                out = po.ap[0:128, :]
                                lhsT = vm[:, kt, 64:192]
                            s.op("pe", lambda q: q.matmul(out, lhsT=lhsT, rhs=pt.ap[:], start=(kt == 0), stop=(kt == NKT - 1)),
                                 reads=[vmb, pt], pwrites=[po])
                        fin = None
                        if kt == NKT - 1:
                            fin = (lambda po=po, h=h, tb=tb, ts=ts: self.attn_finish(po, h, OG[1], OGb[1][tb], ts))
                        items.append((front, back, fin))
            self.run_pipeline2(items, LA=3, DL=2)


    def lat_fourier(self, l, c):
        s, d, db = self.s, self.d, self.db
        OG, OGb = c["OG"], c["OGb"]
        with ExitStack() as Fs:
            DC = [self.sb(Fs, "DC%d" % i, [128, 4, 512], BF16) for i in range(2)]
            DCb = [Buf(DC[i]) for i in range(2)]
            DS = [self.sb(Fs, "DS%d" % i, [128, 4, 512], BF16) for i in range(2)]
            DSb = [Buf(DS[i]) for i in range(2)]
            AB = [self.sb(Fs, "AB%d" % i, [128, 4, 1024], BF16) for i in range(2)]
            ABb = [Buf(AB[i]) for i in range(2)]
            it = 0
            for tb in range(2):
                ts = slice(tb * 512, (tb + 1) * 512)
                acc = [self.PS[4 + g] for g in range(4)]
                for nb in range(8):
                    dc, dcb, ds_, dsb, ab, abb = DC[it % 2], DCb[it % 2], DS[it % 2], DSb[it % 2], AB[it % 2], ABb[it % 2]
                    it += 1
                    rows = slice(nb * 512, (nb + 1) * 512)
                    s.dma("sp", lambda q: q.dma_start(out=dc[:], in_=d["dftc"][rows, ts].rearrange("(i p) n -> p i n", p=128)), writes=[dcb])
                    s.dma("sp", lambda q: q.dma_start(out=ds_[:], in_=d["dftns"][rows, ts].rearrange("(i p) n -> p i n", p=128)), writes=[dsb])
                    gn = "g_ab%d" % (nb % 2)
                    grow = slice((nb // 2) * 512, (nb // 2 + 1) * 512)
                    s.dma("sp", lambda q: q.dma_start(out=ab[:], in_=d[gn][l][grow, :].rearrange("(i p) n -> p i n", p=128)),
                          reads=[db[gn][l]], writes=[abb])
                    for i in range(4):
                        for g in range(4):
                            first = (nb == 0 and i == 0)
                            last = (nb == 7 and i == 3)
                            s.op("pe", lambda q: q.matmul(acc[g].ap[:], lhsT=ab[:, i, g * 256:g * 256 + 128], rhs=dc[:, i, :],
                                                          start=first, stop=False),
                                 reads=[abb, dcb], pwrites=[acc[g]])
                            s.op("pe", lambda q: q.matmul(acc[g].ap[:], lhsT=ab[:, i, g * 256 + 128:g * 256 + 256], rhs=ds_[:, i, :],
                                                          start=False, stop=last),
                                 reads=[abb, dsb], pwrites=[acc[g]])
                for g in range(4):
                    s.op("dve", lambda q: q.tensor_tensor(out=OG[2][:, g, ts], in0=acc[g].ap[:], in1=OG[2][:, g, ts], op=ALU.mult),
                         reads=[acc[g], OGb[2][tb]], pwrites=[OGb[2][tb]])


def _rope_perm():
    P = np.array([i + 8 if (i % 16) < 8 else i - 8 for i in range(32)])
    sgn = np.array([-1.0 if (i % 16) < 8 else 1.0 for i in range(32)], np.float32)
    return P, sgn


def _consts():
    bf = ml_dtypes.bfloat16
    c = {}
    c["identb"] = np.eye(128, dtype=np.float32).astype(bf)
    n = np.arange(128)
    ang = 2 * np.pi * np.outer(n, n) / 128.0
    c["cs128"] = (np.concatenate([np.cos(ang), np.sin(ang)], axis=1) / np.sqrt(128.0)).astype(np.float32).astype(bf)
    n = np.arange(256)
    ang = 2 * np.pi * np.outer(n, n) / 256.0
    c["c256"] = (np.cos(ang) / 16.0).astype(np.float32).astype(bf)
    c["ns256"] = (-np.sin(ang) / 16.0).astype(np.float32).astype(bf)
    sel = np.zeros((32, 96), np.float32)
    sel[np.arange(32), 64 + np.arange(32)] = 1.0
    c["sel32"] = sel.astype(bf)
    return c


def _lat_consts(q):
    bf = ml_dtypes.bfloat16
    c = {}
    P, sgn = _rope_perm()
    t = np.arange(1024 * q, 1024 * q + 1024)
    row = (t // 64).astype(np.float32)
    colp = (t % 64).astype(np.float32)
    half = 16
    inv = (10000.0 ** (-np.arange(0, half, 2, dtype=np.float32) / half)).astype(np.float32)
    ar = row[:, None] * inv[None, :]
    ac = colp[:, None] * inv[None, :]
    ang = np.concatenate([ar, ar, ac, ac], axis=-1)
    cos = np.cos(ang).astype(np.float32)
    sin = (np.sin(ang).astype(np.float32) * sgn[None, :]).astype(np.float32)
    rope = np.zeros((2, 128, 1024), np.float32)
    rope[0, 0:32] = cos.T
    rope[0, 64:96] = cos.T
    rope[1, 0:32] = sin.T
    rope[1, 64:96] = sin.T
    c["ropeT"] = rope
    n = np.arange(4096, dtype=np.float64)
    k = np.arange(1024 * q, 1024 * q + 1024, dtype=np.float64)
    ang = 2 * np.pi * ((np.outer(n, k)) % 4096) / 4096.0
    c["dftc"] = (np.cos(ang) / 64.0).astype(np.float32).astype(bf)
    c["dftns"] = (-np.sin(ang) / 64.0).astype(np.float32).astype(bf)
    kk = np.arange(128)
    kr, kcol = kk // 64, kk % 64
    mask = np.zeros((128, 48, 128), np.float32)
    for b in range(8):
        for slot in range(6):
            if slot < 5:
                lt = b - 2 + slot
            elif b == 0:
                lt = 3
            elif b == 7:
                lt = 4
            else:
                continue
            krow = 16 * q + 2 * lt + kr
            r = 16 * q + 2 * b + kr
            rs = np.clip(r - 4, 0, 56)
            row_ok = ((krow[:, None] >= 0) & (krow[:, None] < 64) &
                      (krow[:, None] >= rs[None, :]) & (krow[:, None] < rs[None, :] + 8))
            cs = np.clip(kcol - 8, 0, 48)
            col_ok = (kcol[:, None] >= cs[None, :]) & (kcol[:, None] < cs[None, :] + 16)
            mask[:, b * 6 + slot, :] = np.where(row_ok & col_ok, 1.0, 0.0)
    c["namask"] = mask.astype(bf)
    sel = np.zeros((128, 8), np.float32)
    if q - 1 >= 0:
        sel[:, q - 1] = 1.0
    if q + 1 <= 3:
        sel[:, 4 + q + 1] = 1.0
    c["selr"] = sel
    return c


_NC_CACHE = {}


def _get_nc(key=("full",)):
    if key not in _NC_CACHE:
        if key[0] == "full":
            kb = KB(run_ctx=True, run_lat=True)
        else:
            kb = KB(**dict(key[1]))
        _NC_CACHE[key] = kb.build()
    return _NC_CACHE[key]


def make_in_maps(inp):
    f = lambda a: np.ascontiguousarray(np.asarray(a, dtype=np.float32))
    x_prompt, x_sample = f(inp["x_prompt"]), f(inp["x_sample"])
    P, sgn = _rope_perm()
    w_in = f(inp["w_in"])
    w_uq = f(inp["w_uq"])
    shared = {}
    shared["w_ada"] = f(inp["w_ada"])
    shared["b_adaT"] = f(f(inp["b_ada"]).reshape(L, 24, 128).transpose(2, 0, 1))
    shared["norm_gT"] = f(f(inp["norm_g"]).reshape(L, 8, 128).transpose(2, 0, 1))
    shared["fin_gT"] = f(f(inp["final_norm_g"]).reshape(8, 128).T)
    shared["qn_gT"] = f(f(inp["q_norm_g"]).reshape(L, 2, 128).transpose(2, 0, 1))
    shared["kvn_gT"] = f(f(inp["kv_norm_g"]).T)
    shared["kvn_bc"] = f(np.broadcast_to(f(inp["kv_norm_g"])[None, :, :], (128, L, 128)))
    shared["w_in"] = w_in
    shared["w_krp"] = f(w_in[:, :, C_KR:C_KR + 32][:, :, P])
    shared["w_uq"] = w_uq
    wq = w_uq.reshape(L, 256, 8, 96).copy()
    wq[:, :, :, 64:96] = wq[:, :, :, 64:96][:, :, :, P]
    shared["w_uqp"] = f(wq.reshape(L, 256, 768))
    shared["w_ukv"] = f(inp["w_ukv"])
    shared["w_o_na"] = f(inp["w_o_na"])
    shared["w_o_mla"] = f(inp["w_o_mla"])
    shared["w_o_fn"] = f(inp["w_o_fourier"])
    shared["w_out"] = f(inp["w_out"])
    shared.update(_consts())
    nb = f(inp["na_bias"])
    kr = np.arange(128) // 64
    kc = np.arange(128) % 64
    dj = np.arange(7)
    dr = 2 * (dj[None, :, None] - 3) + kr[:, None, None] - kr[None, None, :]
    dc = np.clip(kc[:, None, None] - kc[None, None, :] + 15, 0, 30) + 0 * dr
    drc = np.clip(dr + 7, 0, 14)
    nab = nb[:, :, drc, dc]
    shared["nab"] = f(nab.transpose(0, 2, 1, 3, 4))
    cna_k, cna_v = f(inp["cache_na_k"]), f(inp["cache_na_v"])
    cckv, ckr = f(inp["cache_mla_ckv"]), f(inp["cache_mla_krope"])
    cvec, c_ctx = f(inp["c"]), f(inp["c_ctx"])
    maps = []
    latc = [_lat_consts(q) for q in range(4)]
    for core in range(8):
        b, q = core // 4, core % 4
        m = dict(shared)
        m["xc"] = f(x_prompt[2 * core:2 * core + 2].reshape(512, D).T)
        m["xl"] = f(x_sample[b, 1024 * q:1024 * q + 1024].T)
        m["cond"] = f(np.stack([c_ctx, cvec[b]], axis=1))
        m.update(latc[q])
        m["cnakT"] = f(cna_k[b].reshape(L, 256, 512).transpose(0, 2, 1))
        m["cnav"] = f(cna_v[b].reshape(L, 256, 512))
        m["cckvT"] = f(cckv[b].transpose(0, 2, 1))
        m["ckrT"] = f(ckr[b].transpose(0, 2, 1))
        maps.append(m)
    return maps


def assemble(res):
    r = res.results
    y_prompt = np.stack([r[c]["yc"].T.reshape(2, 256, D) for c in range(8)]).reshape(16, 256, D)
    y_sample = np.stack([np.concatenate([r[4 * b + q]["yl"].T for q in range(4)], axis=0) for b in range(2)])

    def cache(name, w):
        return np.concatenate([r[c][name].reshape(L, 2, 256, w).transpose(1, 0, 2, 3) for c in range(8)], axis=0)
    nk = cache("onk", 512).reshape(16, L, 256, 8, 64)
    nv = cache("onv", 512).reshape(16, L, 256, 8, 64)
    nckv = cache("ockv", 128)
    nkr = cache("okr", 32)
    return tuple(np.ascontiguousarray(a.astype(np.float32)) for a in (y_prompt, y_sample, nk, nv, nckv, nkr))


def kernel(**inputs):
    nc = _get_nc()
    in_maps = make_in_maps(inputs)
    res = run_bass_kernel_spmd(nc, in_maps, core_ids=list(range(8)))
    return assemble(res)
```

```python
import os
import numpy as np
import ml_dtypes
from contextlib import ExitStack
import concourse.bass as bass
import concourse.mybir as mybir
from concourse.bass_utils import run_bass_kernel_spmd

F32 = mybir.dt.float32
BF16 = mybir.dt.bfloat16
AF = mybir.ActivationFunctionType
ALU = mybir.AluOpType

L = 4
D = 1024
D_IN = 7072
NA_SCALE = 64 ** -0.5
MLA_SCALE = 96 ** -0.5
EPS = 1e-6
C_Q, C_K, C_V, C_GNA, C_QLAT, C_CKV, C_KR, C_GMLA, C_UFN, C_GFN, C_MRG = (
    0, 512, 1024, 1536, 2048, 2304, 2432, 2464, 2976, 3488, 4000)
NEG = -30000.0
GROUPS = [[0, 1, 2, 3], [4, 5, 6, 7]]


class Buf:
    __slots__ = ("ap", "w", "r", "pm", "name", "fw", "excl")

    def __init__(self, ap, name=""):
        self.ap = ap
        self.w = {}
        self.r = {}
        self.pm = False
        self.fw = {}
        self.excl = False
        self.name = name


class Sched:
    LIMIT = 12000
    NLANES = 8

    def __init__(self, nc, es):
        self.nc = nc
        self.es = es
        self.eng = dict(pe=nc.tensor, act=nc.scalar, dve=nc.vector, pool=nc.gpsimd, sp=nc.sync)
        self.sems = []
        self.cur = {}
        self.known = {e: {} for e in self.eng}
        self.lanes = {e: [] for e in self.eng}
        self.rr = {e: 0 for e in self.eng}
        self.pe_sems = set()
        self.nins = 0

    def new_sem(self):
        h = self.es.enter_context(self.nc.semaphore("sm%d" % len(self.sems)))
        self.sems.append(h)
        return len(self.sems) - 1

    def _deps(self, reads, writes, pwrites):
        deps = {}

        def add(d):
            for k, v in d.items():
                if deps.get(k, 0) < v:
                    deps[k] = v
        for b in reads:
            add(b.w)
            if b.excl:
                add(b.r)
        for b in writes:
            add(b.w)
            add(b.r)
        for b in pwrites:
            add(b.r)
            add(b.fw)
            if not b.pm:
                add(b.w)
        return deps

    def _wait(self, e, deps):
        kn = self.known[e]
        for sem, val in deps.items():
            if e == "pe" and sem in self.pe_sems:
                continue
            if kn.get(sem, 0) >= val:
                continue
            self.eng[e].wait_ge(self.sems[sem], val)
            kn[sem] = val
            self.nins += 1

    def _mark(self, stamp, reads, writes, pwrites):
        s, v = stamp
        for b in writes:
            b.w = {s: v}
            b.fw = {s: v}
            b.r = {}
            b.pm = False
        for b in pwrites:
            b.w[s] = v
            b.pm = True
        for b in reads:
            b.r[s] = v

    def op(self, e, fn, reads=(), writes=(), pwrites=()):
        self._wait(e, self._deps(reads, writes, pwrites))
        c = self.cur.get(e)
        if c is None or c[1] >= self.LIMIT:
            c = [self.new_sem(), 0]
            self.cur[e] = c
            if e == "pe":
                self.pe_sems.add(c[0])
        c[1] += 1
        ins = fn(self.eng[e])
        ins.then_inc(self.sems[c[0]], 1)
        self.nins += 1
        self._mark((c[0], c[1]), reads, writes, pwrites)

    def dma(self, e, fn, reads=(), writes=(), pwrites=(), inc=16):
        deps = self._deps(reads, writes, pwrites)
        lanes = self.lanes[e]
        if len(lanes) < self.NLANES:
            lanes.append([self.new_sem(), 0])
            lane = lanes[-1]
        else:
            lane = lanes[self.rr[e] % self.NLANES]
            self.rr[e] += 1
        if lane[1] > 0 and deps.get(lane[0], 0) < lane[1]:
            deps[lane[0]] = lane[1]
        self._wait(e, deps)
        lane[1] += inc
        ins = fn(self.eng[e])
        ins.then_inc(self.sems[lane[0]], inc)
        self.nins += 1
        self._mark((lane[0], lane[1]), reads, writes, pwrites)

    def cc(self, fn, reads=(), writes=()):
        deps = self._deps(reads, writes, ())
        if not hasattr(self, "cclane"):
            self.cclane = [self.new_sem(), 0]
        lane = self.cclane
        if lane[1] > 0 and deps.get(lane[0], 0) < lane[1]:
            deps[lane[0]] = lane[1]
        self._wait("pool", deps)
        lane[1] += 1
        ins = fn(self.eng["pool"])
        ins.then_inc(self.sems[lane[0]], 1)
        self.nins += 1
        self._mark((lane[0], lane[1]), reads, writes, ())

    def barrier(self):
        allst = {}
        for e, c in self.cur.items():
            allst[c[0]] = c[1]
        for e, lanes in self.lanes.items():
            for ln in lanes:
                if ln[1] > 0:
                    allst[ln[0]] = ln[1]
        for e in self.eng:
            d = dict(allst)
            c = self.cur.get(e)
            if e == "pe" and c is not None:
                d.pop(c[0], None)
            self._wait(e, d)

    def finish(self):
        allst = {}
        for e, lanes in self.lanes.items():
            for ln in lanes:
                if ln[1] > 0:
                    allst[ln[0]] = ln[1]
        for e, c in self.cur.items():
            allst[c[0]] = c[1]
        if hasattr(self, "cclane") and self.cclane[1] > 0:
            allst[self.cclane[0]] = self.cclane[1]
        d = dict(allst)
        self._wait("sp", d)


class KB:
    def __init__(self, run_ctx=True, run_lat=True, nlayers=L, dbg=False, lw=L, stage=99):
        self.LW = lw
        self.stage = stage
        self.run_ctx = run_ctx
        self.run_lat = run_lat
        self.nlayers = nlayers
        self.nc = bass.Bass("TRN2", target_bir_lowering=False)
        self.es = ExitStack()
        self.s = Sched(self.nc, self.es)
        self.tcount = 0

    def din(self, name, shape, dt=F32):
        return self.nc.dram_tensor(name, list(shape), dt, kind="ExternalInput").ap()

    def dout(self, name, shape, dt=F32):
        return self.nc.dram_tensor(name, list(shape), dt, kind="ExternalOutput").ap()

    def dint(self, name, shape, dt=BF16):
        return self.nc.dram_tensor(name, list(shape), dt).ap()

    def sb(self, st, name, shape, dt):
        self.tcount += 1
        return st.enter_context(self.nc.sbuf_tensor("%s_%d" % (name, self.tcount), list(shape), dt))

    def ps(self, grp):
        idxs = self.psgrp[grp]
        i = idxs[self.psrr[grp] % len(idxs)]
        self.psrr[grp] += 1
        return self.PS[i]

    def build(self):
        nc, s, es = self.nc, self.s, self.es
        NL = self.nlayers
        d = {}
        d["xc"] = self.din("xc", [D, 512])
        d["xl"] = self.din("xl", [D, 1024])
        d["cond"] = self.din("cond", [D, 2])
        d["w_ada"] = self.din("w_ada", [self.LW, D, 3 * D])
        d["b_adaT"] = self.din("b_adaT", [128, L, 24])
        d["norm_gT"] = self.din("norm_gT", [128, L, 8])
        d["fin_gT"] = self.din("fin_gT", [128, 8])
        d["qn_gT"] = self.din("qn_gT", [128, L, 2])
        d["kvn_gT"] = self.din("kvn_gT", [128, L])
        d["kvn_bc"] = self.din("kvn_bc", [128, L, 128])
        d["w_in"] = self.din("w_in", [self.LW, D, D_IN])
        d["w_krp"] = self.din("w_krp", [self.LW, D, 32])
        d["w_uq"] = self.din("w_uq", [self.LW, 256, 768])
        d["w_uqp"] = self.din("w_uqp", [self.LW, 256, 768])
        d["w_ukv"] = self.din("w_ukv", [self.LW, 128, 1024])
        d["w_o_na"] = self.din("w_o_na", [self.LW, 512, D])
        d["w_o_mla"] = self.din("w_o_mla", [self.LW, 512, D])
        d["w_o_fn"] = self.din("w_o_fn", [self.LW, 512, D])
        d["w_out"] = self.din("w_out", [self.LW, D, D])
        d["identb"] = self.din("identb", [128, 128], BF16)
        d["cs128"] = self.din("cs128", [128, 256], BF16)
        d["c256"] = self.din("c256", [256, 256], BF16)
        d["ns256"] = self.din("ns256", [256, 256], BF16)
        d["sel32"] = self.din("sel32", [32, 128], BF16)
        d["ropeT"] = self.din("ropeT", [2, 128, 1024])
        d["dftc"] = self.din("dftc", [4096, 1024], BF16)
        d["dftns"] = self.din("dftns", [4096, 1024], BF16)
        d["nab"] = self.din("nab", [self.LW, 128, 8, 7, 128])
        d["namask"] = self.din("namask", [128, 48, 128], BF16)
        d["selr"] = self.din("selr", [128, 8])
        d["cnakT"] = self.din("cnakT", [L, 512, 256])
        d["cnav"] = self.din("cnav", [L, 256, 512])
        d["cckvT"] = self.din("cckvT", [L, 128, 256])
        d["ckrT"] = self.din("ckrT", [L, 32, 256])
        d["yc"] = self.dout("yc", [D, 512])
        d["yl"] = self.dout("yl", [D, 1024])
        d["onk"] = self.dout("onk", [L, 512, 512])
        d["onv"] = self.dout("onv", [L, 512, 512])
        d["ockv"] = self.dout("ockv", [L, 512, 128])
        d["okr"] = self.dout("okr", [L, 512, 32])
        d["pay_mla"] = [self.dint("pay_mla%d" % l, [160, 1024]) for l in range(L)]
        d["g_mla"] = [self.dint("g_mla%d" % l, [640, 1024]) for l in range(L)]
        d["pay_k"] = [self.dint("pay_k%d" % l, [512, 512]) for l in range(L)]
        d["g_k"] = [self.dint("g_k%d" % l, [2048, 512]) for l in range(L)]
        d["pay_v"] = [self.dint("pay_v%d" % l, [512, 768]) for l in range(L)]
        d["g_v"] = [self.dint("g_v%d" % l, [2048, 768]) for l in range(L)]
        d["pay_ab0"] = [self.dint("pay_ab0_%d" % l, [512, 1024]) for l in range(L)]
        d["pay_ab1"] = [self.dint("pay_ab1_%d" % l, [512, 1024]) for l in range(L)]
        d["g_ab0"] = [self.dint("g_ab0_%d" % l, [2048, 1024]) for l in range(L)]
        d["g_ab1"] = [self.dint("g_ab1_%d" % l, [2048, 1024]) for l in range(L)]
        self.d = d
        self.db = {k: [Buf(a) for a in d[k]] for k in ("pay_mla", "g_mla", "pay_k", "g_k", "pay_v", "g_v", "pay_ab0", "pay_ab1", "g_ab0", "g_ab1")}

        self.PS = [Buf(es.enter_context(nc.psum_tensor("ps%d" % i, [128, 512], F32)), "ps%d" % i) for i in range(8)]
        for b in self.PS:
            b.excl = True
        self.psgrp = {"g": [0, 1, 2, 3], "o": [4, 5], "x": [6, 7], "all": list(range(8))}
        self.psrr = {k: 0 for k in self.psgrp}

        P = es
        self.identb = Buf(self.sb(P, "identb", [128, 128], BF16))
        self.onesb = Buf(self.sb(P, "onesb", [128, 3, 128], BF16))
        self.onesf = Buf(self.sb(P, "onesf", [128, 128], F32))
        self.cs128 = Buf(self.sb(P, "cs128", [128, 256], BF16))
        self.c256 = Buf(self.sb(P, "c256", [128, 2, 256], BF16))
        self.ns256 = Buf(self.sb(P, "ns256", [128, 2, 256], BF16))
        self.sel32 = Buf(self.sb(P, "sel32", [32, 128], BF16))
        self.condt = Buf(self.sb(P, "condt", [128, 8, 2], F32))
        self.condb = Buf(self.sb(P, "condb", [128, 8, 2], BF16))
        self.bada = Buf(self.sb(P, "bada", [128, L, 24], F32))
        self.normg = Buf(self.sb(P, "normg", [128, L, 8], F32))
        self.fing = Buf(self.sb(P, "fing", [128, 8], F32))
        self.qng = Buf(self.sb(P, "qng", [128, L, 2], F32))
        self.kvng = Buf(self.sb(P, "kvng", [128, L], F32))
        self.kvnbc = Buf(self.sb(P, "kvnbc", [128, L, 128], F32))
        self.MOD = Buf(self.sb(P, "mod", [128, L, 24, 2], F32))
        self.GS = Buf(self.sb(P, "gs", [128, L, 8, 2], F32))
        self.WB = [Buf(self.sb(P, "wb%d" % i, [128, 8, 512], BF16)) for i in range(3)]
        self.wbi = 0
        self.SQ = [Buf(self.sb(P, "sq%d" % i, [128, 512], BF16)) for i in range(2)]
        self.TMP = [Buf(self.sb(P, "tmp%d" % i, [128, 512], F32)) for i in range(2)]
        self.RS = Buf(self.sb(P, "rs", [128, 512], F32))
        self.RD = Buf(self.sb(P, "rd", [128, 512], F32))
        self.BCS = Buf(self.sb(P, "bcs", [128, 512], F32))
        self.TMPO = Buf(self.sb(P, "tmpo", [128, 512], F32))
        self.PT = [Buf(self.sb(P, "pt%d" % i, [128, 512], BF16)) for i in range(5)]
        self.pending = None
        self.pti = 0
        self.sqi = 0
        self.tmi = 0

        def ld(buf, src, e="sp"):
            s.dma(e, lambda q: q.dma_start(out=buf.ap[:], in_=src), writes=[buf])
        ld(self.identb, d["identb"][:, :])
        ld(self.cs128, d["cs128"][:, :])
        ld(self.c256, d["c256"].rearrange("(t p) n -> p t n", p=128))
        ld(self.ns256, d["ns256"].rearrange("(t p) n -> p t n", p=128))
        ld(self.sel32, d["sel32"][:, :])
        ld(self.condt, d["cond"].rearrange("(c p) n -> p c n", p=128))
        ld(self.bada, d["b_adaT"][:, :, :])
        ld(self.normg, d["norm_gT"][:, :, :])
        ld(self.fing, d["fin_gT"][:, :])
        ld(self.qng, d["qn_gT"][:, :, :])
        ld(self.kvng, d["kvn_gT"][:, :])
        ld(self.kvnbc, d["kvn_bc"][:, :, :])
        for i, v in enumerate((1.0 / 1024, 1.0 / 256, 1.0 / 128)):
            s.op("dve", lambda q: q.memset(self.onesb.ap[:, i, :], v), pwrites=[self.onesb])
        s.op("dve", lambda q: q.memset(self.onesf.ap[:], 1.0), writes=[self.onesf])
        s.op("act", lambda q: q.activation(out=self.condb.ap[:], in_=self.condt.ap[:], func=AF.Silu),
             reads=[self.condt], writes=[self.condb])

        for l in range(NL):
            for jb in range(6):
                wb = self.next_wb()
                s.dma("pool", lambda q: q.dma_start(
                    out=wb.ap[:], in_=d["w_ada"][l, :, jb * 512:(jb + 1) * 512].rearrange("(c p) n -> p c n", p=128)),
                    writes=[wb])
                for jj in range(4):
                    j = jb * 4 + jj
                    ps = self.ps("g")
                    for kc in range(8):
                        s.op("pe", lambda q: q.matmul(ps.ap[:, 0:2], lhsT=wb.ap[:, kc, jj * 128:(jj + 1) * 128],
                                                      rhs=self.condb.ap[:, kc, :], start=(kc == 0), stop=(kc == 7)),
                             reads=[wb, self.condb], pwrites=[ps])
                    s.op("dve", lambda q: q.tensor_scalar(out=self.MOD.ap[:, l, j, :], in0=ps.ap[:, 0:2],
                                                          scalar1=self.bada.ap[:, l, j:j + 1], scalar2=0.0,
                                                          op0=ALU.add, op1=ALU.add),
                         reads=[ps, self.bada], pwrites=[self.MOD])
            for kc in range(8):
                s.op("dve", lambda q: q.tensor_scalar(out=self.GS.ap[:, l, kc, :], in0=self.MOD.ap[:, l, 8 + kc, :],
                                                      scalar1=1.0, scalar2=self.normg.ap[:, l, kc:kc + 1],
                                                      op0=ALU.add, op1=ALU.mult),
                     reads=[self.MOD, self.normg], pwrites=[self.GS])

        if self.run_ctx and self.stage >= 1:
            self.chain("ctx")
        if self.run_lat and self.stage >= 1:
            self.chain("lat")
        s.finish()
        return nc

    def tick_wb_hook(self):
        hooks = getattr(self, "wb_hooks", [])
        self.wb_hooks = []
        for h in hooks:
            h[0] -= 1
            if h[0] <= 0:
                h[1]()
            else:
                self.wb_hooks.append(h)

    def flush_wb_hook(self):
        hooks = getattr(self, "wb_hooks", [])
        self.wb_hooks = []
        for h in hooks:
            h[1]()

    def next_wb(self):
        wb = self.WB[self.wbi % 3]
        self.wbi += 1
        return wb

    def next_pt(self):
        b = self.PT[self.pti % 5]
        self.pti += 1
        return b

    def run_pipeline(self, items, LA=3):
        n = len(items)
        for step in range(n + LA):
            if step < n:
                items[step][0]()
            if step == min(2, n - 1):
                self.flush_pending()
            if step >= LA:
                items[step - LA][1]()

    def run_pipeline2(self, items, LA=2, DL=2):
        n = len(items)
        due = []
        for step in range(n + LA + DL + 1):
            if step < n:
                items[step][0]()
            if LA <= step < n + LA:
                it = items[step - LA]
                it[1]()
                if it[2] is not None:
                    due.append((step + DL, it[2]))
            while due and due[0][0] <= step:
                due.pop(0)[1]()

    def flush_pending(self):
        if self.pending is not None:
            p = self.pending
            self.pending = None
            p()

    def next_sq(self):
        b = self.SQ[self.sqi % 2]
        self.sqi += 1
        return b

    def next_tmp(self):
        b = self.TMP[self.tmi % 2]
        self.tmi += 1
        return b

    def rstd_of(self, chunks, reads, ones_idx, n=512):
        s = self.s
        ps = self.ps("x")
        nk = len(chunks)
        for i, ch in enumerate(chunks):
            sq = self.next_sq()
            s.op("act", lambda q: q.activation(out=sq.ap[:, 0:n], in_=ch, func=AF.Square), reads=reads, writes=[sq])
            s.op("pe", lambda q: q.matmul(ps.ap[:, 0:n], lhsT=self.onesb.ap[:, ones_idx, :], rhs=sq.ap[:, 0:n],
                                          start=(i == 0), stop=(i == nk - 1)),
                 reads=[sq, self.onesb], pwrites=[ps])
        s.op("act", lambda q: q.activation(out=self.RS.ap[:, 0:n], in_=ps.ap[:, 0:n], func=AF.Sqrt, bias=EPS, scale=1.0),
             reads=[ps], writes=[self.RS])
        s.op("dve", lambda q: q.reciprocal(out=self.RS.ap[:, 0:n], in_=self.RS.ap[:, 0:n]),
             reads=[self.RS], writes=[self.RS])
        return self.RS

    def attn_finish(self, po, h, OG, OGb, ts, n=512):
        s = self.s
        j, par = h // 2, h % 2
        base = 64 * par
        dp = 64 if par == 0 else 0
        s.op("dve", lambda q: q.reciprocal(out=self.RD.ap[dp:dp + 1, 0:n], in_=po.ap[dp:dp + 1, 0:n]),
             reads=[po], writes=[self.RD])
        bc = self.ps("x")
        s.op("pe", lambda q: q.matmul(bc.ap[:, 0:n], lhsT=self.onesf.ap[dp:dp + 1, :], rhs=self.RD.ap[dp:dp + 1, 0:n],
                                      start=True, stop=True),
             reads=[self.RD, self.onesf], writes=[bc])
        s.op("dve", lambda q: q.tensor_copy(out=self.BCS.ap[base:base + 64, 0:n], in_=bc.ap[base:base + 64, 0:n]),
             reads=[bc], writes=[self.BCS])
        s.op("dve", lambda q: q.tensor_tensor(out=self.TMPO.ap[base:base + 64, 0:n], in0=po.ap[base:base + 64, 0:n],
                                              in1=self.BCS.ap[base:base + 64, 0:n], op=ALU.mult),
             reads=[po, self.BCS], writes=[self.TMPO])
        s.op("dve", lambda q: q.tensor_tensor(out=OG[base:base + 64, j, ts], in0=self.TMPO.ap[base:base + 64, 0:n],
                                              in1=OG[base:base + 64, j, ts], op=ALU.mult),
             reads=[self.TMPO, OGb], pwrites=[OGb])

    def chain(self, mode):
        nc, s, d = self.nc, self.s, self.d
        lat = (mode == "lat")
        T = 1024 if lat else 512
        NTB = T // 512
        NT = T // 128
        col = 1 if lat else 0
        with ExitStack() as C:
            X = self.sb(C, "X", [128, 8, T], F32)
            Xb = [Buf(X) for _ in range(NTB)]
            XM = self.sb(C, "XM", [128, 8, T], BF16)
            XMb = [Buf(XM) for _ in range(NTB)]
            OG = [self.sb(C, "OG%d" % r, [128, 4, T], BF16) for r in range(3)]
            OGb = [[Buf(OG[r]) for _ in range(NTB)] for r in range(3)]
            xin = d["xl"] if lat else d["xc"]
            for tb in range(NTB):
                s.dma("sp", lambda q: q.dma_start(
                    out=X[:, :, tb * 512:(tb + 1) * 512],
                    in_=xin[:, tb * 512:(tb + 1) * 512].rearrange("(c p) n -> p c n", p=128)), writes=[Xb[tb]])
            ctxs = dict(lat=lat, T=T, NTB=NTB, NT=NT, col=col, X=X, Xb=Xb, XM=XM, XMb=XMb, OG=OG, OGb=OGb)
            if lat:
                ROPE = self.sb(C, "rope", [128, 2, 1024], F32)
                ROPEb = Buf(ROPE)
                s.dma("sp", lambda q: q.dma_start(out=ROPE[:], in_=d["ropeT"].rearrange("a p n -> p a n")), writes=[ROPEb])
                SELR = self.sb(C, "selr", [128, 8], F32)
                SELRb = Buf(SELR)
                s.dma("sp", lambda q: q.dma_start(out=SELR[:], in_=d["selr"][:, :]), writes=[SELRb])
                ctxs.update(ROPE=ROPE, ROPEb=ROPEb, SELR=SELR, SELRb=SELRb)
            for l in range(self.nlayers):
                self.layer(l, ctxs)
            yout = d["yl"] if lat else d["yc"]
            for tb in range(NTB):
                ts = slice(tb * 512, (tb + 1) * 512)
                rs = self.rstd_of([X[:, kc, ts] for kc in range(8)], [Xb[tb]], 0)
                for kc in range(8):
                    tmp = self.next_tmp()
                    s.op("dve", lambda q: q.tensor_tensor(out=tmp.ap[:], in0=X[:, kc, ts], in1=rs.ap[:], op=ALU.mult),
                         reads=[Xb[tb], rs], writes=[tmp])
                    s.op("act", lambda q: q.activation(out=tmp.ap[:], in_=tmp.ap[:], func=AF.Identity,
                                                       scale=self.fing.ap[:, kc:kc + 1]),
                         reads=[tmp, self.fing], writes=[tmp])
                    s.dma("sp", lambda q: q.dma_start(out=yout[kc * 128:(kc + 1) * 128, ts], in_=tmp.ap[:]), reads=[tmp])
            s.barrier()

    def layer(self, l, c):
        nc, s, d = self.nc, self.s, self.d
        lat, T, NTB, NT, col = c["lat"], c["T"], c["NTB"], c["NT"], c["col"]
        X, Xb, XM, XMb, OG, OGb = c["X"], c["Xb"], c["XM"], c["XMb"], c["OG"], c["OGb"]
        w_in = d["w_in"]

        def TS(tb):
            return slice(tb * 512, (tb + 1) * 512)

        for tb in range(NTB):
            ts = TS(tb)
            rs = self.rstd_of([X[:, kc, ts] for kc in range(8)], [Xb[tb]], 0)
            for kc in range(8):
                tmp = self.next_tmp()
                s.op("dve", lambda q: q.tensor_tensor(out=tmp.ap[:], in0=X[:, kc, ts], in1=rs.ap[:], op=ALU.mult),
                     reads=[Xb[tb], rs], writes=[tmp])
                s.op("act", lambda q: q.activation(out=XM[:, kc, ts], in_=tmp.ap[:], func=AF.Identity,
                                                   scale=self.GS.ap[:, l, kc, col:col + 1],
                                                   bias=self.MOD.ap[:, l, kc, col:col + 1]),
                     reads=[tmp, self.GS, self.MOD], pwrites=[XMb[tb]])

        if self.stage < 3:
            return
        A0 = ExitStack()
        A1 = ExitStack()
        QL = self.sb(A0, "QL", [128, 2, T], BF16)
        QLb = [Buf(QL) for _ in range(NTB)]
        QT = self.sb(A1, "QT", [128, 4, T], BF16)
        QTb = [Buf(QT) for _ in range(NTB)]
        KT = self.sb(A1, "KT", [128, 4, T], BF16)
        KTb = [Buf(KT) for _ in range(NTB)]
        V = self.sb(A1, "V", [128, NT, 768], BF16)
        Vb = [Buf(V) for _ in range(NT)]
        nap = None
        if lat and self.stage >= 4.1:
            nap = self.lat_na_prefetch(l, A1)
        with ExitStack() as A:
            CK = self.sb(A, "CK", [128, T], BF16)
            CKb = [Buf(CK) for _ in range(NTB)]
            KR = self.sb(A, "KR", [32, T], BF16)
            KRb = [Buf(KR) for _ in range(NTB)]
            UF = self.sb(A, "UF", [128, 4, 512], BF16)
            UFb = Buf(UF)
            QLR = self.sb(A, "QLR", [128, 3, 512], F32)
            QLRb = Buf(QLR)
            WKRP = self.sb(A, "WKRP", [128, 8, 32], BF16)
            WKRPb = Buf(WKRP)
            ABS = [self.sb(A, "ABS%d" % i, [128, 1024], BF16) for i in range(2)]
            ABSb = [Buf(ABS[i]) for i in range(2)]
            if not lat:
                ABC = self.sb(A, "ABC", [128, 4, 1024], BF16)
                ABCb = [Buf(ABC) for _ in range(4)]
                STG = [self.sb(A, "STG%d" % i, [128, 512], F32) for i in range(2)]
                STGb = [Buf(STG[i]) for i in range(2)]
                STK = self.sb(A, "STK", [128, 160], F32)
                STKb = Buf(STK)
                STS = self.sb(A, "STS", [128, 4], F32)
                STSb = Buf(STS)
            s.op("dve", lambda q: q.memset(V[:, :, :], 0.0), writes=Vb)
            for p in range(4):
                s.op("dve", lambda q: q.memset(V[:, :, p * 192 + 64:p * 192 + 65], 1.0), pwrites=Vb)
            if lat:
                s.dma("pool", lambda q: q.dma_start(out=WKRP[:], in_=d["w_krp"][l].rearrange("(c p) n -> p c n", p=128)),
                      writes=[WKRPb])

            def load_w(c0, n):
                self.tick_wb_hook()
                wb = self.next_wb()
                s.dma("pool", lambda q: q.dma_start(
                    out=wb.ap[:, :, 0:n], in_=w_in[l, :, c0:c0 + n].rearrange("(c p) n -> p c n", p=128)), writes=[wb])
                return wb

            def mm_fm(ps, M, wb, wsl, tb, extra_reads=()):
                for kc in range(8):
                    s.op("pe", lambda q: q.matmul(ps.ap[0:M, :], lhsT=wb.ap[:, kc, wsl], rhs=XM[:, kc, TS(tb)],
                                                  start=(kc == 0), stop=(kc == 7)),
                         reads=[wb, XMb[tb]], pwrites=[ps])

            def blk_qk(sel=None):
                for (c0, dst, dstb, scl) in ((C_Q, QT, QTb, NA_SCALE), (C_K, KT, KTb, 1.0)) if self.stage >= 3.1 else ():
                    if sel is not None and c0 != sel:
                        continue
                    wb = load_w(c0, 512)
                    for tb in range(NTB):
                        for j in range(4):
                            ps = self.ps("g")
                            mm_fm(ps, 128, wb, slice(j * 128, (j + 1) * 128), tb)
                            if j % 2 == 0:
                                s.op("dve", lambda q: q.tensor_scalar(out=dst[:, j, TS(tb)], in0=ps.ap[:], scalar1=scl,
                                                                      scalar2=0.0, op0=ALU.mult, op1=ALU.add),
                                     reads=[ps], pwrites=[dstb[tb]])
                            else:
                                s.op("act", lambda q: q.activation(out=dst[:, j, TS(tb)], in_=ps.ap[:], func=AF.Identity,
                                                                   scale=scl),
                                     reads=[ps], pwrites=[dstb[tb]])
                    if (not lat) and c0 == C_K:
                        for t in range(NT):
                            ps = self.ps("g")
                            for kc in range(8):
                                s.op("pe", lambda q: q.matmul(ps.ap[:], lhsT=XM[:, kc, t * 128:(t + 1) * 128],
                                                              rhs=wb.ap[:, kc, :], start=(kc == 0), stop=(kc == 7)),
                                     reads=[wb, XMb[0]], pwrites=[ps])
                            st = STGb[t % 2]
                            s.op("act", lambda q: q.activation(out=st.ap[:], in_=ps.ap[:], func=AF.Identity),
                                 reads=[ps], writes=[st])
                            s.dma("sp", lambda q: q.dma_start(out=d["onk"][l, t * 128:(t + 1) * 128, :], in_=st.ap[:]),
                                  reads=[st])
            def blk_v(sel=None):
                wb = load_w(C_V, 512)
                for t in range(NT) if self.stage >= 3.2 else ():
                    ps = self.ps("g")
                    tb = t // 4
                    for kc in range(8):
                        s.op("pe", lambda q: q.matmul(ps.ap[:], lhsT=XM[:, kc, t * 128:(t + 1) * 128], rhs=wb.ap[:, kc, :],
                                                      start=(kc == 0), stop=(kc == 7)),
                             reads=[wb, XMb[tb]], pwrites=[ps])
                    vv = V[:, t, :].rearrange("p (a b) -> p a b", b=192)
                    pv = ps.ap[:].rearrange("p (a e x) -> p a e x", e=2, x=64)
                    s.op("dve", lambda q: q.tensor_copy(out=vv[:, :, 0:64], in_=pv[:, :, 0, :]), reads=[ps], pwrites=[Vb[t]])
                    s.op("dve", lambda q: q.tensor_copy(out=vv[:, :, 128:192], in_=pv[:, :, 1, :]), reads=[ps], pwrites=[Vb[t]])
                    if not lat and not os.environ.get("KDBG_NOONV"):
                        st = STGb[t % 2]
                        s.op("dve", lambda q: q.tensor_copy(out=st.ap[:], in_=ps.ap[:]), reads=[ps], writes=[st])
                        s.dma("sp", lambda q: q.dma_start(out=d["onv"][l, t * 128:(t + 1) * 128, :], in_=st.ap[:]), reads=[st])
            def blk_gates(sel=None):
                for (c0, r) in ((C_GNA, 0), (C_GMLA, 1), (C_GFN, 2)) if self.stage >= 3.3 else ():
                    wb = load_w(c0, 512)
                    for tb in range(NTB):
                        for j in range(4):
                            ps = self.ps("g")
                            mm_fm(ps, 128, wb, slice(j * 128, (j + 1) * 128), tb)
                            s.op("act", lambda q: q.activation(out=OG[r][:, j, TS(tb)], in_=ps.ap[:], func=AF.Silu),
                                 reads=[ps], pwrites=[OGb[r][tb]])
            def blk_qlat(sel=None):
                wb = load_w(C_QLAT, 416)
                for tb in range(NTB) if self.stage >= 3.4 else ():
                    ts = TS(tb)
                    for j in range(3):
                        ps = self.ps("g")
                        mm_fm(ps, 128, wb, slice(j * 128, (j + 1) * 128), tb)
                        s.op("dve", lambda q: q.tensor_copy(out=QLR[:, j, :], in_=ps.ap[:]), reads=[ps], pwrites=[QLRb])
                    rs = self.rstd_of([QLR[:, 0, :], QLR[:, 1, :]], [QLRb], 1)
                    for j in range(2):
                        tmp = self.next_tmp()
                        s.op("dve", lambda q: q.tensor_tensor(out=tmp.ap[:], in0=QLR[:, j, :], in1=rs.ap[:], op=ALU.mult),
                             reads=[QLRb, rs], writes=[tmp])
                        s.op("act", lambda q: q.activation(out=QL[:, j, ts], in_=tmp.ap[:], func=AF.Identity,
                                                           scale=self.qng.ap[:, l, j:j + 1]),
                             reads=[tmp, self.qng], pwrites=[QLb[tb]])
                    rs = self.rstd_of([QLR[:, 2, :]], [QLRb], 2)
                    tmp = self.next_tmp()
                    s.op("dve", lambda q: q.tensor_tensor(out=tmp.ap[:], in0=QLR[:, 2, :], in1=rs.ap[:], op=ALU.mult),
                         reads=[QLRb, rs], writes=[tmp])
                    s.op("act", lambda q: q.activation(out=CK[:, ts], in_=tmp.ap[:], func=AF.Identity,
                                                       scale=self.kvng.ap[:, l:l + 1]),
                         reads=[tmp, self.kvng], pwrites=[CKb[tb]])
                    ps = self.ps("g")
                    mm_fm(ps, 32, wb, slice(384, 416), tb)
                    if lat:
                        ps2 = self.ps("g")
                        for kc in range(8):
                            s.op("pe", lambda q: q.matmul(ps2.ap[0:32, :], lhsT=WKRP[:, kc, :], rhs=XM[:, kc, ts],
                                                          start=(kc == 0), stop=(kc == 7)),
                                 reads=[WKRPb, XMb[tb]], pwrites=[ps2])
                        t1 = self.next_tmp()
                        t2 = self.next_tmp()
                        s.op("dve", lambda q: q.tensor_tensor(out=t1.ap[0:32, :], in0=ps.ap[0:32, :],
                                                              in1=c["ROPE"][0:32, 0, ts], op=ALU.mult),
                             reads=[ps, c["ROPEb"]], writes=[t1])
                        s.op("dve", lambda q: q.tensor_tensor(out=t2.ap[0:32, :], in0=ps2.ap[0:32, :],
                                                              in1=c["ROPE"][0:32, 1, ts], op=ALU.mult),
                             reads=[ps2, c["ROPEb"]], writes=[t2])
                        s.op("dve", lambda q: q.tensor_tensor(out=KR[0:32, ts], in0=t1.ap[0:32, :], in1=t2.ap[0:32, :],
                                                              op=ALU.add),
                             reads=[t1, t2], pwrites=[KRb[tb]])
                    else:
                        s.op("act", lambda q: q.activation(out=KR[0:32, ts], in_=ps.ap[0:32, :], func=AF.Identity),
                             reads=[ps], pwrites=[KRb[tb]])
                    if not lat:
                        for t in range(4):
                            ps = self.ps("g")
                            for kc in range(8):
                                s.op("pe", lambda q: q.matmul(ps.ap[:, 0:160], lhsT=XM[:, kc, t * 128:(t + 1) * 128],
                                                              rhs=wb.ap[:, kc, 256:416], start=(kc == 0), stop=(kc == 7)),
                                     reads=[wb, XMb[0]], pwrites=[ps])
                            s.op("act", lambda q: q.activation(out=STK[:, 0:128], in_=ps.ap[:, 0:128], func=AF.Square),
                                 reads=[ps], writes=[STKb])
                            s.op("dve", lambda q: q.reduce_sum(out=STS[:, 0:1], in_=STK[:, 0:128], axis=mybir.AxisListType.X),
                                 reads=[STKb], writes=[STSb])
                            s.op("act", lambda q: q.activation(out=STS[:, 1:2], in_=STS[:, 0:1], func=AF.Sqrt, bias=EPS,
                                                               scale=1.0 / 128),
                                 reads=[STSb], writes=[STSb])
                            s.op("dve", lambda q: q.reciprocal(out=STS[:, 2:3], in_=STS[:, 1:2]), reads=[STSb], writes=[STSb])
                            s.op("dve", lambda q: q.tensor_scalar(out=STK[:, 0:128], in0=ps.ap[:, 0:128],
                                                                  scalar1=STS[:, 2:3], scalar2=0.0, op0=ALU.mult, op1=ALU.add),
                                 reads=[ps, STSb], writes=[STKb])
                            s.op("dve", lambda q: q.tensor_tensor(out=STK[:, 0:128], in0=STK[:, 0:128],
                                                                  in1=self.kvnbc.ap[:, l, :], op=ALU.mult),
                                 reads=[STKb, self.kvnbc], writes=[STKb])
                            s.op("act", lambda q: q.activation(out=STK[:, 128:160], in_=ps.ap[:, 128:160], func=AF.Identity),
                                 reads=[ps], pwrites=[STKb])
                            s.dma("sp", lambda q: q.dma_start(out=d["ockv"][l, t * 128:(t + 1) * 128, :], in_=STK[:, 0:128]),
                                  reads=[STKb])
                            s.dma("sp", lambda q: q.dma_start(out=d["okr"][l, t * 128:(t + 1) * 128, :], in_=STK[:, 128:160]),
                                  reads=[STKb])
            def blk_ufn(sel=None):
                wb = load_w(C_UFN, 512)
                for tb in range(NTB) if self.stage >= 3.5 else ():
                    for j in range(4):
                        ps = self.ps("g")
                        mm_fm(ps, 128, wb, slice(j * 128, (j + 1) * 128), tb)
                        s.op("dve", lambda q: q.tensor_copy(out=UF[:, j, :], in_=ps.ap[:]), reads=[ps], pwrites=[UFb])
                    for tt in range(4):
                        t = tb * 4 + tt
                        if lat:
                            ab = ABS[t % 2]
                            abb = ABSb[t % 2]
                        else:
                            ab = ABC[:, t, :]
                            abb = ABCb[t]
                        for half in range(2):
                            ps = self.ps("g")
                            for gg in range(2):
                                g = half * 2 + gg
                                s.op("pe", lambda q: q.matmul(ps.ap[:, gg * 256:(gg + 1) * 256],
                                                              lhsT=UF[:, g, tt * 128:(tt + 1) * 128], rhs=self.cs128.ap[:],
                                                              start=True, stop=True),
                                     reads=[UFb, self.cs128], pwrites=[ps])
                            dst = ab[:, half * 512:(half + 1) * 512]
                            if half == 0:
                                s.op("dve", lambda q: q.tensor_copy(out=dst, in_=ps.ap[:]), reads=[ps], pwrites=[abb])
                            else:
                                s.op("act", lambda q: q.activation(out=dst, in_=ps.ap[:], func=AF.Identity),
                                     reads=[ps], pwrites=[abb])
                        if lat:
                            pn = "pay_ab%d" % (t // 4)
                            s.dma("sp", lambda q: q.dma_start(out=d[pn][l][(t % 4) * 128:(t % 4 + 1) * 128, :], in_=ab[:]),
                                  reads=[abb], pwrites=[self.db[pn][l]])

            def emit_cc(names):
                for nm in names:
                    pay, g = d['pay_' + nm][l], d['g_' + nm][l]
                    s.cc(lambda q: q.collective_compute('AllGather', ALU.bypass, replica_groups=GROUPS,
                                                        ins=[pay[:, :]], outs=[g[:, :]]),
                         reads=[self.db['pay_' + nm][l]], writes=[self.db['g_' + nm][l]])
            if not lat:
                blk_qk()
                blk_v()
                blk_gates()
                blk_qlat()
                blk_ufn()
            else:
                self.wb_hooks = []
                blk_qk(sel=C_K)
                blk_v()
                self.lat_pay_kv(l, c, KT, KTb, V, Vb)
                self.wb_hooks.append([2, lambda: emit_cc(('k',))])
                self.wb_hooks.append([3, lambda: emit_cc(('v',))])
                blk_qlat()
                self.lat_pay_mla(l, c, CK, CKb, KR, KRb)
                blk_ufn()
                blk_qk(sel=C_Q)
                blk_gates()
                self.flush_wb_hook()
                self.cc_late = [(lambda: emit_cc(('mla',))), (lambda: emit_cc(('ab0',))), (lambda: emit_cc(('ab1',)))]
            if self.stage >= 4:
                if not lat:
                    self.ctx_attention(l, c, A, QT, QTb, KT, KTb, V, Vb, QL, QLb, CK, CKb, KR, KRb, ABC, ABCb)
            s.barrier()
        if lat and self.stage >= 4.1:
            self.lat_na(l, c, QT, QTb, KT, KTb, V, Vb, nap)
            s.barrier()
        A1.close()
        if lat and self.stage >= 4.2:
            self.lat_mla(l, c, QL, QLb)
            s.barrier()
        if lat and self.stage >= 4.3:
            self.lat_fourier(l, c)
            s.barrier()
        A0.close()
        if self.stage < 5:
            return

        with ExitStack() as Fz:
            WO = [self.sb(Fz, "WO%d" % r, [128, 4, 1024], BF16) for r in range(3)]
            WOb = [Buf(WO[r]) for r in range(3)]
            MG = self.sb(Fz, "MG", [128, 8, T], BF16)
            MGb = [Buf(MG) for _ in range(NTB)]
            SG = [self.sb(Fz, "SG%d" % r, [128, 512], F32) for r in range(3)]
            SGb = [Buf(SG[r]) for r in range(3)]
            MT = self.sb(Fz, "MT", [128, 512], F32)
            MTb = Buf(MT)
            MT2 = self.sb(Fz, "MT2", [128, 512], F32)
            MT2b = Buf(MT2)
            wm = w_in[l, :, C_MRG:D_IN].rearrange("(c p) (r n) -> p c r n", p=128, r=3)
            for cch in range(8):
                wb = self.next_wb()
                wv = wb.ap[:, :, 0:384].rearrange("p c (r n) -> p c r n", r=3)
                for r in range(3):
                    s.dma("pool", lambda q: q.dma_start(out=wv[:, :, r, :], in_=wm[:, :, r, cch * 128:(cch + 1) * 128]),
                          pwrites=[wb])
                if cch == 0:
                    for r, nm in enumerate(("w_o_na", "w_o_mla", "w_o_fn")):
                        s.dma("pool", lambda q: q.dma_start(out=WO[r][:], in_=d[nm][l].rearrange("(c p) n -> p c n", p=128)),
                              writes=[WOb[r]])
                for tb in range(NTB):
                    ts = TS(tb)
                    for r in range(3):
                        ps = self.ps("all")
                        for kc in range(8):
                            s.op("pe", lambda q: q.matmul(ps.ap[:], lhsT=wv[:, kc, r, :], rhs=XM[:, kc, ts],
                                                          start=(kc == 0), stop=(kc == 7)),
                                 reads=[wb, XMb[tb]], pwrites=[ps])
                        s.op("act", lambda q: q.activation(out=SG[r][:], in_=ps.ap[:], func=AF.Sigmoid),
                             reads=[ps], writes=[SGb[r]])
                    for r in range(3):
                        ps = self.ps("all")
                        for kc in range(4):
                            s.op("pe", lambda q: q.matmul(ps.ap[:], lhsT=WO[r][:, kc, cch * 128:(cch + 1) * 128],
                                                          rhs=OG[r][:, kc, ts], start=(kc == 0), stop=(kc == 3)),
                                 reads=[WOb[r], OGb[r][tb]], pwrites=[ps])
                        if r == 0:
                            s.op("dve", lambda q: q.tensor_tensor(out=MT[:], in0=ps.ap[:], in1=SG[0][:], op=ALU.mult),
                                 reads=[ps, SGb[0]], writes=[MTb])
                        else:
                            s.op("dve", lambda q: q.tensor_tensor(out=MT2[:], in0=ps.ap[:], in1=SG[r][:], op=ALU.mult),
                                 reads=[ps, SGb[r]], writes=[MT2b])
                            if r == 1:
                                s.op("dve", lambda q: q.tensor_tensor(out=MT[:], in0=MT[:], in1=MT2[:], op=ALU.add),
                                     reads=[MTb, MT2b], writes=[MTb])
                            else:
                                s.op("dve", lambda q: q.tensor_tensor(out=MG[:, cch, ts], in0=MT[:], in1=MT2[:], op=ALU.add),
                                     reads=[MTb, MT2b], pwrites=[MGb[tb]])
            for half in range(2):
                wb = self.next_wb()
                s.dma("pool", lambda q: q.dma_start(
                    out=wb.ap[:], in_=d["w_out"][l, :, half * 512:(half + 1) * 512].rearrange("(c p) n -> p c n", p=128)),
                    writes=[wb])
                for tb in range(NTB):
                    ts = TS(tb)
                    for jj in range(4):
                        cch = half * 4 + jj
                        ps = self.ps("all")
                        for kc in range(8):
                            s.op("pe", lambda q: q.matmul(ps.ap[:], lhsT=wb.ap[:, kc, jj * 128:(jj + 1) * 128],
                                                          rhs=MG[:, kc, ts], start=(kc == 0), stop=(kc == 7)),
                                 reads=[wb, MGb[tb]], pwrites=[ps])
                        s.op("dve", lambda q: q.scalar_tensor_tensor(out=X[:, cch, ts], in0=ps.ap[:],
                                                                     scalar=self.MOD.ap[:, l, 16 + cch, col:col + 1],
                                                                     in1=X[:, cch, ts], op0=ALU.mult, op1=ALU.add),
                             reads=[ps, self.MOD, Xb[tb]], pwrites=[Xb[tb]])
            s.barrier()

    def ctx_attention(self, l, c, A, QT, QTb, KT, KTb, V, Vb, QL, QLb, CK, CKb, KR, KRb, ABC, ABCb):
        nc, s, d = self.nc, self.s, self.d
        OG, OGb = c["OG"], c["OGb"]
        ts = slice(0, 512)
        items = []
        for h in range(8):
            j, par = h // 2, h % 2
            base = 64 * par
            hst = {}
            for bb in range(2):
                st = {}

                def front(h=h, j=j, par=par, base=base, bb=bb, st=st, hst=hst):
                    if bb == 0:
                        hst["po"] = self.ps("o")
                    ps = self.ps("g")
                    for kt in range(2):
                        k0 = bb * 256 + kt * 128
                        s.op("pe", lambda q: q.matmul(ps.ap[:, kt * 256:(kt + 1) * 256], lhsT=KT[base:base + 64, j, k0:k0 + 128],
                                                      rhs=QT[base:base + 64, j, bb * 256:(bb + 1) * 256], start=True, stop=True),
                             reads=[KTb[0], QTb[0]], pwrites=[ps])
                    pt = self.next_pt()
                    s.op("act", lambda q: q.activation(out=pt.ap[:], in_=ps.ap[:], func=AF.Exp), reads=[ps], writes=[pt])
                    st["pt"] = pt

                def back(h=h, j=j, par=par, bb=bb, st=st, hst=hst):
                    po, pt = hst["po"], st["pt"]
                    for kt in range(2):
                        t = bb * 2 + kt
                        if par == 0:
                            out = po.ap[0:65, bb * 256:(bb + 1) * 256]
                            lhsT = V[:, t, j * 192:j * 192 + 65]
                        else:
                            out = po.ap[0:128, bb * 256:(bb + 1) * 256]
                            lhsT = V[:, t, j * 192 + 64:j * 192 + 192]
                        s.op("pe", lambda q: q.matmul(out, lhsT=lhsT, rhs=pt.ap[:, kt * 256:(kt + 1) * 256],
                                                      start=(kt == 0), stop=(kt == 1)),
                             reads=[Vb[t], pt], pwrites=[po])
                fin = None
                if bb == 1:
                    fin = (lambda h=h, hst=hst: self.attn_finish(hst["po"], h, OG[0], OGb[0][0], ts))
                items.append((front, back, fin))
        self.run_pipeline2(items, LA=2, DL=2)

        WUQ = self.sb(A, "WUQ", [128, 2, 768], BF16)
        WUQb = Buf(WUQ)
        WK = self.sb(A, "WK", [128, 8, 128], BF16)
        WKb = Buf(WK)
        WV = self.sb(A, "WV", [128, 512], BF16)
        WVb = Buf(WV)
        VM = self.sb(A, "VM", [128, 4, 192], BF16)
        VMb = Buf(VM)
        KHs = [self.sb(A, "KH%d" % i, [96, 512], BF16) for i in range(2)]
        KHbs = [Buf(KHs[i]) for i in range(2)]
        QHs = [self.sb(A, "QH%d" % i, [96, 512], BF16) for i in range(2)]
        QHbs = [Buf(QHs[i]) for i in range(2)]
        VM2 = self.sb(A, "VM2", [128, 4, 192], BF16)
        VM2b = Buf(VM2)
        s.dma("pool", lambda q: q.dma_start(out=WUQ[:], in_=d["w_uq"][l].rearrange("(c p) n -> p c n", p=128)), writes=[WUQb])
        s.op("dve", lambda q: q.memset(WK[:], 0.0), writes=[WKb])
        wukv = d["w_ukv"][l].rearrange("c (h t x) -> c h t x", t=2, x=64)
        s.dma("pool", lambda q: q.dma_start(out=WK[:, :, 0:64], in_=wukv[:, :, 0, :]), pwrites=[WKb])
        s.dma("pool", lambda q: q.dma_start(out=WV[:].rearrange("p (h x) -> p h x", x=64), in_=wukv[:, :, 1, :]), writes=[WVb])
        VMs, VMbs = [VM, VM2], [VMb, VM2b]
        for i in range(2):
            s.op("dve", lambda q: q.memset(VMs[i][:], 0.0), writes=[VMbs[i]])
            s.op("dve", lambda q: q.memset(VMs[i][:, :, 64:65], 1.0), pwrites=[VMbs[i]])
        items = []
        for h in range(8):
            p, par = h // 2, h % 2
            vm, vmb = VMs[p % 2], VMbs[p % 2]
            KH, KHb, QH, QHb = KHs[h % 2], KHbs[h % 2], QHs[h % 2], QHbs[h % 2]
            hst = {}
            for bb in range(2):
                st = {}

                def front(h=h, p=p, par=par, bb=bb, st=st, hst=hst, vm=vm, vmb=vmb, KH=KH, KHb=KHb, QH=QH, QHb=QHb):
                    if bb == 0 and par == 0:
                        ps = self.ps("g")
                        for t in range(4):
                            s.op("pe", lambda q: q.matmul(ps.ap[:, t * 128:(t + 1) * 128], lhsT=CK[:, t * 128:(t + 1) * 128],
                                                          rhs=WV[:, p * 128:(p + 1) * 128], start=True, stop=True),
                                 reads=[CKb[0], WVb], pwrites=[ps])
                        pv = ps.ap[:].rearrange("p (t e x) -> p t e x", e=2, x=64)
                        s.op("dve", lambda q: q.tensor_copy(out=vm[:, :, 0:64], in_=pv[:, :, 0, :]), reads=[ps], pwrites=[vmb])
                        s.op("dve", lambda q: q.tensor_copy(out=vm[:, :, 128:192], in_=pv[:, :, 1, :]), reads=[ps], pwrites=[vmb])
                    if bb == 0:
                        hst["po"] = self.ps("o")
                        ps = self.ps("g")
                        s.op("pe", lambda q: q.matmul(ps.ap[0:128, :], lhsT=WK[:, h, :], rhs=CK[:, 0:512], start=True, stop=False),
                             reads=[WKb, CKb[0]], pwrites=[ps])
                        s.op("pe", lambda q: q.matmul(ps.ap[0:128, :], lhsT=self.sel32.ap[:, :], rhs=KR[0:32, 0:512],
                                                      start=False, stop=True),
                             reads=[self.sel32, KRb[0]], pwrites=[ps])
                        s.op("dve", lambda q: q.tensor_copy(out=KH[:, :], in_=ps.ap[0:96, :]), reads=[ps], writes=[KHb])
                        ps = self.ps("g")
                        for kc in range(2):
                            s.op("pe", lambda q: q.matmul(ps.ap[0:96, :], lhsT=WUQ[:, kc, h * 96:(h + 1) * 96], rhs=QL[:, kc, 0:512],
                                                          start=(kc == 0), stop=(kc == 1)),
                                 reads=[WUQb, QLb[0]], pwrites=[ps])
                        s.op("dve", lambda q: q.tensor_copy(out=QH[:, :], in_=ps.ap[0:96, :]), reads=[ps], writes=[QHb])
                    ps = self.ps("g")
                    for kt in range(2):
                        k0 = bb * 256 + kt * 128
                        s.op("pe", lambda q: q.matmul(ps.ap[:, kt * 256:(kt + 1) * 256], lhsT=KH[:, k0:k0 + 128],
                                                      rhs=QH[:, bb * 256:(bb + 1) * 256], start=True, stop=True),
                             reads=[KHb, QHb], pwrites=[ps])
                    pt = self.next_pt()
                    s.op("act", lambda q: q.activation(out=pt.ap[:], in_=ps.ap[:], func=AF.Exp, scale=MLA_SCALE),
                         reads=[ps], writes=[pt])
                    st["pt"] = pt

                def back(par=par, bb=bb, st=st, hst=hst, vm=vm, vmb=vmb):
                    po, pt = hst["po"], st["pt"]
                    for kt in range(2):
                        t = bb * 2 + kt
                        if par == 0:
                            out = po.ap[0:65, bb * 256:(bb + 1) * 256]
                            lhsT = vm[:, t, 0:65]
                        else:
                            out = po.ap[0:128, bb * 256:(bb + 1) * 256]
                            lhsT = vm[:, t, 64:192]
                        s.op("pe", lambda q: q.matmul(out, lhsT=lhsT, rhs=pt.ap[:, kt * 256:(kt + 1) * 256],
                                                      start=(kt == 0), stop=(kt == 1)),
                             reads=[vmb, pt], pwrites=[po])
                fin = None
                if bb == 1:
                    fin = (lambda h=h, hst=hst: self.attn_finish(hst["po"], h, OG[1], OGb[1][0], ts))
                items.append((front, back, fin))
        self.run_pipeline2(items, LA=2, DL=2)

        for g in range(4):
            po = self.ps("o")
            for bb in range(2):
                n = 0
                for nt in range(2):
                    t = bb * 2 + nt
                    for (off, mat) in ((0, self.c256), (128, self.ns256)):
                        s.op("pe", lambda q: q.matmul(po.ap[:, bb * 256:(bb + 1) * 256],
                                                      lhsT=ABC[:, t, g * 256 + off:g * 256 + off + 128],
                                                      rhs=mat.ap[:, nt, :], start=(n == 0), stop=(n == 3)),
                             reads=[ABCb[t], mat], pwrites=[po])
                        n += 1
            s.op("dve", lambda q: q.tensor_tensor(out=OG[2][:, g, ts], in0=po.ap[:], in1=OG[2][:, g, ts], op=ALU.mult),
                 reads=[po, OGb[2][0]], pwrites=[OGb[2][0]])


    def lat_pay_kv(self, l, c, KT, KTb, V, Vb):
        s, d, db = self.s, self.d, self.db
        pk = d["pay_k"][l].rearrange("(j p) n -> p j n", p=128)
        s.dma("sp", lambda q: q.dma_start(out=pk[:, :, 0:256], in_=KT[:, :, 0:256]), reads=[KTb[0]], pwrites=[db["pay_k"][l]])
        s.dma("sp", lambda q: q.dma_start(out=pk[:, :, 256:512], in_=KT[:, :, 768:1024]), reads=[KTb[1]], pwrites=[db["pay_k"][l]])
        pv = d["pay_v"][l].rearrange("(t p) n -> p t n", p=128)
        s.dma("sp", lambda q: q.dma_start(out=pv[:, 0:2, :], in_=V[:, 0:2, :]), reads=[Vb[0], Vb[1]], pwrites=[db["pay_v"][l]])
        s.dma("sp", lambda q: q.dma_start(out=pv[:, 2:4, :], in_=V[:, 6:8, :]), reads=[Vb[6], Vb[7]], pwrites=[db["pay_v"][l]])

    def lat_pay_mla(self, l, c, CK, CKb, KR, KRb):
        s, d, db = self.s, self.d, self.db
        s.dma("sp", lambda q: q.dma_start(out=d["pay_mla"][l][0:128, :], in_=CK[:, :]), reads=CKb, pwrites=[db["pay_mla"][l]])
        s.dma("sp", lambda q: q.dma_start(out=d["pay_mla"][l][128:160, :], in_=KR[0:32, :]), reads=KRb, pwrites=[db["pay_mla"][l]])

    def lat_na_prefetch(self, l, st):
        s, d = self.s, self.d
        KCT = self.sb(st, "KCT", [128, 4, 256], BF16)
        KCTb = Buf(KCT)
        VCX = self.sb(st, "VCX", [128, 2, 768], BF16)
        VCXb = Buf(VCX)
        BT = [self.sb(st, "BT%d" % i, [128, 2, 7, 128], BF16) for i in range(2)]
        BTb = [Buf(BT[i]) for i in range(2)]
        s.dma("pool", lambda q: q.dma_start(out=KCT[:], in_=d["cnakT"][l].rearrange("(j p) n -> p j n", p=128)), writes=[KCTb])
        s.op("dve", lambda q: q.memset(VCX[:], 0.0), writes=[VCXb])
        for p in range(4):
            s.op("dve", lambda q: q.memset(VCX[:, :, p * 192 + 64:p * 192 + 65], 1.0), pwrites=[VCXb])
        cv = d["cnav"][l].rearrange("(t p) (a e x) -> p t a e x", p=128, e=2, x=64)
        for t in range(2):
            vx = VCX[:, t, :].rearrange("p (a b) -> p a b", b=192)
            s.dma("pool", lambda q: q.dma_start(out=vx[:, :, 0:64], in_=cv[:, t, :, 0, :]), pwrites=[VCXb])
            s.dma("pool", lambda q: q.dma_start(out=vx[:, :, 128:192], in_=cv[:, t, :, 1, :]), pwrites=[VCXb])
        for p in range(2):
            s.dma("pool", lambda q: q.dma_start(out=BT[p][:], in_=d["nab"][l, :, 2 * p:2 * p + 2, :, :]), writes=[BTb[p]])
            s.op("act", lambda q: q.activation(out=BT[p][:], in_=BT[p][:], func=AF.Exp), reads=[BTb[p]], writes=[BTb[p]])
        return dict(KCT=KCT, KCTb=KCTb, VCX=VCX, VCXb=VCXb, BT=BT, BTb=BTb)

    def lat_na(self, l, c, QT, QTb, KT, KTb, V, Vb, nap):
        s, d, db = self.s, self.d, self.db
        KCT, KCTb, VCX, VCXb, BT, BTb = nap["KCT"], nap["KCTb"], nap["VCX"], nap["VCXb"], nap["BT"], nap["BTb"]
        OG, OGb = c["OG"], c["OGb"]
        SELR, SELRb = c["SELR"], c["SELRb"]
        with ExitStack() as N:
            HK = self.sb(N, "HK", [128, 4, 2, 256], BF16)
            HKb = Buf(HK)
            HV = self.sb(N, "HV", [128, 4, 768], BF16)
            HVb = Buf(HV)
            with ExitStack() as N2:
                KC = [self.sb(N2, "KC%d" % i, [128, 4, 512], BF16) for i in range(2)]
                KCb = [Buf(KC[i]) for i in range(2)]
                VC = [self.sb(N2, "VC%d" % i, [128, 4, 768], BF16) for i in range(2)]
                VCb = [Buf(VC[i]) for i in range(2)]
                gk = d["g_k"][l].rearrange("(c j p) n -> p j c n", c=4, j=4)
                for j in range(4):
                    kc, kcb = KC[j % 2], KCb[j % 2]
                    s.dma("sp", lambda q: q.dma_start(out=kc[:], in_=gk[:, j]), reads=[db["g_k"][l]], writes=[kcb])
                    for side in range(2):
                        cols = slice(256, 512) if side == 0 else slice(0, 256)
                        for cc in range(4):
                            sc = SELR[:, side * 4 + cc:side * 4 + cc + 1]
                            if cc == 0:
                                s.op("dve", lambda q: q.tensor_scalar(out=HK[:, j, side, :], in0=kc[:, cc, cols], scalar1=sc,
                                                                      scalar2=0.0, op0=ALU.mult, op1=ALU.add),
                                     reads=[kcb, SELRb], pwrites=[HKb])
                            else:
                                s.op("dve", lambda q: q.scalar_tensor_tensor(out=HK[:, j, side, :], in0=kc[:, cc, cols], scalar=sc,
                                                                             in1=HK[:, j, side, :], op0=ALU.mult, op1=ALU.add),
                                     reads=[kcb, SELRb, HKb], pwrites=[HKb])
                gv = d["g_v"][l].rearrange("(c t p) n -> p t c n", c=4, t=4)
                for ht in range(4):
                    side = ht // 2
                    src_t = (2 + ht) if side == 0 else (ht - 2)
                    vc, vcb = VC[ht % 2], VCb[ht % 2]
                    s.dma("sp", lambda q: q.dma_start(out=vc[:], in_=gv[:, src_t]), reads=[db["g_v"][l]], writes=[vcb])
                    for cc in range(4):
                        sc = SELR[:, side * 4 + cc:side * 4 + cc + 1]
                        if cc == 0:
                            s.op("dve", lambda q: q.tensor_scalar(out=HV[:, ht, :], in0=vc[:, cc, :], scalar1=sc, scalar2=0.0,
                                                                  op0=ALU.mult, op1=ALU.add),
                                 reads=[vcb, SELRb], pwrites=[HVb])
                        else:
                            s.op("dve", lambda q: q.scalar_tensor_tensor(out=HV[:, ht, :], in0=vc[:, cc, :], scalar=sc,
                                                                         in1=HV[:, ht, :], op0=ALU.mult, op1=ALU.add),
                                 reads=[vcb, SELRb, HVb], pwrites=[HVb])
                s.barrier()
            MK = self.sb(N, "MK", [128, 48, 128], BF16)
            MKb = Buf(MK)
            s.dma("sp", lambda q: q.dma_start(out=MK[:], in_=d["namask"][:, :, :]), writes=[MKb])
            all_items = []
            for p in range(4):
                bt, btb = BT[p % 2], BTb[p % 2]
                need_bt = [p >= 2]
                for h in (2 * p, 2 * p + 1):
                    par = h % 2
                    base = 64 * par
                    for grp in range(2):
                        po = self.ps("o")
                        items = []
                        for bi in range(4):
                            b = grp * 4 + bi
                            qs = slice(b * 128, (b + 1) * 128)
                            tiles = []
                            lts = [(b - 2 + jj, jj) for jj in range(5)]
                            if b == 0:
                                lts.append((3, 5))
                            if b == 7:
                                lts.append((4, 5))
                            for (lt, slot) in lts:
                                if 0 <= lt <= 7:
                                    kap = KT[base:base + 64, p, lt * 128:(lt + 1) * 128]
                                    kb_ = KTb[lt // 4]
                                    vt = V[:, lt, :]
                                    vb_ = Vb[lt]
                                elif lt < 0:
                                    kap = HK[base:base + 64, p, 0, (lt + 2) * 128:(lt + 3) * 128]
                                    kb_ = HKb
                                    vt = HV[:, lt + 2, :]
                                    vb_ = HVb
                                else:
                                    kap = HK[base:base + 64, p, 1, (lt - 8) * 128:(lt - 7) * 128]
                                    kb_ = HKb
                                    vt = HV[:, 2 + lt - 8, :]
                                    vb_ = HVb
                                tiles.append((kap, kb_, vt, vb_, lt - b + 3, b * 6 + slot))
                            for t in range(2):
                                tiles.append((KCT[base:base + 64, p, t * 128:(t + 1) * 128], KCTb, VCX[:, t, :], VCXb, None, None))
                            nt = len(tiles)
                            for g0 in range(0, nt, 4):
                                grpt = tiles[g0:g0 + 4]
                                st = {}

                                def front(grpt=grpt, st=st, qs=qs, grp=grp, base=base, p=p, par=par, bt=bt, btb=btb, need_bt=need_bt):
                                    if need_bt[0]:
                                        need_bt[0] = False
                                        s.dma("pool", lambda q: q.dma_start(out=bt[:], in_=d["nab"][l, :, 2 * p:2 * p + 2, :, :]),
                                              writes=[btb])
                                        s.op("act", lambda q: q.activation(out=bt[:], in_=bt[:], func=AF.Exp), reads=[btb], writes=[btb])
                                    ps = self.ps("g")
                                    for i, (kap, kb_, vt, vb_, dj, ms) in enumerate(grpt):
                                        reg = ps.ap[:, i * 128:(i + 1) * 128]
                                        s.op("pe", lambda q: q.matmul(reg, lhsT=kap, rhs=QT[base:base + 64, p, qs], start=True,
                                                                      stop=True),
                                             reads=[kb_, QTb[grp]], pwrites=[ps])
                                    w = len(grpt) * 128
                                    pt = self.next_pt()
                                    s.op("act", lambda q: q.activation(out=pt.ap[:, 0:w], in_=ps.ap[:, 0:w], func=AF.Exp),
                                         reads=[ps], writes=[pt])
                                    i = 0
                                    while i < len(grpt):
                                        dj, ms = grpt[i][4], grpt[i][5]
                                        if dj is None:
                                            i += 1
                                            continue
                                        n = 1
                                        while (i + n < len(grpt) and grpt[i + n][4] is not None
                                               and grpt[i + n][4] == dj + n and grpt[i + n][5] == ms + n):
                                            n += 1
                                        pv3 = pt.ap[:, i * 128:(i + n) * 128].rearrange("p (a b) -> p a b", b=128)
                                        s.op("dve", lambda q: q.tensor_tensor(out=pv3, in0=pv3, in1=bt[:, par, dj:dj + n, :], op=ALU.mult),
                                             reads=[pt, btb], writes=[pt])
                                        s.op("pool", lambda q: q.tensor_tensor(out=pv3, in0=pv3, in1=MK[:, ms:ms + n, :], op=ALU.mult),
                                             reads=[pt, MKb], writes=[pt])
                                        i += n
                                    st["pt"] = pt

                                def back(grpt=grpt, st=st, g0=g0, nt=nt, bi=bi, po=po, par=par, p=p):
                                    pt = st["pt"]
                                    for i, (kap, kb_, vt, vb_, dj, ms) in enumerate(grpt):
                                        n = g0 + i
                                        if par == 0:
                                            out = po.ap[0:65, bi * 128:(bi + 1) * 128]
                                            lhsT = vt[:, p * 192:p * 192 + 65]
                                        else:
                                            out = po.ap[0:128, bi * 128:(bi + 1) * 128]
                                            lhsT = vt[:, p * 192 + 64:p * 192 + 192]
                                        s.op("pe", lambda q: q.matmul(out, lhsT=lhsT, rhs=pt.ap[:, i * 128:(i + 1) * 128],
                                                                      start=(n == 0), stop=(n == nt - 1)),
                                             reads=[vb_, pt], pwrites=[po])
                                items.append([front, back, None])
                        items[-1][2] = (lambda po=po, h=h, grp=grp: self.attn_finish(po, h, OG[0], OGb[0][grp],
                                                                                      slice(grp * 512, (grp + 1) * 512)))
                        all_items.extend(items)
            late = getattr(self, "cc_late", None) or []
            self.cc_late = None
            npos = len(all_items)
            for i, f in enumerate(late):
                pos = (i * npos) // 4
                fr = all_items[pos][0]
                all_items[pos][0] = (lambda fr=fr, f=f: (f(), fr()))
            self.run_pipeline2(all_items, LA=4, DL=2)

    def lat_mla(self, l, c, QL, QLb):
        s, d, db = self.s, self.d, self.db
        OG, OGb = c["OG"], c["OGb"]
        ROPE, ROPEb = c["ROPE"], c["ROPEb"]
        NK = 4352
        NKT = 34
        with ExitStack() as M:
            CKA = self.sb(M, "CKA", [128, NK], BF16)
            CKAb = Buf(CKA)
            KRA = self.sb(M, "KRA", [32, NK], BF16)
            KRAb = Buf(KRA)
            KH = self.sb(M, "KH", [96, NK], BF16)
            KHb = Buf(KH)
            VM = [self.sb(M, "VM%d" % i, [128, NKT, 192], BF16) for i in range(2)]
            VMb = [Buf(VM[i]) for i in range(2)]
            WUQ = self.sb(M, "WUQ", [128, 2, 768], BF16)
            WUQb = Buf(WUQ)
            WUQP = self.sb(M, "WUQP", [128, 2, 768], BF16)
            WUQPb = Buf(WUQP)
            WK = self.sb(M, "WK", [128, 8, 128], BF16)
            WKb = Buf(WK)
            WV = self.sb(M, "WV", [128, 512], BF16)
            WVb = Buf(WV)
            QH = [self.sb(M, "QH%d" % i, [96, 1024], BF16) for i in range(2)]
            QHb = [Buf(QH[i]) for i in range(2)]
            for cc in range(4):
                s.dma("sp", lambda q: q.dma_start(out=CKA[:, cc * 1024:(cc + 1) * 1024], in_=d["g_mla"][l][cc * 160:cc * 160 + 128, :]),
                      reads=[db["g_mla"][l]], pwrites=[CKAb])
                s.dma("sp", lambda q: q.dma_start(out=KRA[0:32, cc * 1024:(cc + 1) * 1024],
                                                  in_=d["g_mla"][l][cc * 160 + 128:cc * 160 + 160, :]),
                      reads=[db["g_mla"][l]], pwrites=[KRAb])
            s.dma("pool", lambda q: q.dma_start(out=CKA[:, 4096:NK], in_=d["cckvT"][l]), pwrites=[CKAb])
            s.dma("pool", lambda q: q.dma_start(out=KRA[0:32, 4096:NK], in_=d["ckrT"][l]), pwrites=[KRAb])
            s.dma("pool", lambda q: q.dma_start(out=WUQ[:], in_=d["w_uq"][l].rearrange("(c p) n -> p c n", p=128)), writes=[WUQb])
            s.dma("pool", lambda q: q.dma_start(out=WUQP[:], in_=d["w_uqp"][l].rearrange("(c p) n -> p c n", p=128)), writes=[WUQPb])
            s.op("dve", lambda q: q.memset(WK[:], 0.0), writes=[WKb])
            wukv = d["w_ukv"][l].rearrange("c (h t x) -> c h t x", t=2, x=64)
            s.dma("pool", lambda q: q.dma_start(out=WK[:, :, 0:64], in_=wukv[:, :, 0, :]), pwrites=[WKb])
            s.dma("pool", lambda q: q.dma_start(out=WV[:].rearrange("p (h x) -> p h x", x=64), in_=wukv[:, :, 1, :]), writes=[WVb])
            for i in range(2):
                s.op("dve", lambda q: q.memset(VM[i][:], 0.0), writes=[VMb[i]])
                s.op("dve", lambda q: q.memset(VM[i][:, :, 64:65], 1.0), pwrites=[VMb[i]])
            KH2 = self.sb(M, "KH2", [96, NK], BF16)
            KHs, KHbs = [KH, KH2], [KHb, Buf(KH2)]

            def gen_v(p):
                vm, vmb = VM[p % 2], VMb[p % 2]
                for k0 in range(0, NKT, 4):
                    nt = min(4, NKT - k0)
                    ps = self.ps("g")
                    for t in range(nt):
                        kt = k0 + t
                        s.op("pe", lambda q: q.matmul(ps.ap[:, t * 128:(t + 1) * 128], lhsT=CKA[:, kt * 128:(kt + 1) * 128],
                                                      rhs=WV[:, p * 128:(p + 1) * 128], start=True, stop=True),
                             reads=[CKAb, WVb], pwrites=[ps])
                    pv = ps.ap[:, 0:nt * 128].rearrange("p (t e x) -> p t e x", e=2, x=64)
                    s.op("dve", lambda q: q.tensor_copy(out=vm[:, k0:k0 + nt, 0:64], in_=pv[:, :, 0, :]), reads=[ps], pwrites=[vmb])
                    s.op("dve", lambda q: q.tensor_copy(out=vm[:, k0:k0 + nt, 128:192], in_=pv[:, :, 1, :]), reads=[ps], pwrites=[vmb])

            def gen_kq(h):
                kh, khb = KHs[h % 2], KHbs[h % 2]
                qh, qhb = QH[h % 2], QHb[h % 2]
                for k0 in range(0, NK, 512):
                    n = min(512, NK - k0)
                    ps = self.ps("g")
                    s.op("pe", lambda q: q.matmul(ps.ap[0:128, 0:n], lhsT=WK[:, h, :], rhs=CKA[:, k0:k0 + n], start=True, stop=False),
                         reads=[WKb, CKAb], pwrites=[ps])
                    s.op("pe", lambda q: q.matmul(ps.ap[0:128, 0:n], lhsT=self.sel32.ap[:, :], rhs=KRA[0:32, k0:k0 + n],
                                                  start=False, stop=True),
                         reads=[self.sel32, KRAb], pwrites=[ps])
                    s.op("dve", lambda q: q.tensor_copy(out=kh[:, k0:k0 + n], in_=ps.ap[0:96, 0:n]), reads=[ps], pwrites=[khb])
                for tb in range(2):
                    ts = slice(tb * 512, (tb + 1) * 512)
                    ps = self.ps("g")
                    ps2 = self.ps("g")
                    for (pp, ww, wwb) in ((ps, WUQ, WUQb), (ps2, WUQP, WUQPb)):
                        for kc in range(2):
                            s.op("pe", lambda q: q.matmul(pp.ap[0:96, :], lhsT=ww[:, kc, h * 96:(h + 1) * 96], rhs=QL[:, kc, ts],
                                                          start=(kc == 0), stop=(kc == 1)),
                                 reads=[wwb, QLb[tb]], pwrites=[pp])
                    s.op("dve", lambda q: q.tensor_copy(out=qh[0:64, ts], in_=ps.ap[0:64, :]), reads=[ps], pwrites=[qhb])
                    t1 = self.next_tmp()
                    t2 = self.next_tmp()
                    s.op("dve", lambda q: q.tensor_tensor(out=t1.ap[64:96, :], in0=ps.ap[64:96, :], in1=ROPE[64:96, 0, ts], op=ALU.mult),
                         reads=[ps, ROPEb], writes=[t1])
                    s.op("dve", lambda q: q.tensor_tensor(out=t2.ap[64:96, :], in0=ps2.ap[64:96, :], in1=ROPE[64:96, 1, ts], op=ALU.mult),
                         reads=[ps2, ROPEb], writes=[t2])
                    s.op("dve", lambda q: q.tensor_tensor(out=qh[64:96, ts], in0=t1.ap[64:96, :], in1=t2.ap[64:96, :], op=ALU.add),
                         reads=[t1, t2], pwrites=[qhb])

            gen_v(0)
            gen_kq(0)
            items = []
            for h in range(8):
                p, par = h // 2, h % 2
                vm, vmb = VM[p % 2], VMb[p % 2]
                kh, khb = KHs[h % 2], KHbs[h % 2]
                qh, qhb = QH[h % 2], QHb[h % 2]
                for tb in range(2):
                    ts = slice(tb * 512, (tb + 1) * 512)
                    po = self.ps("o")
                    for kt in range(NKT):
                        st = {}
                        pre = None
                        if par == 1 and tb == 0 and kt == 0 and p + 1 < 4:
                            pre = (lambda p=p: gen_v(p + 1))
                        if tb == 1 and kt == NKT - 12 and h + 1 < 8:
                            pre = (lambda h=h: gen_kq(h + 1))

                        def front(kt=kt, st=st, ts=ts, kh=kh, khb=khb, qh=qh, qhb=qhb, pre=pre):
                            if pre is not None:
                                pre()
                            ps = self.ps("g")
                            s.op("pe", lambda q: q.matmul(ps.ap[:], lhsT=kh[:, kt * 128:(kt + 1) * 128], rhs=qh[:, ts],
                                                          start=True, stop=True),
                                 reads=[khb, qhb], writes=[ps])
                            pt = self.next_pt()
                            s.op("act", lambda q: q.activation(out=pt.ap[:], in_=ps.ap[:], func=AF.Exp, scale=MLA_SCALE),
                                 reads=[ps], writes=[pt])
                            st["pt"] = pt

                        def back(kt=kt, st=st, po=po, par=par, vm=vm, vmb=vmb):
                            pt = st["pt"]
                            if par == 0:
                                out = po.ap[0:65, :]
                                lhsT = vm[:, kt, 0:65]
                            else:
                                out = po.ap[0:128, :]
                                lhsT = vm[:, kt, 64:192]
                            s.op("pe", lambda q: q.matmul(out, lhsT=lhsT, rhs=pt.ap[:], start=(kt == 0), stop=(kt == NKT - 1)),
                                 reads=[vmb, pt], pwrites=[po])
                        fin = None
                        if kt == NKT - 1:
                            fin = (lambda po=po, h=h, tb=tb, ts=ts: self.attn_finish(po, h, OG[1], OGb[1][tb], ts))
                        items.append((front, back, fin))
            self.run_pipeline2(items, LA=3, DL=2)


    def lat_fourier(self, l, c):
        s, d, db = self.s, self.d, self.db
        OG, OGb = c["OG"], c["OGb"]
        with ExitStack() as Fs:
            DC = [self.sb(Fs, "DC%d" % i, [128, 4, 512], BF16) for i in range(2)]
            DCb = [Buf(DC[i]) for i in range(2)]
            DS = [self.sb(Fs, "DS%d" % i, [128, 4, 512], BF16) for i in range(2)]
            DSb = [Buf(DS[i]) for i in range(2)]
            AB = [self.sb(Fs, "AB%d" % i, [128, 4, 1024], BF16) for i in range(2)]
            ABb = [Buf(AB[i]) for i in range(2)]
            it = 0
            for tb in range(2):
                ts = slice(tb * 512, (tb + 1) * 512)
                acc = [self.PS[4 + g] for g in range(4)]
                for nb in range(8):
                    dc, dcb, ds_, dsb, ab, abb = DC[it % 2], DCb[it % 2], DS[it % 2], DSb[it % 2], AB[it % 2], ABb[it % 2]
                    it += 1
                    rows = slice(nb * 512, (nb + 1) * 512)
                    s.dma("sp", lambda q: q.dma_start(out=dc[:], in_=d["dftc"][rows, ts].rearrange("(i p) n -> p i n", p=128)), writes=[dcb])
                    s.dma("sp", lambda q: q.dma_start(out=ds_[:], in_=d["dftns"][rows, ts].rearrange("(i p) n -> p i n", p=128)), writes=[dsb])
                    gn = "g_ab%d" % (nb % 2)
                    grow = slice((nb // 2) * 512, (nb // 2 + 1) * 512)
                    s.dma("sp", lambda q: q.dma_start(out=ab[:], in_=d[gn][l][grow, :].rearrange("(i p) n -> p i n", p=128)),
                          reads=[db[gn][l]], writes=[abb])
                    for i in range(4):
                        for g in range(4):
                            first = (nb == 0 and i == 0)
                            last = (nb == 7 and i == 3)
                            s.op("pe", lambda q: q.matmul(acc[g].ap[:], lhsT=ab[:, i, g * 256:g * 256 + 128], rhs=dc[:, i, :],
                                                          start=first, stop=False),
                                 reads=[abb, dcb], pwrites=[acc[g]])
                            s.op("pe", lambda q: q.matmul(acc[g].ap[:], lhsT=ab[:, i, g * 256 + 128:g * 256 + 256], rhs=ds_[:, i, :],
                                                          start=False, stop=last),
                                 reads=[abb, dsb], pwrites=[acc[g]])
                for g in range(4):
                    s.op("dve", lambda q: q.tensor_tensor(out=OG[2][:, g, ts], in0=acc[g].ap[:], in1=OG[2][:, g, ts], op=ALU.mult),
                         reads=[acc[g], OGb[2][tb]], pwrites=[OGb[2][tb]])


def _rope_perm():
    P = np.array([i + 8 if (i % 16) < 8 else i - 8 for i in range(32)])
    sgn = np.array([-1.0 if (i % 16) < 8 else 1.0 for i in range(32)], np.float32)
    return P, sgn


def _consts():
    bf = ml_dtypes.bfloat16
    c = {}
    c["identb"] = np.eye(128, dtype=np.float32).astype(bf)
    n = np.arange(128)
    ang = 2 * np.pi * np.outer(n, n) / 128.0
    c["cs128"] = (np.concatenate([np.cos(ang), np.sin(ang)], axis=1) / np.sqrt(128.0)).astype(np.float32).astype(bf)
    n = np.arange(256)
    ang = 2 * np.pi * np.outer(n, n) / 256.0
    c["c256"] = (np.cos(ang) / 16.0).astype(np.float32).astype(bf)
    c["ns256"] = (-np.sin(ang) / 16.0).astype(np.float32).astype(bf)
    sel = np.zeros((32, 128), np.float32)
    sel[np.arange(32), 64 + np.arange(32)] = 1.0
    c["sel32"] = sel.astype(bf)
    return c


def _lat_consts(q):
    bf = ml_dtypes.bfloat16
    c = {}
    P, sgn = _rope_perm()
    t = np.arange(1024 * q, 1024 * q + 1024)
    row = (t // 64).astype(np.float32)
    colp = (t % 64).astype(np.float32)
    half = 16
    inv = (10000.0 ** (-np.arange(0, half, 2, dtype=np.float32) / half)).astype(np.float32)
    ar = row[:, None] * inv[None, :]
    ac = colp[:, None] * inv[None, :]
    ang = np.concatenate([ar, ar, ac, ac], axis=-1)
    cos = np.cos(ang).astype(np.float32)
    sin = (np.sin(ang).astype(np.float32) * sgn[None, :]).astype(np.float32)
    rope = np.zeros((2, 128, 1024), np.float32)
    rope[0, 0:32] = cos.T
    rope[0, 64:96] = cos.T
    rope[1, 0:32] = sin.T
    rope[1, 64:96] = sin.T
    c["ropeT"] = rope
    n = np.arange(4096, dtype=np.float64)
    k = np.arange(1024 * q, 1024 * q + 1024, dtype=np.float64)
    ang = 2 * np.pi * ((np.outer(n, k)) % 4096) / 4096.0
    c["dftc"] = (np.cos(ang) / 64.0).astype(np.float32).astype(bf)
    c["dftns"] = (-np.sin(ang) / 64.0).astype(np.float32).astype(bf)
    kk = np.arange(128)
    kr, kcol = kk // 64, kk % 64
    mask = np.zeros((128, 48, 128), np.float32)
    for b in range(8):
        for slot in range(6):
            if slot < 5:
                lt = b - 2 + slot
            elif b == 0:
                lt = 3
            elif b == 7:
                lt = 4
            else:
                continue
            krow = 16 * q + 2 * lt + kr
            r = 16 * q + 2 * b + kr
            rs = np.clip(r - 4, 0, 56)
            row_ok = ((krow[:, None] >= 0) & (krow[:, None] < 64) &
                      (krow[:, None] >= rs[None, :]) & (krow[:, None] < rs[None, :] + 8))
            cs = np.clip(kcol - 8, 0, 48)
            col_ok = (kcol[:, None] >= cs[None, :]) & (kcol[:, None] < cs[None, :] + 16)
            mask[:, b * 6 + slot, :] = np.where(row_ok & col_ok, 1.0, 0.0)
    c["namask"] = mask.astype(bf)
    sel = np.zeros((128, 8), np.float32)
    if q - 1 >= 0:
        sel[:, q - 1] = 1.0
    if q + 1 <= 3:
        sel[:, 4 + q + 1] = 1.0
    c["selr"] = sel
    return c


_NC_CACHE = {}


def _get_nc(key=("full",)):
    if key not in _NC_CACHE:
        if key[0] == "full":
            kb = KB(run_ctx=True, run_lat=True)
        else:
            kb = KB(**dict(key[1]))
        _NC_CACHE[key] = kb.build()
    return _NC_CACHE[key]


def make_in_maps(inp):
    f = lambda a: np.ascontiguousarray(np.asarray(a, dtype=np.float32))
    x_prompt, x_sample = f(inp["x_prompt"]), f(inp["x_sample"])
    P, sgn = _rope_perm()
    w_in = f(inp["w_in"])
    w_uq = f(inp["w_uq"])
    shared = {}
    shared["w_ada"] = f(inp["w_ada"])
    shared["b_adaT"] = f(f(inp["b_ada"]).reshape(L, 24, 128).transpose(2, 0, 1))
    shared["norm_gT"] = f(f(inp["norm_g"]).reshape(L, 8, 128).transpose(2, 0, 1))
    shared["fin_gT"] = f(f(inp["final_norm_g"]).reshape(8, 128).T)
    shared["qn_gT"] = f(f(inp["q_norm_g"]).reshape(L, 2, 128).transpose(2, 0, 1))
    shared["kvn_gT"] = f(f(inp["kv_norm_g"]).T)
    shared["kvn_bc"] = f(np.broadcast_to(f(inp["kv_norm_g"])[None, :, :], (128, L, 128)))
    shared["w_in"] = w_in
    shared["w_krp"] = f(w_in[:, :, C_KR:C_KR + 32][:, :, P])
    shared["w_uq"] = w_uq
    wq = w_uq.reshape(L, 256, 8, 96).copy()
    wq[:, :, :, 64:96] = wq[:, :, :, 64:96][:, :, :, P]
    shared["w_uqp"] = f(wq.reshape(L, 256, 768))
    shared["w_ukv"] = f(inp["w_ukv"])
    shared["w_o_na"] = f(inp["w_o_na"])
    shared["w_o_mla"] = f(inp["w_o_mla"])
    shared["w_o_fn"] = f(inp["w_o_fourier"])
    shared["w_out"] = f(inp["w_out"])
    shared.update(_consts())
    nb = f(inp["na_bias"])
    kr = np.arange(128) // 64
    kc = np.arange(128) % 64
    dj = np.arange(7)
    dr = 2 * (dj[None, :, None] - 3) + kr[:, None, None] - kr[None, None, :]
    dc = np.clip(kc[:, None, None] - kc[None, None, :] + 15, 0, 30) + 0 * dr
    drc = np.clip(dr + 7, 0, 14)
    nab = nb[:, :, drc, dc]
    shared["nab"] = f(nab.transpose(0, 2, 1, 3, 4))
    cna_k, cna_v = f(inp["cache_na_k"]), f(inp["cache_na_v"])
    cckv, ckr = f(inp["cache_mla_ckv"]), f(inp["cache_mla_krope"])
    cvec, c_ctx = f(inp["c"]), f(inp["c_ctx"])
    maps = []
    latc = [_lat_consts(q) for q in range(4)]
    for core in range(8):
        b, q = core // 4, core % 4
        m = dict(shared)
        m["xc"] = f(x_prompt[2 * core:2 * core + 2].reshape(512, D).T)
        m["xl"] = f(x_sample[b, 1024 * q:1024 * q + 1024].T)
        m["cond"] = f(np.stack([c_ctx, cvec[b]], axis=1))
        m.update(latc[q])
        m["cnakT"] = f(cna_k[b].reshape(L, 256, 512).transpose(0, 2, 1))
        m["cnav"] = f(cna_v[b].reshape(L, 256, 512))
        m["cckvT"] = f(cckv[b].transpose(0, 2, 1))
        m["ckrT"] = f(ckr[b].transpose(0, 2, 1))
        maps.append(m)
    return maps


def assemble(res):
    r = res.results
    y_prompt = np.stack([r[c]["yc"].T.reshape(2, 256, D) for c in range(8)]).reshape(16, 256, D)
    y_sample = np.stack([np.concatenate([r[4 * b + q]["yl"].T for q in range(4)], axis=0) for b in range(2)])

    def cache(name, w):
        return np.concatenate([r[c][name].reshape(L, 2, 256, w).transpose(1, 0, 2, 3) for c in range(8)], axis=0)
    nk = cache("onk", 512).reshape(16, L, 256, 8, 64)
    nv = cache("onv", 512).reshape(16, L, 256, 8, 64)
    nckv = cache("ockv", 128)
    nkr = cache("okr", 32)
    return tuple(np.ascontiguousarray(a.astype(np.float32)) for a in (y_prompt, y_sample, nk, nv, nckv, nkr))


def kernel(**inputs):
    nc = _get_nc()
    in_maps = make_in_maps(inputs)
    res = run_bass_kernel_spmd(nc, in_maps, core_ids=list(range(8)))
    return assemble(res)
```

```python
import os
import numpy as np
import ml_dtypes
from contextlib import ExitStack
import concourse.bass as bass
import concourse.mybir as mybir
from concourse.bass_utils import run_bass_kernel_spmd

F32 = mybir.dt.float32
BF16 = mybir.dt.bfloat16
AF = mybir.ActivationFunctionType
ALU = mybir.AluOpType

L = 4
D = 1024
D_IN = 7072
NA_SCALE = 64 ** -0.5
MLA_SCALE = 96 ** -0.5
EPS = 1e-6
C_Q, C_K, C_V, C_GNA, C_QLAT, C_CKV, C_KR, C_GMLA, C_UFN, C_GFN, C_MRG = (
    0, 512, 1024, 1536, 2048, 2304, 2432, 2464, 2976, 3488, 4000)
NEG = -30000.0
GROUPS = [[0, 1, 2, 3], [4, 5, 6, 7]]


class Buf:
    __slots__ = ("ap", "w", "r", "pm", "name", "fw", "excl")

    def __init__(self, ap, name=""):
        self.ap = ap
        self.w = {}
        self.r = {}
        self.pm = False
        self.fw = {}
        self.excl = False
        self.name = name


class Sched:
    LIMIT = 12000
    NLANES = 8

    def __init__(self, nc, es):
        self.nc = nc
        self.es = es
        self.eng = dict(pe=nc.tensor, act=nc.scalar, dve=nc.vector, pool=nc.gpsimd, sp=nc.sync)
        self.sems = []
        self.cur = {}
        self.known = {e: {} for e in self.eng}
        self.lanes = {e: [] for e in self.eng}
        self.rr = {e: 0 for e in self.eng}
        self.pe_sems = set()
        self.nins = 0

    def new_sem(self):
        h = self.es.enter_context(self.nc.semaphore("sm%d" % len(self.sems)))
        self.sems.append(h)
        return len(self.sems) - 1

    def _deps(self, reads, writes, pwrites):
        deps = {}

        def add(d):
            for k, v in d.items():
                if deps.get(k, 0) < v:
                    deps[k] = v
        for b in reads:
            add(b.w)
            if b.excl:
                add(b.r)
        for b in writes:
            add(b.w)
            add(b.r)
        for b in pwrites:
            add(b.r)
            add(b.fw)
            if not b.pm:
                add(b.w)
        return deps

    def _wait(self, e, deps):
        kn = self.known[e]
        for sem, val in deps.items():
            if e == "pe" and sem in self.pe_sems:
                continue
            if kn.get(sem, 0) >= val:
                continue
            self.eng[e].wait_ge(self.sems[sem], val)
            kn[sem] = val
            self.nins += 1

    def _mark(self, stamp, reads, writes, pwrites):
        s, v = stamp
        for b in writes:
            b.w = {s: v}
            b.fw = {s: v}
            b.r = {}
            b.pm = False
        for b in pwrites:
            b.w[s] = v
            b.pm = True
        for b in reads:
            b.r[s] = v

    def op(self, e, fn, reads=(), writes=(), pwrites=()):
        self._wait(e, self._deps(reads, writes, pwrites))
        c = self.cur.get(e)
        if c is None or c[1] >= self.LIMIT:
            c = [self.new_sem(), 0]
            self.cur[e] = c
            if e == "pe":
                self.pe_sems.add(c[0])
        c[1] += 1
        ins = fn(self.eng[e])
        ins.then_inc(self.sems[c[0]], 1)
        self.nins += 1
        self._mark((c[0], c[1]), reads, writes, pwrites)

    def dma(self, e, fn, reads=(), writes=(), pwrites=(), inc=16):
        deps = self._deps(reads, writes, pwrites)
        lanes = self.lanes[e]
        if len(lanes) < self.NLANES:
            lanes.append([self.new_sem(), 0])
            lane = lanes[-1]
        else:
            lane = lanes[self.rr[e] % self.NLANES]
            self.rr[e] += 1
        if lane[1] > 0 and deps.get(lane[0], 0) < lane[1]:
            deps[lane[0]] = lane[1]
        self._wait(e, deps)
        lane[1] += inc
        ins = fn(self.eng[e])
        ins.then_inc(self.sems[lane[0]], inc)
        self.nins += 1
        self._mark((lane[0], lane[1]), reads, writes, pwrites)

    def cc(self, fn, reads=(), writes=()):
        deps = self._deps(reads, writes, ())
        if not hasattr(self, "cclane"):
            self.cclane = [self.new_sem(), 0]
        lane = self.cclane
        if lane[1] > 0 and deps.get(lane[0], 0) < lane[1]:
            deps[lane[0]] = lane[1]
        self._wait("pool", deps)
        lane[1] += 1
        ins = fn(self.eng["pool"])
        ins.then_inc(self.sems[lane[0]], 1)
        self.nins += 1
        self._mark((lane[0], lane[1]), reads, writes, ())

    def barrier(self):
        allst = {}
        for e, c in self.cur.items():
            allst[c[0]] = c[1]
        for e, lanes in self.lanes.items():
            for ln in lanes:
                if ln[1] > 0:
                    allst[ln[0]] = ln[1]
        for e in self.eng:
            d = dict(allst)
            c = self.cur.get(e)
            if e == "pe" and c is not None:
                d.pop(c[0], None)
            self._wait(e, d)

    def finish(self):
        allst = {}
        for e, lanes in self.lanes.items():
            for ln in lanes:
                if ln[1] > 0:
                    allst[ln[0]] = ln[1]
        for e, c in self.cur.items():
            allst[c[0]] = c[1]
        if hasattr(self, "cclane") and self.cclane[1] > 0:
            allst[self.cclane[0]] = self.cclane[1]
        d = dict(allst)
        self._wait("sp", d)


class KB:
    def __init__(self, run_ctx=True, run_lat=True, nlayers=L, dbg=False, lw=L, stage=99):
        self.LW = lw
        self.stage = stage
        self.run_ctx = run_ctx
        self.run_lat = run_lat
        self.nlayers = nlayers
        self.nc = bass.Bass("TRN2", target_bir_lowering=False)
        self.es = ExitStack()
        self.s = Sched(self.nc, self.es)
        self.tcount = 0

    def din(self, name, shape, dt=F32):
        return self.nc.dram_tensor(name, list(shape), dt, kind="ExternalInput").ap()

    def dout(self, name, shape, dt=F32):
        return self.nc.dram_tensor(name, list(shape), dt, kind="ExternalOutput").ap()

    def dint(self, name, shape, dt=BF16):
        return self.nc.dram_tensor(name, list(shape), dt).ap()

    def sb(self, st, name, shape, dt):
        self.tcount += 1
        return st.enter_context(self.nc.sbuf_tensor("%s_%d" % (name, self.tcount), list(shape), dt))

    def ps(self, grp):
        idxs = self.psgrp[grp]
        i = idxs[self.psrr[grp] % len(idxs)]
        self.psrr[grp] += 1
        return self.PS[i]

    def build(self):
        nc, s, es = self.nc, self.s, self.es
        NL = self.nlayers
        d = {}
        d["xc"] = self.din("xc", [D, 512])
        d["xl"] = self.din("xl", [D, 1024])
        d["cond"] = self.din("cond", [D, 2])
        d["w_ada"] = self.din("w_ada", [self.LW, D, 3 * D])
        d["b_adaT"] = self.din("b_adaT", [128, L, 24])
        d["norm_gT"] = self.din("norm_gT", [128, L, 8])
        d["fin_gT"] = self.din("fin_gT", [128, 8])
        d["qn_gT"] = self.din("qn_gT", [128, L, 2])
        d["kvn_gT"] = self.din("kvn_gT", [128, L])
        d["kvn_bc"] = self.din("kvn_bc", [128, L, 128])
        d["w_in"] = self.din("w_in", [self.LW, D, D_IN])
        d["w_krp"] = self.din("w_krp", [self.LW, D, 32])
        d["w_uq"] = self.din("w_uq", [self.LW, 256, 768])
        d["w_uqp"] = self.din("w_uqp", [self.LW, 256, 768])
        d["w_ukv"] = self.din("w_ukv", [self.LW, 128, 1024])
        d["w_o_na"] = self.din("w_o_na", [self.LW, 512, D])
        d["w_o_mla"] = self.din("w_o_mla", [self.LW, 512, D])
        d["w_o_fn"] = self.din("w_o_fn", [self.LW, 512, D])
        d["w_out"] = self.din("w_out", [self.LW, D, D])
        d["identb"] = self.din("identb", [128, 128], BF16)
        d["cs128"] = self.din("cs128", [128, 256], BF16)
        d["c256"] = self.din("c256", [256, 256], BF16)
        d["ns256"] = self.din("ns256", [256, 256], BF16)
        d["sel32"] = self.din("sel32", [32, 128], BF16)
        d["ropeT"] = self.din("ropeT", [2, 128, 1024])
        d["dftc"] = self.din("dftc", [4096, 1024], BF16)
        d["dftns"] = self.din("dftns", [4096, 1024], BF16)
        d["nab"] = self.din("nab", [self.LW, 128, 8, 7, 128])
        d["namask"] = self.din("namask", [128, 48, 128], BF16)
        d["selr"] = self.din("selr", [128, 8])
        d["cnakT"] = self.din("cnakT", [L, 512, 256])
        d["cnav"] = self.din("cnav", [L, 256, 512])
        d["cckvT"] = self.din("cckvT", [L, 128, 256])
        d["ckrT"] = self.din("ckrT", [L, 32, 256])
        d["yc"] = self.dout("yc", [D, 512])
        d["yl"] = self.dout("yl", [D, 1024])
        d["onk"] = self.dout("onk", [L, 512, 512])
        d["onv"] = self.dout("onv", [L, 512, 512])
        d["ockv"] = self.dout("ockv", [L, 512, 128])
        d["okr"] = self.dout("okr", [L, 512, 32])
        d["pay_mla"] = [self.dint("pay_mla%d" % l, [160, 1024]) for l in range(L)]
        d["g_mla"] = [self.dint("g_mla%d" % l, [640, 1024]) for l in range(L)]
        d["pay_k"] = [self.dint("pay_k%d" % l, [512, 512]) for l in range(L)]
        d["g_k"] = [self.dint("g_k%d" % l, [2048, 512]) for l in range(L)]
        d["pay_v"] = [self.dint("pay_v%d" % l, [512, 768]) for l in range(L)]
        d["g_v"] = [self.dint("g_v%d" % l, [2048, 768]) for l in range(L)]
        d["pay_ab0"] = [self.dint("pay_ab0_%d" % l, [512, 1024]) for l in range(L)]
        d["pay_ab1"] = [self.dint("pay_ab1_%d" % l, [512, 1024]) for l in range(L)]
        d["g_ab0"] = [self.dint("g_ab0_%d" % l, [2048, 1024]) for l in range(L)]
        d["g_ab1"] = [self.dint("g_ab1_%d" % l, [2048, 1024]) for l in range(L)]
        self.d = d
        self.db = {k: [Buf(a) for a in d[k]] for k in ("pay_mla", "g_mla", "pay_k", "g_k", "pay_v", "g_v", "pay_ab0", "pay_ab1", "g_ab0", "g_ab1")}

        self.PS = [Buf(es.enter_context(nc.psum_tensor("ps%d" % i, [128, 512], F32)), "ps%d" % i) for i in range(8)]
        for b in self.PS:
            b.excl = True
        self.psgrp = {"g": [0, 1, 2, 3], "o": [4, 5], "x": [6, 7], "all": list(range(8))}
        self.psrr = {k: 0 for k in self.psgrp}

        P = es
        self.identb = Buf(self.sb(P, "identb", [128, 128], BF16))
        self.onesb = Buf(self.sb(P, "onesb", [128, 3, 128], BF16))
        self.onesf = Buf(self.sb(P, "onesf", [128, 128], F32))
        self.cs128 = Buf(self.sb(P, "cs128", [128, 256], BF16))
        self.c256 = Buf(self.sb(P, "c256", [128, 2, 256], BF16))
        self.ns256 = Buf(self.sb(P, "ns256", [128, 2, 256], BF16))
        self.sel32 = Buf(self.sb(P, "sel32", [32, 128], BF16))
        self.condt = Buf(self.sb(P, "condt", [128, 8, 2], F32))
        self.condb = Buf(self.sb(P, "condb", [128, 8, 2], BF16))
        self.bada = Buf(self.sb(P, "bada", [128, L, 24], F32))
        self.normg = Buf(self.sb(P, "normg", [128, L, 8], F32))
        self.fing = Buf(self.sb(P, "fing", [128, 8], F32))
        self.qng = Buf(self.sb(P, "qng", [128, L, 2], F32))
        self.kvng = Buf(self.sb(P, "kvng", [128, L], F32))
        self.kvnbc = Buf(self.sb(P, "kvnbc", [128, L, 128], F32))
        self.MOD = Buf(self.sb(P, "mod", [128, L, 24, 2], F32))
        self.GS = Buf(self.sb(P, "gs", [128, L, 8, 2], F32))
        self.WB = [Buf(self.sb(P, "wb%d" % i, [128, 8, 512], BF16)) for i in range(3)]
        self.wbi = 0
        self.SQ = [Buf(self.sb(P, "sq%d" % i, [128, 512], BF16)) for i in range(2)]
        self.TMP = [Buf(self.sb(P, "tmp%d" % i, [128, 512], F32)) for i in range(2)]
        self.RS = Buf(self.sb(P, "rs", [128, 512], F32))
        self.RD = Buf(self.sb(P, "rd", [128, 512], F32))
        self.BCS = Buf(self.sb(P, "bcs", [128, 512], F32))
        self.TMPO = Buf(self.sb(P, "tmpo", [128, 512], F32))
        self.PT = [Buf(self.sb(P, "pt%d" % i, [128, 512], BF16)) for i in range(5)]
        self.pending = None
        self.pti = 0
        self.sqi = 0
        self.tmi = 0

        def ld(buf, src, e="sp"):
            s.dma(e, lambda q: q.dma_start(out=buf.ap[:], in_=src), writes=[buf])
        ld(self.identb, d["identb"][:, :])
        ld(self.cs128, d["cs128"][:, :])
        ld(self.c256, d["c256"].rearrange("(t p) n -> p t n", p=128))
        ld(self.ns256, d["ns256"].rearrange("(t p) n -> p t n", p=128))
        ld(self.sel32, d["sel32"][:, :])
        ld(self.condt, d["cond"].rearrange("(c p) n -> p c n", p=128))
        ld(self.bada, d["b_adaT"][:, :, :])
        ld(self.normg, d["norm_gT"][:, :, :])
        ld(self.fing, d["fin_gT"][:, :])
        ld(self.qng, d["qn_gT"][:, :, :])
        ld(self.kvng, d["kvn_gT"][:, :])
        ld(self.kvnbc, d["kvn_bc"][:, :, :])
        for i, v in enumerate((1.0 / 1024, 1.0 / 256, 1.0 / 128)):
            s.op("dve", lambda q: q.memset(self.onesb.ap[:, i, :], v), pwrites=[self.onesb])
        s.op("dve", lambda q: q.memset(self.onesf.ap[:], 1.0), writes=[self.onesf])
        s.op("act", lambda q: q.activation(out=self.condb.ap[:], in_=self.condt.ap[:], func=AF.Silu),
             reads=[self.condt], writes=[self.condb])

        for l in range(NL if not self.run_ctx else min(1, NL)):
            for jb in range(6):
                self.adaln_compute(l, jb, self.adaln_load(l, jb))
            self.adaln_gs(l)

        if self.run_ctx and self.stage >= 1:
            self.chain("ctx")
        if self.run_lat and self.stage >= 1:
            self.chain("lat")
        s.finish()
        return nc

    def tick_wb_hook(self):
        hooks = getattr(self, "wb_hooks", [])
        self.wb_hooks = []
        for h in hooks:
            h[0] -= 1
            if h[0] <= 0:
                h[1]()
            else:
                self.wb_hooks.append(h)

    def flush_wb_hook(self):
        hooks = getattr(self, "wb_hooks", [])
        self.wb_hooks = []
        for h in hooks:
            h[1]()

    def adaln_load(self, l, jb):
        s, d = self.s, self.d
        wb = self.next_wb()
        s.dma("pool", lambda q: q.dma_start(
            out=wb.ap[:], in_=d["w_ada"][l, :, jb * 512:(jb + 1) * 512].rearrange("(c p) n -> p c n", p=128)),
            writes=[wb])
        return wb

    def adaln_compute(self, l, jb, wb):
        s = self.s
        for jj in range(4):
            j = jb * 4 + jj
            ps = self.ps("g")
            for kc in range(8):
                s.op("pe", lambda q: q.matmul(ps.ap[:, 0:2], lhsT=wb.ap[:, kc, jj * 128:(jj + 1) * 128],
                                              rhs=self.condb.ap[:, kc, :], start=(kc == 0), stop=(kc == 7)),
                     reads=[wb, self.condb], pwrites=[ps])
            s.op("dve", lambda q: q.tensor_scalar(out=self.MOD.ap[:, l, j, :], in0=ps.ap[:, 0:2],
                                                  scalar1=self.bada.ap[:, l, j:j + 1], scalar2=0.0,
                                                  op0=ALU.add, op1=ALU.add),
                 reads=[ps, self.bada], pwrites=[self.MOD])

    def adaln_gs(self, l):
        s = self.s
        for kc in range(8):
            s.op("dve", lambda q: q.tensor_scalar(out=self.GS.ap[:, l, kc, :], in0=self.MOD.ap[:, l, 8 + kc, :],
                                                  scalar1=1.0, scalar2=self.normg.ap[:, l, kc:kc + 1],
                                                  op0=ALU.add, op1=ALU.mult),
                 reads=[self.MOD, self.normg], pwrites=[self.GS])

    def next_wb(self):
        wb = self.WB[self.wbi % 3]
        self.wbi += 1
        return wb

    def next_pt(self):
        b = self.PT[self.pti % 5]
        self.pti += 1
        return b

    def run_pipeline(self, items, LA=3):
        n = len(items)
        for step in range(n + LA):
            if step < n:
                items[step][0]()
            if step == min(2, n - 1):
                self.flush_pending()
            if step >= LA:
                items[step - LA][1]()

    def run_pipeline2(self, items, LA=2, DL=2):
        n = len(items)
        due = []
        for step in range(n + LA + DL + 1):
            if step < n:
                items[step][0]()
            if LA <= step < n + LA:
                it = items[step - LA]
                it[1]()
                if it[2] is not None:
                    due.append((step + DL, it[2]))
            while due and due[0][0] <= step:
                due.pop(0)[1]()

    def flush_pending(self):
        if self.pending is not None:
            p = self.pending
            self.pending = None
            p()

    def next_sq(self):
        b = self.SQ[self.sqi % 2]
        self.sqi += 1
        return b

    def next_tmp(self):
        b = self.TMP[self.tmi % 2]
        self.tmi += 1
        return b

    def rstd_of(self, chunks, reads, ones_idx, n=512):
        s = self.s
        ps = self.ps("x")
        nk = len(chunks)
        for i, ch in enumerate(chunks):
            sq = self.next_sq()
            s.op("act", lambda q: q.activation(out=sq.ap[:, 0:n], in_=ch, func=AF.Square), reads=reads, writes=[sq])
            s.op("pe", lambda q: q.matmul(ps.ap[:, 0:n], lhsT=self.onesb.ap[:, ones_idx, :], rhs=sq.ap[:, 0:n],
                                          start=(i == 0), stop=(i == nk - 1)),
                 reads=[sq, self.onesb], pwrites=[ps])
        s.op("act", lambda q: q.activation(out=self.RS.ap[:, 0:n], in_=ps.ap[:, 0:n], func=AF.Sqrt, bias=EPS, scale=1.0),
             reads=[ps], writes=[self.RS])
        s.op("dve", lambda q: q.reciprocal(out=self.RS.ap[:, 0:n], in_=self.RS.ap[:, 0:n]),
             reads=[self.RS], writes=[self.RS])
        return self.RS

    def attn_finish(self, po, h, OG, OGb, ts, n=512):
        s = self.s
        j, par = h // 2, h % 2
        base = 64 * par
        dp = 64 if par == 0 else 0
        s.op("dve", lambda q: q.reciprocal(out=self.RD.ap[dp:dp + 1, 0:n], in_=po.ap[dp:dp + 1, 0:n]),
             reads=[po], writes=[self.RD])
        bc = self.ps("x")
        s.op("pe", lambda q: q.matmul(bc.ap[:, 0:n], lhsT=self.onesf.ap[dp:dp + 1, :], rhs=self.RD.ap[dp:dp + 1, 0:n],
                                      start=True, stop=True),
             reads=[self.RD, self.onesf], writes=[bc])
        s.op("dve", lambda q: q.tensor_copy(out=self.BCS.ap[base:base + 64, 0:n], in_=bc.ap[base:base + 64, 0:n]),
             reads=[bc], writes=[self.BCS])
        s.op("dve", lambda q: q.tensor_tensor(out=self.TMPO.ap[base:base + 64, 0:n], in0=po.ap[base:base + 64, 0:n],
                                              in1=self.BCS.ap[base:base + 64, 0:n], op=ALU.mult),
             reads=[po, self.BCS], writes=[self.TMPO])
        s.op("dve", lambda q: q.tensor_tensor(out=OG[base:base + 64, j, ts], in0=self.TMPO.ap[base:base + 64, 0:n],
                                              in1=OG[base:base + 64, j, ts], op=ALU.mult),
             reads=[self.TMPO, OGb], pwrites=[OGb])

    def chain(self, mode):
        nc, s, d = self.nc, self.s, self.d
        lat = (mode == "lat")
        T = 1024 if lat else 512
        NTB = T // 512
        NT = T // 128
        col = 1 if lat else 0
        with ExitStack() as C:
            X = self.sb(C, "X", [128, 8, T], F32)
            Xb = [Buf(X) for _ in range(NTB)]
            XM = self.sb(C, "XM", [128, 8, T], BF16)
            XMb = [Buf(XM) for _ in range(NTB)]
            OG = [self.sb(C, "OG%d" % r, [128, 4, T], BF16) for r in range(3)]
            OGb = [[Buf(OG[r]) for _ in range(NTB)] for r in range(3)]
            xin = d["xl"] if lat else d["xc"]
            for tb in range(NTB):
                s.dma("sp", lambda q: q.dma_start(
                    out=X[:, :, tb * 512:(tb + 1) * 512],
                    in_=xin[:, tb * 512:(tb + 1) * 512].rearrange("(c p) n -> p c n", p=128)), writes=[Xb[tb]])
            ctxs = dict(lat=lat, T=T, NTB=NTB, NT=NT, col=col, X=X, Xb=Xb, XM=XM, XMb=XMb, OG=OG, OGb=OGb)
            if lat:
                ROPE = self.sb(C, "rope", [128, 2, 1024], F32)
                ROPEb = Buf(ROPE)
                s.dma("sp", lambda q: q.dma_start(out=ROPE[:], in_=d["ropeT"].rearrange("a p n -> p a n")), writes=[ROPEb])
                SELR = self.sb(C, "selr", [128, 8], F32)
                SELRb = Buf(SELR)
                s.dma("sp", lambda q: q.dma_start(out=SELR[:], in_=d["selr"][:, :]), writes=[SELRb])
                ctxs.update(ROPE=ROPE, ROPEb=ROPEb, SELR=SELR, SELRb=SELRb)
            for l in range(self.nlayers):
                self.layer(l, ctxs)
            yout = d["yl"] if lat else d["yc"]
            for tb in range(NTB):
                ts = slice(tb * 512, (tb + 1) * 512)
                rs = self.rstd_of([X[:, kc, ts] for kc in range(8)], [Xb[tb]], 0)
                for kc in range(8):
                    tmp = self.next_tmp()
                    s.op("dve", lambda q: q.tensor_tensor(out=tmp.ap[:], in0=X[:, kc, ts], in1=rs.ap[:], op=ALU.mult),
                         reads=[Xb[tb], rs], writes=[tmp])
                    s.op("act", lambda q: q.activation(out=tmp.ap[:], in_=tmp.ap[:], func=AF.Identity,
                                                       scale=self.fing.ap[:, kc:kc + 1]),
                         reads=[tmp, self.fing], writes=[tmp])
                    s.dma("sp", lambda q: q.dma_start(out=yout[kc * 128:(kc + 1) * 128, ts], in_=tmp.ap[:]), reads=[tmp])
            s.barrier()

    def layer(self, l, c):
        nc, s, d = self.nc, self.s, self.d
        lat, T, NTB, NT, col = c["lat"], c["T"], c["NTB"], c["NT"], c["col"]
        X, Xb, XM, XMb, OG, OGb = c["X"], c["Xb"], c["XM"], c["XMb"], c["OG"], c["OGb"]
        w_in = d["w_in"]

        def TS(tb):
            return slice(tb * 512, (tb + 1) * 512)

        for tb in range(NTB):
            ts = TS(tb)
            rs = self.rstd_of([X[:, kc, ts] for kc in range(8)], [Xb[tb]], 0)
            for kc in range(8):
                tmp = self.next_tmp()
                s.op("dve", lambda q: q.tensor_tensor(out=tmp.ap[:], in0=X[:, kc, ts], in1=rs.ap[:], op=ALU.mult),
                     reads=[Xb[tb], rs], writes=[tmp])
                s.op("act", lambda q: q.activation(out=XM[:, kc, ts], in_=tmp.ap[:], func=AF.Identity,
                                                   scale=self.GS.ap[:, l, kc, col:col + 1],
                                                   bias=self.MOD.ap[:, l, kc, col:col + 1]),
                     reads=[tmp, self.GS, self.MOD], pwrites=[XMb[tb]])

        if self.stage < 3:
            return
        A0 = ExitStack()
        A1 = ExitStack()
        QL = self.sb(A0, "QL", [128, 2, T], BF16)
        QLb = [Buf(QL) for _ in range(NTB)]
        QT = self.sb(A1, "QT", [128, 4, T], BF16)
        QTb = [Buf(QT) for _ in range(NTB)]
        KT = self.sb(A1, "KT", [128, 4, T], BF16)
        KTb = [Buf(KT) for _ in range(NTB)]
        V = self.sb(A1, "V", [128, NT, 768], BF16)
        Vb = [Buf(V) for _ in range(NT)]
        nap = None
        if lat and self.stage >= 4.1:
            nap = self.lat_na_prefetch(l, A1)
        with ExitStack() as A:
            CK = self.sb(A, "CK", [128, T], BF16)
            CKb = [Buf(CK) for _ in range(NTB)]
            KR = self.sb(A, "KR", [32, T], BF16)
            KRb = [Buf(KR) for _ in range(NTB)]
            UF = self.sb(A, "UF", [128, 4, 512], BF16)
            UFb = Buf(UF)
            QLR = self.sb(A, "QLR", [128, 3, 512], F32)
            QLRb = Buf(QLR)
            WKRP = self.sb(A, "WKRP", [128, 8, 32], BF16)
            WKRPb = Buf(WKRP)
            ABS = [self.sb(A, "ABS%d" % i, [128, 1024], BF16) for i in range(2)]
            ABSb = [Buf(ABS[i]) for i in range(2)]
            if not lat:
                ABC = self.sb(A, "ABC", [128, 4, 1024], BF16)
                ABCb = [Buf(ABC) for _ in range(4)]
                STG = [self.sb(A, "STG%d" % i, [128, 512], F32) for i in range(2)]
                STGb = [Buf(STG[i]) for i in range(2)]
                STK = self.sb(A, "STK", [128, 160], F32)
                STKb = Buf(STK)
                STS = self.sb(A, "STS", [128, 4], F32)
                STSb = Buf(STS)
            s.op("dve", lambda q: q.memset(V[:, :, :], 0.0), writes=Vb)
            for p in range(4):
                s.op("dve", lambda q: q.memset(V[:, :, p * 192 + 64:p * 192 + 65], 1.0), pwrites=Vb)
            if lat:
                s.dma("pool", lambda q: q.dma_start(out=WKRP[:], in_=d["w_krp"][l].rearrange("(c p) n -> p c n", p=128)),
                      writes=[WKRPb])

            def load_w(c0, n):
                self.tick_wb_hook()
                wb = self.next_wb()
                s.dma("pool", lambda q: q.dma_start(
                    out=wb.ap[:, :, 0:n], in_=w_in[l, :, c0:c0 + n].rearrange("(c p) n -> p c n", p=128)), writes=[wb])
                return wb

            def mm_fm(ps, M, wb, wsl, tb, extra_reads=()):
                for kc in range(8):
                    s.op("pe", lambda q: q.matmul(ps.ap[0:M, :], lhsT=wb.ap[:, kc, wsl], rhs=XM[:, kc, TS(tb)],
                                                  start=(kc == 0), stop=(kc == 7)),
                         reads=[wb, XMb[tb]], pwrites=[ps])

            def blk_qk(sel=None):
                for (c0, dst, dstb, scl) in ((C_Q, QT, QTb, NA_SCALE), (C_K, KT, KTb, 1.0)) if self.stage >= 3.1 else ():
                    if sel is not None and c0 != sel:
                        continue
                    wb = load_w(c0, 512)
                    for tb in range(NTB):
                        for j in range(4):
                            ps = self.ps("g")
                            mm_fm(ps, 128, wb, slice(j * 128, (j + 1) * 128), tb)
                            if j % 2 == 0:
                                s.op("dve", lambda q: q.tensor_scalar(out=dst[:, j, TS(tb)], in0=ps.ap[:], scalar1=scl,
                                                                      scalar2=0.0, op0=ALU.mult, op1=ALU.add),
                                     reads=[ps], pwrites=[dstb[tb]])
                            else:
                                s.op("act", lambda q: q.activation(out=dst[:, j, TS(tb)], in_=ps.ap[:], func=AF.Identity,
                                                                   scale=scl),
                                     reads=[ps], pwrites=[dstb[tb]])
                    if (not lat) and c0 == C_K:
                        for t in range(NT):
                            ps = self.ps("g")
                            for kc in range(8):
                                s.op("pe", lambda q: q.matmul(ps.ap[:], lhsT=XM[:, kc, t * 128:(t + 1) * 128],
                                                              rhs=wb.ap[:, kc, :], start=(kc == 0), stop=(kc == 7)),
                                     reads=[wb, XMb[0]], pwrites=[ps])
                            st = STGb[t % 2]
                            s.op("act", lambda q: q.activation(out=st.ap[:], in_=ps.ap[:], func=AF.Identity),
                                 reads=[ps], writes=[st])
                            s.dma("sp", lambda q: q.dma_start(out=d["onk"][l, t * 128:(t + 1) * 128, :], in_=st.ap[:]),
                                  reads=[st])
            def blk_v(sel=None):
                wb = load_w(C_V, 512)
                for t in range(NT) if self.stage >= 3.2 else ():
                    ps = self.ps("g")
                    tb = t // 4
                    for kc in range(8):
                        s.op("pe", lambda q: q.matmul(ps.ap[:], lhsT=XM[:, kc, t * 128:(t + 1) * 128], rhs=wb.ap[:, kc, :],
                                                      start=(kc == 0), stop=(kc == 7)),
                             reads=[wb, XMb[tb]], pwrites=[ps])
                    vv = V[:, t, :].rearrange("p (a b) -> p a b", b=192)
                    pv = ps.ap[:].rearrange("p (a e x) -> p a e x", e=2, x=64)
                    s.op("dve", lambda q: q.tensor_copy(out=vv[:, :, 0:64], in_=pv[:, :, 0, :]), reads=[ps], pwrites=[Vb[t]])
                    s.op("dve", lambda q: q.tensor_copy(out=vv[:, :, 128:192], in_=pv[:, :, 1, :]), reads=[ps], pwrites=[Vb[t]])
                    if not lat and not os.environ.get("KDBG_NOONV"):
                        st = STGb[t % 2]
                        s.op("dve", lambda q: q.tensor_copy(out=st.ap[:], in_=ps.ap[:]), reads=[ps], writes=[st])
                        s.dma("sp", lambda q: q.dma_start(out=d["onv"][l, t * 128:(t + 1) * 128, :], in_=st.ap[:]), reads=[st])
            def blk_gates(sel=None):
                for (c0, r) in ((C_GNA, 0), (C_GMLA, 1), (C_GFN, 2)) if self.stage >= 3.3 else ():
                    wb = load_w(c0, 512)
                    for tb in range(NTB):
                        for j in range(4):
                            ps = self.ps("g")
                            mm_fm(ps, 128, wb, slice(j * 128, (j + 1) * 128), tb)
                            s.op("act", lambda q: q.activation(out=OG[r][:, j, TS(tb)], in_=ps.ap[:], func=AF.Silu),
                                 reads=[ps], pwrites=[OGb[r][tb]])
            def blk_qlat(sel=None):
                wb = load_w(C_QLAT, 416)
                for tb in range(NTB) if self.stage >= 3.4 else ():
                    ts = TS(tb)
                    for j in range(3):
                        ps = self.ps("g")
                        mm_fm(ps, 128, wb, slice(j * 128, (j + 1) * 128), tb)
                        s.op("dve", lambda q: q.tensor_copy(out=QLR[:, j, :], in_=ps.ap[:]), reads=[ps], pwrites=[QLRb])
                    rs = self.rstd_of([QLR[:, 0, :], QLR[:, 1, :]], [QLRb], 1)
                    for j in range(2):
                        tmp = self.next_tmp()
                        s.op("dve", lambda q: q.tensor_tensor(out=tmp.ap[:], in0=QLR[:, j, :], in1=rs.ap[:], op=ALU.mult),
                             reads=[QLRb, rs], writes=[tmp])
                        s.op("act", lambda q: q.activation(out=QL[:, j, ts], in_=tmp.ap[:], func=AF.Identity,
                                                           scale=self.qng.ap[:, l, j:j + 1]),
                             reads=[tmp, self.qng], pwrites=[QLb[tb]])
                    rs = self.rstd_of([QLR[:, 2, :]], [QLRb], 2)
                    tmp = self.next_tmp()
                    s.op("dve", lambda q: q.tensor_tensor(out=tmp.ap[:], in0=QLR[:, 2, :], in1=rs.ap[:], op=ALU.mult),
                         reads=[QLRb, rs], writes=[tmp])
                    s.op("act", lambda q: q.activation(out=CK[:, ts], in_=tmp.ap[:], func=AF.Identity,
                                                       scale=self.kvng.ap[:, l:l + 1]),
                         reads=[tmp, self.kvng], pwrites=[CKb[tb]])
                    ps = self.ps("g")
                    mm_fm(ps, 32, wb, slice(384, 416), tb)
                    if lat:
                        ps2 = self.ps("g")
                        for kc in range(8):
                            s.op("pe", lambda q: q.matmul(ps2.ap[0:32, :], lhsT=WKRP[:, kc, :], rhs=XM[:, kc, ts],
                                                          start=(kc == 0), stop=(kc == 7)),
                                 reads=[WKRPb, XMb[tb]], pwrites=[ps2])
                        t1 = self.next_tmp()
                        t2 = self.next_tmp()
                        s.op("dve", lambda q: q.tensor_tensor(out=t1.ap[0:32, :], in0=ps.ap[0:32, :],
                                                              in1=c["ROPE"][0:32, 0, ts], op=ALU.mult),
                             reads=[ps, c["ROPEb"]], writes=[t1])
                        s.op("dve", lambda q: q.tensor_tensor(out=t2.ap[0:32, :], in0=ps2.ap[0:32, :],
                                                              in1=c["ROPE"][0:32, 1, ts], op=ALU.mult),
                             reads=[ps2, c["ROPEb"]], writes=[t2])
                        s.op("dve", lambda q: q.tensor_tensor(out=KR[0:32, ts], in0=t1.ap[0:32, :], in1=t2.ap[0:32, :],
                                                              op=ALU.add),
                             reads=[t1, t2], pwrites=[KRb[tb]])
                    else:
                        s.op("act", lambda q: q.activation(out=KR[0:32, ts], in_=ps.ap[0:32, :], func=AF.Identity),
                             reads=[ps], pwrites=[KRb[tb]])
                    if not lat:
                        for t in range(4):
                            ps = self.ps("g")
                            for kc in range(8):
                                s.op("pe", lambda q: q.matmul(ps.ap[:, 0:160], lhsT=XM[:, kc, t * 128:(t + 1) * 128],
                                                              rhs=wb.ap[:, kc, 256:416], start=(kc == 0), stop=(kc == 7)),
                                     reads=[wb, XMb[0]], pwrites=[ps])
                            s.op("act", lambda q: q.activation(out=STK[:, 0:128], in_=ps.ap[:, 0:128], func=AF.Square),
                                 reads=[ps], writes=[STKb])
                            s.op("dve", lambda q: q.reduce_sum(out=STS[:, 0:1], in_=STK[:, 0:128], axis=mybir.AxisListType.X),
                                 reads=[STKb], writes=[STSb])
                            s.op("act", lambda q: q.activation(out=STS[:, 1:2], in_=STS[:, 0:1], func=AF.Sqrt, bias=EPS,
                                                               scale=1.0 / 128),
                                 reads=[STSb], writes=[STSb])
                            s.op("dve", lambda q: q.reciprocal(out=STS[:, 2:3], in_=STS[:, 1:2]), reads=[STSb], writes=[STSb])
                            s.op("dve", lambda q: q.tensor_scalar(out=STK[:, 0:128], in0=ps.ap[:, 0:128],
                                                                  scalar1=STS[:, 2:3], scalar2=0.0, op0=ALU.mult, op1=ALU.add),
                                 reads=[ps, STSb], writes=[STKb])
                            s.op("dve", lambda q: q.tensor_tensor(out=STK[:, 0:128], in0=STK[:, 0:128],
                                                                  in1=self.kvnbc.ap[:, l, :], op=ALU.mult),
                                 reads=[STKb, self.kvnbc], writes=[STKb])
                            s.op("act", lambda q: q.activation(out=STK[:, 128:160], in_=ps.ap[:, 128:160], func=AF.Identity),
                                 reads=[ps], pwrites=[STKb])
                            s.dma("sp", lambda q: q.dma_start(out=d["ockv"][l, t * 128:(t + 1) * 128, :], in_=STK[:, 0:128]),
                                  reads=[STKb])
                            s.dma("sp", lambda q: q.dma_start(out=d["okr"][l, t * 128:(t + 1) * 128, :], in_=STK[:, 128:160]),
                                  reads=[STKb])
            def blk_ufn(sel=None):
                wb = load_w(C_UFN, 512)
                for tb in range(NTB) if self.stage >= 3.5 else ():
                    for j in range(4):
                        ps = self.ps("g")
                        mm_fm(ps, 128, wb, slice(j * 128, (j + 1) * 128), tb)
                        s.op("dve", lambda q: q.tensor_copy(out=UF[:, j, :], in_=ps.ap[:]), reads=[ps], pwrites=[UFb])
                    for tt in range(4):
                        t = tb * 4 + tt
                        if lat:
                            ab = ABS[t % 2]
                            abb = ABSb[t % 2]
                        else:
                            ab = ABC[:, t, :]
                            abb = ABCb[t]
                        for half in range(2):
                            ps = self.ps("g")
                            for gg in range(2):
                                g = half * 2 + gg
                                s.op("pe", lambda q: q.matmul(ps.ap[:, gg * 256:(gg + 1) * 256],
                                                              lhsT=UF[:, g, tt * 128:(tt + 1) * 128], rhs=self.cs128.ap[:],
                                                              start=True, stop=True),
                                     reads=[UFb, self.cs128], pwrites=[ps])
                            dst = ab[:, half * 512:(half + 1) * 512]
                            if half == 0:
                                s.op("dve", lambda q: q.tensor_copy(out=dst, in_=ps.ap[:]), reads=[ps], pwrites=[abb])
                            else:
                                s.op("act", lambda q: q.activation(out=dst, in_=ps.ap[:], func=AF.Identity),
                                     reads=[ps], pwrites=[abb])
                        if lat:
                            pn = "pay_ab%d" % (t // 4)
                            s.dma("sp", lambda q: q.dma_start(out=d[pn][l][(t % 4) * 128:(t % 4 + 1) * 128, :], in_=ab[:]),
                                  reads=[abb], pwrites=[self.db[pn][l]])

            def emit_cc(names):
                for nm in names:
                    pay, g = d['pay_' + nm][l], d['g_' + nm][l]
                    s.cc(lambda q: q.collective_compute('AllGather', ALU.bypass, replica_groups=GROUPS,
                                                        ins=[pay[:, :]], outs=[g[:, :]]),
                         reads=[self.db['pay_' + nm][l]], writes=[self.db['g_' + nm][l]])
            if not lat:
                blk_qk()
                blk_v()
                blk_gates()
                blk_qlat()
                blk_ufn()
            else:
                self.wb_hooks = []
                blk_qk(sel=C_K)
                blk_v()
                self.lat_pay_kv(l, c, KT, KTb, V, Vb)
                self.wb_hooks.append([2, lambda: emit_cc(('k',))])
                self.wb_hooks.append([3, lambda: emit_cc(('v',))])
                blk_qlat()
                self.lat_pay_mla(l, c, CK, CKb, KR, KRb)
                blk_ufn()
                blk_qk(sel=C_Q)
                blk_gates()
                self.flush_wb_hook()
                self.cc_late = [(lambda: emit_cc(('mla',))), (lambda: emit_cc(('ab0',))), (lambda: emit_cc(('ab1',)))]
            if self.stage >= 4:
                if not lat:
                    self.ctx_attention(l, c, A, QT, QTb, KT, KTb, V, Vb, QL, QLb, CK, CKb, KR, KRb, ABC, ABCb)
            s.barrier()
        if lat and self.stage >= 4.1:
            self.lat_na(l, c, QT, QTb, KT, KTb, V, Vb, nap)
            s.barrier()
        A1.close()
        if lat and self.stage >= 4.2:
            self.lat_mla(l, c, QL, QLb)
            s.barrier()
        if lat and self.stage >= 4.3:
            self.lat_fourier(l, c)
            s.barrier()
        A0.close()
        if self.stage < 5:
            return

        with ExitStack() as Fz:
            WO = [self.sb(Fz, "WO%d" % r, [128, 4, 1024], BF16) for r in range(3)]
            WOb = [Buf(WO[r]) for r in range(3)]
            MG = self.sb(Fz, "MG", [128, 8, T], BF16)
            MGb = [Buf(MG) for _ in range(NTB)]
            SG = [self.sb(Fz, "SG%d" % r, [128, 512], F32) for r in range(3)]
            SGb = [Buf(SG[r]) for r in range(3)]
            MT = self.sb(Fz, "MT", [128, 512], F32)
            MTb = Buf(MT)
            MT2 = self.sb(Fz, "MT2", [128, 512], F32)
            MT2b = Buf(MT2)
            wm = w_in[l, :, C_MRG:D_IN].rearrange("(c p) (r n) -> p c r n", p=128, r=3)
            for cch in range(8):
                wb = self.next_wb()
                wv = wb.ap[:, :, 0:384].rearrange("p c (r n) -> p c r n", r=3)
                for r in range(3):
                    s.dma("pool", lambda q: q.dma_start(out=wv[:, :, r, :], in_=wm[:, :, r, cch * 128:(cch + 1) * 128]),
                          pwrites=[wb])
                if cch == 0:
                    for r, nm in enumerate(("w_o_na", "w_o_mla", "w_o_fn")):
                        s.dma("pool", lambda q: q.dma_start(out=WO[r][:], in_=d[nm][l].rearrange("(c p) n -> p c n", p=128)),
                              writes=[WOb[r]])
                for tb in range(NTB):
                    ts = TS(tb)
                    for r in range(3):
                        ps = self.ps("all")
                        for kc in range(8):
                            s.op("pe", lambda q: q.matmul(ps.ap[:], lhsT=wv[:, kc, r, :], rhs=XM[:, kc, ts],
                                                          start=(kc == 0), stop=(kc == 7)),
                                 reads=[wb, XMb[tb]], pwrites=[ps])
                        s.op("act", lambda q: q.activation(out=SG[r][:], in_=ps.ap[:], func=AF.Sigmoid),
                             reads=[ps], writes=[SGb[r]])
                    for r in range(3):
                        ps = self.ps("all")
                        for kc in range(4):
                            s.op("pe", lambda q: q.matmul(ps.ap[:], lhsT=WO[r][:, kc, cch * 128:(cch + 1) * 128],
                                                          rhs=OG[r][:, kc, ts], start=(kc == 0), stop=(kc == 3)),
                                 reads=[WOb[r], OGb[r][tb]], pwrites=[ps])
                        if r == 0:
                            s.op("dve", lambda q: q.tensor_tensor(out=MT[:], in0=ps.ap[:], in1=SG[0][:], op=ALU.mult),
                                 reads=[ps, SGb[0]], writes=[MTb])
                        else:
                            s.op("dve", lambda q: q.tensor_tensor(out=MT2[:], in0=ps.ap[:], in1=SG[r][:], op=ALU.mult),
                                 reads=[ps, SGb[r]], writes=[MT2b])
                            if r == 1:
                                s.op("dve", lambda q: q.tensor_tensor(out=MT[:], in0=MT[:], in1=MT2[:], op=ALU.add),
                                     reads=[MTb, MT2b], writes=[MTb])
                            else:
                                s.op("dve", lambda q: q.tensor_tensor(out=MG[:, cch, ts], in0=MT[:], in1=MT2[:], op=ALU.add),
                                     reads=[MTb, MT2b], pwrites=[MGb[tb]])
            for half in range(2):
                wb = self.next_wb()
                s.dma("pool", lambda q: q.dma_start(
                    out=wb.ap[:], in_=d["w_out"][l, :, half * 512:(half + 1) * 512].rearrange("(c p) n -> p c n", p=128)),
                    writes=[wb])
                for tb in range(NTB):
                    ts = TS(tb)
                    for jj in range(4):
                        cch = half * 4 + jj
                        ps = self.ps("all")
                        for kc in range(8):
                            s.op("pe", lambda q: q.matmul(ps.ap[:], lhsT=wb.ap[:, kc, jj * 128:(jj + 1) * 128],
                                                          rhs=MG[:, kc, ts], start=(kc == 0), stop=(kc == 7)),
                                 reads=[wb, MGb[tb]], pwrites=[ps])
                        s.op("dve", lambda q: q.scalar_tensor_tensor(out=X[:, cch, ts], in0=ps.ap[:],
                                                                     scalar=self.MOD.ap[:, l, 16 + cch, col:col + 1],
                                                                     in1=X[:, cch, ts], op0=ALU.mult, op1=ALU.add),
                             reads=[ps, self.MOD, Xb[tb]], pwrites=[Xb[tb]])
            s.barrier()

    def ctx_attention(self, l, c, A, QT, QTb, KT, KTb, V, Vb, QL, QLb, CK, CKb, KR, KRb, ABC, ABCb):
        nc, s, d = self.nc, self.s, self.d
        OG, OGb = c["OG"], c["OGb"]
        ts = slice(0, 512)
        items = []
        for h in range(8):
            j, par = h // 2, h % 2
            base = 64 * par
            hst = {}
            for bb in range(2):
                st = {}

                def front(h=h, j=j, par=par, base=base, bb=bb, st=st, hst=hst):
                    if bb == 0:
                        hst["po"] = self.ps("o")
                    ps = self.ps("g")
                    for kt in range(2):
                        k0 = bb * 256 + kt * 128
                        s.op("pe", lambda q: q.matmul(ps.ap[:, kt * 256:(kt + 1) * 256], lhsT=KT[base:base + 64, j, k0:k0 + 128],
                                                      rhs=QT[base:base + 64, j, bb * 256:(bb + 1) * 256], start=True, stop=True),
                             reads=[KTb[0], QTb[0]], pwrites=[ps])
                    pt = self.next_pt()
                    s.op("act", lambda q: q.activation(out=pt.ap[:], in_=ps.ap[:], func=AF.Exp), reads=[ps], writes=[pt])
                    st["pt"] = pt

                def back(h=h, j=j, par=par, bb=bb, st=st, hst=hst):
                    po, pt = hst["po"], st["pt"]
                    for kt in range(2):
                        t = bb * 2 + kt
                        if par == 0:
                            out = po.ap[0:65, bb * 256:(bb + 1) * 256]
                            lhsT = V[:, t, j * 192:j * 192 + 65]
                        else:
                            out = po.ap[0:128, bb * 256:(bb + 1) * 256]
                            lhsT = V[:, t, j * 192 + 64:j * 192 + 192]
                        s.op("pe", lambda q: q.matmul(out, lhsT=lhsT, rhs=pt.ap[:, kt * 256:(kt + 1) * 256],
                                                      start=(kt == 0), stop=(kt == 1)),
                             reads=[Vb[t], pt], pwrites=[po])
                fin = None
                if bb == 1:
                    fin = (lambda h=h, hst=hst: self.attn_finish(hst["po"], h, OG[0], OGb[0][0], ts))
                items.append((front, back, fin))
        ada_next = (l + 1) if (l + 1 < self.nlayers) else None
        ada_wbs = [self.adaln_load(ada_next, jb) for jb in range(3)] if ada_next is not None else []
        self.run_pipeline2(items, LA=2, DL=2)
        for jb, wb in enumerate(ada_wbs):
            self.adaln_compute(ada_next, jb, wb)

        WUQ = self.sb(A, "WUQ", [128, 2, 768], BF16)
        WUQb = Buf(WUQ)
        WK = self.sb(A, "WK", [128, 8, 128], BF16)
        WKb = Buf(WK)
        WV = self.sb(A, "WV", [128, 512], BF16)
        WVb = Buf(WV)
        VM = self.sb(A, "VM", [128, 4, 192], BF16)
        VMb = Buf(VM)
        KHs = [self.sb(A, "KH%d" % i, [96, 512], BF16) for i in range(2)]
        KHbs = [Buf(KHs[i]) for i in range(2)]
        QHs = [self.sb(A, "QH%d" % i, [96, 512], BF16) for i in range(2)]
        QHbs = [Buf(QHs[i]) for i in range(2)]
        VM2 = self.sb(A, "VM2", [128, 4, 192], BF16)
        VM2b = Buf(VM2)
        s.dma("pool", lambda q: q.dma_start(out=WUQ[:], in_=d["w_uq"][l].rearrange("(c p) n -> p c n", p=128)), writes=[WUQb])
        s.op("dve", lambda q: q.memset(WK[:], 0.0), writes=[WKb])
        wukv = d["w_ukv"][l].rearrange("c (h t x) -> c h t x", t=2, x=64)
        s.dma("pool", lambda q: q.dma_start(out=WK[:, :, 0:64], in_=wukv[:, :, 0, :]), pwrites=[WKb])
        s.dma("pool", lambda q: q.dma_start(out=WV[:].rearrange("p (h x) -> p h x", x=64), in_=wukv[:, :, 1, :]), writes=[WVb])
        VMs, VMbs = [VM, VM2], [VMb, VM2b]
        for i in range(2):
            s.op("dve", lambda q: q.memset(VMs[i][:], 0.0), writes=[VMbs[i]])
            s.op("dve", lambda q: q.memset(VMs[i][:, :, 64:65], 1.0), pwrites=[VMbs[i]])
        items = []
        for h in range(8):
            p, par = h // 2, h % 2
            vm, vmb = VMs[p % 2], VMbs[p % 2]
            KH, KHb, QH, QHb = KHs[h % 2], KHbs[h % 2], QHs[h % 2], QHbs[h % 2]
            hst = {}
            for bb in range(2):
                st = {}

                def front(h=h, p=p, par=par, bb=bb, st=st, hst=hst, vm=vm, vmb=vmb, KH=KH, KHb=KHb, QH=QH, QHb=QHb):
                    if bb == 0 and par == 0:
                        ps = self.ps("g")
                        for t in range(4):
                            s.op("pe", lambda q: q.matmul(ps.ap[:, t * 128:(t + 1) * 128], lhsT=CK[:, t * 128:(t + 1) * 128],
                                                          rhs=WV[:, p * 128:(p + 1) * 128], start=True, stop=True),
                                 reads=[CKb[0], WVb], pwrites=[ps])
                        pv = ps.ap[:].rearrange("p (t e x) -> p t e x", e=2, x=64)
                        s.op("dve", lambda q: q.tensor_copy(out=vm[:, :, 0:64], in_=pv[:, :, 0, :]), reads=[ps], pwrites=[vmb])
                        s.op("dve", lambda q: q.tensor_copy(out=vm[:, :, 128:192], in_=pv[:, :, 1, :]), reads=[ps], pwrites=[vmb])
                    if bb == 0:
                        hst["po"] = self.ps("o")
                        ps = self.ps("g")
                        s.op("pe", lambda q: q.matmul(ps.ap[0:128, :], lhsT=WK[:, h, :], rhs=CK[:, 0:512], start=True, stop=False),
                             reads=[WKb, CKb[0]], pwrites=[ps])
                        s.op("pe", lambda q: q.matmul(ps.ap[0:128, :], lhsT=self.sel32.ap[:, :], rhs=KR[0:32, 0:512],
                                                      start=False, stop=True),
                             reads=[self.sel32, KRb[0]], pwrites=[ps])
                        s.op("dve", lambda q: q.tensor_copy(out=KH[:, :], in_=ps.ap[0:96, :]), reads=[ps], writes=[KHb])
                        ps = self.ps("g")
                        for kc in range(2):
                            s.op("pe", lambda q: q.matmul(ps.ap[0:96, :], lhsT=WUQ[:, kc, h * 96:(h + 1) * 96], rhs=QL[:, kc, 0:512],
                                                          start=(kc == 0), stop=(kc == 1)),
                                 reads=[WUQb, QLb[0]], pwrites=[ps])
                        s.op("dve", lambda q: q.tensor_copy(out=QH[:, :], in_=ps.ap[0:96, :]), reads=[ps], writes=[QHb])
                    ps = self.ps("g")
                    for kt in range(2):
                        k0 = bb * 256 + kt * 128
                        s.op("pe", lambda q: q.matmul(ps.ap[:, kt * 256:(kt + 1) * 256], lhsT=KH[:, k0:k0 + 128],
                                                      rhs=QH[:, bb * 256:(bb + 1) * 256], start=True, stop=True),
                             reads=[KHb, QHb], pwrites=[ps])
                    pt = self.next_pt()
                    s.op("act", lambda q: q.activation(out=pt.ap[:], in_=ps.ap[:], func=AF.Exp, scale=MLA_SCALE),
                         reads=[ps], writes=[pt])
                    st["pt"] = pt

                def back(par=par, bb=bb, st=st, hst=hst, vm=vm, vmb=vmb):
                    po, pt = hst["po"], st["pt"]
                    for kt in range(2):
                        t = bb * 2 + kt
                        if par == 0:
                            out = po.ap[0:65, bb * 256:(bb + 1) * 256]
                            lhsT = vm[:, t, 0:65]
                        else:
                            out = po.ap[0:128, bb * 256:(bb + 1) * 256]
                            lhsT = vm[:, t, 64:192]
                        s.op("pe", lambda q: q.matmul(out, lhsT=lhsT, rhs=pt.ap[:, kt * 256:(kt + 1) * 256],
                                                      start=(kt == 0), stop=(kt == 1)),
                             reads=[vmb, pt], pwrites=[po])
                fin = None
                if bb == 1:
                    fin = (lambda h=h, hst=hst: self.attn_finish(hst["po"], h, OG[1], OGb[1][0], ts))
                items.append((front, back, fin))
        ada_wbs = [self.adaln_load(ada_next, jb) for jb in range(3, 6)] if ada_next is not None else []
        self.run_pipeline2(items, LA=2, DL=2)
        for jb, wb in enumerate(ada_wbs):
            self.adaln_compute(ada_next, 3 + jb, wb)
        if ada_next is not None:
            self.adaln_gs(ada_next)

        for g in range(4):
            po = self.ps("o")
            for bb in range(2):
                n = 0
                for nt in range(2):
                    t = bb * 2 + nt
                    for (off, mat) in ((0, self.c256), (128, self.ns256)):
                        s.op("pe", lambda q: q.matmul(po.ap[:, bb * 256:(bb + 1) * 256],
                                                      lhsT=ABC[:, t, g * 256 + off:g * 256 + off + 128],
                                                      rhs=mat.ap[:, nt, :], start=(n == 0), stop=(n == 3)),
                             reads=[ABCb[t], mat], pwrites=[po])
                        n += 1
            s.op("dve", lambda q: q.tensor_tensor(out=OG[2][:, g, ts], in0=po.ap[:], in1=OG[2][:, g, ts], op=ALU.mult),
                 reads=[po, OGb[2][0]], pwrites=[OGb[2][0]])


    def lat_pay_kv(self, l, c, KT, KTb, V, Vb):
        s, d, db = self.s, self.d, self.db
        pk = d["pay_k"][l].rearrange("(j p) n -> p j n", p=128)
        s.dma("sp", lambda q: q.dma_start(out=pk[:, :, 0:256], in_=KT[:, :, 0:256]), reads=[KTb[0]], pwrites=[db["pay_k"][l]])
        s.dma("sp", lambda q: q.dma_start(out=pk[:, :, 256:512], in_=KT[:, :, 768:1024]), reads=[KTb[1]], pwrites=[db["pay_k"][l]])
        pv = d["pay_v"][l].rearrange("(t p) n -> p t n", p=128)
        s.dma("sp", lambda q: q.dma_start(out=pv[:, 0:2, :], in_=V[:, 0:2, :]), reads=[Vb[0], Vb[1]], pwrites=[db["pay_v"][l]])
        s.dma("sp", lambda q: q.dma_start(out=pv[:, 2:4, :], in_=V[:, 6:8, :]), reads=[Vb[6], Vb[7]], pwrites=[db["pay_v"][l]])

    def lat_pay_mla(self, l, c, CK, CKb, KR, KRb):
        s, d, db = self.s, self.d, self.db
        s.dma("sp", lambda q: q.dma_start(out=d["pay_mla"][l][0:128, :], in_=CK[:, :]), reads=CKb, pwrites=[db["pay_mla"][l]])
        s.dma("sp", lambda q: q.dma_start(out=d["pay_mla"][l][128:160, :], in_=KR[0:32, :]), reads=KRb, pwrites=[db["pay_mla"][l]])

    def lat_na_prefetch(self, l, st):
        s, d = self.s, self.d
        KCT = self.sb(st, "KCT", [128, 4, 256], BF16)
        KCTb = Buf(KCT)
        VCX = self.sb(st, "VCX", [128, 2, 768], BF16)
        VCXb = Buf(VCX)
        BT = [self.sb(st, "BT%d" % i, [128, 2, 7, 128], BF16) for i in range(2)]
        BTb = [Buf(BT[i]) for i in range(2)]
        s.dma("pool", lambda q: q.dma_start(out=KCT[:], in_=d["cnakT"][l].rearrange("(j p) n -> p j n", p=128)), writes=[KCTb])
        s.op("dve", lambda q: q.memset(VCX[:], 0.0), writes=[VCXb])
        for p in range(4):
            s.op("dve", lambda q: q.memset(VCX[:, :, p * 192 + 64:p * 192 + 65], 1.0), pwrites=[VCXb])
        cv = d["cnav"][l].rearrange("(t p) (a e x) -> p t a e x", p=128, e=2, x=64)
        for t in range(2):
            vx = VCX[:, t, :].rearrange("p (a b) -> p a b", b=192)
            s.dma("pool", lambda q: q.dma_start(out=vx[:, :, 0:64], in_=cv[:, t, :, 0, :]), pwrites=[VCXb])
            s.dma("pool", lambda q: q.dma_start(out=vx[:, :, 128:192], in_=cv[:, t, :, 1, :]), pwrites=[VCXb])
        for p in range(2):
            s.dma("pool", lambda q: q.dma_start(out=BT[p][:], in_=d["nab"][l, :, 2 * p:2 * p + 2, :, :]), writes=[BTb[p]])
            s.op("act", lambda q: q.activation(out=BT[p][:], in_=BT[p][:], func=AF.Exp), reads=[BTb[p]], writes=[BTb[p]])
        return dict(KCT=KCT, KCTb=KCTb, VCX=VCX, VCXb=VCXb, BT=BT, BTb=BTb)

    def lat_na(self, l, c, QT, QTb, KT, KTb, V, Vb, nap):
        s, d, db = self.s, self.d, self.db
        KCT, KCTb, VCX, VCXb, BT, BTb = nap["KCT"], nap["KCTb"], nap["VCX"], nap["VCXb"], nap["BT"], nap["BTb"]
        OG, OGb = c["OG"], c["OGb"]
        SELR, SELRb = c["SELR"], c["SELRb"]
        with ExitStack() as N:
            HK = self.sb(N, "HK", [128, 4, 2, 256], BF16)
            HKb = Buf(HK)
            HV = self.sb(N, "HV", [128, 4, 768], BF16)
            HVb = Buf(HV)
            with ExitStack() as N2:
                KC = [self.sb(N2, "KC%d" % i, [128, 4, 512], BF16) for i in range(2)]
                KCb = [Buf(KC[i]) for i in range(2)]
                VC = [self.sb(N2, "VC%d" % i, [128, 4, 768], BF16) for i in range(2)]
                VCb = [Buf(VC[i]) for i in range(2)]
                gk = d["g_k"][l].rearrange("(c j p) n -> p j c n", c=4, j=4)
                for j in range(4):
                    kc, kcb = KC[j % 2], KCb[j % 2]
                    s.dma("sp", lambda q: q.dma_start(out=kc[:], in_=gk[:, j]), reads=[db["g_k"][l]], writes=[kcb])
                    for side in range(2):
                        cols = slice(256, 512) if side == 0 else slice(0, 256)
                        for cc in range(4):
                            sc = SELR[:, side * 4 + cc:side * 4 + cc + 1]
                            if cc == 0:
                                s.op("dve", lambda q: q.tensor_scalar(out=HK[:, j, side, :], in0=kc[:, cc, cols], scalar1=sc,
                                                                      scalar2=0.0, op0=ALU.mult, op1=ALU.add),
                                     reads=[kcb, SELRb], pwrites=[HKb])
                            else:
                                s.op("dve", lambda q: q.scalar_tensor_tensor(out=HK[:, j, side, :], in0=kc[:, cc, cols], scalar=sc,
                                                                             in1=HK[:, j, side, :], op0=ALU.mult, op1=ALU.add),
                                     reads=[kcb, SELRb, HKb], pwrites=[HKb])
                gv = d["g_v"][l].rearrange("(c t p) n -> p t c n", c=4, t=4)
                for ht in range(4):
                    side = ht // 2
                    src_t = (2 + ht) if side == 0 else (ht - 2)
                    vc, vcb = VC[ht % 2], VCb[ht % 2]
                    s.dma("sp", lambda q: q.dma_start(out=vc[:], in_=gv[:, src_t]), reads=[db["g_v"][l]], writes=[vcb])
                    for cc in range(4):
                        sc = SELR[:, side * 4 + cc:side * 4 + cc + 1]
                        if cc == 0:
                            s.op("dve", lambda q: q.tensor_scalar(out=HV[:, ht, :], in0=vc[:, cc, :], scalar1=sc, scalar2=0.0,
                                                                  op0=ALU.mult, op1=ALU.add),
                                 reads=[vcb, SELRb], pwrites=[HVb])
                        else:
                            s.op("dve", lambda q: q.scalar_tensor_tensor(out=HV[:, ht, :], in0=vc[:, cc, :], scalar=sc,
                                                                         in1=HV[:, ht, :], op0=ALU.mult, op1=ALU.add),
                                 reads=[vcb, SELRb, HVb], pwrites=[HVb])
                s.barrier()
            MK = self.sb(N, "MK", [128, 48, 128], BF16)
            MKb = Buf(MK)
            s.dma("sp", lambda q: q.dma_start(out=MK[:], in_=d["namask"][:, :, :]), writes=[MKb])
            all_items = []
            for p in range(4):
                bt, btb = BT[p % 2], BTb[p % 2]
                need_bt = [p >= 2]
                for h in (2 * p, 2 * p + 1):
                    par = h % 2
                    base = 64 * par
                    for grp in range(2):
                        po = self.ps("o")
                        items = []
                        for bi in range(4):
                            b = grp * 4 + bi
                            qs = slice(b * 128, (b + 1) * 128)
                            tiles = []
                            lts = [(b - 2 + jj, jj) for jj in range(5)]
                            if b == 0:
                                lts.append((3, 5))
                            if b == 7:
                                lts.append((4, 5))
                            for (lt, slot) in lts:
                                if 0 <= lt <= 7:
                                    kap = KT[base:base + 64, p, lt * 128:(lt + 1) * 128]
                                    kb_ = KTb[lt // 4]
                                    vt = V[:, lt, :]
                                    vb_ = Vb[lt]
                                elif lt < 0:
                                    kap = HK[base:base + 64, p, 0, (lt + 2) * 128:(lt + 3) * 128]
                                    kb_ = HKb
                                    vt = HV[:, lt + 2, :]
                                    vb_ = HVb
                                else:
                                    kap = HK[base:base + 64, p, 1, (lt - 8) * 128:(lt - 7) * 128]
                                    kb_ = HKb
                                    vt = HV[:, 2 + lt - 8, :]
                                    vb_ = HVb
                                tiles.append((kap, kb_, vt, vb_, lt - b + 3, b * 6 + slot))
                            for t in range(2):
                                tiles.append((KCT[base:base + 64, p, t * 128:(t + 1) * 128], KCTb, VCX[:, t, :], VCXb, None, None))
                            nt = len(tiles)
                            for g0 in range(0, nt, 4):
                                grpt = tiles[g0:g0 + 4]
                                st = {}

                                def front(grpt=grpt, st=st, qs=qs, grp=grp, base=base, p=p, par=par, bt=bt, btb=btb, need_bt=need_bt):
                                    if need_bt[0]:
                                        need_bt[0] = False
                                        s.dma("pool", lambda q: q.dma_start(out=bt[:], in_=d["nab"][l, :, 2 * p:2 * p + 2, :, :]),
                                              writes=[btb])
                                        s.op("act", lambda q: q.activation(out=bt[:], in_=bt[:], func=AF.Exp), reads=[btb], writes=[btb])
                                    ps = self.ps("g")
                                    for i, (kap, kb_, vt, vb_, dj, ms) in enumerate(grpt):
                                        reg = ps.ap[:, i * 128:(i + 1) * 128]
                                        s.op("pe", lambda q: q.matmul(reg, lhsT=kap, rhs=QT[base:base + 64, p, qs], start=True,
                                                                      stop=True),
                                             reads=[kb_, QTb[grp]], pwrites=[ps])
                                    w = len(grpt) * 128
                                    pt = self.next_pt()
                                    s.op("act", lambda q: q.activation(out=pt.ap[:, 0:w], in_=ps.ap[:, 0:w], func=AF.Exp),
                                         reads=[ps], writes=[pt])
                                    i = 0
                                    while i < len(grpt):
                                        dj, ms = grpt[i][4], grpt[i][5]
                                        if dj is None:
                                            i += 1
                                            continue
                                        n = 1
                                        while (i + n < len(grpt) and grpt[i + n][4] is not None
                                               and grpt[i + n][4] == dj + n and grpt[i + n][5] == ms + n):
                                            n += 1
                                        pv3 = pt.ap[:, i * 128:(i + n) * 128].rearrange("p (a b) -> p a b", b=128)
                                        s.op("dve", lambda q: q.tensor_tensor(out=pv3, in0=pv3, in1=bt[:, par, dj:dj + n, :], op=ALU.mult),
                                             reads=[pt, btb], writes=[pt])
                                        s.op("pool", lambda q: q.tensor_tensor(out=pv3, in0=pv3, in1=MK[:, ms:ms + n, :], op=ALU.mult),
                                             reads=[pt, MKb], writes=[pt])
                                        i += n
                                    st["pt"] = pt

                                def back(grpt=grpt, st=st, g0=g0, nt=nt, bi=bi, po=po, par=par, p=p):
                                    pt = st["pt"]
                                    for i, (kap, kb_, vt, vb_, dj, ms) in enumerate(grpt):
                                        n = g0 + i
                                        if par == 0:
                                            out = po.ap[0:65, bi * 128:(bi + 1) * 128]
                                            lhsT = vt[:, p * 192:p * 192 + 65]
                                        else:
                                            out = po.ap[0:128, bi * 128:(bi + 1) * 128]
                                            lhsT = vt[:, p * 192 + 64:p * 192 + 192]
                                        s.op("pe", lambda q: q.matmul(out, lhsT=lhsT, rhs=pt.ap[:, i * 128:(i + 1) * 128],
                                                                      start=(n == 0), stop=(n == nt - 1)),
                                             reads=[vb_, pt], pwrites=[po])
                                items.append([front, back, None])
                        items[-1][2] = (lambda po=po, h=h, grp=grp: self.attn_finish(po, h, OG[0], OGb[0][grp],
                                                                                      slice(grp * 512, (grp + 1) * 512)))
                        all_items.extend(items)
            late = getattr(self, "cc_late", None) or []
            self.cc_late = None
            npos = len(all_items)
            for i, f in enumerate(late):
                pos = (i * npos) // 4
                fr = all_items[pos][0]
                all_items[pos][0] = (lambda fr=fr, f=f: (f(), fr()))
            self.run_pipeline2(all_items, LA=4, DL=2)

    def lat_mla(self, l, c, QL, QLb):
        s, d, db = self.s, self.d, self.db
        OG, OGb = c["OG"], c["OGb"]
        ROPE, ROPEb = c["ROPE"], c["ROPEb"]
        NK = 4352
        NKT = 34
        with ExitStack() as M:
            CKA = self.sb(M, "CKA", [128, NK], BF16)
            CKAb = Buf(CKA)
            KRA = self.sb(M, "KRA", [32, NK], BF16)
            KRAb = Buf(KRA)
            KH = self.sb(M, "KH", [96, NK], BF16)
            KHb = Buf(KH)
            VM = [self.sb(M, "VM%d" % i, [128, NKT, 192], BF16) for i in range(2)]
            VMb = [Buf(VM[i]) for i in range(2)]
            WUQ = self.sb(M, "WUQ", [128, 2, 768], BF16)
            WUQb = Buf(WUQ)
            WUQP = self.sb(M, "WUQP", [128, 2, 768], BF16)
            WUQPb = Buf(WUQP)
            WK = self.sb(M, "WK", [128, 8, 128], BF16)
            WKb = Buf(WK)
            WV = self.sb(M, "WV", [128, 512], BF16)
            WVb = Buf(WV)
            QH = [self.sb(M, "QH%d" % i, [96, 1024], BF16) for i in range(2)]
            QHb = [Buf(QH[i]) for i in range(2)]
            for cc in range(4):
                s.dma("sp", lambda q: q.dma_start(out=CKA[:, cc * 1024:(cc + 1) * 1024], in_=d["g_mla"][l][cc * 160:cc * 160 + 128, :]),
                      reads=[db["g_mla"][l]], pwrites=[CKAb])
                s.dma("sp", lambda q: q.dma_start(out=KRA[0:32, cc * 1024:(cc + 1) * 1024],
                                                  in_=d["g_mla"][l][cc * 160 + 128:cc * 160 + 160, :]),
                      reads=[db["g_mla"][l]], pwrites=[KRAb])
            s.dma("pool", lambda q: q.dma_start(out=CKA[:, 4096:NK], in_=d["cckvT"][l]), pwrites=[CKAb])
            s.dma("pool", lambda q: q.dma_start(out=KRA[0:32, 4096:NK], in_=d["ckrT"][l]), pwrites=[KRAb])
            s.dma("pool", lambda q: q.dma_start(out=WUQ[:], in_=d["w_uq"][l].rearrange("(c p) n -> p c n", p=128)), writes=[WUQb])
            s.dma("pool", lambda q: q.dma_start(out=WUQP[:], in_=d["w_uqp"][l].rearrange("(c p) n -> p c n", p=128)), writes=[WUQPb])
            s.op("dve", lambda q: q.memset(WK[:], 0.0), writes=[WKb])
            wukv = d["w_ukv"][l].rearrange("c (h t x) -> c h t x", t=2, x=64)
            s.dma("pool", lambda q: q.dma_start(out=WK[:, :, 0:64], in_=wukv[:, :, 0, :]), pwrites=[WKb])
            s.dma("pool", lambda q: q.dma_start(out=WV[:].rearrange("p (h x) -> p h x", x=64), in_=wukv[:, :, 1, :]), writes=[WVb])
            for i in range(2):
                s.op("dve", lambda q: q.memset(VM[i][:], 0.0), writes=[VMb[i]])
                s.op("dve", lambda q: q.memset(VM[i][:, :, 64:65], 1.0), pwrites=[VMb[i]])
            KH2 = self.sb(M, "KH2", [96, NK], BF16)
            KHs, KHbs = [KH, KH2], [KHb, Buf(KH2)]

            def gen_v(p):
                vm, vmb = VM[p % 2], VMb[p % 2]
                for k0 in range(0, NKT, 4):
                    nt = min(4, NKT - k0)
                    ps = self.ps("g")
                    for t in range(nt):
                        kt = k0 + t
                        s.op("pe", lambda q: q.matmul(ps.ap[:, t * 128:(t + 1) * 128], lhsT=CKA[:, kt * 128:(kt + 1) * 128],
                                                      rhs=WV[:, p * 128:(p + 1) * 128], start=True, stop=True),
                             reads=[CKAb, WVb], pwrites=[ps])
                    pv = ps.ap[:, 0:nt * 128].rearrange("p (t e x) -> p t e x", e=2, x=64)
                    s.op("dve", lambda q: q.tensor_copy(out=vm[:, k0:k0 + nt, 0:64], in_=pv[:, :, 0, :]), reads=[ps], pwrites=[vmb])
                    s.op("dve", lambda q: q.tensor_copy(out=vm[:, k0:k0 + nt, 128:192], in_=pv[:, :, 1, :]), reads=[ps], pwrites=[vmb])

            def gen_kq(h):
                kh, khb = KHs[h % 2], KHbs[h % 2]
                qh, qhb = QH[h % 2], QHb[h % 2]
                for k0 in range(0, NK, 512):
                    n = min(512, NK - k0)
                    ps = self.ps("g")
                    s.op("pe", lambda q: q.matmul(ps.ap[0:128, 0:n], lhsT=WK[:, h, :], rhs=CKA[:, k0:k0 + n], start=True, stop=False),
                         reads=[WKb, CKAb], pwrites=[ps])
                    s.op("pe", lambda q: q.matmul(ps.ap[0:128, 0:n], lhsT=self.sel32.ap[:, :], rhs=KRA[0:32, k0:k0 + n],
                                                  start=False, stop=True),
                         reads=[self.sel32, KRAb], pwrites=[ps])
                    s.op("dve", lambda q: q.tensor_copy(out=kh[:, k0:k0 + n], in_=ps.ap[0:96, 0:n]), reads=[ps], pwrites=[khb])
                for tb in range(2):
                    ts = slice(tb * 512, (tb + 1) * 512)
                    ps = self.ps("g")
                    ps2 = self.ps("g")
                    for (pp, ww, wwb) in ((ps, WUQ, WUQb), (ps2, WUQP, WUQPb)):
                        for kc in range(2):
                            s.op("pe", lambda q: q.matmul(pp.ap[0:96, :], lhsT=ww[:, kc, h * 96:(h + 1) * 96], rhs=QL[:, kc, ts],
                                                          start=(kc == 0), stop=(kc == 1)),
                                 reads=[wwb, QLb[tb]], pwrites=[pp])
                    s.op("dve", lambda q: q.tensor_copy(out=qh[0:64, ts], in_=ps.ap[0:64, :]), reads=[ps], pwrites=[qhb])
                    t1 = self.next_tmp()
                    t2 = self.next_tmp()
                    s.op("dve", lambda q: q.tensor_tensor(out=t1.ap[64:96, :], in0=ps.ap[64:96, :], in1=ROPE[64:96, 0, ts], op=ALU.mult),
                         reads=[ps, ROPEb], writes=[t1])
                    s.op("dve", lambda q: q.tensor_tensor(out=t2.ap[64:96, :], in0=ps2.ap[64:96, :], in1=ROPE[64:96, 1, ts], op=ALU.mult),
                         reads=[ps2, ROPEb], writes=[t2])
                    s.op("dve", lambda q: q.tensor_tensor(out=qh[64:96, ts], in0=t1.ap[64:96, :], in1=t2.ap[64:96, :], op=ALU.add),
                         reads=[t1, t2], pwrites=[qhb])

            gen_v(0)
            gen_kq(0)
            items = []
            for h in range(8):
                p, par = h // 2, h % 2
                vm, vmb = VM[p % 2], VMb[p % 2]
                kh, khb = KHs[h % 2], KHbs[h % 2]
                qh, qhb = QH[h % 2], QHb[h % 2]
                for tb in range(2):
                    ts = slice(tb * 512, (tb + 1) * 512)
                    po = self.ps("o")
                    for kt in range(NKT):
                        st = {}
                        pre = None
                        if par == 1 and tb == 0 and kt == 0 and p + 1 < 4:
                            pre = (lambda p=p: gen_v(p + 1))
                        if tb == 1 and kt == NKT - 12 and h + 1 < 8:
                            pre = (lambda h=h: gen_kq(h + 1))

                        def front(kt=kt, st=st, ts=ts, kh=kh, khb=khb, qh=qh, qhb=qhb, pre=pre):
                            if pre is not None:
                                pre()
                            ps = self.ps("g")
                            s.op("pe", lambda q: q.matmul(ps.ap[:], lhsT=kh[:, kt * 128:(kt + 1) * 128], rhs=qh[:, ts],
                                                          start=True, stop=True),
                                 reads=[khb, qhb], writes=[ps])
                            pt = self.next_pt()
                            s.op("act", lambda q: q.activation(out=pt.ap[:], in_=ps.ap[:], func=AF.Exp, scale=MLA_SCALE),
                                 reads=[ps], writes=[pt])
                            st["pt"] = pt

                        def back(kt=kt, st=st, po=po, par=par, vm=vm, vmb=vmb):
                            pt = st["pt"]
                            if par == 0:
                                out = po.ap[0:65, :]
                                lhsT = vm[:, kt, 0:65]
                            else:
                                out = po.ap[0:128, :]
                                lhsT = vm[:, kt, 64:192]
                            s.op("pe", lambda q: q.matmul(out, lhsT=lhsT, rhs=pt.ap[:], start=(kt == 0), stop=(kt == NKT - 1)),
                                 reads=[vmb, pt], pwrites=[po])
                        fin = None
                        if kt == NKT - 1:
                            fin = (lambda po=po, h=h, tb=tb, ts=ts: self.attn_finish(po, h, OG[1], OGb[1][tb], ts))
                        items.append((front, back, fin))
            self.run_pipeline2(items, LA=3, DL=2)


    def lat_fourier(self, l, c):
        s, d, db = self.s, self.d, self.db
        OG, OGb = c["OG"], c["OGb"]
        with ExitStack() as Fs:
            DC = [self.sb(Fs, "DC%d" % i, [128, 4, 512], BF16) for i in range(2)]
            DCb = [Buf(DC[i]) for i in range(2)]
            DS = [self.sb(Fs, "DS%d" % i, [128, 4, 512], BF16) for i in range(2)]
            DSb = [Buf(DS[i]) for i in range(2)]
            AB = [self.sb(Fs, "AB%d" % i, [128, 4, 1024], BF16) for i in range(2)]
            ABb = [Buf(AB[i]) for i in range(2)]
            it = 0
            for tb in range(2):
                ts = slice(tb * 512, (tb + 1) * 512)
                acc = [self.PS[4 + g] for g in range(4)]
                for nb in range(8):
                    dc, dcb, ds_, dsb, ab, abb = DC[it % 2], DCb[it % 2], DS[it % 2], DSb[it % 2], AB[it % 2], ABb[it % 2]
                    it += 1
                    rows = slice(nb * 512, (nb + 1) * 512)
                    s.dma("sp", lambda q: q.dma_start(out=dc[:], in_=d["dftc"][rows, ts].rearrange("(i p) n -> p i n", p=128)), writes=[dcb])
                    s.dma("sp", lambda q: q.dma_start(out=ds_[:], in_=d["dftns"][rows, ts].rearrange("(i p) n -> p i n", p=128)), writes=[dsb])
                    gn = "g_ab%d" % (nb % 2)
                    grow = slice((nb // 2) * 512, (nb // 2 + 1) * 512)
                    s.dma("sp", lambda q: q.dma_start(out=ab[:], in_=d[gn][l][grow, :].rearrange("(i p) n -> p i n", p=128)),
                          reads=[db[gn][l]], writes=[abb])
                    for i in range(4):
                        for g in range(4):
                            first = (nb == 0 and i == 0)
                            last = (nb == 7 and i == 3)
                            s.op("pe", lambda q: q.matmul(acc[g].ap[:], lhsT=ab[:, i, g * 256:g * 256 + 128], rhs=dc[:, i, :],
                                                          start=first, stop=False),
                                 reads=[abb, dcb], pwrites=[acc[g]])
                            s.op("pe", lambda q: q.matmul(acc[g].ap[:], lhsT=ab[:, i, g * 256 + 128:g * 256 + 256], rhs=ds_[:, i, :],
                                                          start=False, stop=last),
                                 reads=[abb, dsb], pwrites=[acc[g]])
                for g in range(4):
                    s.op("dve", lambda q: q.tensor_tensor(out=OG[2][:, g, ts], in0=acc[g].ap[:], in1=OG[2][:, g, ts], op=ALU.mult),
                         reads=[acc[g], OGb[2][tb]], pwrites=[OGb[2][tb]])


def _rope_perm():
    P = np.array([i + 8 if (i % 16) < 8 else i - 8 for i in range(32)])
    sgn = np.array([-1.0 if (i % 16) < 8 else 1.0 for i in range(32)], np.float32)
    return P, sgn


def _consts():
    bf = ml_dtypes.bfloat16
    c = {}
    c["identb"] = np.eye(128, dtype=np.float32).astype(bf)
    n = np.arange(128)
    ang = 2 * np.pi * np.outer(n, n) / 128.0
    c["cs128"] = (np.concatenate([np.cos(ang), np.sin(ang)], axis=1) / np.sqrt(128.0)).astype(np.float32).astype(bf)
    n = np.arange(256)
    ang = 2 * np.pi * np.outer(n, n) / 256.0
    c["c256"] = (np.cos(ang) / 16.0).astype(np.float32).astype(bf)
    c["ns256"] = (-np.sin(ang) / 16.0).astype(np.float32).astype(bf)
    sel = np.zeros((32, 128), np.float32)
    sel[np.arange(32), 64 + np.arange(32)] = 1.0
    c["sel32"] = sel.astype(bf)
    return c


def _lat_consts(q):
    bf = ml_dtypes.bfloat16
    c = {}
    P, sgn = _rope_perm()
    t = np.arange(1024 * q, 1024 * q + 1024)
    row = (t // 64).astype(np.float32)
    colp = (t % 64).astype(np.float32)
    half = 16
    inv = (10000.0 ** (-np.arange(0, half, 2, dtype=np.float32) / half)).astype(np.float32)
    ar = row[:, None] * inv[None, :]
    ac = colp[:, None] * inv[None, :]
    ang = np.concatenate([ar, ar, ac, ac], axis=-1)
    cos = np.cos(ang).astype(np.float32)
    sin = (np.sin(ang).astype(np.float32) * sgn[None, :]).astype(np.float32)
    rope = np.zeros((2, 128, 1024), np.float32)
    rope[0, 0:32] = cos.T
    rope[0, 64:96] = cos.T
    rope[1, 0:32] = sin.T
    rope[1, 64:96] = sin.T
    c["ropeT"] = rope
    n = np.arange(4096, dtype=np.float64)
    k = np.arange(1024 * q, 1024 * q + 1024, dtype=np.float64)
    ang = 2 * np.pi * ((np.outer(n, k)) % 4096) / 4096.0
    c["dftc"] = (np.cos(ang) / 64.0).astype(np.float32).astype(bf)
    c["dftns"] = (-np.sin(ang) / 64.0).astype(np.float32).astype(bf)
    kk = np.arange(128)
    kr, kcol = kk // 64, kk % 64
    mask = np.zeros((128, 48, 128), np.float32)
    for b in range(8):
        for slot in range(6):
            if slot < 5:
                lt = b - 2 + slot
            elif b == 0:
                lt = 3
            elif b == 7:
                lt = 4
            else:
                continue
            krow = 16 * q + 2 * lt + kr
            r = 16 * q + 2 * b + kr
            rs = np.clip(r - 4, 0, 56)
            row_ok = ((krow[:, None] >= 0) & (krow[:, None] < 64) &
                      (krow[:, None] >= rs[None, :]) & (krow[:, None] < rs[None, :] + 8))
            cs = np.clip(kcol - 8, 0, 48)
            col_ok = (kcol[:, None] >= cs[None, :]) & (kcol[:, None] < cs[None, :] + 16)
            mask[:, b * 6 + slot, :] = np.where(row_ok & col_ok, 1.0, 0.0)
    c["namask"] = mask.astype(bf)
    sel = np.zeros((128, 8), np.float32)
    if q - 1 >= 0:
        sel[:, q - 1] = 1.0
    if q + 1 <= 3:
        sel[:, 4 + q + 1] = 1.0
    c["selr"] = sel
    return c


_NC_CACHE = {}


def _get_nc(key=("full",)):
    if key not in _NC_CACHE:
        if key[0] == "full":
            kb = KB(run_ctx=True, run_lat=True)
        else:
            kb = KB(**dict(key[1]))
        _NC_CACHE[key] = kb.build()
    return _NC_CACHE[key]


def make_in_maps(inp):
    f = lambda a: np.ascontiguousarray(np.asarray(a, dtype=np.float32))
    x_prompt, x_sample = f(inp["x_prompt"]), f(inp["x_sample"])
    P, sgn = _rope_perm()
    w_in = f(inp["w_in"])
    w_uq = f(inp["w_uq"])
    shared = {}
    shared["w_ada"] = f(inp["w_ada"])
    shared["b_adaT"] = f(f(inp["b_ada"]).reshape(L, 24, 128).transpose(2, 0, 1))
    shared["norm_gT"] = f(f(inp["norm_g"]).reshape(L, 8, 128).transpose(2, 0, 1))
    shared["fin_gT"] = f(f(inp["final_norm_g"]).reshape(8, 128).T)
    shared["qn_gT"] = f(f(inp["q_norm_g"]).reshape(L, 2, 128).transpose(2, 0, 1))
    shared["kvn_gT"] = f(f(inp["kv_norm_g"]).T)
    shared["kvn_bc"] = f(np.broadcast_to(f(inp["kv_norm_g"])[None, :, :], (128, L, 128)))
    shared["w_in"] = w_in
    shared["w_krp"] = f(w_in[:, :, C_KR:C_KR + 32][:, :, P])
    shared["w_uq"] = w_uq
    wq = w_uq.reshape(L, 256, 8, 96).copy()
    wq[:, :, :, 64:96] = wq[:, :, :, 64:96][:, :, :, P]
    shared["w_uqp"] = f(wq.reshape(L, 256, 768))
    shared["w_ukv"] = f(inp["w_ukv"])
    shared["w_o_na"] = f(inp["w_o_na"])
    shared["w_o_mla"] = f(inp["w_o_mla"])
    shared["w_o_fn"] = f(inp["w_o_fourier"])
    shared["w_out"] = f(inp["w_out"])
    shared.update(_consts())
    nb = f(inp["na_bias"])
    kr = np.arange(128) // 64
    kc = np.arange(128) % 64
    dj = np.arange(7)
    dr = 2 * (dj[None, :, None] - 3) + kr[:, None, None] - kr[None, None, :]
    dc = np.clip(kc[:, None, None] - kc[None, None, :] + 15, 0, 30) + 0 * dr
    drc = np.clip(dr + 7, 0, 14)
    nab = nb[:, :, drc, dc]
    shared["nab"] = f(nab.transpose(0, 2, 1, 3, 4))
    cna_k, cna_v = f(inp["cache_na_k"]), f(inp["cache_na_v"])
    cckv, ckr = f(inp["cache_mla_ckv"]), f(inp["cache_mla_krope"])
    cvec, c_ctx = f(inp["c"]), f(inp["c_ctx"])
    maps = []
    latc = [_lat_consts(q) for q in range(4)]
    for core in range(8):
        b, q = core // 4, core % 4
        m = dict(shared)
        m["xc"] = f(x_prompt[2 * core:2 * core + 2].reshape(512, D).T)
        m["xl"] = f(x_sample[b, 1024 * q:1024 * q + 1024].T)
        m["cond"] = f(np.stack([c_ctx, cvec[b]], axis=1))
        m.update(latc[q])
        m["cnakT"] = f(cna_k[b].reshape(L, 256, 512).transpose(0, 2, 1))
        m["cnav"] = f(cna_v[b].reshape(L, 256, 512))
        m["cckvT"] = f(cckv[b].transpose(0, 2, 1))
        m["ckrT"] = f(ckr[b].transpose(0, 2, 1))
        maps.append(m)
    return maps


def assemble(res):
    r = res.results
    y_prompt = np.stack([r[c]["yc"].T.reshape(2, 256, D) for c in range(8)]).reshape(16, 256, D)
    y_sample = np.stack([np.concatenate([r[4 * b + q]["yl"].T for q in range(4)], axis=0) for b in range(2)])

    def cache(name, w):
        return np.concatenate([r[c][name].reshape(L, 2, 256, w).transpose(1, 0, 2, 3) for c in range(8)], axis=0)
    nk = cache("onk", 512).reshape(16, L, 256, 8, 64)
    nv = cache("onv", 512).reshape(16, L, 256, 8, 64)
    nckv = cache("ockv", 128)
    nkr = cache("okr", 32)
    return tuple(np.ascontiguousarray(a.astype(np.float32)) for a in (y_prompt, y_sample, nk, nv, nckv, nkr))


def kernel(**inputs):
    nc = _get_nc()
    in_maps = make_in_maps(inputs)
    res = run_bass_kernel_spmd(nc, in_maps, core_ids=list(range(8)))
    return assemble(res)
```

```python
import os
import numpy as np
import ml_dtypes
from contextlib import ExitStack
import concourse.bass as bass
import concourse.mybir as mybir
from concourse.bass_utils import run_bass_kernel_spmd

F32 = mybir.dt.float32
BF16 = mybir.dt.bfloat16
AF = mybir.ActivationFunctionType
ALU = mybir.AluOpType

L = 4
D = 1024
D_IN = 7072
NA_SCALE = 64 ** -0.5
MLA_SCALE = 96 ** -0.5
EPS = 1e-6
C_Q, C_K, C_V, C_GNA, C_QLAT, C_CKV, C_KR, C_GMLA, C_UFN, C_GFN, C_MRG = (
    0, 512, 1024, 1536, 2048, 2304, 2432, 2464, 2976, 3488, 4000)
NEG = -30000.0
GROUPS = [[0, 1, 2, 3], [4, 5, 6, 7]]


class Buf:
    __slots__ = ("ap", "w", "r", "pm", "name", "fw", "excl")

    def __init__(self, ap, name=""):
        self.ap = ap
        self.w = {}
        self.r = {}
        self.pm = False
        self.fw = {}
        self.excl = False
        self.name = name


class Sched:
    LIMIT = 12000
    NLANES = 8

    def __init__(self, nc, es):
        self.nc = nc
        self.es = es
        self.eng = dict(pe=nc.tensor, act=nc.scalar, dve=nc.vector, pool=nc.gpsimd, sp=nc.sync)
        self.sems = []
        self.cur = {}
        self.known = {e: {} for e in self.eng}
        self.lanes = {e: [] for e in self.eng}
        self.rr = {e: 0 for e in self.eng}
        self.pe_sems = set()
        self.nins = 0

    def new_sem(self):
        h = self.es.enter_context(self.nc.semaphore("sm%d" % len(self.sems)))
        self.sems.append(h)
        return len(self.sems) - 1

    def _deps(self, reads, writes, pwrites):
        deps = {}

        def add(d):
            for k, v in d.items():
                if deps.get(k, 0) < v:
                    deps[k] = v
        for b in reads:
            add(b.w)
            if b.excl:
                add(b.r)
        for b in writes:
            add(b.w)
            add(b.r)
        for b in pwrites:
            add(b.r)
            add(b.fw)
            if not b.pm:
                add(b.w)
        return deps

    def _wait(self, e, deps):
        kn = self.known[e]
        for sem, val in deps.items():
            if e == "pe" and sem in self.pe_sems:
                continue
            if kn.get(sem, 0) >= val:
                continue
            self.eng[e].wait_ge(self.sems[sem], val)
            kn[sem] = val
            self.nins += 1

    def _mark(self, stamp, reads, writes, pwrites):
        s, v = stamp
        for b in writes:
            b.w = {s: v}
            b.fw = {s: v}
            b.r = {}
            b.pm = False
        for b in pwrites:
            b.w[s] = v
            b.pm = True
        for b in reads:
            b.r[s] = v

    def op(self, e, fn, reads=(), writes=(), pwrites=()):
        self._wait(e, self._deps(reads, writes, pwrites))
        c = self.cur.get(e)
        if c is None or c[1] >= self.LIMIT:
            c = [self.new_sem(), 0]
            self.cur[e] = c
            if e == "pe":
                self.pe_sems.add(c[0])
        c[1] += 1
        ins = fn(self.eng[e])
        ins.then_inc(self.sems[c[0]], 1)
        self.nins += 1
        self._mark((c[0], c[1]), reads, writes, pwrites)

    def dma(self, e, fn, reads=(), writes=(), pwrites=(), inc=16):
        deps = self._deps(reads, writes, pwrites)
        lanes = self.lanes[e]
        if len(lanes) < self.NLANES:
            lanes.append([self.new_sem(), 0])
            lane = lanes[-1]
        else:
            lane = lanes[self.rr[e] % self.NLANES]
            self.rr[e] += 1
        if lane[1] > 0 and deps.get(lane[0], 0) < lane[1]:
            deps[lane[0]] = lane[1]
        self._wait(e, deps)
        lane[1] += inc
        ins = fn(self.eng[e])
        ins.then_inc(self.sems[lane[0]], inc)
        self.nins += 1
        self._mark((lane[0], lane[1]), reads, writes, pwrites)

    def cc(self, fn, reads=(), writes=()):
        deps = self._deps(reads, writes, ())
        if not hasattr(self, "cclane"):
            self.cclane = [self.new_sem(), 0]
        lane = self.cclane
        if lane[1] > 0 and deps.get(lane[0], 0) < lane[1]:
            deps[lane[0]] = lane[1]
        self._wait("pool", deps)
        lane[1] += 1
        ins = fn(self.eng["pool"])
        ins.then_inc(self.sems[lane[0]], 1)
        self.nins += 1
        self._mark((lane[0], lane[1]), reads, writes, ())

    def barrier(self):
        allst = {}
        for e, c in self.cur.items():
            allst[c[0]] = c[1]
        for e, lanes in self.lanes.items():
            for ln in lanes:
                if ln[1] > 0:
                    allst[ln[0]] = ln[1]
        for e in self.eng:
            d = dict(allst)
            c = self.cur.get(e)
            if e == "pe" and c is not None:
                d.pop(c[0], None)
            self._wait(e, d)

    def finish(self):
        allst = {}
        for e, lanes in self.lanes.items():
            for ln in lanes:
                if ln[1] > 0:
                    allst[ln[0]] = ln[1]
        for e, c in self.cur.items():
            allst[c[0]] = c[1]
        if hasattr(self, "cclane") and self.cclane[1] > 0:
            allst[self.cclane[0]] = self.cclane[1]
        d = dict(allst)
        self._wait("sp", d)


class KB:
    def __init__(self, run_ctx=True, run_lat=True, nlayers=L, dbg=False, lw=L, stage=99):
        self.LW = lw
        self.stage = stage
        self.run_ctx = run_ctx
        self.run_lat = run_lat
        self.nlayers = nlayers
        self.nc = bass.Bass("TRN2", target_bir_lowering=False)
        self.es = ExitStack()
        self.s = Sched(self.nc, self.es)
        self.tcount = 0

    def din(self, name, shape, dt=F32):
        return self.nc.dram_tensor(name, list(shape), dt, kind="ExternalInput").ap()

    def dout(self, name, shape, dt=F32):
        return self.nc.dram_tensor(name, list(shape), dt, kind="ExternalOutput").ap()

    def dint(self, name, shape, dt=BF16):
        return self.nc.dram_tensor(name, list(shape), dt).ap()

    def sb(self, st, name, shape, dt):
        self.tcount += 1
        return st.enter_context(self.nc.sbuf_tensor("%s_%d" % (name, self.tcount), list(shape), dt))

    def ps(self, grp):
        idxs = self.psgrp[grp]
        i = idxs[self.psrr[grp] % len(idxs)]
        self.psrr[grp] += 1
        return self.PS[i]

    def build(self):
        nc, s, es = self.nc, self.s, self.es
        NL = self.nlayers
        d = {}
        d["xc"] = self.din("xc", [D, 512])
        d["xl"] = self.din("xl", [D, 1024])
        d["cond"] = self.din("cond", [D, 2])
        d["w_ada"] = self.din("w_ada", [self.LW, D, 3 * D])
        d["b_adaT"] = self.din("b_adaT", [128, L, 24])
        d["norm_gT"] = self.din("norm_gT", [128, L, 8])
        d["fin_gT"] = self.din("fin_gT", [128, 8])
        d["qn_gT"] = self.din("qn_gT", [128, L, 2])
        d["kvn_gT"] = self.din("kvn_gT", [128, L])
        d["kvn_bc"] = self.din("kvn_bc", [128, L, 128])
        d["w_in"] = self.din("w_in", [self.LW, D, D_IN])
        d["w_krp"] = self.din("w_krp", [self.LW, D, 32])
        d["w_uq"] = self.din("w_uq", [self.LW, 256, 768])
        d["w_uqp"] = self.din("w_uqp", [self.LW, 256, 768])
        d["w_ukv"] = self.din("w_ukv", [self.LW, 128, 1024])
        d["w_o_na"] = self.din("w_o_na", [self.LW, 512, D])
        d["w_o_mla"] = self.din("w_o_mla", [self.LW, 512, D])
        d["w_o_fn"] = self.din("w_o_fn", [self.LW, 512, D])
        d["w_out"] = self.din("w_out", [self.LW, D, D])
        d["identb"] = self.din("identb", [128, 128], BF16)
        d["cs128"] = self.din("cs128", [128, 256], BF16)
        d["c256"] = self.din("c256", [256, 256], BF16)
        d["ns256"] = self.din("ns256", [256, 256], BF16)
        d["sel32"] = self.din("sel32", [32, 128], BF16)
        d["ropeT"] = self.din("ropeT", [2, 128, 1024])
        d["dftc"] = self.din("dftc", [4096, 1024], BF16)
        d["dftns"] = self.din("dftns", [4096, 1024], BF16)
        d["nab"] = self.din("nab", [self.LW, 128, 8, 7, 128])
        d["namask"] = self.din("namask", [128, 48, 128], BF16)
        d["selr"] = self.din("selr", [128, 8])
        d["cnakT"] = self.din("cnakT", [L, 512, 256])
        d["cnav"] = self.din("cnav", [L, 256, 512])
        d["cckvT"] = self.din("cckvT", [L, 128, 256])
        d["ckrT"] = self.din("ckrT", [L, 32, 256])
        d["yc"] = self.dout("yc", [D, 512])
        d["yl"] = self.dout("yl", [D, 1024])
        d["onk"] = self.dout("onk", [L, 512, 512])
        d["onv"] = self.dout("onv", [L, 512, 512])
        d["ockv"] = self.dout("ockv", [L, 512, 128])
        d["okr"] = self.dout("okr", [L, 512, 32])
        d["pay_mla"] = [self.dint("pay_mla%d" % l, [160, 1024]) for l in range(L)]
        d["g_mla"] = [self.dint("g_mla%d" % l, [640, 1024]) for l in range(L)]
        d["pay_k"] = [self.dint("pay_k%d" % l, [512, 512]) for l in range(L)]
        d["g_k"] = [self.dint("g_k%d" % l, [2048, 512]) for l in range(L)]
        d["pay_v"] = [self.dint("pay_v%d" % l, [512, 768]) for l in range(L)]
        d["g_v"] = [self.dint("g_v%d" % l, [2048, 768]) for l in range(L)]
        d["pay_ab0"] = [self.dint("pay_ab0_%d" % l, [512, 1024]) for l in range(L)]
        d["pay_ab1"] = [self.dint("pay_ab1_%d" % l, [512, 1024]) for l in range(L)]
        d["g_ab0"] = [self.dint("g_ab0_%d" % l, [2048, 1024]) for l in range(L)]
        d["g_ab1"] = [self.dint("g_ab1_%d" % l, [2048, 1024]) for l in range(L)]
        self.d = d
        self.db = {k: [Buf(a) for a in d[k]] for k in ("pay_mla", "g_mla", "pay_k", "g_k", "pay_v", "g_v", "pay_ab0", "pay_ab1", "g_ab0", "g_ab1")}

        self.PS = [Buf(es.enter_context(nc.psum_tensor("ps%d" % i, [128, 512], F32)), "ps%d" % i) for i in range(8)]
        for b in self.PS:
            b.excl = True
        self.psgrp = {"g": [0, 1, 2, 3], "o": [4, 5], "x": [6, 7], "all": list(range(8))}
        self.psrr = {k: 0 for k in self.psgrp}

        P = es
        self.identb = Buf(self.sb(P, "identb", [128, 128], BF16))
        self.onesb = Buf(self.sb(P, "onesb", [128, 3, 128], BF16))
        self.onesf = Buf(self.sb(P, "onesf", [128, 128], F32))
        self.cs128 = Buf(self.sb(P, "cs128", [128, 256], BF16))
        self.c256 = Buf(self.sb(P, "c256", [128, 2, 256], BF16))
        self.ns256 = Buf(self.sb(P, "ns256", [128, 2, 256], BF16))
        self.sel32 = Buf(self.sb(P, "sel32", [32, 128], BF16))
        self.condt = Buf(self.sb(P, "condt", [128, 8, 2], F32))
        self.condb = Buf(self.sb(P, "condb", [128, 8, 2], BF16))
        self.bada = Buf(self.sb(P, "bada", [128, L, 24], F32))
        self.normg = Buf(self.sb(P, "normg", [128, L, 8], F32))
        self.fing = Buf(self.sb(P, "fing", [128, 8], F32))
        self.qng = Buf(self.sb(P, "qng", [128, L, 2], F32))
        self.kvng = Buf(self.sb(P, "kvng", [128, L], F32))
        self.kvnbc = Buf(self.sb(P, "kvnbc", [128, L, 128], F32))
        self.MOD = Buf(self.sb(P, "mod", [128, L, 24, 2], F32))
        self.GS = Buf(self.sb(P, "gs", [128, L, 8, 2], F32))
        self.WB = [Buf(self.sb(P, "wb%d" % i, [128, 8, 512], BF16)) for i in range(3)]
        self.wbi = 0
        self.SQ = [Buf(self.sb(P, "sq%d" % i, [128, 512], BF16)) for i in range(2)]
        self.TMP = [Buf(self.sb(P, "tmp%d" % i, [128, 512], F32)) for i in range(2)]
        self.RS = Buf(self.sb(P, "rs", [128, 512], F32))
        self.RD = Buf(self.sb(P, "rd", [128, 512], F32))
        self.BCS = Buf(self.sb(P, "bcs", [128, 512], F32))
        self.TMPO = Buf(self.sb(P, "tmpo", [128, 512], F32))
        self.PT = [Buf(self.sb(P, "pt%d" % i, [128, 512], BF16)) for i in range(5)]
        self.pending = None
        self.pti = 0
        self.sqi = 0
        self.tmi = 0

        def ld(buf, src, e="sp"):
            s.dma(e, lambda q: q.dma_start(out=buf.ap[:], in_=src), writes=[buf])
        ld(self.identb, d["identb"][:, :])
        ld(self.cs128, d["cs128"][:, :])
        ld(self.c256, d["c256"].rearrange("(t p) n -> p t n", p=128))
        ld(self.ns256, d["ns256"].rearrange("(t p) n -> p t n", p=128))
        ld(self.sel32, d["sel32"][:, :])
        ld(self.condt, d["cond"].rearrange("(c p) n -> p c n", p=128))
        ld(self.bada, d["b_adaT"][:, :, :])
        ld(self.normg, d["norm_gT"][:, :, :])
        ld(self.fing, d["fin_gT"][:, :])
        ld(self.qng, d["qn_gT"][:, :, :])
        ld(self.kvng, d["kvn_gT"][:, :])
        ld(self.kvnbc, d["kvn_bc"][:, :, :])
        for i, v in enumerate((1.0 / 1024, 1.0 / 256, 1.0 / 128)):
            s.op("dve", lambda q: q.memset(self.onesb.ap[:, i, :], v), pwrites=[self.onesb])
        s.op("dve", lambda q: q.memset(self.onesf.ap[:], 1.0), writes=[self.onesf])
        s.op("act", lambda q: q.activation(out=self.condb.ap[:], in_=self.condt.ap[:], func=AF.Silu),
             reads=[self.condt], writes=[self.condb])

        for l in range(NL if not self.run_ctx else min(1, NL)):
            for jb in range(6):
                self.adaln_compute(l, jb, self.adaln_load(l, jb))
            self.adaln_gs(l)

        if self.run_ctx and self.stage >= 1:
            self.chain("ctx")
        if self.run_lat and self.stage >= 1:
            self.chain("lat")
        s.finish()
        return nc

    def tick_wb_hook(self):
        hooks = getattr(self, "wb_hooks", [])
        self.wb_hooks = []
        for h in hooks:
            h[0] -= 1
            if h[0] <= 0:
                h[1]()
            else:
                self.wb_hooks.append(h)

    def flush_wb_hook(self):
        hooks = getattr(self, "wb_hooks", [])
        self.wb_hooks = []
        for h in hooks:
            h[1]()

    def adaln_load(self, l, jb):
        s, d = self.s, self.d
        wb = self.next_wb()
        s.dma("pool", lambda q: q.dma_start(
            out=wb.ap[:], in_=d["w_ada"][l, :, jb * 512:(jb + 1) * 512].rearrange("(c p) n -> p c n", p=128)),
            writes=[wb])
        return wb

    def adaln_compute(self, l, jb, wb):
        s = self.s
        for jj in range(4):
            j = jb * 4 + jj
            ps = self.ps("g")
            for kc in range(8):
                s.op("pe", lambda q: q.matmul(ps.ap[:, 0:2], lhsT=wb.ap[:, kc, jj * 128:(jj + 1) * 128],
                                              rhs=self.condb.ap[:, kc, :], start=(kc == 0), stop=(kc == 7)),
                     reads=[wb, self.condb], pwrites=[ps])
            s.op("dve", lambda q: q.tensor_scalar(out=self.MOD.ap[:, l, j, :], in0=ps.ap[:, 0:2],
                                                  scalar1=self.bada.ap[:, l, j:j + 1], scalar2=0.0,
                                                  op0=ALU.add, op1=ALU.add),
                 reads=[ps, self.bada], pwrites=[self.MOD])

    def adaln_gs(self, l):
        s = self.s
        for kc in range(8):
            s.op("dve", lambda q: q.tensor_scalar(out=self.GS.ap[:, l, kc, :], in0=self.MOD.ap[:, l, 8 + kc, :],
                                                  scalar1=1.0, scalar2=self.normg.ap[:, l, kc:kc + 1],
                                                  op0=ALU.add, op1=ALU.mult),
                 reads=[self.MOD, self.normg], pwrites=[self.GS])

    def next_wb(self):
        wb = self.WB[self.wbi % 3]
        self.wbi += 1
        return wb

    def next_pt(self):
        b = self.PT[self.pti % 5]
        self.pti += 1
        return b

    def run_pipeline(self, items, LA=3):
        n = len(items)
        for step in range(n + LA):
            if step < n:
                items[step][0]()
            if step == min(2, n - 1):
                self.flush_pending()
            if step >= LA:
                items[step - LA][1]()

    def run_pipeline2(self, items, LA=2, DL=2):
        n = len(items)
        due = []
        for step in range(n + LA + DL + 1):
            if step < n:
                items[step][0]()
            if LA <= step < n + LA:
                it = items[step - LA]
                it[1]()
                if it[2] is not None:
                    due.append((step + DL, it[2]))
            while due and due[0][0] <= step:
                due.pop(0)[1]()

    def flush_pending(self):
        if self.pending is not None:
            p = self.pending
            self.pending = None
            p()

    def next_sq(self):
        b = self.SQ[self.sqi % 2]
        self.sqi += 1
        return b

    def next_tmp(self):
        b = self.TMP[self.tmi % 2]
        self.tmi += 1
        return b

    def rstd_of(self, chunks, reads, ones_idx, n=512):
        s = self.s
        ps = self.ps("x")
        nk = len(chunks)
        for i, ch in enumerate(chunks):
            sq = self.next_sq()
            s.op("act", lambda q: q.activation(out=sq.ap[:, 0:n], in_=ch, func=AF.Square), reads=reads, writes=[sq])
            s.op("pe", lambda q: q.matmul(ps.ap[:, 0:n], lhsT=self.onesb.ap[:, ones_idx, :], rhs=sq.ap[:, 0:n],
                                          start=(i == 0), stop=(i == nk - 1)),
                 reads=[sq, self.onesb], pwrites=[ps])
        s.op("act", lambda q: q.activation(out=self.RS.ap[:, 0:n], in_=ps.ap[:, 0:n], func=AF.Sqrt, bias=EPS, scale=1.0),
             reads=[ps], writes=[self.RS])
        s.op("dve", lambda q: q.reciprocal(out=self.RS.ap[:, 0:n], in_=self.RS.ap[:, 0:n]),
             reads=[self.RS], writes=[self.RS])
        return self.RS

    def attn_finish(self, po, h, OG, OGb, ts, n=512):
        s = self.s
        j, par = h // 2, h % 2
        base = 64 * par
        dp = 64 if par == 0 else 0
        s.op("dve", lambda q: q.reciprocal(out=self.RD.ap[dp:dp + 1, 0:n], in_=po.ap[dp:dp + 1, 0:n]),
             reads=[po], writes=[self.RD])
        bc = self.ps("x")
        s.op("pe", lambda q: q.matmul(bc.ap[:, 0:n], lhsT=self.onesf.ap[dp:dp + 1, :], rhs=self.RD.ap[dp:dp + 1, 0:n],
                                      start=True, stop=True),
             reads=[self.RD, self.onesf], writes=[bc])
        s.op("dve", lambda q: q.tensor_copy(out=self.BCS.ap[base:base + 64, 0:n], in_=bc.ap[base:base + 64, 0:n]),
             reads=[bc], writes=[self.BCS])
        s.op("dve", lambda q: q.tensor_tensor(out=self.TMPO.ap[base:base + 64, 0:n], in0=po.ap[base:base + 64, 0:n],
                                              in1=self.BCS.ap[base:base + 64, 0:n], op=ALU.mult),
             reads=[po, self.BCS], writes=[self.TMPO])
        s.op("dve", lambda q: q.tensor_tensor(out=OG[base:base + 64, j, ts], in0=self.TMPO.ap[base:base + 64, 0:n],
                                              in1=OG[base:base + 64, j, ts], op=ALU.mult),
             reads=[self.TMPO, OGb], pwrites=[OGb])

    def chain(self, mode):
        nc, s, d = self.nc, self.s, self.d
        lat = (mode == "lat")
        T = 1024 if lat else 512
        NTB = T // 512
        NT = T // 128
        col = 1 if lat else 0
        with ExitStack() as C:
            X = self.sb(C, "X", [128, 8, T], F32)
            Xb = [Buf(X) for _ in range(NTB)]
            XM = self.sb(C, "XM", [128, 8, T], BF16)
            XMb = [Buf(XM) for _ in range(NTB)]
            OG = [self.sb(C, "OG%d" % r, [128, 4, T], BF16) for r in range(3)]
            OGb = [[Buf(OG[r]) for _ in range(NTB)] for r in range(3)]
            xin = d["xl"] if lat else d["xc"]
            for tb in range(NTB):
                s.dma("sp", lambda q: q.dma_start(
                    out=X[:, :, tb * 512:(tb + 1) * 512],
                    in_=xin[:, tb * 512:(tb + 1) * 512].rearrange("(c p) n -> p c n", p=128)), writes=[Xb[tb]])
            ctxs = dict(lat=lat, T=T, NTB=NTB, NT=NT, col=col, X=X, Xb=Xb, XM=XM, XMb=XMb, OG=OG, OGb=OGb)
            if lat:
                ROPE = self.sb(C, "rope", [128, 2, 1024], F32)
                ROPEb = Buf(ROPE)
                s.dma("sp", lambda q: q.dma_start(out=ROPE[:], in_=d["ropeT"].rearrange("a p n -> p a n")), writes=[ROPEb])
                SELR = self.sb(C, "selr", [128, 8], F32)
                SELRb = Buf(SELR)
                s.dma("sp", lambda q: q.dma_start(out=SELR[:], in_=d["selr"][:, :]), writes=[SELRb])
                ctxs.update(ROPE=ROPE, ROPEb=ROPEb, SELR=SELR, SELRb=SELRb)
            for l in range(self.nlayers):
                self.layer(l, ctxs)
            yout = d["yl"] if lat else d["yc"]
            for tb in range(NTB):
                ts = slice(tb * 512, (tb + 1) * 512)
                rs = self.rstd_of([X[:, kc, ts] for kc in range(8)], [Xb[tb]], 0)
                for kc in range(8):
                    tmp = self.next_tmp()
                    s.op("dve", lambda q: q.tensor_tensor(out=tmp.ap[:], in0=X[:, kc, ts], in1=rs.ap[:], op=ALU.mult),
                         reads=[Xb[tb], rs], writes=[tmp])
                    s.op("act", lambda q: q.activation(out=tmp.ap[:], in_=tmp.ap[:], func=AF.Identity,
                                                       scale=self.fing.ap[:, kc:kc + 1]),
                         reads=[tmp, self.fing], writes=[tmp])
                    s.dma("sp", lambda q: q.dma_start(out=yout[kc * 128:(kc + 1) * 128, ts], in_=tmp.ap[:]), reads=[tmp])
            s.barrier()

    def layer(self, l, c):
        nc, s, d = self.nc, self.s, self.d
        lat, T, NTB, NT, col = c["lat"], c["T"], c["NTB"], c["NT"], c["col"]
        X, Xb, XM, XMb, OG, OGb = c["X"], c["Xb"], c["XM"], c["XMb"], c["OG"], c["OGb"]
        w_in = d["w_in"]

        def TS(tb):
            return slice(tb * 512, (tb + 1) * 512)

        for tb in range(NTB):
            ts = TS(tb)
            rs = self.rstd_of([X[:, kc, ts] for kc in range(8)], [Xb[tb]], 0)
            for kc in range(8):
                tmp = self.next_tmp()
                s.op("dve", lambda q: q.tensor_tensor(out=tmp.ap[:], in0=X[:, kc, ts], in1=rs.ap[:], op=ALU.mult),
                     reads=[Xb[tb], rs], writes=[tmp])
                s.op("act", lambda q: q.activation(out=XM[:, kc, ts], in_=tmp.ap[:], func=AF.Identity,
                                                   scale=self.GS.ap[:, l, kc, col:col + 1],
                                                   bias=self.MOD.ap[:, l, kc, col:col + 1]),
                     reads=[tmp, self.GS, self.MOD], pwrites=[XMb[tb]])

        if self.stage < 3:
            return
        WOS = ExitStack()
        wo = {}

        def ensure_wo(load=True):
            if "WO" not in wo:
                wo["WO"] = [self.sb(WOS, "WO%d" % r, [128, 4, 1024], BF16) for r in range(3)]
                wo["WOb"] = [Buf(wo["WO"][r]) for r in range(3)]
            if load and "loaded" not in wo:
                wo["loaded"] = True
                for r, nm in enumerate(("w_o_na", "w_o_mla", "w_o_fn")):
                    s.dma("pool", lambda q: q.dma_start(out=wo["WO"][r][:], in_=d[nm][l].rearrange("(c p) n -> p c n", p=128)),
                          writes=[wo["WOb"][r]])
        if not lat:
            ensure_wo(load=False)
        A0 = ExitStack()
        A1 = ExitStack()
        QL = self.sb(A0, "QL", [128, 2, T], BF16)
        QLb = [Buf(QL) for _ in range(NTB)]
        QT = self.sb(A1, "QT", [128, 4, T], BF16)
        QTb = [Buf(QT) for _ in range(NTB)]
        KT = self.sb(A1, "KT", [128, 4, T], BF16)
        KTb = [Buf(KT) for _ in range(NTB)]
        V = self.sb(A1, "V", [128, NT, 768], BF16)
        Vb = [Buf(V) for _ in range(NT)]
        nap = None
        if lat and self.stage >= 4.1:
            nap = self.lat_na_prefetch(l, A1)
        with ExitStack() as A:
            CK = self.sb(A, "CK", [128, T], BF16)
            CKb = [Buf(CK) for _ in range(NTB)]
            KR = self.sb(A, "KR", [32, T], BF16)
            KRb = [Buf(KR) for _ in range(NTB)]
            UF = self.sb(A, "UF", [128, 4, 512], BF16)
            UFb = Buf(UF)
            QLR = self.sb(A, "QLR", [128, 3, 512], F32)
            QLRb = Buf(QLR)
            WKRP = self.sb(A, "WKRP", [128, 8, 32], BF16)
            WKRPb = Buf(WKRP)
            ABS = [self.sb(A, "ABS%d" % i, [128, 1024], BF16) for i in range(2)]
            ABSb = [Buf(ABS[i]) for i in range(2)]
            if not lat:
                ABC = self.sb(A, "ABC", [128, 4, 1024], BF16)
                ABCb = [Buf(ABC) for _ in range(4)]
                STG = [self.sb(A, "STG%d" % i, [128, 512], F32) for i in range(2)]
                STGb = [Buf(STG[i]) for i in range(2)]
                STK = self.sb(A, "STK", [128, 160], F32)
                STKb = Buf(STK)
                STS = self.sb(A, "STS", [128, 4], F32)
                STSb = Buf(STS)
            s.op("dve", lambda q: q.memset(V[:, :, :], 0.0), writes=Vb)
            for p in range(4):
                s.op("dve", lambda q: q.memset(V[:, :, p * 192 + 64:p * 192 + 65], 1.0), pwrites=Vb)
            if lat:
                s.dma("pool", lambda q: q.dma_start(out=WKRP[:], in_=d["w_krp"][l].rearrange("(c p) n -> p c n", p=128)),
                      writes=[WKRPb])

            def load_w(c0, n):
                self.tick_wb_hook()
                wb = self.next_wb()
                s.dma("pool", lambda q: q.dma_start(
                    out=wb.ap[:, :, 0:n], in_=w_in[l, :, c0:c0 + n].rearrange("(c p) n -> p c n", p=128)), writes=[wb])
                return wb

            def mm_fm(ps, M, wb, wsl, tb, extra_reads=()):
                for kc in range(8):
                    s.op("pe", lambda q: q.matmul(ps.ap[0:M, :], lhsT=wb.ap[:, kc, wsl], rhs=XM[:, kc, TS(tb)],
                                                  start=(kc == 0), stop=(kc == 7)),
                         reads=[wb, XMb[tb]], pwrites=[ps])

            def blk_qk(sel=None):
                for (c0, dst, dstb, scl) in ((C_Q, QT, QTb, NA_SCALE), (C_K, KT, KTb, 1.0)) if self.stage >= 3.1 else ():
                    if sel is not None and c0 != sel:
                        continue
                    wb = load_w(c0, 512)
                    for tb in range(NTB):
                        for j in range(4):
                            ps = self.ps("g")
                            mm_fm(ps, 128, wb, slice(j * 128, (j + 1) * 128), tb)
                            if j % 2 == 0:
                                s.op("dve", lambda q: q.tensor_scalar(out=dst[:, j, TS(tb)], in0=ps.ap[:], scalar1=scl,
                                                                      scalar2=0.0, op0=ALU.mult, op1=ALU.add),
                                     reads=[ps], pwrites=[dstb[tb]])
                            else:
                                s.op("act", lambda q: q.activation(out=dst[:, j, TS(tb)], in_=ps.ap[:], func=AF.Identity,
                                                                   scale=scl),
                                     reads=[ps], pwrites=[dstb[tb]])
                    if (not lat) and c0 == C_K:
                        for t in range(NT):
                            ps = self.ps("g")
                            for kc in range(8):
                                s.op("pe", lambda q: q.matmul(ps.ap[:], lhsT=XM[:, kc, t * 128:(t + 1) * 128],
                                                              rhs=wb.ap[:, kc, :], start=(kc == 0), stop=(kc == 7)),
                                     reads=[wb, XMb[0]], pwrites=[ps])
                            st = STGb[t % 2]
                            s.op("act", lambda q: q.activation(out=st.ap[:], in_=ps.ap[:], func=AF.Identity),
                                 reads=[ps], writes=[st])
                            s.dma("sp", lambda q: q.dma_start(out=d["onk"][l, t * 128:(t + 1) * 128, :], in_=st.ap[:]),
                                  reads=[st])
            def blk_v(sel=None):
                wb = load_w(C_V, 512)
                for t in range(NT) if self.stage >= 3.2 else ():
                    ps = self.ps("g")
                    tb = t // 4
                    for kc in range(8):
                        s.op("pe", lambda q: q.matmul(ps.ap[:], lhsT=XM[:, kc, t * 128:(t + 1) * 128], rhs=wb.ap[:, kc, :],
                                                      start=(kc == 0), stop=(kc == 7)),
                             reads=[wb, XMb[tb]], pwrites=[ps])
                    vv = V[:, t, :].rearrange("p (a b) -> p a b", b=192)
                    pv = ps.ap[:].rearrange("p (a e x) -> p a e x", e=2, x=64)
                    s.op("dve", lambda q: q.tensor_copy(out=vv[:, :, 0:64], in_=pv[:, :, 0, :]), reads=[ps], pwrites=[Vb[t]])
                    s.op("dve", lambda q: q.tensor_copy(out=vv[:, :, 128:192], in_=pv[:, :, 1, :]), reads=[ps], pwrites=[Vb[t]])
                    if not lat and not os.environ.get("KDBG_NOONV"):
                        st = STGb[t % 2]
                        s.op("dve", lambda q: q.tensor_copy(out=st.ap[:], in_=ps.ap[:]), reads=[ps], writes=[st])
                        s.dma("sp", lambda q: q.dma_start(out=d["onv"][l, t * 128:(t + 1) * 128, :], in_=st.ap[:]), reads=[st])
            def blk_gates(sel=None):
                for (c0, r) in ((C_GNA, 0), (C_GMLA, 1), (C_GFN, 2)) if self.stage >= 3.3 else ():
                    wb = load_w(c0, 512)
                    for tb in range(NTB):
                        for j in range(4):
                            ps = self.ps("g")
                            mm_fm(ps, 128, wb, slice(j * 128, (j + 1) * 128), tb)
                            s.op("act", lambda q: q.activation(out=OG[r][:, j, TS(tb)], in_=ps.ap[:], func=AF.Silu),
                                 reads=[ps], pwrites=[OGb[r][tb]])
            def blk_qlat(sel=None):
                wb = load_w(C_QLAT, 416)
                for tb in range(NTB) if self.stage >= 3.4 else ():
                    ts = TS(tb)
                    for j in range(3):
                        ps = self.ps("g")
                        mm_fm(ps, 128, wb, slice(j * 128, (j + 1) * 128), tb)
                        s.op("dve", lambda q: q.tensor_copy(out=QLR[:, j, :], in_=ps.ap[:]), reads=[ps], pwrites=[QLRb])
                    rs = self.rstd_of([QLR[:, 0, :], QLR[:, 1, :]], [QLRb], 1)
                    for j in range(2):
                        tmp = self.next_tmp()
                        s.op("dve", lambda q: q.tensor_tensor(out=tmp.ap[:], in0=QLR[:, j, :], in1=rs.ap[:], op=ALU.mult),
                             reads=[QLRb, rs], writes=[tmp])
                        s.op("act", lambda q: q.activation(out=QL[:, j, ts], in_=tmp.ap[:], func=AF.Identity,
                                                           scale=self.qng.ap[:, l, j:j + 1]),
                             reads=[tmp, self.qng], pwrites=[QLb[tb]])
                    rs = self.rstd_of([QLR[:, 2, :]], [QLRb], 2)
                    tmp = self.next_tmp()
                    s.op("dve", lambda q: q.tensor_tensor(out=tmp.ap[:], in0=QLR[:, 2, :], in1=rs.ap[:], op=ALU.mult),
                         reads=[QLRb, rs], writes=[tmp])
                    s.op("act", lambda q: q.activation(out=CK[:, ts], in_=tmp.ap[:], func=AF.Identity,
                                                       scale=self.kvng.ap[:, l:l + 1]),
                         reads=[tmp, self.kvng], pwrites=[CKb[tb]])
                    ps = self.ps("g")
                    mm_fm(ps, 32, wb, slice(384, 416), tb)
                    if lat:
                        ps2 = self.ps("g")
                        for kc in range(8):
                            s.op("pe", lambda q: q.matmul(ps2.ap[0:32, :], lhsT=WKRP[:, kc, :], rhs=XM[:, kc, ts],
                                                          start=(kc == 0), stop=(kc == 7)),
                                 reads=[WKRPb, XMb[tb]], pwrites=[ps2])
                        t1 = self.next_tmp()
                        t2 = self.next_tmp()
                        s.op("dve", lambda q: q.tensor_tensor(out=t1.ap[0:32, :], in0=ps.ap[0:32, :],
                                                              in1=c["ROPE"][0:32, 0, ts], op=ALU.mult),
                             reads=[ps, c["ROPEb"]], writes=[t1])
                        s.op("dve", lambda q: q.tensor_tensor(out=t2.ap[0:32, :], in0=ps2.ap[0:32, :],
                                                              in1=c["ROPE"][0:32, 1, ts], op=ALU.mult),
                             reads=[ps2, c["ROPEb"]], writes=[t2])
                        s.op("dve", lambda q: q.tensor_tensor(out=KR[0:32, ts], in0=t1.ap[0:32, :], in1=t2.ap[0:32, :],
                                                              op=ALU.add),
                             reads=[t1, t2], pwrites=[KRb[tb]])
                    else:
                        s.op("act", lambda q: q.activation(out=KR[0:32, ts], in_=ps.ap[0:32, :], func=AF.Identity),
                             reads=[ps], pwrites=[KRb[tb]])
                    if not lat:
                        for t in range(4):
                            ps = self.ps("g")
                            for kc in range(8):
                                s.op("pe", lambda q: q.matmul(ps.ap[:, 0:160], lhsT=XM[:, kc, t * 128:(t + 1) * 128],
                                                              rhs=wb.ap[:, kc, 256:416], start=(kc == 0), stop=(kc == 7)),
                                     reads=[wb, XMb[0]], pwrites=[ps])
                            s.op("act", lambda q: q.activation(out=STK[:, 0:128], in_=ps.ap[:, 0:128], func=AF.Square),
                                 reads=[ps], writes=[STKb])
                            s.op("dve", lambda q: q.reduce_sum(out=STS[:, 0:1], in_=STK[:, 0:128], axis=mybir.AxisListType.X),
                                 reads=[STKb], writes=[STSb])
                            s.op("act", lambda q: q.activation(out=STS[:, 1:2], in_=STS[:, 0:1], func=AF.Sqrt, bias=EPS,
                                                               scale=1.0 / 128),
                                 reads=[STSb], writes=[STSb])
                            s.op("dve", lambda q: q.reciprocal(out=STS[:, 2:3], in_=STS[:, 1:2]), reads=[STSb], writes=[STSb])
                            s.op("dve", lambda q: q.tensor_scalar(out=STK[:, 0:128], in0=ps.ap[:, 0:128],
                                                                  scalar1=STS[:, 2:3], scalar2=0.0, op0=ALU.mult, op1=ALU.add),
                                 reads=[ps, STSb], writes=[STKb])
                            s.op("dve", lambda q: q.tensor_tensor(out=STK[:, 0:128], in0=STK[:, 0:128],
                                                                  in1=self.kvnbc.ap[:, l, :], op=ALU.mult),
                                 reads=[STKb, self.kvnbc], writes=[STKb])
                            s.op("act", lambda q: q.activation(out=STK[:, 128:160], in_=ps.ap[:, 128:160], func=AF.Identity),
                                 reads=[ps], pwrites=[STKb])
                            s.dma("sp", lambda q: q.dma_start(out=d["ockv"][l, t * 128:(t + 1) * 128, :], in_=STK[:, 0:128]),
                                  reads=[STKb])
                            s.dma("sp", lambda q: q.dma_start(out=d["okr"][l, t * 128:(t + 1) * 128, :], in_=STK[:, 128:160]),
                                  reads=[STKb])
            def blk_ufn(sel=None):
                wb = load_w(C_UFN, 512)
                for tb in range(NTB) if self.stage >= 3.5 else ():
                    for j in range(4):
                        ps = self.ps("g")
                        mm_fm(ps, 128, wb, slice(j * 128, (j + 1) * 128), tb)
                        s.op("dve", lambda q: q.tensor_copy(out=UF[:, j, :], in_=ps.ap[:]), reads=[ps], pwrites=[UFb])
                    for tt in range(4):
                        t = tb * 4 + tt
                        if lat:
                            ab = ABS[t % 2]
                            abb = ABSb[t % 2]
                        else:
                            ab = ABC[:, t, :]
                            abb = ABCb[t]
                        for half in range(2):
                            ps = self.ps("g")
                            for gg in range(2):
                                g = half * 2 + gg
                                s.op("pe", lambda q: q.matmul(ps.ap[:, gg * 256:(gg + 1) * 256],
                                                              lhsT=UF[:, g, tt * 128:(tt + 1) * 128], rhs=self.cs128.ap[:],
                                                              start=True, stop=True),
                                     reads=[UFb, self.cs128], pwrites=[ps])
                            dst = ab[:, half * 512:(half + 1) * 512]
                            if half == 0:
                                s.op("dve", lambda q: q.tensor_copy(out=dst, in_=ps.ap[:]), reads=[ps], pwrites=[abb])
                            else:
                                s.op("act", lambda q: q.activation(out=dst, in_=ps.ap[:], func=AF.Identity),
                                     reads=[ps], pwrites=[abb])
                        if lat:
                            pn = "pay_ab%d" % (t // 4)
                            s.dma("sp", lambda q: q.dma_start(out=d[pn][l][(t % 4) * 128:(t % 4 + 1) * 128, :], in_=ab[:]),
                                  reads=[abb], pwrites=[self.db[pn][l]])

            def emit_cc(names):
                for nm in names:
                    pay, g = d['pay_' + nm][l], d['g_' + nm][l]
                    s.cc(lambda q: q.collective_compute('AllGather', ALU.bypass, replica_groups=GROUPS,
                                                        ins=[pay[:, :]], outs=[g[:, :]]),
                         reads=[self.db['pay_' + nm][l]], writes=[self.db['g_' + nm][l]])
            if not lat:
                blk_qk()
                blk_v()
                blk_gates()
                blk_qlat()
                blk_ufn()
            else:
                self.wb_hooks = []
                blk_qk(sel=C_K)
                blk_v()
                self.lat_pay_kv(l, c, KT, KTb, V, Vb)
                self.wb_hooks.append([2, lambda: emit_cc(('k',))])
                self.wb_hooks.append([3, lambda: emit_cc(('v',))])
                blk_qlat()
                self.lat_pay_mla(l, c, CK, CKb, KR, KRb)
                blk_ufn()
                blk_qk(sel=C_Q)
                blk_gates()
                self.flush_wb_hook()
                self.cc_late = [(lambda: emit_cc(('mla',))), (lambda: emit_cc(('ab0',))), (lambda: emit_cc(('ab1',)))]
            if self.stage >= 4:
                if not lat:
                    ensure_wo()
                    self.ctx_attention(l, c, A, QT, QTb, KT, KTb, V, Vb, QL, QLb, CK, CKb, KR, KRb, ABC, ABCb)
            s.barrier()
        if lat and self.stage >= 4.1:
            self.lat_na(l, c, QT, QTb, KT, KTb, V, Vb, nap)
            s.barrier()
        A1.close()
        if lat and self.stage >= 4.2:
            self.lat_mla(l, c, QL, QLb)
            s.barrier()
        if lat and self.stage >= 4.3:
            ensure_wo()
            self.lat_fourier(l, c)
            s.barrier()
        if not lat:
            A0.close()
        if self.stage < 5:
            return

        ensure_wo()
        WO, WOb = wo["WO"], wo["WOb"]
        with ExitStack() as Fz:
            MG = self.sb(Fz, "MG", [128, 8, T], BF16)
            MGb = [Buf(MG) for _ in range(NTB)]
            SG = [self.sb(Fz, "SG%d" % r, [128, 512], F32) for r in range(3)]
            SGb = [Buf(SG[r]) for r in range(3)]
            MT = self.sb(Fz, "MT", [128, 512], F32)
            MTb = Buf(MT)
            MT2 = self.sb(Fz, "MT2", [128, 512], F32)
            MT2b = Buf(MT2)
            wm = w_in[l, :, C_MRG:D_IN].rearrange("(c p) (r n) -> p c r n", p=128, r=3)
            for cch in range(8):
                wb = self.next_wb()
                wv = wb.ap[:, :, 0:384].rearrange("p c (r n) -> p c r n", r=3)
                for r in range(3):
                    s.dma("pool", lambda q: q.dma_start(out=wv[:, :, r, :], in_=wm[:, :, r, cch * 128:(cch + 1) * 128]),
                          pwrites=[wb])
                for tb in range(NTB):
                    ts = TS(tb)
                    for r in range(3):
                        ps = self.ps("all")
                        for kc in range(8):
                            s.op("pe", lambda q: q.matmul(ps.ap[:], lhsT=wv[:, kc, r, :], rhs=XM[:, kc, ts],
                                                          start=(kc == 0), stop=(kc == 7)),
                                 reads=[wb, XMb[tb]], pwrites=[ps])
                        s.op("act", lambda q: q.activation(out=SG[r][:], in_=ps.ap[:], func=AF.Sigmoid),
                             reads=[ps], writes=[SGb[r]])
                    for r in range(3):
                        ps = self.ps("all")
                        for kc in range(4):
                            s.op("pe", lambda q: q.matmul(ps.ap[:], lhsT=WO[r][:, kc, cch * 128:(cch + 1) * 128],
                                                          rhs=OG[r][:, kc, ts], start=(kc == 0), stop=(kc == 3)),
                                 reads=[WOb[r], OGb[r][tb]], pwrites=[ps])
                        if r == 0:
                            s.op("dve", lambda q: q.tensor_tensor(out=MT[:], in0=ps.ap[:], in1=SG[0][:], op=ALU.mult),
                                 reads=[ps, SGb[0]], writes=[MTb])
                        else:
                            s.op("dve", lambda q: q.tensor_tensor(out=MT2[:], in0=ps.ap[:], in1=SG[r][:], op=ALU.mult),
                                 reads=[ps, SGb[r]], writes=[MT2b])
                            if r == 1:
                                s.op("dve", lambda q: q.tensor_tensor(out=MT[:], in0=MT[:], in1=MT2[:], op=ALU.add),
                                     reads=[MTb, MT2b], writes=[MTb])
                            else:
                                s.op("dve", lambda q: q.tensor_tensor(out=MG[:, cch, ts], in0=MT[:], in1=MT2[:], op=ALU.add),
                                     reads=[MTb, MT2b], pwrites=[MGb[tb]])
            for half in range(2):
                wb = self.next_wb()
                s.dma("pool", lambda q: q.dma_start(
                    out=wb.ap[:], in_=d["w_out"][l, :, half * 512:(half + 1) * 512].rearrange("(c p) n -> p c n", p=128)),
                    writes=[wb])
                for tb in range(NTB):
                    ts = TS(tb)
                    for jj in range(4):
                        cch = half * 4 + jj
                        ps = self.ps("all")
                        for kc in range(8):
                            s.op("pe", lambda q: q.matmul(ps.ap[:], lhsT=wb.ap[:, kc, jj * 128:(jj + 1) * 128],
                                                          rhs=MG[:, kc, ts], start=(kc == 0), stop=(kc == 7)),
                                 reads=[wb, MGb[tb]], pwrites=[ps])
                        s.op("dve", lambda q: q.scalar_tensor_tensor(out=X[:, cch, ts], in0=ps.ap[:],
                                                                     scalar=self.MOD.ap[:, l, 16 + cch, col:col + 1],
                                                                     in1=X[:, cch, ts], op0=ALU.mult, op1=ALU.add),
                             reads=[ps, self.MOD, Xb[tb]], pwrites=[Xb[tb]])
            s.barrier()
        WOS.close()
        if lat:
            A0.close()

    def ctx_attention(self, l, c, A, QT, QTb, KT, KTb, V, Vb, QL, QLb, CK, CKb, KR, KRb, ABC, ABCb):
        nc, s, d = self.nc, self.s, self.d
        OG, OGb = c["OG"], c["OGb"]
        ts = slice(0, 512)
        items = []
        for h in range(8):
            j, par = h // 2, h % 2
            base = 64 * par
            hst = {}
            for bb in range(2):
                st = {}

                def front(h=h, j=j, par=par, base=base, bb=bb, st=st, hst=hst):
                    if bb == 0:
                        hst["po"] = self.ps("o")
                    ps = self.ps("g")
                    for kt in range(2):
                        k0 = bb * 256 + kt * 128
                        s.op("pe", lambda q: q.matmul(ps.ap[:, kt * 256:(kt + 1) * 256], lhsT=KT[base:base + 64, j, k0:k0 + 128],
                                                      rhs=QT[base:base + 64, j, bb * 256:(bb + 1) * 256], start=True, stop=True),
                             reads=[KTb[0], QTb[0]], pwrites=[ps])
                    pt = self.next_pt()
                    s.op("act", lambda q: q.activation(out=pt.ap[:], in_=ps.ap[:], func=AF.Exp), reads=[ps], writes=[pt])
                    st["pt"] = pt

                def back(h=h, j=j, par=par, bb=bb, st=st, hst=hst):
                    po, pt = hst["po"], st["pt"]
                    for kt in range(2):
                        t = bb * 2 + kt
                        if par == 0:
                            out = po.ap[0:65, bb * 256:(bb + 1) * 256]
                            lhsT = V[:, t, j * 192:j * 192 + 65]
                        else:
                            out = po.ap[0:128, bb * 256:(bb + 1) * 256]
                            lhsT = V[:, t, j * 192 + 64:j * 192 + 192]
                        s.op("pe", lambda q: q.matmul(out, lhsT=lhsT, rhs=pt.ap[:, kt * 256:(kt + 1) * 256],
                                                      start=(kt == 0), stop=(kt == 1)),
                             reads=[Vb[t], pt], pwrites=[po])
                fin = None
                if bb == 1:
                    fin = (lambda h=h, hst=hst: self.attn_finish(hst["po"], h, OG[0], OGb[0][0], ts))
                items.append((front, back, fin))
        ada_next = (l + 1) if (l + 1 < self.nlayers) else None
        ada_wbs = [self.adaln_load(ada_next, jb) for jb in range(3)] if ada_next is not None else []
        self.run_pipeline2(items, LA=2, DL=2)
        for jb, wb in enumerate(ada_wbs):
            self.adaln_compute(ada_next, jb, wb)

        WUQ = self.sb(A, "WUQ", [128, 2, 768], BF16)
        WUQb = Buf(WUQ)
        WK = self.sb(A, "WK", [128, 8, 128], BF16)
        WKb = Buf(WK)
        WV = self.sb(A, "WV", [128, 512], BF16)
        WVb = Buf(WV)
        VM = self.sb(A, "VM", [128, 4, 192], BF16)
        VMb = Buf(VM)
        KHs = [self.sb(A, "KH%d" % i, [96, 512], BF16) for i in range(2)]
        KHbs = [Buf(KHs[i]) for i in range(2)]
        QHs = [self.sb(A, "QH%d" % i, [96, 512], BF16) for i in range(2)]
        QHbs = [Buf(QHs[i]) for i in range(2)]
        VM2 = self.sb(A, "VM2", [128, 4, 192], BF16)
        VM2b = Buf(VM2)
        s.dma("pool", lambda q: q.dma_start(out=WUQ[:], in_=d["w_uq"][l].rearrange("(c p) n -> p c n", p=128)), writes=[WUQb])
        s.op("dve", lambda q: q.memset(WK[:], 0.0), writes=[WKb])
        wukv = d["w_ukv"][l].rearrange("c (h t x) -> c h t x", t=2, x=64)
        s.dma("pool", lambda q: q.dma_start(out=WK[:, :, 0:64], in_=wukv[:, :, 0, :]), pwrites=[WKb])
        s.dma("pool", lambda q: q.dma_start(out=WV[:].rearrange("p (h x) -> p h x", x=64), in_=wukv[:, :, 1, :]), writes=[WVb])
        VMs, VMbs = [VM, VM2], [VMb, VM2b]
        for i in range(2):
            s.op("dve", lambda q: q.memset(VMs[i][:], 0.0), writes=[VMbs[i]])
            s.op("dve", lambda q: q.memset(VMs[i][:, :, 64:65], 1.0), pwrites=[VMbs[i]])
        items = []
        for h in range(8):
            p, par = h // 2, h % 2
            vm, vmb = VMs[p % 2], VMbs[p % 2]
            KH, KHb, QH, QHb = KHs[h % 2], KHbs[h % 2], QHs[h % 2], QHbs[h % 2]
            hst = {}
            for bb in range(2):
                st = {}

                def front(h=h, p=p, par=par, bb=bb, st=st, hst=hst, vm=vm, vmb=vmb, KH=KH, KHb=KHb, QH=QH, QHb=QHb):
                    if bb == 0 and par == 0:
                        ps = self.ps("g")
                        for t in range(4):
                            s.op("pe", lambda q: q.matmul(ps.ap[:, t * 128:(t + 1) * 128], lhsT=CK[:, t * 128:(t + 1) * 128],
                                                          rhs=WV[:, p * 128:(p + 1) * 128], start=True, stop=True),
                                 reads=[CKb[0], WVb], pwrites=[ps])
                        pv = ps.ap[:].rearrange("p (t e x) -> p t e x", e=2, x=64)
                        s.op("dve", lambda q: q.tensor_copy(out=vm[:, :, 0:64], in_=pv[:, :, 0, :]), reads=[ps], pwrites=[vmb])
                        s.op("dve", lambda q: q.tensor_copy(out=vm[:, :, 128:192], in_=pv[:, :, 1, :]), reads=[ps], pwrites=[vmb])
                    if bb == 0:
                        hst["po"] = self.ps("o")
                        ps = self.ps("g")
                        s.op("pe", lambda q: q.matmul(ps.ap[0:128, :], lhsT=WK[:, h, :], rhs=CK[:, 0:512], start=True, stop=False),
                             reads=[WKb, CKb[0]], pwrites=[ps])
                        s.op("pe", lambda q: q.matmul(ps.ap[0:128, :], lhsT=self.sel32.ap[:, :], rhs=KR[0:32, 0:512],
                                                      start=False, stop=True),
                             reads=[self.sel32, KRb[0]], pwrites=[ps])
                        s.op("dve", lambda q: q.tensor_copy(out=KH[:, :], in_=ps.ap[0:96, :]), reads=[ps], writes=[KHb])
                        ps = self.ps("g")
                        for kc in range(2):
                            s.op("pe", lambda q: q.matmul(ps.ap[0:96, :], lhsT=WUQ[:, kc, h * 96:(h + 1) * 96], rhs=QL[:, kc, 0:512],
                                                          start=(kc == 0), stop=(kc == 1)),
                                 reads=[WUQb, QLb[0]], pwrites=[ps])
                        s.op("dve", lambda q: q.tensor_copy(out=QH[:, :], in_=ps.ap[0:96, :]), reads=[ps], writes=[QHb])
                    ps = self.ps("g")
                    for kt in range(2):
                        k0 = bb * 256 + kt * 128
                        s.op("pe", lambda q: q.matmul(ps.ap[:, kt * 256:(kt + 1) * 256], lhsT=KH[:, k0:k0 + 128],
                                                      rhs=QH[:, bb * 256:(bb + 1) * 256], start=True, stop=True),
                             reads=[KHb, QHb], pwrites=[ps])
                    pt = self.next_pt()
                    s.op("act", lambda q: q.activation(out=pt.ap[:], in_=ps.ap[:], func=AF.Exp, scale=MLA_SCALE),
                         reads=[ps], writes=[pt])
                    st["pt"] = pt

                def back(par=par, bb=bb, st=st, hst=hst, vm=vm, vmb=vmb):
                    po, pt = hst["po"], st["pt"]
                    for kt in range(2):
                        t = bb * 2 + kt
                        if par == 0:
                            out = po.ap[0:65, bb * 256:(bb + 1) * 256]
                            lhsT = vm[:, t, 0:65]
                        else:
                            out = po.ap[0:128, bb * 256:(bb + 1) * 256]
                            lhsT = vm[:, t, 64:192]
                        s.op("pe", lambda q: q.matmul(out, lhsT=lhsT, rhs=pt.ap[:, kt * 256:(kt + 1) * 256],
                                                      start=(kt == 0), stop=(kt == 1)),
                             reads=[vmb, pt], pwrites=[po])
                fin = None
                if bb == 1:
                    fin = (lambda h=h, hst=hst: self.attn_finish(hst["po"], h, OG[1], OGb[1][0], ts))
                items.append((front, back, fin))
        ada_wbs = [self.adaln_load(ada_next, jb) for jb in range(3, 6)] if ada_next is not None else []
        self.run_pipeline2(items, LA=2, DL=2)
        for jb, wb in enumerate(ada_wbs):
            self.adaln_compute(ada_next, 3 + jb, wb)
        if ada_next is not None:
            self.adaln_gs(ada_next)

        for g in range(4):
            po = self.ps("o")
            for bb in range(2):
                n = 0
                for nt in range(2):
                    t = bb * 2 + nt
                    for (off, mat) in ((0, self.c256), (128, self.ns256)):
                        s.op("pe", lambda q: q.matmul(po.ap[:, bb * 256:(bb + 1) * 256],
                                                      lhsT=ABC[:, t, g * 256 + off:g * 256 + off + 128],
                                                      rhs=mat.ap[:, nt, :], start=(n == 0), stop=(n == 3)),
                             reads=[ABCb[t], mat], pwrites=[po])
                        n += 1
            s.op("dve", lambda q: q.tensor_tensor(out=OG[2][:, g, ts], in0=po.ap[:], in1=OG[2][:, g, ts], op=ALU.mult),
                 reads=[po, OGb[2][0]], pwrites=[OGb[2][0]])


    def lat_pay_kv(self, l, c, KT, KTb, V, Vb):
        s, d, db = self.s, self.d, self.db
        pk = d["pay_k"][l].rearrange("(j p) n -> p j n", p=128)
        s.dma("sp", lambda q: q.dma_start(out=pk[:, :, 0:256], in_=KT[:, :, 0:256]), reads=[KTb[0]], pwrites=[db["pay_k"][l]])
        s.dma("sp", lambda q: q.dma_start(out=pk[:, :, 256:512], in_=KT[:, :, 768:1024]), reads=[KTb[1]], pwrites=[db["pay_k"][l]])
        pv = d["pay_v"][l].rearrange("(t p) n -> p t n", p=128)
        s.dma("sp", lambda q: q.dma_start(out=pv[:, 0:2, :], in_=V[:, 0:2, :]), reads=[Vb[0], Vb[1]], pwrites=[db["pay_v"][l]])
        s.dma("sp", lambda q: q.dma_start(out=pv[:, 2:4, :], in_=V[:, 6:8, :]), reads=[Vb[6], Vb[7]], pwrites=[db["pay_v"][l]])

    def lat_pay_mla(self, l, c, CK, CKb, KR, KRb):
        s, d, db = self.s, self.d, self.db
        s.dma("sp", lambda q: q.dma_start(out=d["pay_mla"][l][0:128, :], in_=CK[:, :]), reads=CKb, pwrites=[db["pay_mla"][l]])
        s.dma("sp", lambda q: q.dma_start(out=d["pay_mla"][l][128:160, :], in_=KR[0:32, :]), reads=KRb, pwrites=[db["pay_mla"][l]])

    def lat_na_prefetch(self, l, st):
        s, d = self.s, self.d
        KCT = self.sb(st, "KCT", [128, 4, 256], BF16)
        KCTb = Buf(KCT)
        VCX = self.sb(st, "VCX", [128, 2, 768], BF16)
        VCXb = Buf(VCX)
        BT = [self.sb(st, "BT%d" % i, [128, 2, 7, 128], BF16) for i in range(2)]
        BTb = [Buf(BT[i]) for i in range(2)]
        s.dma("pool", lambda q: q.dma_start(out=KCT[:], in_=d["cnakT"][l].rearrange("(j p) n -> p j n", p=128)), writes=[KCTb])
        s.op("dve", lambda q: q.memset(VCX[:], 0.0), writes=[VCXb])
        for p in range(4):
            s.op("dve", lambda q: q.memset(VCX[:, :, p * 192 + 64:p * 192 + 65], 1.0), pwrites=[VCXb])
        cv = d["cnav"][l].rearrange("(t p) (a e x) -> p t a e x", p=128, e=2, x=64)
        for t in range(2):
            vx = VCX[:, t, :].rearrange("p (a b) -> p a b", b=192)
            s.dma("pool", lambda q: q.dma_start(out=vx[:, :, 0:64], in_=cv[:, t, :, 0, :]), pwrites=[VCXb])
            s.dma("pool", lambda q: q.dma_start(out=vx[:, :, 128:192], in_=cv[:, t, :, 1, :]), pwrites=[VCXb])
        for p in range(2):
            s.dma("pool", lambda q: q.dma_start(out=BT[p][:], in_=d["nab"][l, :, 2 * p:2 * p + 2, :, :]), writes=[BTb[p]])
            s.op("act", lambda q: q.activation(out=BT[p][:], in_=BT[p][:], func=AF.Exp), reads=[BTb[p]], writes=[BTb[p]])
        return dict(KCT=KCT, KCTb=KCTb, VCX=VCX, VCXb=VCXb, BT=BT, BTb=BTb)

    def lat_na(self, l, c, QT, QTb, KT, KTb, V, Vb, nap):
        s, d, db = self.s, self.d, self.db
        KCT, KCTb, VCX, VCXb, BT, BTb = nap["KCT"], nap["KCTb"], nap["VCX"], nap["VCXb"], nap["BT"], nap["BTb"]
        OG, OGb = c["OG"], c["OGb"]
        SELR, SELRb = c["SELR"], c["SELRb"]
        with ExitStack() as N:
            HK = self.sb(N, "HK", [128, 4, 2, 256], BF16)
            HKb = Buf(HK)
            HV = self.sb(N, "HV", [128, 4, 768], BF16)
            HVb = Buf(HV)
            with ExitStack() as N2:
                KC = [self.sb(N2, "KC%d" % i, [128, 4, 512], BF16) for i in range(2)]
                KCb = [Buf(KC[i]) for i in range(2)]
                VC = [self.sb(N2, "VC%d" % i, [128, 4, 768], BF16) for i in range(2)]
                VCb = [Buf(VC[i]) for i in range(2)]
                gk = d["g_k"][l].rearrange("(c j p) n -> p j c n", c=4, j=4)
                for j in range(4):
                    kc, kcb = KC[j % 2], KCb[j % 2]
                    s.dma("sp", lambda q: q.dma_start(out=kc[:], in_=gk[:, j]), reads=[db["g_k"][l]], writes=[kcb])
                    for side in range(2):
                        cols = slice(256, 512) if side == 0 else slice(0, 256)
                        for cc in range(4):
                            sc = SELR[:, side * 4 + cc:side * 4 + cc + 1]
                            if cc == 0:
                                s.op("dve", lambda q: q.tensor_scalar(out=HK[:, j, side, :], in0=kc[:, cc, cols], scalar1=sc,
                                                                      scalar2=0.0, op0=ALU.mult, op1=ALU.add),
                                     reads=[kcb, SELRb], pwrites=[HKb])
                            else:
                                s.op("dve", lambda q: q.scalar_tensor_tensor(out=HK[:, j, side, :], in0=kc[:, cc, cols], scalar=sc,
                                                                             in1=HK[:, j, side, :], op0=ALU.mult, op1=ALU.add),
                                     reads=[kcb, SELRb, HKb], pwrites=[HKb])
                gv = d["g_v"][l].rearrange("(c t p) n -> p t c n", c=4, t=4)
                for ht in range(4):
                    side = ht // 2
                    src_t = (2 + ht) if side == 0 else (ht - 2)
                    vc, vcb = VC[ht % 2], VCb[ht % 2]
                    s.dma("sp", lambda q: q.dma_start(out=vc[:], in_=gv[:, src_t]), reads=[db["g_v"][l]], writes=[vcb])
                    for cc in range(4):
                        sc = SELR[:, side * 4 + cc:side * 4 + cc + 1]
                        if cc == 0:
                            s.op("dve", lambda q: q.tensor_scalar(out=HV[:, ht, :], in0=vc[:, cc, :], scalar1=sc, scalar2=0.0,
                                                                  op0=ALU.mult, op1=ALU.add),
                                 reads=[vcb, SELRb], pwrites=[HVb])
                        else:
                            s.op("dve", lambda q: q.scalar_tensor_tensor(out=HV[:, ht, :], in0=vc[:, cc, :], scalar=sc,
                                                                         in1=HV[:, ht, :], op0=ALU.mult, op1=ALU.add),
                                 reads=[vcb, SELRb, HVb], pwrites=[HVb])
                s.barrier()
            MK = self.sb(N, "MK", [128, 48, 128], BF16)
            MKb = Buf(MK)
            s.dma("sp", lambda q: q.dma_start(out=MK[:], in_=d["namask"][:, :, :]), writes=[MKb])
            all_items = []
            for p in range(4):
                bt, btb = BT[p % 2], BTb[p % 2]
                need_bt = [p >= 2]
                for h in (2 * p, 2 * p + 1):
                    par = h % 2
                    base = 64 * par
                    for grp in range(2):
                        po = self.ps("o")
                        items = []
                        for bi in range(4):
                            b = grp * 4 + bi
                            qs = slice(b * 128, (b + 1) * 128)
                            tiles = []
                            lts = [(b - 2 + jj, jj) for jj in range(5)]
                            if b == 0:
                                lts.append((3, 5))
                            if b == 7:
                                lts.append((4, 5))
                            for (lt, slot) in lts:
                                if 0 <= lt <= 7:
                                    kap = KT[base:base + 64, p, lt * 128:(lt + 1) * 128]
                                    kb_ = KTb[lt // 4]
                                    vt = V[:, lt, :]
                                    vb_ = Vb[lt]
                                elif lt < 0:
                                    kap = HK[base:base + 64, p, 0, (lt + 2) * 128:(lt + 3) * 128]
                                    kb_ = HKb
                                    vt = HV[:, lt + 2, :]
                                    vb_ = HVb
                                else:
                                    kap = HK[base:base + 64, p, 1, (lt - 8) * 128:(lt - 7) * 128]
                                    kb_ = HKb
                                    vt = HV[:, 2 + lt - 8, :]
                                    vb_ = HVb
                                tiles.append((kap, kb_, vt, vb_, lt - b + 3, b * 6 + slot))
                            for t in range(2):
                                tiles.append((KCT[base:base + 64, p, t * 128:(t + 1) * 128], KCTb, VCX[:, t, :], VCXb, None, None))
                            nt = len(tiles)
                            for g0 in range(0, nt, 4):
                                grpt = tiles[g0:g0 + 4]
                                st = {}

                                def front(grpt=grpt, st=st, qs=qs, grp=grp, base=base, p=p, par=par, bt=bt, btb=btb, need_bt=need_bt):
                                    if need_bt[0]:
                                        need_bt[0] = False
                                        s.dma("pool", lambda q: q.dma_start(out=bt[:], in_=d["nab"][l, :, 2 * p:2 * p + 2, :, :]),
                                              writes=[btb])
                                        s.op("act", lambda q: q.activation(out=bt[:], in_=bt[:], func=AF.Exp), reads=[btb], writes=[btb])
                                    ps = self.ps("g")
                                    for i, (kap, kb_, vt, vb_, dj, ms) in enumerate(grpt):
                                        reg = ps.ap[:, i * 128:(i + 1) * 128]
                                        s.op("pe", lambda q: q.matmul(reg, lhsT=kap, rhs=QT[base:base + 64, p, qs], start=True,
                                                                      stop=True),
                                             reads=[kb_, QTb[grp]], pwrites=[ps])
                                    w = len(grpt) * 128
                                    pt = self.next_pt()
                                    s.op("act", lambda q: q.activation(out=pt.ap[:, 0:w], in_=ps.ap[:, 0:w], func=AF.Exp),
                                         reads=[ps], writes=[pt])
                                    i = 0
                                    while i < len(grpt):
                                        dj, ms = grpt[i][4], grpt[i][5]
                                        if dj is None:
                                            i += 1
                                            continue
                                        n = 1
                                        while (i + n < len(grpt) and grpt[i + n][4] is not None
                                               and grpt[i + n][4] == dj + n and grpt[i + n][5] == ms + n):
                                            n += 1
                                        pv3 = pt.ap[:, i * 128:(i + n) * 128].rearrange("p (a b) -> p a b", b=128)
                                        s.op("dve", lambda q: q.tensor_tensor(out=pv3, in0=pv3, in1=bt[:, par, dj:dj + n, :], op=ALU.mult),
                                             reads=[pt, btb], writes=[pt])
                                        s.op("pool", lambda q: q.tensor_tensor(out=pv3, in0=pv3, in1=MK[:, ms:ms + n, :], op=ALU.mult),
                                             reads=[pt, MKb], writes=[pt])
                                        i += n
                                    st["pt"] = pt

                                def back(grpt=grpt, st=st, g0=g0, nt=nt, bi=bi, po=po, par=par, p=p):
                                    pt = st["pt"]
                                    for i, (kap, kb_, vt, vb_, dj, ms) in enumerate(grpt):
                                        n = g0 + i
                                        if par == 0:
                                            out = po.ap[0:65, bi * 128:(bi + 1) * 128]
                                            lhsT = vt[:, p * 192:p * 192 + 65]
                                        else:
                                            out = po.ap[0:128, bi * 128:(bi + 1) * 128]
                                            lhsT = vt[:, p * 192 + 64:p * 192 + 192]
                                        s.op("pe", lambda q: q.matmul(out, lhsT=lhsT, rhs=pt.ap[:, i * 128:(i + 1) * 128],
                                                                      start=(n == 0), stop=(n == nt - 1)),
                                             reads=[vb_, pt], pwrites=[po])
                                items.append([front, back, None])
                        items[-1][2] = (lambda po=po, h=h, grp=grp: self.attn_finish(po, h, OG[0], OGb[0][grp],
                                                                                      slice(grp * 512, (grp + 1) * 512)))
                        all_items.extend(items)
            late = getattr(self, "cc_late", None) or []
            self.cc_late = None
            npos = len(all_items)
            for i, f in enumerate(late):
                pos = (i * npos) // 4
                fr = all_items[pos][0]
                all_items[pos][0] = (lambda fr=fr, f=f: (f(), fr()))
            self.run_pipeline2(all_items, LA=4, DL=2)

    def lat_mla(self, l, c, QL, QLb):
        s, d, db = self.s, self.d, self.db
        OG, OGb = c["OG"], c["OGb"]
        ROPE, ROPEb = c["ROPE"], c["ROPEb"]
        NK = 4352
        NKT = 34
        with ExitStack() as M:
            CKA = self.sb(M, "CKA", [128, NK], BF16)
            CKAb = Buf(CKA)
            KRA = self.sb(M, "KRA", [32, NK], BF16)
            KRAb = Buf(KRA)
            KH = self.sb(M, "KH", [96, NK], BF16)
            KHb = Buf(KH)
            VM = [self.sb(M, "VM%d" % i, [128, NKT, 192], BF16) for i in range(2)]
            VMb = [Buf(VM[i]) for i in range(2)]
            WUQ = self.sb(M, "WUQ", [128, 2, 768], BF16)
            WUQb = Buf(WUQ)
            WUQP = self.sb(M, "WUQP", [128, 2, 768], BF16)
            WUQPb = Buf(WUQP)
            WK = self.sb(M, "WK", [128, 8, 128], BF16)
            WKb = Buf(WK)
            WV = self.sb(M, "WV", [128, 512], BF16)
            WVb = Buf(WV)
            QH = [self.sb(M, "QH%d" % i, [96, 1024], BF16) for i in range(2)]
            QHb = [Buf(QH[i]) for i in range(2)]
            for cc in range(4):
                s.dma("sp", lambda q: q.dma_start(out=CKA[:, cc * 1024:(cc + 1) * 1024], in_=d["g_mla"][l][cc * 160:cc * 160 + 128, :]),
                      reads=[db["g_mla"][l]], pwrites=[CKAb])
                s.dma("sp", lambda q: q.dma_start(out=KRA[0:32, cc * 1024:(cc + 1) * 1024],
                                                  in_=d["g_mla"][l][cc * 160 + 128:cc * 160 + 160, :]),
                      reads=[db["g_mla"][l]], pwrites=[KRAb])
            s.dma("pool", lambda q: q.dma_start(out=CKA[:, 4096:NK], in_=d["cckvT"][l]), pwrites=[CKAb])
            s.dma("pool", lambda q: q.dma_start(out=KRA[0:32, 4096:NK], in_=d["ckrT"][l]), pwrites=[KRAb])
            s.dma("pool", lambda q: q.dma_start(out=WUQ[:], in_=d["w_uq"][l].rearrange("(c p) n -> p c n", p=128)), writes=[WUQb])
            s.dma("pool", lambda q: q.dma_start(out=WUQP[:], in_=d["w_uqp"][l].rearrange("(c p) n -> p c n", p=128)), writes=[WUQPb])
            s.op("dve", lambda q: q.memset(WK[:], 0.0), writes=[WKb])
            wukv = d["w_ukv"][l].rearrange("c (h t x) -> c h t x", t=2, x=64)
            s.dma("pool", lambda q: q.dma_start(out=WK[:, :, 0:64], in_=wukv[:, :, 0, :]), pwrites=[WKb])
            s.dma("pool", lambda q: q.dma_start(out=WV[:].rearrange("p (h x) -> p h x", x=64), in_=wukv[:, :, 1, :]), writes=[WVb])
            for i in range(2):
                s.op("dve", lambda q: q.memset(VM[i][:], 0.0), writes=[VMb[i]])
                s.op("dve", lambda q: q.memset(VM[i][:, :, 64:65], 1.0), pwrites=[VMb[i]])
            KH2 = self.sb(M, "KH2", [96, NK], BF16)
            KHs, KHbs = [KH, KH2], [KHb, Buf(KH2)]

            def gen_v(p):
                vm, vmb = VM[p % 2], VMb[p % 2]
                for k0 in range(0, NKT, 4):
                    nt = min(4, NKT - k0)
                    ps = self.ps("g")
                    for t in range(nt):
                        kt = k0 + t
                        s.op("pe", lambda q: q.matmul(ps.ap[:, t * 128:(t + 1) * 128], lhsT=CKA[:, kt * 128:(kt + 1) * 128],
                                                      rhs=WV[:, p * 128:(p + 1) * 128], start=True, stop=True),
                             reads=[CKAb, WVb], pwrites=[ps])
                    pv = ps.ap[:, 0:nt * 128].rearrange("p (t e x) -> p t e x", e=2, x=64)
                    s.op("dve", lambda q: q.tensor_copy(out=vm[:, k0:k0 + nt, 0:64], in_=pv[:, :, 0, :]), reads=[ps], pwrites=[vmb])
                    s.op("dve", lambda q: q.tensor_copy(out=vm[:, k0:k0 + nt, 128:192], in_=pv[:, :, 1, :]), reads=[ps], pwrites=[vmb])

            def gen_kq(h):
                kh, khb = KHs[h % 2], KHbs[h % 2]
                qh, qhb = QH[h % 2], QHb[h % 2]
                for k0 in range(0, NK, 512):
                    n = min(512, NK - k0)
                    ps = self.ps("g")
                    s.op("pe", lambda q: q.matmul(ps.ap[0:128, 0:n], lhsT=WK[:, h, :], rhs=CKA[:, k0:k0 + n], start=True, stop=False),
                         reads=[WKb, CKAb], pwrites=[ps])
                    s.op("pe", lambda q: q.matmul(ps.ap[0:128, 0:n], lhsT=self.sel32.ap[:, :], rhs=KRA[0:32, k0:k0 + n],
                                                  start=False, stop=True),
                         reads=[self.sel32, KRAb], pwrites=[ps])
                    s.op("dve", lambda q: q.tensor_copy(out=kh[:, k0:k0 + n], in_=ps.ap[0:96, 0:n]), reads=[ps], pwrites=[khb])
                for tb in range(2):
                    ts = slice(tb * 512, (tb + 1) * 512)
                    ps = self.ps("g")
                    ps2 = self.ps("g")
                    for (pp, ww, wwb) in ((ps, WUQ, WUQb), (ps2, WUQP, WUQPb)):
                        for kc in range(2):
                            s.op("pe", lambda q: q.matmul(pp.ap[0:96, :], lhsT=ww[:, kc, h * 96:(h + 1) * 96], rhs=QL[:, kc, ts],
                                                          start=(kc == 0), stop=(kc == 1)),
                                 reads=[wwb, QLb[tb]], pwrites=[pp])
                    s.op("dve", lambda q: q.tensor_copy(out=qh[0:64, ts], in_=ps.ap[0:64, :]), reads=[ps], pwrites=[qhb])
                    t1 = self.next_tmp()
                    t2 = self.next_tmp()
                    s.op("dve", lambda q: q.tensor_tensor(out=t1.ap[64:96, :], in0=ps.ap[64:96, :], in1=ROPE[64:96, 0, ts], op=ALU.mult),
                         reads=[ps, ROPEb], writes=[t1])
                    s.op("dve", lambda q: q.tensor_tensor(out=t2.ap[64:96, :], in0=ps2.ap[64:96, :], in1=ROPE[64:96, 1, ts], op=ALU.mult),
                         reads=[ps2, ROPEb], writes=[t2])
                    s.op("dve", lambda q: q.tensor_tensor(out=qh[64:96, ts], in0=t1.ap[64:96, :], in1=t2.ap[64:96, :], op=ALU.add),
                         reads=[t1, t2], pwrites=[qhb])

            gen_v(0)
            gen_kq(0)
            items = []
            for h in range(8):
                p, par = h // 2, h % 2
                vm, vmb = VM[p % 2], VMb[p % 2]
                kh, khb = KHs[h % 2], KHbs[h % 2]
                qh, qhb = QH[h % 2], QHb[h % 2]
                for tb in range(2):
                    ts = slice(tb * 512, (tb + 1) * 512)
                    po = self.ps("o")
                    for kt in range(NKT):
                        st = {}
                        pre = None
                        if par == 1 and tb == 0 and kt == 0 and p + 1 < 4:
                            pre = (lambda p=p: gen_v(p + 1))
                        if tb == 1 and kt == NKT - 12 and h + 1 < 8:
                            pre = (lambda h=h: gen_kq(h + 1))

                        def front(kt=kt, st=st, ts=ts, kh=kh, khb=khb, qh=qh, qhb=qhb, pre=pre):
                            if pre is not None:
                                pre()
                            ps = self.ps("g")
                            s.op("pe", lambda q: q.matmul(ps.ap[:], lhsT=kh[:, kt * 128:(kt + 1) * 128], rhs=qh[:, ts],
                                                          start=True, stop=True),
                                 reads=[khb, qhb], writes=[ps])
                            pt = self.next_pt()
                            s.op("act", lambda q: q.activation(out=pt.ap[:], in_=ps.ap[:], func=AF.Exp, scale=MLA_SCALE),
                                 reads=[ps], writes=[pt])
                            st["pt"] = pt

                        def back(kt=kt, st=st, po=po, par=par, vm=vm, vmb=vmb):
                            pt = st["pt"]
                            if par == 0:
                                out = po.ap[0:65, :]
                                lhsT = vm[:, kt, 0:65]
                            else:
                                out = po.ap[0:128, :]
                                lhsT = vm[:, kt, 64:192]
                            s.op("pe", lambda q: q.matmul(out, lhsT=lhsT, rhs=pt.ap[:], start=(kt == 0), stop=(kt == NKT - 1)),
                                 reads=[vmb, pt], pwrites=[po])
                        fin = None
                        if kt == NKT - 1:
                            fin = (lambda po=po, h=h, tb=tb, ts=ts: self.attn_finish(po, h, OG[1], OGb[1][tb], ts))
                        items.append((front, back, fin))
            self.run_pipeline2(items, LA=3, DL=2)


    def lat_fourier(self, l, c):
        s, d, db = self.s, self.d, self.db
        OG, OGb = c["OG"], c["OGb"]
        with ExitStack() as Fs:
            DC = [self.sb(Fs, "DC%d" % i, [128, 4, 512], BF16) for i in range(2)]
            DCb = [Buf(DC[i]) for i in range(2)]
            DS = [self.sb(Fs, "DS%d" % i, [128, 4, 512], BF16) for i in range(2)]
            DSb = [Buf(DS[i]) for i in range(2)]
            AB = [self.sb(Fs, "AB%d" % i, [128, 4, 1024], BF16) for i in range(2)]
            ABb = [Buf(AB[i]) for i in range(2)]
            it = 0
            for tb in range(2):
                ts = slice(tb * 512, (tb + 1) * 512)
                acc = [self.PS[4 + g] for g in range(4)]
                for nb in range(8):
                    dc, dcb, ds_, dsb, ab, abb = DC[it % 2], DCb[it % 2], DS[it % 2], DSb[it % 2], AB[it % 2], ABb[it % 2]
                    it += 1
                    rows = slice(nb * 512, (nb + 1) * 512)
                    s.dma("sp", lambda q: q.dma_start(out=dc[:], in_=d["dftc"][rows, ts].rearrange("(i p) n -> p i n", p=128)), writes=[dcb])
                    s.dma("sp", lambda q: q.dma_start(out=ds_[:], in_=d["dftns"][rows, ts].rearrange("(i p) n -> p i n", p=128)), writes=[dsb])
                    gn = "g_ab%d" % (nb % 2)
                    grow = slice((nb // 2) * 512, (nb // 2 + 1) * 512)
                    s.dma("sp", lambda q: q.dma_start(out=ab[:], in_=d[gn][l][grow, :].rearrange("(i p) n -> p i n", p=128)),
                          reads=[db[gn][l]], writes=[abb])
                    for i in range(4):
                        for g in range(4):
                            first = (nb == 0 and i == 0)
                            last = (nb == 7 and i == 3)
                            s.op("pe", lambda q: q.matmul(acc[g].ap[:], lhsT=ab[:, i, g * 256:g * 256 + 128], rhs=dc[:, i, :],
                                                          start=first, stop=False),
                                 reads=[abb, dcb], pwrites=[acc[g]])
                            s.op("pe", lambda q: q.matmul(acc[g].ap[:], lhsT=ab[:, i, g * 256 + 128:g * 256 + 256], rhs=ds_[:, i, :],
                                                          start=False, stop=last),
                                 reads=[abb, dsb], pwrites=[acc[g]])
                for g in range(4):
                    s.op("dve", lambda q: q.tensor_tensor(out=OG[2][:, g, ts], in0=acc[g].ap[:], in1=OG[2][:, g, ts], op=ALU.mult),
                         reads=[acc[g], OGb[2][tb]], pwrites=[OGb[2][tb]])


def _rope_perm():
    P = np.array([i + 8 if (i % 16) < 8 else i - 8 for i in range(32)])
    sgn = np.array([-1.0 if (i % 16) < 8 else 1.0 for i in range(32)], np.float32)
    return P, sgn


def _consts():
    bf = ml_dtypes.bfloat16
    c = {}
    c["identb"] = np.eye(128, dtype=np.float32).astype(bf)
    n = np.arange(128)
    ang = 2 * np.pi * np.outer(n, n) / 128.0
    c["cs128"] = (np.concatenate([np.cos(ang), np.sin(ang)], axis=1) / np.sqrt(128.0)).astype(np.float32).astype(bf)
    n = np.arange(256)
    ang = 2 * np.pi * np.outer(n, n) / 256.0
    c["c256"] = (np.cos(ang) / 16.0).astype(np.float32).astype(bf)
    c["ns256"] = (-np.sin(ang) / 16.0).astype(np.float32).astype(bf)
    sel = np.zeros((32, 128), np.float32)
    sel[np.arange(32), 64 + np.arange(32)] = 1.0
    c["sel32"] = sel.astype(bf)
    return c


def _lat_consts(q):
    bf = ml_dtypes.bfloat16
    c = {}
    P, sgn = _rope_perm()
    t = np.arange(1024 * q, 1024 * q + 1024)
    row = (t // 64).astype(np.float32)
    colp = (t % 64).astype(np.float32)
    half = 16
    inv = (10000.0 ** (-np.arange(0, half, 2, dtype=np.float32) / half)).astype(np.float32)
    ar = row[:, None] * inv[None, :]
    ac = colp[:, None] * inv[None, :]
    ang = np.concatenate([ar, ar, ac, ac], axis=-1)
    cos = np.cos(ang).astype(np.float32)
    sin = (np.sin(ang).astype(np.float32) * sgn[None, :]).astype(np.float32)
    rope = np.zeros((2, 128, 1024), np.float32)
    rope[0, 0:32] = cos.T
    rope[0, 64:96] = cos.T
    rope[1, 0:32] = sin.T
    rope[1, 64:96] = sin.T
    c["ropeT"] = rope
    n = np.arange(4096, dtype=np.float64)
    k = np.arange(1024 * q, 1024 * q + 1024, dtype=np.float64)
    ang = 2 * np.pi * ((np.outer(n, k)) % 4096) / 4096.0
    c["dftc"] = (np.cos(ang) / 64.0).astype(np.float32).astype(bf)
    c["dftns"] = (-np.sin(ang) / 64.0).astype(np.float32).astype(bf)
    kk = np.arange(128)
    kr, kcol = kk // 64, kk % 64
    mask = np.zeros((128, 48, 128), np.float32)
    for b in range(8):
        for slot in range(6):
            if slot < 5:
                lt = b - 2 + slot
            elif b == 0:
                lt = 3
            elif b == 7:
                lt = 4
            else:
                continue
            krow = 16 * q + 2 * lt + kr
            r = 16 * q + 2 * b + kr
            rs = np.clip(r - 4, 0, 56)
            row_ok = ((krow[:, None] >= 0) & (krow[:, None] < 64) &
                      (krow[:, None] >= rs[None, :]) & (krow[:, None] < rs[None, :] + 8))
            cs = np.clip(kcol - 8, 0, 48)
            col_ok = (kcol[:, None] >= cs[None, :]) & (kcol[:, None] < cs[None, :] + 16)
            mask[:, b * 6 + slot, :] = np.where(row_ok & col_ok, 1.0, 0.0)
    c["namask"] = mask.astype(bf)
    sel = np.zeros((128, 8), np.float32)
    if q - 1 >= 0:
        sel[:, q - 1] = 1.0
    if q + 1 <= 3:
        sel[:, 4 + q + 1] = 1.0
    c["selr"] = sel
    return c


_NC_CACHE = {}


def _get_nc(key=("full",)):
    if key not in _NC_CACHE:
        if key[0] == "full":
            kb = KB(run_ctx=True, run_lat=True)
        else:
            kb = KB(**dict(key[1]))
        _NC_CACHE[key] = kb.build()
    return _NC_CACHE[key]


def make_in_maps(inp):
    f = lambda a: np.ascontiguousarray(np.asarray(a, dtype=np.float32))
    x_prompt, x_sample = f(inp["x_prompt"]), f(inp["x_sample"])
    P, sgn = _rope_perm()
    w_in = f(inp["w_in"])
    w_uq = f(inp["w_uq"])
    shared = {}
    shared["w_ada"] = f(inp["w_ada"])
    shared["b_adaT"] = f(f(inp["b_ada"]).reshape(L, 24, 128).transpose(2, 0, 1))
    shared["norm_gT"] = f(f(inp["norm_g"]).reshape(L, 8, 128).transpose(2, 0, 1))
    shared["fin_gT"] = f(f(inp["final_norm_g"]).reshape(8, 128).T)
    shared["qn_gT"] = f(f(inp["q_norm_g"]).reshape(L, 2, 128).transpose(2, 0, 1))
    shared["kvn_gT"] = f(f(inp["kv_norm_g"]).T)
    shared["kvn_bc"] = f(np.broadcast_to(f(inp["kv_norm_g"])[None, :, :], (128, L, 128)))
    shared["w_in"] = w_in
    shared["w_krp"] = f(w_in[:, :, C_KR:C_KR + 32][:, :, P])
    shared["w_uq"] = w_uq
    wq = w_uq.reshape(L, 256, 8, 96).copy()
    wq[:, :, :, 64:96] = wq[:, :, :, 64:96][:, :, :, P]
    shared["w_uqp"] = f(wq.reshape(L, 256, 768))
    shared["w_ukv"] = f(inp["w_ukv"])
    shared["w_o_na"] = f(inp["w_o_na"])
    shared["w_o_mla"] = f(inp["w_o_mla"])
    shared["w_o_fn"] = f(inp["w_o_fourier"])
    shared["w_out"] = f(inp["w_out"])
    shared.update(_consts())
    nb = f(inp["na_bias"])
    kr = np.arange(128) // 64
    kc = np.arange(128) % 64
    dj = np.arange(7)
    dr = 2 * (dj[None, :, None] - 3) + kr[:, None, None] - kr[None, None, :]
    dc = np.clip(kc[:, None, None] - kc[None, None, :] + 15, 0, 30) + 0 * dr
    drc = np.clip(dr + 7, 0, 14)
    nab = nb[:, :, drc, dc]
    shared["nab"] = f(nab.transpose(0, 2, 1, 3, 4))
    cna_k, cna_v = f(inp["cache_na_k"]), f(inp["cache_na_v"])
    cckv, ckr = f(inp["cache_mla_ckv"]), f(inp["cache_mla_krope"])
    cvec, c_ctx = f(inp["c"]), f(inp["c_ctx"])
    maps = []
    latc = [_lat_consts(q) for q in range(4)]
    for core in range(8):
        b, q = core // 4, core % 4
        m = dict(shared)
        m["xc"] = f(x_prompt[2 * core:2 * core + 2].reshape(512, D).T)
        m["xl"] = f(x_sample[b, 1024 * q:1024 * q + 1024].T)
        m["cond"] = f(np.stack([c_ctx, cvec[b]], axis=1))
        m.update(latc[q])
        m["cnakT"] = f(cna_k[b].reshape(L, 256, 512).transpose(0, 2, 1))
        m["cnav"] = f(cna_v[b].reshape(L, 256, 512))
        m["cckvT"] = f(cckv[b].transpose(0, 2, 1))
        m["ckrT"] = f(ckr[b].transpose(0, 2, 1))
        maps.append(m)
    return maps


def assemble(res):
    r = res.results
    y_prompt = np.stack([r[c]["yc"].T.reshape(2, 256, D) for c in range(8)]).reshape(16, 256, D)
    y_sample = np.stack([np.concatenate([r[4 * b + q]["yl"].T for q in range(4)], axis=0) for b in range(2)])

    def cache(name, w):
        return np.concatenate([r[c][name].reshape(L, 2, 256, w).transpose(1, 0, 2, 3) for c in range(8)], axis=0)
    nk = cache("onk", 512).reshape(16, L, 256, 8, 64)
    nv = cache("onv", 512).reshape(16, L, 256, 8, 64)
    nckv = cache("ockv", 128)
    nkr = cache("okr", 32)
    return tuple(np.ascontiguousarray(a.astype(np.float32)) for a in (y_prompt, y_sample, nk, nv, nckv, nkr))


def kernel(**inputs):
    nc = _get_nc()
    in_maps = make_in_maps(inputs)
    res = run_bass_kernel_spmd(nc, in_maps, core_ids=list(range(8)))
    return assemble(res)
```

```python
import os
import numpy as np
import ml_dtypes
from contextlib import ExitStack
import concourse.bass as bass
import concourse.mybir as mybir
from concourse.bass_utils import run_bass_kernel_spmd

F32 = mybir.dt.float32
BF16 = mybir.dt.bfloat16
AF = mybir.ActivationFunctionType
ALU = mybir.AluOpType

L = 4
D = 1024
D_IN = 7072
NA_SCALE = 64 ** -0.5
MLA_SCALE = 96 ** -0.5
EPS = 1e-6
C_Q, C_K, C_V, C_GNA, C_QLAT, C_CKV, C_KR, C_GMLA, C_UFN, C_GFN, C_MRG = (
    0, 512, 1024, 1536, 2048, 2304, 2432, 2464, 2976, 3488, 4000)
NEG = -30000.0
GROUPS = [[0, 1, 2, 3], [4, 5, 6, 7]]


class Buf:
    __slots__ = ("ap", "w", "r", "pm", "name", "fw", "excl")

    def __init__(self, ap, name=""):
        self.ap = ap
        self.w = {}
        self.r = {}
        self.pm = False
        self.fw = {}
        self.excl = False
        self.name = name


class Sched:
    LIMIT = 12000
    NLANES = 8

    def __init__(self, nc, es):
        self.nc = nc
        self.es = es
        self.eng = dict(pe=nc.tensor, act=nc.scalar, dve=nc.vector, pool=nc.gpsimd, sp=nc.sync)
        self.sems = []
        self.cur = {}
        self.known = {e: {} for e in self.eng}
        self.lanes = {e: [] for e in self.eng}
        self.rr = {e: 0 for e in self.eng}
        self.pe_sems = set()
        self.nins = 0

    def new_sem(self):
        h = self.es.enter_context(self.nc.semaphore("sm%d" % len(self.sems)))
        self.sems.append(h)
        return len(self.sems) - 1

    def _deps(self, reads, writes, pwrites):
        deps = {}

        def add(d):
            for k, v in d.items():
                if deps.get(k, 0) < v:
                    deps[k] = v
        for b in reads:
            add(b.w)
            if b.excl:
                add(b.r)
        for b in writes:
            add(b.w)
            add(b.r)
        for b in pwrites:
            add(b.r)
            add(b.fw)
            if not b.pm:
                add(b.w)
        return deps

    def _wait(self, e, deps):
        kn = self.known[e]
        for sem, val in deps.items():
            if e == "pe" and sem in self.pe_sems:
                continue
            if kn.get(sem, 0) >= val:
                continue
            self.eng[e].wait_ge(self.sems[sem], val)
            kn[sem] = val
            self.nins += 1

    def _mark(self, stamp, reads, writes, pwrites):
        s, v = stamp
        for b in writes:
            b.w = {s: v}
            b.fw = {s: v}
            b.r = {}
            b.pm = False
        for b in pwrites:
            b.w[s] = v
            b.pm = True
        for b in reads:
            b.r[s] = v

    def op(self, e, fn, reads=(), writes=(), pwrites=()):
        self._wait(e, self._deps(reads, writes, pwrites))
        c = self.cur.get(e)
        if c is None or c[1] >= self.LIMIT:
            c = [self.new_sem(), 0]
            self.cur[e] = c
            if e == "pe":
                self.pe_sems.add(c[0])
        c[1] += 1
        ins = fn(self.eng[e])
        ins.then_inc(self.sems[c[0]], 1)
        self.nins += 1
        self._mark((c[0], c[1]), reads, writes, pwrites)

    def dma(self, e, fn, reads=(), writes=(), pwrites=(), inc=16):
        deps = self._deps(reads, writes, pwrites)
        lanes = self.lanes[e]
        if len(lanes) < self.NLANES:
            lanes.append([self.new_sem(), 0])
            lane = lanes[-1]
        else:
            lane = lanes[self.rr[e] % self.NLANES]
            self.rr[e] += 1
        if lane[1] > 0 and deps.get(lane[0], 0) < lane[1]:
            deps[lane[0]] = lane[1]
        self._wait(e, deps)
        lane[1] += inc
        ins = fn(self.eng[e])
        ins.then_inc(self.sems[lane[0]], inc)
        self.nins += 1
        self._mark((lane[0], lane[1]), reads, writes, pwrites)

    def cc(self, fn, reads=(), writes=()):
        deps = self._deps(reads, writes, ())
        if not hasattr(self, "cclane"):
            self.cclane = [self.new_sem(), 0]
        lane = self.cclane
        if lane[1] > 0 and deps.get(lane[0], 0) < lane[1]:
            deps[lane[0]] = lane[1]
        self._wait("pool", deps)
        lane[1] += 1
        ins = fn(self.eng["pool"])
        ins.then_inc(self.sems[lane[0]], 1)
        self.nins += 1
        self._mark((lane[0], lane[1]), reads, writes, ())

    def barrier(self):
        allst = {}
        for e, c in self.cur.items():
            allst[c[0]] = c[1]
        for e, lanes in self.lanes.items():
            for ln in lanes:
                if ln[1] > 0:
                    allst[ln[0]] = ln[1]
        for e in self.eng:
            d = dict(allst)
            c = self.cur.get(e)
            if e == "pe" and c is not None:
                d.pop(c[0], None)
            self._wait(e, d)

    def finish(self):
        allst = {}
        for e, lanes in self.lanes.items():
            for ln in lanes:
                if ln[1] > 0:
                    allst[ln[0]] = ln[1]
        for e, c in self.cur.items():
            allst[c[0]] = c[1]
        if hasattr(self, "cclane") and self.cclane[1] > 0:
            allst[self.cclane[0]] = self.cclane[1]
        d = dict(allst)
        self._wait("sp", d)


class KB:
    def __init__(self, run_ctx=True, run_lat=True, nlayers=L, dbg=False, lw=L, stage=99):
        self.LW = lw
        self.stage = stage
        self.run_ctx = run_ctx
        self.run_lat = run_lat
        self.nlayers = nlayers
        self.nc = bass.Bass("TRN2", target_bir_lowering=False)
        self.es = ExitStack()
        self.s = Sched(self.nc, self.es)
        self.tcount = 0

    def din(self, name, shape, dt=F32):
        return self.nc.dram_tensor(name, list(shape), dt, kind="ExternalInput").ap()

    def dout(self, name, shape, dt=F32):
        return self.nc.dram_tensor(name, list(shape), dt, kind="ExternalOutput").ap()

    def dint(self, name, shape, dt=BF16):
        return self.nc.dram_tensor(name, list(shape), dt).ap()

    def sb(self, st, name, shape, dt):
        self.tcount += 1
        return st.enter_context(self.nc.sbuf_tensor("%s_%d" % (name, self.tcount), list(shape), dt))

    def ps(self, grp):
        idxs = self.psgrp[grp]
        i = idxs[self.psrr[grp] % len(idxs)]
        self.psrr[grp] += 1
        return self.PS[i]

    def build(self):
        nc, s, es = self.nc, self.s, self.es
        NL = self.nlayers
        d = {}
        d["xc"] = self.din("xc", [D, 512])
        d["xl"] = self.din("xl", [D, 1024])
        d["cond"] = self.din("cond", [D, 2])
        d["w_ada"] = self.din("w_ada", [self.LW, D, 3 * D])
        d["b_adaT"] = self.din("b_adaT", [128, L, 24])
        d["norm_gT"] = self.din("norm_gT", [128, L, 8])
        d["fin_gT"] = self.din("fin_gT", [128, 8])
        d["qn_gT"] = self.din("qn_gT", [128, L, 2])
        d["kvn_gT"] = self.din("kvn_gT", [128, L])
        d["kvn_bc"] = self.din("kvn_bc", [128, L, 128])
        d["w_in"] = self.din("w_in", [self.LW, D, D_IN])
        d["w_krp"] = self.din("w_krp", [self.LW, D, 32])
        d["w_uq"] = self.din("w_uq", [self.LW, 256, 768])
        d["w_uqp"] = self.din("w_uqp", [self.LW, 256, 768])
        d["w_ukv"] = self.din("w_ukv", [self.LW, 128, 1024])
        d["w_o_na"] = self.din("w_o_na", [self.LW, 512, D])
        d["w_o_mla"] = self.din("w_o_mla", [self.LW, 512, D])
        d["w_o_fn"] = self.din("w_o_fn", [self.LW, 512, D])
        d["w_out"] = self.din("w_out", [self.LW, D, D])
        d["identb"] = self.din("identb", [128, 128], BF16)
        d["cs128"] = self.din("cs128", [128, 256], BF16)
        d["c256"] = self.din("c256", [256, 256], BF16)
        d["ns256"] = self.din("ns256", [256, 256], BF16)
        d["sel32"] = self.din("sel32", [32, 128], BF16)
        d["ropeT"] = self.din("ropeT", [2, 128, 1024])
        d["dftc"] = self.din("dftc", [4096, 1024], BF16)
        d["dftns"] = self.din("dftns", [4096, 1024], BF16)
        d["nab"] = self.din("nab", [self.LW, 128, 8, 7, 128])
        d["namask"] = self.din("namask", [128, 48, 128], BF16)
        d["selr"] = self.din("selr", [128, 8])
        d["cnakT"] = self.din("cnakT", [L, 512, 256])
        d["cnav"] = self.din("cnav", [L, 256, 512])
        d["cckvT"] = self.din("cckvT", [L, 128, 256])
        d["ckrT"] = self.din("ckrT", [L, 32, 256])
        d["yc"] = self.dout("yc", [D, 512])
        d["yl"] = self.dout("yl", [D, 1024])
        d["onk"] = self.dout("onk", [L, 512, 512])
        d["onv"] = self.dout("onv", [L, 512, 512])
        d["ockv"] = self.dout("ockv", [L, 512, 128])
        d["okr"] = self.dout("okr", [L, 512, 32])
        d["pay_mla"] = [self.dint("pay_mla%d" % l, [160, 1024]) for l in range(L)]
        d["g_mla"] = [self.dint("g_mla%d" % l, [640, 1024]) for l in range(L)]
        d["pay_k"] = [self.dint("pay_k%d" % l, [512, 512]) for l in range(L)]
        d["g_k"] = [self.dint("g_k%d" % l, [2048, 512]) for l in range(L)]
        d["pay_v"] = [self.dint("pay_v%d" % l, [512, 768]) for l in range(L)]
        d["g_v"] = [self.dint("g_v%d" % l, [2048, 768]) for l in range(L)]
        d["pay_ab0"] = [self.dint("pay_ab0_%d" % l, [512, 1024]) for l in range(L)]
        d["pay_ab1"] = [self.dint("pay_ab1_%d" % l, [512, 1024]) for l in range(L)]
        d["g_ab0"] = [self.dint("g_ab0_%d" % l, [2048, 1024]) for l in range(L)]
        d["g_ab1"] = [self.dint("g_ab1_%d" % l, [2048, 1024]) for l in range(L)]
        self.d = d
        self.db = {k: [Buf(a) for a in d[k]] for k in ("pay_mla", "g_mla", "pay_k", "g_k", "pay_v", "g_v", "pay_ab0", "pay_ab1", "g_ab0", "g_ab1")}

        self.PS = [Buf(es.enter_context(nc.psum_tensor("ps%d" % i, [128, 512], F32)), "ps%d" % i) for i in range(8)]
        for b in self.PS:
            b.excl = True
        self.psgrp = {"g": [0, 1, 2, 3], "o": [4, 5], "x": [6, 7], "all": list(range(8))}
        self.psrr = {k: 0 for k in self.psgrp}

        P = es
        self.identb = Buf(self.sb(P, "identb", [128, 128], BF16))
        self.onesb = Buf(self.sb(P, "onesb", [128, 3, 128], BF16))
        self.onesf = Buf(self.sb(P, "onesf", [128, 128], F32))
        self.cs128 = Buf(self.sb(P, "cs128", [128, 256], BF16))
        self.c256 = Buf(self.sb(P, "c256", [128, 2, 256], BF16))
        self.ns256 = Buf(self.sb(P, "ns256", [128, 2, 256], BF16))
        self.sel32 = Buf(self.sb(P, "sel32", [32, 128], BF16))
        self.condt = Buf(self.sb(P, "condt", [128, 8, 2], F32))
        self.condb = Buf(self.sb(P, "condb", [128, 8, 2], BF16))
        self.bada = Buf(self.sb(P, "bada", [128, L, 24], F32))
        self.normg = Buf(self.sb(P, "normg", [128, L, 8], F32))
        self.fing = Buf(self.sb(P, "fing", [128, 8], F32))
        self.qng = Buf(self.sb(P, "qng", [128, L, 2], F32))
        self.kvng = Buf(self.sb(P, "kvng", [128, L], F32))
        self.kvnbc = Buf(self.sb(P, "kvnbc", [128, L, 128], F32))
        self.MOD = Buf(self.sb(P, "mod", [128, L, 24, 2], F32))
        self.GS = Buf(self.sb(P, "gs", [128, L, 8, 2], F32))
        self.WB = [Buf(self.sb(P, "wb%d" % i, [128, 8, 512], BF16)) for i in range(3)]
        self.wbi = 0
        self.SQ = [Buf(self.sb(P, "sq%d" % i, [128, 512], BF16)) for i in range(2)]
        self.TMP = [Buf(self.sb(P, "tmp%d" % i, [128, 512], F32)) for i in range(2)]
        self.RS = Buf(self.sb(P, "rs", [128, 512], F32))
        self.RD = Buf(self.sb(P, "rd", [128, 512], F32))
        self.BCS = Buf(self.sb(P, "bcs", [128, 512], F32))
        self.TMPO = Buf(self.sb(P, "tmpo", [128, 512], F32))
        self.PT = [Buf(self.sb(P, "pt%d" % i, [128, 512], BF16)) for i in range(5)]
        self.pending = None
        self.pti = 0
        self.sqi = 0
        self.tmi = 0

        def ld(buf, src, e="sp"):
            s.dma(e, lambda q: q.dma_start(out=buf.ap[:], in_=src), writes=[buf])
        ld(self.identb, d["identb"][:, :])
        ld(self.cs128, d["cs128"][:, :])
        ld(self.c256, d["c256"].rearrange("(t p) n -> p t n", p=128))
        ld(self.ns256, d["ns256"].rearrange("(t p) n -> p t n", p=128))
        ld(self.sel32, d["sel32"][:, :])
        ld(self.condt, d["cond"].rearrange("(c p) n -> p c n", p=128))
        ld(self.bada, d["b_adaT"][:, :, :])
        ld(self.normg, d["norm_gT"][:, :, :])
        ld(self.fing, d["fin_gT"][:, :])
        ld(self.qng, d["qn_gT"][:, :, :])
        ld(self.kvng, d["kvn_gT"][:, :])
        ld(self.kvnbc, d["kvn_bc"][:, :, :])
        for i, v in enumerate((1.0 / 1024, 1.0 / 256, 1.0 / 128)):
            s.op("dve", lambda q: q.memset(self.onesb.ap[:, i, :], v), pwrites=[self.onesb])
        s.op("dve", lambda q: q.memset(self.onesf.ap[:], 1.0), writes=[self.onesf])
        s.op("act", lambda q: q.activation(out=self.condb.ap[:], in_=self.condt.ap[:], func=AF.Silu),
             reads=[self.condt], writes=[self.condb])

        for l in range(NL if not self.run_ctx else min(1, NL)):
            for jb in range(6):
                self.adaln_compute(l, jb, self.adaln_load(l, jb))
            self.adaln_gs(l)

        if self.run_ctx and self.stage >= 1:
            self.chain("ctx")
        if self.run_lat and self.stage >= 1:
            self.chain("lat")
        s.finish()
        return nc

    def tick_wb_hook(self):
        hooks = getattr(self, "wb_hooks", [])
        self.wb_hooks = []
        for h in hooks:
            h[0] -= 1
            if h[0] <= 0:
                h[1]()
            else:
                self.wb_hooks.append(h)

    def flush_wb_hook(self):
        hooks = getattr(self, "wb_hooks", [])
        self.wb_hooks = []
        for h in hooks:
            h[1]()

    def adaln_load(self, l, jb):
        s, d = self.s, self.d
        wb = self.next_wb()
        s.dma("pool", lambda q: q.dma_start(
            out=wb.ap[:], in_=d["w_ada"][l, :, jb * 512:(jb + 1) * 512].rearrange("(c p) n -> p c n", p=128)),
            writes=[wb])
        return wb

    def adaln_compute(self, l, jb, wb):
        s = self.s
        for jj in range(4):
            j = jb * 4 + jj
            ps = self.ps("g")
            for kc in range(8):
                s.op("pe", lambda q: q.matmul(ps.ap[:, 0:2], lhsT=wb.ap[:, kc, jj * 128:(jj + 1) * 128],
                                              rhs=self.condb.ap[:, kc, :], start=(kc == 0), stop=(kc == 7)),
                     reads=[wb, self.condb], pwrites=[ps])
            s.op("dve", lambda q: q.tensor_scalar(out=self.MOD.ap[:, l, j, :], in0=ps.ap[:, 0:2],
                                                  scalar1=self.bada.ap[:, l, j:j + 1], scalar2=0.0,
                                                  op0=ALU.add, op1=ALU.add),
                 reads=[ps, self.bada], pwrites=[self.MOD])

    def adaln_gs(self, l):
        s = self.s
        for kc in range(8):
            s.op("dve", lambda q: q.tensor_scalar(out=self.GS.ap[:, l, kc, :], in0=self.MOD.ap[:, l, 8 + kc, :],
                                                  scalar1=1.0, scalar2=self.normg.ap[:, l, kc:kc + 1],
                                                  op0=ALU.add, op1=ALU.mult),
                 reads=[self.MOD, self.normg], pwrites=[self.GS])

    def next_wb(self):
        wb = self.WB[self.wbi % 3]
        self.wbi += 1
        return wb

    def next_pt(self):
        b = self.PT[self.pti % 5]
        self.pti += 1
        return b

    def run_pipeline(self, items, LA=3):
        n = len(items)
        for step in range(n + LA):
            if step < n:
                items[step][0]()
            if step == min(2, n - 1):
                self.flush_pending()
            if step >= LA:
                items[step - LA][1]()

    def run_pipeline2(self, items, LA=2, DL=2):
        n = len(items)
        due = []
        for step in range(n + LA + DL + 1):
            if step < n:
                items[step][0]()
            if LA <= step < n + LA:
                it = items[step - LA]
                it[1]()
                if it[2] is not None:
                    due.append((step + DL, it[2]))
            while due and due[0][0] <= step:
                due.pop(0)[1]()

    def flush_pending(self):
        if self.pending is not None:
            p = self.pending
            self.pending = None
            p()

    def next_sq(self):
        b = self.SQ[self.sqi % 2]
        self.sqi += 1
        return b

    def next_tmp(self):
        b = self.TMP[self.tmi % 2]
        self.tmi += 1
        return b

    def rstd_of(self, chunks, reads, ones_idx, n=512):
        s = self.s
        ps = self.ps("x")
        nk = len(chunks)
        for i, ch in enumerate(chunks):
            sq = self.next_sq()
            s.op("act", lambda q: q.activation(out=sq.ap[:, 0:n], in_=ch, func=AF.Square), reads=reads, writes=[sq])
            s.op("pe", lambda q: q.matmul(ps.ap[:, 0:n], lhsT=self.onesb.ap[:, ones_idx, :], rhs=sq.ap[:, 0:n],
                                          start=(i == 0), stop=(i == nk - 1)),
                 reads=[sq, self.onesb], pwrites=[ps])
        s.op("act", lambda q: q.activation(out=self.RS.ap[:, 0:n], in_=ps.ap[:, 0:n], func=AF.Sqrt, bias=EPS, scale=1.0),
             reads=[ps], writes=[self.RS])
        s.op("dve", lambda q: q.reciprocal(out=self.RS.ap[:, 0:n], in_=self.RS.ap[:, 0:n]),
             reads=[self.RS], writes=[self.RS])
        return self.RS

    def attn_finish(self, po, h, OG, OGb, ts, n=512):
        s = self.s
        j, par = h // 2, h % 2
        base = 64 * par
        dp = 64 if par == 0 else 0
        s.op("dve", lambda q: q.reciprocal(out=self.RD.ap[dp:dp + 1, 0:n], in_=po.ap[dp:dp + 1, 0:n]),
             reads=[po], writes=[self.RD])
        bc = self.ps("x")
        s.op("pe", lambda q: q.matmul(bc.ap[:, 0:n], lhsT=self.onesf.ap[dp:dp + 1, :], rhs=self.RD.ap[dp:dp + 1, 0:n],
                                      start=True, stop=True),
             reads=[self.RD, self.onesf], writes=[bc])
        s.op("dve", lambda q: q.tensor_copy(out=self.BCS.ap[base:base + 64, 0:n], in_=bc.ap[base:base + 64, 0:n]),
             reads=[bc], writes=[self.BCS])
        s.op("dve", lambda q: q.tensor_tensor(out=self.TMPO.ap[base:base + 64, 0:n], in0=po.ap[base:base + 64, 0:n],
                                              in1=self.BCS.ap[base:base + 64, 0:n], op=ALU.mult),
             reads=[po, self.BCS], writes=[self.TMPO])
        s.op("dve", lambda q: q.tensor_tensor(out=OG[base:base + 64, j, ts], in0=self.TMPO.ap[base:base + 64, 0:n],
                                              in1=OG[base:base + 64, j, ts], op=ALU.mult),
             reads=[self.TMPO, OGb], pwrites=[OGb])

    def chain(self, mode):
        nc, s, d = self.nc, self.s, self.d
        lat = (mode == "lat")
        T = 1024 if lat else 512
        NTB = T // 512
        NT = T // 128
        col = 1 if lat else 0
        with ExitStack() as C:
            X = self.sb(C, "X", [128, 8, T], F32)
            Xb = [Buf(X) for _ in range(NTB)]
            XM = self.sb(C, "XM", [128, 8, T], BF16)
            XMb = [Buf(XM) for _ in range(NTB)]
            OG = [self.sb(C, "OG%d" % r, [128, 4, T], BF16) for r in range(3)]
            OGb = [[Buf(OG[r]) for _ in range(NTB)] for r in range(3)]
            xin = d["xl"] if lat else d["xc"]
            for tb in range(NTB):
                s.dma("sp", lambda q: q.dma_start(
                    out=X[:, :, tb * 512:(tb + 1) * 512],
                    in_=xin[:, tb * 512:(tb + 1) * 512].rearrange("(c p) n -> p c n", p=128)), writes=[Xb[tb]])
            ctxs = dict(lat=lat, T=T, NTB=NTB, NT=NT, col=col, X=X, Xb=Xb, XM=XM, XMb=XMb, OG=OG, OGb=OGb)
            if lat:
                ROPE = self.sb(C, "rope", [128, 2, 1024], F32)
                ROPEb = Buf(ROPE)
                s.dma("sp", lambda q: q.dma_start(out=ROPE[:], in_=d["ropeT"].rearrange("a p n -> p a n")), writes=[ROPEb])
                SELR = self.sb(C, "selr", [128, 8], F32)
                SELRb = Buf(SELR)
                s.dma("sp", lambda q: q.dma_start(out=SELR[:], in_=d["selr"][:, :]), writes=[SELRb])
                ctxs.update(ROPE=ROPE, ROPEb=ROPEb, SELR=SELR, SELRb=SELRb)
            for l in range(self.nlayers):
                self.layer(l, ctxs)
            yout = d["yl"] if lat else d["yc"]
            for tb in range(NTB):
                ts = slice(tb * 512, (tb + 1) * 512)
                rs = self.rstd_of([X[:, kc, ts] for kc in range(8)], [Xb[tb]], 0)
                for kc in range(8):
                    tmp = self.next_tmp()
                    s.op("dve", lambda q: q.tensor_tensor(out=tmp.ap[:], in0=X[:, kc, ts], in1=rs.ap[:], op=ALU.mult),
                         reads=[Xb[tb], rs], writes=[tmp])
                    s.op("act", lambda q: q.activation(out=tmp.ap[:], in_=tmp.ap[:], func=AF.Identity,
                                                       scale=self.fing.ap[:, kc:kc + 1]),
                         reads=[tmp, self.fing], writes=[tmp])
                    s.dma("sp", lambda q: q.dma_start(out=yout[kc * 128:(kc + 1) * 128, ts], in_=tmp.ap[:]), reads=[tmp])
            s.barrier()

    def layer(self, l, c):
        nc, s, d = self.nc, self.s, self.d
        lat, T, NTB, NT, col = c["lat"], c["T"], c["NTB"], c["NT"], c["col"]
        X, Xb, XM, XMb, OG, OGb = c["X"], c["Xb"], c["XM"], c["XMb"], c["OG"], c["OGb"]
        w_in = d["w_in"]

        def TS(tb):
            return slice(tb * 512, (tb + 1) * 512)

        for tb in range(NTB):
            ts = TS(tb)
            rs = self.rstd_of([X[:, kc, ts] for kc in range(8)], [Xb[tb]], 0)
            for kc in range(8):
                tmp = self.next_tmp()
                s.op("dve", lambda q: q.tensor_tensor(out=tmp.ap[:], in0=X[:, kc, ts], in1=rs.ap[:], op=ALU.mult),
                     reads=[Xb[tb], rs], writes=[tmp])
                s.op("act", lambda q: q.activation(out=XM[:, kc, ts], in_=tmp.ap[:], func=AF.Identity,
                                                   scale=self.GS.ap[:, l, kc, col:col + 1],
                                                   bias=self.MOD.ap[:, l, kc, col:col + 1]),
                     reads=[tmp, self.GS, self.MOD], pwrites=[XMb[tb]])

        if self.stage < 3:
            return
        WOS = ExitStack()
        wo = {}

        def ensure_wo(load=True):
            if "WO" not in wo:
                wo["WO"] = [self.sb(WOS, "WO%d" % r, [128, 4, 1024], BF16) for r in range(3)]
                wo["WOb"] = [Buf(wo["WO"][r]) for r in range(3)]
            if load and "loaded" not in wo:
                wo["loaded"] = True
                for r, nm in enumerate(("w_o_na", "w_o_mla", "w_o_fn")):
                    s.dma("pool", lambda q: q.dma_start(out=wo["WO"][r][:], in_=d[nm][l].rearrange("(c p) n -> p c n", p=128)),
                          writes=[wo["WOb"][r]])
        mrg_pre = {}

        def prefetch_merge0():
            if "wb" in mrg_pre:
                return
            wb0 = self.next_wb()
            wv0 = wb0.ap[:, :, 0:384].rearrange("p c (r n) -> p c r n", r=3)
            wm0 = w_in[l, :, C_MRG:D_IN].rearrange("(c p) (r n) -> p c r n", p=128, r=3)
            for r in range(3):
                s.dma("pool", lambda q: q.dma_start(out=wv0[:, :, r, :], in_=wm0[:, :, r, 0:128]), pwrites=[wb0])
            mrg_pre["wb"] = wb0
        c["pre_merge"] = None if lat else prefetch_merge0
        if not lat:
            ensure_wo(load=False)
        A0 = ExitStack()
        A1 = ExitStack()
        QL = self.sb(A0, "QL", [128, 2, T], BF16)
        QLb = [Buf(QL) for _ in range(NTB)]
        QT = self.sb(A1, "QT", [128, 4, T], BF16)
        QTb = [Buf(QT) for _ in range(NTB)]
        KT = self.sb(A1, "KT", [128, 4, T], BF16)
        KTb = [Buf(KT) for _ in range(NTB)]
        V = self.sb(A1, "V", [128, NT, 768], BF16)
        Vb = [Buf(V) for _ in range(NT)]
        nap = None
        if lat and self.stage >= 4.1:
            nap = self.lat_na_prefetch(l, A1)
        with ExitStack() as A:
            CK = self.sb(A, "CK", [128, T], BF16)
            CKb = [Buf(CK) for _ in range(NTB)]
            KR = self.sb(A, "KR", [32, T], BF16)
            KRb = [Buf(KR) for _ in range(NTB)]
            UF = self.sb(A, "UF", [128, 4, 512], BF16)
            UFb = Buf(UF)
            QLR = self.sb(A, "QLR", [128, 3, 512], F32)
            QLRb = Buf(QLR)
            WKRP = self.sb(A, "WKRP", [128, 8, 32], BF16)
            WKRPb = Buf(WKRP)
            ABS = [self.sb(A, "ABS%d" % i, [128, 1024], BF16) for i in range(2)]
            ABSb = [Buf(ABS[i]) for i in range(2)]
            if not lat:
                ABC = self.sb(A, "ABC", [128, 4, 1024], BF16)
                ABCb = [Buf(ABC) for _ in range(4)]
                STG = [self.sb(A, "STG%d" % i, [128, 512], F32) for i in range(2)]
                STGb = [Buf(STG[i]) for i in range(2)]
                STK = self.sb(A, "STK", [128, 160], F32)
                STKb = Buf(STK)
                STS = self.sb(A, "STS", [128, 4], F32)
                STSb = Buf(STS)
            s.op("dve", lambda q: q.memset(V[:, :, :], 0.0), writes=Vb)
            for p in range(4):
                s.op("dve", lambda q: q.memset(V[:, :, p * 192 + 64:p * 192 + 65], 1.0), pwrites=Vb)
            if lat:
                s.dma("pool", lambda q: q.dma_start(out=WKRP[:], in_=d["w_krp"][l].rearrange("(c p) n -> p c n", p=128)),
                      writes=[WKRPb])

            def load_w(c0, n):
                self.tick_wb_hook()
                wb = self.next_wb()
                s.dma("pool", lambda q: q.dma_start(
                    out=wb.ap[:, :, 0:n], in_=w_in[l, :, c0:c0 + n].rearrange("(c p) n -> p c n", p=128)), writes=[wb])
                return wb

            def mm_fm(ps, M, wb, wsl, tb, extra_reads=()):
                for kc in range(8):
                    s.op("pe", lambda q: q.matmul(ps.ap[0:M, :], lhsT=wb.ap[:, kc, wsl], rhs=XM[:, kc, TS(tb)],
                                                  start=(kc == 0), stop=(kc == 7)),
                         reads=[wb, XMb[tb]], pwrites=[ps])

            def blk_qk(sel=None):
                for (c0, dst, dstb, scl) in ((C_Q, QT, QTb, NA_SCALE), (C_K, KT, KTb, 1.0)) if self.stage >= 3.1 else ():
                    if sel is not None and c0 != sel:
                        continue
                    wb = load_w(c0, 512)
                    for tb in range(NTB):
                        for j in range(4):
                            ps = self.ps("g")
                            mm_fm(ps, 128, wb, slice(j * 128, (j + 1) * 128), tb)
                            if j % 2 == 0:
                                s.op("dve", lambda q: q.tensor_scalar(out=dst[:, j, TS(tb)], in0=ps.ap[:], scalar1=scl,
                                                                      scalar2=0.0, op0=ALU.mult, op1=ALU.add),
                                     reads=[ps], pwrites=[dstb[tb]])
                            else:
                                s.op("act", lambda q: q.activation(out=dst[:, j, TS(tb)], in_=ps.ap[:], func=AF.Identity,
                                                                   scale=scl),
                                     reads=[ps], pwrites=[dstb[tb]])
                    if (not lat) and c0 == C_K:
                        for t in range(NT):
                            ps = self.ps("g")
                            for kc in range(8):
                                s.op("pe", lambda q: q.matmul(ps.ap[:], lhsT=XM[:, kc, t * 128:(t + 1) * 128],
                                                              rhs=wb.ap[:, kc, :], start=(kc == 0), stop=(kc == 7)),
                                     reads=[wb, XMb[0]], pwrites=[ps])
                            st = STGb[t % 2]
                            s.op("act", lambda q: q.activation(out=st.ap[:], in_=ps.ap[:], func=AF.Identity),
                                 reads=[ps], writes=[st])
                            s.dma("sp", lambda q: q.dma_start(out=d["onk"][l, t * 128:(t + 1) * 128, :], in_=st.ap[:]),
                                  reads=[st])
            def blk_v(sel=None):
                wb = load_w(C_V, 512)
                for t in range(NT) if self.stage >= 3.2 else ():
                    ps = self.ps("g")
                    tb = t // 4
                    for kc in range(8):
                        s.op("pe", lambda q: q.matmul(ps.ap[:], lhsT=XM[:, kc, t * 128:(t + 1) * 128], rhs=wb.ap[:, kc, :],
                                                      start=(kc == 0), stop=(kc == 7)),
                             reads=[wb, XMb[tb]], pwrites=[ps])
                    vv = V[:, t, :].rearrange("p (a b) -> p a b", b=192)
                    pv = ps.ap[:].rearrange("p (a e x) -> p a e x", e=2, x=64)
                    s.op("dve", lambda q: q.tensor_copy(out=vv[:, :, 0:64], in_=pv[:, :, 0, :]), reads=[ps], pwrites=[Vb[t]])
                    s.op("dve", lambda q: q.tensor_copy(out=vv[:, :, 128:192], in_=pv[:, :, 1, :]), reads=[ps], pwrites=[Vb[t]])
                    if not lat and not os.environ.get("KDBG_NOONV"):
                        st = STGb[t % 2]
                        s.op("dve", lambda q: q.tensor_copy(out=st.ap[:], in_=ps.ap[:]), reads=[ps], writes=[st])
                        s.dma("sp", lambda q: q.dma_start(out=d["onv"][l, t * 128:(t + 1) * 128, :], in_=st.ap[:]), reads=[st])
            def blk_gates(sel=None):
                for (c0, r) in ((C_GNA, 0), (C_GMLA, 1), (C_GFN, 2)) if self.stage >= 3.3 else ():
                    wb = load_w(c0, 512)
                    for tb in range(NTB):
                        for j in range(4):
                            ps = self.ps("g")
                            mm_fm(ps, 128, wb, slice(j * 128, (j + 1) * 128), tb)
                            s.op("act", lambda q: q.activation(out=OG[r][:, j, TS(tb)], in_=ps.ap[:], func=AF.Silu),
                                 reads=[ps], pwrites=[OGb[r][tb]])
            def blk_qlat(sel=None):
                wb = load_w(C_QLAT, 416)
                for tb in range(NTB) if self.stage >= 3.4 else ():
                    ts = TS(tb)
                    for j in range(3):
                        ps = self.ps("g")
                        mm_fm(ps, 128, wb, slice(j * 128, (j + 1) * 128), tb)
                        s.op("dve", lambda q: q.tensor_copy(out=QLR[:, j, :], in_=ps.ap[:]), reads=[ps], pwrites=[QLRb])
                    rs = self.rstd_of([QLR[:, 0, :], QLR[:, 1, :]], [QLRb], 1)
                    for j in range(2):
                        tmp = self.next_tmp()
                        s.op("dve", lambda q: q.tensor_tensor(out=tmp.ap[:], in0=QLR[:, j, :], in1=rs.ap[:], op=ALU.mult),
                             reads=[QLRb, rs], writes=[tmp])
                        s.op("act", lambda q: q.activation(out=QL[:, j, ts], in_=tmp.ap[:], func=AF.Identity,
                                                           scale=self.qng.ap[:, l, j:j + 1]),
                             reads=[tmp, self.qng], pwrites=[QLb[tb]])
                    rs = self.rstd_of([QLR[:, 2, :]], [QLRb], 2)
                    tmp = self.next_tmp()
                    s.op("dve", lambda q: q.tensor_tensor(out=tmp.ap[:], in0=QLR[:, 2, :], in1=rs.ap[:], op=ALU.mult),
                         reads=[QLRb, rs], writes=[tmp])
                    s.op("act", lambda q: q.activation(out=CK[:, ts], in_=tmp.ap[:], func=AF.Identity,
                                                       scale=self.kvng.ap[:, l:l + 1]),
                         reads=[tmp, self.kvng], pwrites=[CKb[tb]])
                    ps = self.ps("g")
                    mm_fm(ps, 32, wb, slice(384, 416), tb)
                    if lat:
                        ps2 = self.ps("g")
                        for kc in range(8):
                            s.op("pe", lambda q: q.matmul(ps2.ap[0:32, :], lhsT=WKRP[:, kc, :], rhs=XM[:, kc, ts],
                                                          start=(kc == 0), stop=(kc == 7)),
                                 reads=[WKRPb, XMb[tb]], pwrites=[ps2])
                        t1 = self.next_tmp()
                        t2 = self.next_tmp()
                        s.op("dve", lambda q: q.tensor_tensor(out=t1.ap[0:32, :], in0=ps.ap[0:32, :],
                                                              in1=c["ROPE"][0:32, 0, ts], op=ALU.mult),
                             reads=[ps, c["ROPEb"]], writes=[t1])
                        s.op("dve", lambda q: q.tensor_tensor(out=t2.ap[0:32, :], in0=ps2.ap[0:32, :],
                                                              in1=c["ROPE"][0:32, 1, ts], op=ALU.mult),
                             reads=[ps2, c["ROPEb"]], writes=[t2])
                        s.op("dve", lambda q: q.tensor_tensor(out=KR[0:32, ts], in0=t1.ap[0:32, :], in1=t2.ap[0:32, :],
                                                              op=ALU.add),
                             reads=[t1, t2], pwrites=[KRb[tb]])
                    else:
                        s.op("act", lambda q: q.activation(out=KR[0:32, ts], in_=ps.ap[0:32, :], func=AF.Identity),
                             reads=[ps], pwrites=[KRb[tb]])
                    if not lat:
                        for t in range(4):
                            ps = self.ps("g")
                            for kc in range(8):
                                s.op("pe", lambda q: q.matmul(ps.ap[:, 0:160], lhsT=XM[:, kc, t * 128:(t + 1) * 128],
                                                              rhs=wb.ap[:, kc, 256:416], start=(kc == 0), stop=(kc == 7)),
                                     reads=[wb, XMb[0]], pwrites=[ps])
                            s.op("act", lambda q: q.activation(out=STK[:, 0:128], in_=ps.ap[:, 0:128], func=AF.Square),
                                 reads=[ps], writes=[STKb])
                            s.op("dve", lambda q: q.reduce_sum(out=STS[:, 0:1], in_=STK[:, 0:128], axis=mybir.AxisListType.X),
                                 reads=[STKb], writes=[STSb])
                            s.op("act", lambda q: q.activation(out=STS[:, 1:2], in_=STS[:, 0:1], func=AF.Sqrt, bias=EPS,
                                                               scale=1.0 / 128),
                                 reads=[STSb], writes=[STSb])
                            s.op("dve", lambda q: q.reciprocal(out=STS[:, 2:3], in_=STS[:, 1:2]), reads=[STSb], writes=[STSb])
                            s.op("dve", lambda q: q.tensor_scalar(out=STK[:, 0:128], in0=ps.ap[:, 0:128],
                                                                  scalar1=STS[:, 2:3], scalar2=0.0, op0=ALU.mult, op1=ALU.add),
                                 reads=[ps, STSb], writes=[STKb])
                            s.op("dve", lambda q: q.tensor_tensor(out=STK[:, 0:128], in0=STK[:, 0:128],
                                                                  in1=self.kvnbc.ap[:, l, :], op=ALU.mult),
                                 reads=[STKb, self.kvnbc], writes=[STKb])
                            s.op("act", lambda q: q.activation(out=STK[:, 128:160], in_=ps.ap[:, 128:160], func=AF.Identity),
                                 reads=[ps], pwrites=[STKb])
                            s.dma("sp", lambda q: q.dma_start(out=d["ockv"][l, t * 128:(t + 1) * 128, :], in_=STK[:, 0:128]),
                                  reads=[STKb])
                            s.dma("sp", lambda q: q.dma_start(out=d["okr"][l, t * 128:(t + 1) * 128, :], in_=STK[:, 128:160]),
                                  reads=[STKb])
            def blk_ufn(sel=None):
                wb = load_w(C_UFN, 512)
                for tb in range(NTB) if self.stage >= 3.5 else ():
                    for j in range(4):
                        ps = self.ps("g")
                        mm_fm(ps, 128, wb, slice(j * 128, (j + 1) * 128), tb)
                        s.op("dve", lambda q: q.tensor_copy(out=UF[:, j, :], in_=ps.ap[:]), reads=[ps], pwrites=[UFb])
                    for tt in range(4):
                        t = tb * 4 + tt
                        if lat:
                            ab = ABS[t % 2]
                            abb = ABSb[t % 2]
                        else:
                            ab = ABC[:, t, :]
                            abb = ABCb[t]
                        for half in range(2):
                            ps = self.ps("g")
                            for gg in range(2):
                                g = half * 2 + gg
                                s.op("pe", lambda q: q.matmul(ps.ap[:, gg * 256:(gg + 1) * 256],
                                                              lhsT=UF[:, g, tt * 128:(tt + 1) * 128], rhs=self.cs128.ap[:],
                                                              start=True, stop=True),
                                     reads=[UFb, self.cs128], pwrites=[ps])
                            dst = ab[:, half * 512:(half + 1) * 512]
                            if half == 0:
                                s.op("dve", lambda q: q.tensor_copy(out=dst, in_=ps.ap[:]), reads=[ps], pwrites=[abb])
                            else:
                                s.op("act", lambda q: q.activation(out=dst, in_=ps.ap[:], func=AF.Identity),
                                     reads=[ps], pwrites=[abb])
                        if lat:
                            pn = "pay_ab%d" % (t // 4)
                            s.dma("sp", lambda q: q.dma_start(out=d[pn][l][(t % 4) * 128:(t % 4 + 1) * 128, :], in_=ab[:]),
                                  reads=[abb], pwrites=[self.db[pn][l]])

            def emit_cc(names):
                for nm in names:
                    pay, g = d['pay_' + nm][l], d['g_' + nm][l]
                    s.cc(lambda q: q.collective_compute('AllGather', ALU.bypass, replica_groups=GROUPS,
                                                        ins=[pay[:, :]], outs=[g[:, :]]),
                         reads=[self.db['pay_' + nm][l]], writes=[self.db['g_' + nm][l]])
            if not lat:
                blk_qk()
                blk_v()
                blk_gates()
                blk_qlat()
                blk_ufn()
            else:
                self.wb_hooks = []
                blk_qk(sel=C_K)
                blk_v()
                self.lat_pay_kv(l, c, KT, KTb, V, Vb)
                self.wb_hooks.append([2, lambda: emit_cc(('k',))])
                self.wb_hooks.append([3, lambda: emit_cc(('v',))])
                blk_qlat()
                self.lat_pay_mla(l, c, CK, CKb, KR, KRb)
                blk_ufn()
                blk_qk(sel=C_Q)
                blk_gates()
                self.flush_wb_hook()
                self.cc_late = [(lambda: emit_cc(('mla',))), (lambda: emit_cc(('ab0',))), (lambda: emit_cc(('ab1',)))]
            if self.stage >= 4:
                if not lat:
                    ensure_wo()
                    self.ctx_attention(l, c, A, QT, QTb, KT, KTb, V, Vb, QL, QLb, CK, CKb, KR, KRb, ABC, ABCb)
            s.barrier()
        if lat and self.stage >= 4.1:
            self.lat_na(l, c, QT, QTb, KT, KTb, V, Vb, nap)
            s.barrier()
        A1.close()
        if lat and self.stage >= 4.2:
            self.lat_mla(l, c, QL, QLb)
            s.barrier()
        if lat and self.stage >= 4.3:
            ensure_wo()
            prefetch_merge0()
            self.lat_fourier(l, c)
            s.barrier()
        if not lat:
            A0.close()
        if self.stage < 5:
            return

        ensure_wo()
        WO, WOb = wo["WO"], wo["WOb"]
        with ExitStack() as Fz:
            MG = self.sb(Fz, "MG", [128, 8, T], BF16)
            MGb = [Buf(MG) for _ in range(NTB)]
            SG = [self.sb(Fz, "SG%d" % r, [128, 512], F32) for r in range(3)]
            SGb = [Buf(SG[r]) for r in range(3)]
            MT = self.sb(Fz, "MT", [128, 512], F32)
            MTb = Buf(MT)
            MT2 = self.sb(Fz, "MT2", [128, 512], F32)
            MT2b = Buf(MT2)
            wm = w_in[l, :, C_MRG:D_IN].rearrange("(c p) (r n) -> p c r n", p=128, r=3)
            for cch in range(8):
                if cch == 0 and "wb" in mrg_pre:
                    wb = mrg_pre["wb"]
                    wv = wb.ap[:, :, 0:384].rearrange("p c (r n) -> p c r n", r=3)
                else:
                    wb = self.next_wb()
                    wv = wb.ap[:, :, 0:384].rearrange("p c (r n) -> p c r n", r=3)
                    for r in range(3):
                        s.dma("pool", lambda q: q.dma_start(out=wv[:, :, r, :], in_=wm[:, :, r, cch * 128:(cch + 1) * 128]),
                              pwrites=[wb])
                for tb in range(NTB):
                    ts = TS(tb)
                    for r in range(3):
                        ps = self.ps("all")
                        for kc in range(8):
                            s.op("pe", lambda q: q.matmul(ps.ap[:], lhsT=wv[:, kc, r, :], rhs=XM[:, kc, ts],
                                                          start=(kc == 0), stop=(kc == 7)),
                                 reads=[wb, XMb[tb]], pwrites=[ps])
                        s.op("act", lambda q: q.activation(out=SG[r][:], in_=ps.ap[:], func=AF.Sigmoid),
                             reads=[ps], writes=[SGb[r]])
                    for r in range(3):
                        ps = self.ps("all")
                        for kc in range(4):
                            s.op("pe", lambda q: q.matmul(ps.ap[:], lhsT=WO[r][:, kc, cch * 128:(cch + 1) * 128],
                                                          rhs=OG[r][:, kc, ts], start=(kc == 0), stop=(kc == 3)),
                                 reads=[WOb[r], OGb[r][tb]], pwrites=[ps])
                        if r == 0:
                            s.op("dve", lambda q: q.tensor_tensor(out=MT[:], in0=ps.ap[:], in1=SG[0][:], op=ALU.mult),
                                 reads=[ps, SGb[0]], writes=[MTb])
                        else:
                            s.op("dve", lambda q: q.tensor_tensor(out=MT2[:], in0=ps.ap[:], in1=SG[r][:], op=ALU.mult),
                                 reads=[ps, SGb[r]], writes=[MT2b])
                            if r == 1:
                                s.op("dve", lambda q: q.tensor_tensor(out=MT[:], in0=MT[:], in1=MT2[:], op=ALU.add),
                                     reads=[MTb, MT2b], writes=[MTb])
                            else:
                                s.op("dve", lambda q: q.tensor_tensor(out=MG[:, cch, ts], in0=MT[:], in1=MT2[:], op=ALU.add),
                                     reads=[MTb, MT2b], pwrites=[MGb[tb]])
            for half in range(2):
                wb = self.next_wb()
                s.dma("pool", lambda q: q.dma_start(
                    out=wb.ap[:], in_=d["w_out"][l, :, half * 512:(half + 1) * 512].rearrange("(c p) n -> p c n", p=128)),
                    writes=[wb])
                for tb in range(NTB):
                    ts = TS(tb)
                    for jj in range(4):
                        cch = half * 4 + jj
                        ps = self.ps("all")
                        for kc in range(8):
                            s.op("pe", lambda q: q.matmul(ps.ap[:], lhsT=wb.ap[:, kc, jj * 128:(jj + 1) * 128],
                                                          rhs=MG[:, kc, ts], start=(kc == 0), stop=(kc == 7)),
                                 reads=[wb, MGb[tb]], pwrites=[ps])
                        s.op("dve", lambda q: q.scalar_tensor_tensor(out=X[:, cch, ts], in0=ps.ap[:],
                                                                     scalar=self.MOD.ap[:, l, 16 + cch, col:col + 1],
                                                                     in1=X[:, cch, ts], op0=ALU.mult, op1=ALU.add),
                             reads=[ps, self.MOD, Xb[tb]], pwrites=[Xb[tb]])
            s.barrier()
        WOS.close()
        if lat:
            A0.close()

    def ctx_attention(self, l, c, A, QT, QTb, KT, KTb, V, Vb, QL, QLb, CK, CKb, KR, KRb, ABC, ABCb):
        nc, s, d = self.nc, self.s, self.d
        OG, OGb = c["OG"], c["OGb"]
        ts = slice(0, 512)
        items = []
        for h in range(8):
            j, par = h // 2, h % 2
            base = 64 * par
            hst = {}
            for bb in range(2):
                st = {}

                def front(h=h, j=j, par=par, base=base, bb=bb, st=st, hst=hst):
                    if bb == 0:
                        hst["po"] = self.ps("o")
                    ps = self.ps("g")
                    for kt in range(2):
                        k0 = bb * 256 + kt * 128
                        s.op("pe", lambda q: q.matmul(ps.ap[:, kt * 256:(kt + 1) * 256], lhsT=KT[base:base + 64, j, k0:k0 + 128],
                                                      rhs=QT[base:base + 64, j, bb * 256:(bb + 1) * 256], start=True, stop=True),
                             reads=[KTb[0], QTb[0]], pwrites=[ps])
                    pt = self.next_pt()
                    s.op("act", lambda q: q.activation(out=pt.ap[:], in_=ps.ap[:], func=AF.Exp), reads=[ps], writes=[pt])
                    st["pt"] = pt

                def back(h=h, j=j, par=par, bb=bb, st=st, hst=hst):
                    po, pt = hst["po"], st["pt"]
                    for kt in range(2):
                        t = bb * 2 + kt
                        if par == 0:
                            out = po.ap[0:65, bb * 256:(bb + 1) * 256]
                            lhsT = V[:, t, j * 192:j * 192 + 65]
                        else:
                            out = po.ap[0:128, bb * 256:(bb + 1) * 256]
                            lhsT = V[:, t, j * 192 + 64:j * 192 + 192]
                        s.op("pe", lambda q: q.matmul(out, lhsT=lhsT, rhs=pt.ap[:, kt * 256:(kt + 1) * 256],
                                                      start=(kt == 0), stop=(kt == 1)),
                             reads=[Vb[t], pt], pwrites=[po])
                fin = None
                if bb == 1:
                    fin = (lambda h=h, hst=hst: self.attn_finish(hst["po"], h, OG[0], OGb[0][0], ts))
                items.append((front, back, fin))
        ada_next = (l + 1) if (l + 1 < self.nlayers) else None
        ada_wbs = [self.adaln_load(ada_next, jb) for jb in range(3)] if ada_next is not None else []
        self.run_pipeline2(items, LA=2, DL=2)
        for jb, wb in enumerate(ada_wbs):
            self.adaln_compute(ada_next, jb, wb)

        WUQ = self.sb(A, "WUQ", [128, 2, 768], BF16)
        WUQb = Buf(WUQ)
        WK = self.sb(A, "WK", [128, 8, 128], BF16)
        WKb = Buf(WK)
        WV = self.sb(A, "WV", [128, 512], BF16)
        WVb = Buf(WV)
        VM = self.sb(A, "VM", [128, 4, 192], BF16)
        VMb = Buf(VM)
        KHs = [self.sb(A, "KH%d" % i, [96, 512], BF16) for i in range(2)]
        KHbs = [Buf(KHs[i]) for i in range(2)]
        QHs = [self.sb(A, "QH%d" % i, [96, 512], BF16) for i in range(2)]
        QHbs = [Buf(QHs[i]) for i in range(2)]
        VM2 = self.sb(A, "VM2", [128, 4, 192], BF16)
        VM2b = Buf(VM2)
        s.dma("pool", lambda q: q.dma_start(out=WUQ[:], in_=d["w_uq"][l].rearrange("(c p) n -> p c n", p=128)), writes=[WUQb])
        s.op("dve", lambda q: q.memset(WK[:], 0.0), writes=[WKb])
        wukv = d["w_ukv"][l].rearrange("c (h t x) -> c h t x", t=2, x=64)
        s.dma("pool", lambda q: q.dma_start(out=WK[:, :, 0:64], in_=wukv[:, :, 0, :]), pwrites=[WKb])
        s.dma("pool", lambda q: q.dma_start(out=WV[:].rearrange("p (h x) -> p h x", x=64), in_=wukv[:, :, 1, :]), writes=[WVb])
        VMs, VMbs = [VM, VM2], [VMb, VM2b]
        for i in range(2):
            s.op("dve", lambda q: q.memset(VMs[i][:], 0.0), writes=[VMbs[i]])
            s.op("dve", lambda q: q.memset(VMs[i][:, :, 64:65], 1.0), pwrites=[VMbs[i]])
        items = []
        for h in range(8):
            p, par = h // 2, h % 2
            vm, vmb = VMs[p % 2], VMbs[p % 2]
            KH, KHb, QH, QHb = KHs[h % 2], KHbs[h % 2], QHs[h % 2], QHbs[h % 2]
            hst = {}
            for bb in range(2):
                st = {}

                def front(h=h, p=p, par=par, bb=bb, st=st, hst=hst, vm=vm, vmb=vmb, KH=KH, KHb=KHb, QH=QH, QHb=QHb):
                    if bb == 0 and par == 0:
                        ps = self.ps("g")
                        for t in range(4):
                            s.op("pe", lambda q: q.matmul(ps.ap[:, t * 128:(t + 1) * 128], lhsT=CK[:, t * 128:(t + 1) * 128],
                                                          rhs=WV[:, p * 128:(p + 1) * 128], start=True, stop=True),
                                 reads=[CKb[0], WVb], pwrites=[ps])
                        pv = ps.ap[:].rearrange("p (t e x) -> p t e x", e=2, x=64)
                        s.op("dve", lambda q: q.tensor_copy(out=vm[:, :, 0:64], in_=pv[:, :, 0, :]), reads=[ps], pwrites=[vmb])
                        s.op("dve", lambda q: q.tensor_copy(out=vm[:, :, 128:192], in_=pv[:, :, 1, :]), reads=[ps], pwrites=[vmb])
                    if bb == 0:
                        hst["po"] = self.ps("o")
                        ps = self.ps("g")
                        s.op("pe", lambda q: q.matmul(ps.ap[0:128, :], lhsT=WK[:, h, :], rhs=CK[:, 0:512], start=True, stop=False),
                             reads=[WKb, CKb[0]], pwrites=[ps])
                        s.op("pe", lambda q: q.matmul(ps.ap[0:128, :], lhsT=self.sel32.ap[:, :], rhs=KR[0:32, 0:512],
                                                      start=False, stop=True),
                             reads=[self.sel32, KRb[0]], pwrites=[ps])
                        s.op("dve", lambda q: q.tensor_copy(out=KH[:, :], in_=ps.ap[0:96, :]), reads=[ps], writes=[KHb])
                        ps = self.ps("g")
                        for kc in range(2):
                            s.op("pe", lambda q: q.matmul(ps.ap[0:96, :], lhsT=WUQ[:, kc, h * 96:(h + 1) * 96], rhs=QL[:, kc, 0:512],
                                                          start=(kc == 0), stop=(kc == 1)),
                                 reads=[WUQb, QLb[0]], pwrites=[ps])
                        s.op("dve", lambda q: q.tensor_copy(out=QH[:, :], in_=ps.ap[0:96, :]), reads=[ps], writes=[QHb])
                    ps = self.ps("g")
                    for kt in range(2):
                        k0 = bb * 256 + kt * 128
                        s.op("pe", lambda q: q.matmul(ps.ap[:, kt * 256:(kt + 1) * 256], lhsT=KH[:, k0:k0 + 128],
                                                      rhs=QH[:, bb * 256:(bb + 1) * 256], start=True, stop=True),
                             reads=[KHb, QHb], pwrites=[ps])
                    pt = self.next_pt()
                    s.op("act", lambda q: q.activation(out=pt.ap[:], in_=ps.ap[:], func=AF.Exp, scale=MLA_SCALE),
                         reads=[ps], writes=[pt])
                    st["pt"] = pt

                def back(par=par, bb=bb, st=st, hst=hst, vm=vm, vmb=vmb):
                    po, pt = hst["po"], st["pt"]
                    for kt in range(2):
                        t = bb * 2 + kt
                        if par == 0:
                            out = po.ap[0:65, bb * 256:(bb + 1) * 256]
                            lhsT = vm[:, t, 0:65]
                        else:
                            out = po.ap[0:128, bb * 256:(bb + 1) * 256]
                            lhsT = vm[:, t, 64:192]
                        s.op("pe", lambda q: q.matmul(out, lhsT=lhsT, rhs=pt.ap[:, kt * 256:(kt + 1) * 256],
                                                      start=(kt == 0), stop=(kt == 1)),
                             reads=[vmb, pt], pwrites=[po])
                fin = None
                if bb == 1:
                    fin = (lambda h=h, hst=hst: self.attn_finish(hst["po"], h, OG[1], OGb[1][0], ts))
                items.append((front, back, fin))
        ada_wbs = [self.adaln_load(ada_next, jb) for jb in range(3, 6)] if ada_next is not None else []
        self.run_pipeline2(items, LA=2, DL=2)
        for jb, wb in enumerate(ada_wbs):
            self.adaln_compute(ada_next, 3 + jb, wb)
        if ada_next is not None:
            self.adaln_gs(ada_next)
        if c.get("pre_merge") is not None:
            c["pre_merge"]()

        for g in range(4):
            po = self.ps("o")
            for bb in range(2):
                n = 0
                for nt in range(2):
                    t = bb * 2 + nt
                    for (off, mat) in ((0, self.c256), (128, self.ns256)):
                        s.op("pe", lambda q: q.matmul(po.ap[:, bb * 256:(bb + 1) * 256],
                                                      lhsT=ABC[:, t, g * 256 + off:g * 256 + off + 128],
                                                      rhs=mat.ap[:, nt, :], start=(n == 0), stop=(n == 3)),
                             reads=[ABCb[t], mat], pwrites=[po])
                        n += 1
            s.op("dve", lambda q: q.tensor_tensor(out=OG[2][:, g, ts], in0=po.ap[:], in1=OG[2][:, g, ts], op=ALU.mult),
                 reads=[po, OGb[2][0]], pwrites=[OGb[2][0]])


    def lat_pay_kv(self, l, c, KT, KTb, V, Vb):
        s, d, db = self.s, self.d, self.db
        pk = d["pay_k"][l].rearrange("(j p) n -> p j n", p=128)
        s.dma("sp", lambda q: q.dma_start(out=pk[:, :, 0:256], in_=KT[:, :, 0:256]), reads=[KTb[0]], pwrites=[db["pay_k"][l]])
        s.dma("sp", lambda q: q.dma_start(out=pk[:, :, 256:512], in_=KT[:, :, 768:1024]), reads=[KTb[1]], pwrites=[db["pay_k"][l]])
        pv = d["pay_v"][l].rearrange("(t p) n -> p t n", p=128)
        s.dma("sp", lambda q: q.dma_start(out=pv[:, 0:2, :], in_=V[:, 0:2, :]), reads=[Vb[0], Vb[1]], pwrites=[db["pay_v"][l]])
        s.dma("sp", lambda q: q.dma_start(out=pv[:, 2:4, :], in_=V[:, 6:8, :]), reads=[Vb[6], Vb[7]], pwrites=[db["pay_v"][l]])

    def lat_pay_mla(self, l, c, CK, CKb, KR, KRb):
        s, d, db = self.s, self.d, self.db
        s.dma("sp", lambda q: q.dma_start(out=d["pay_mla"][l][0:128, :], in_=CK[:, :]), reads=CKb, pwrites=[db["pay_mla"][l]])
        s.dma("sp", lambda q: q.dma_start(out=d["pay_mla"][l][128:160, :], in_=KR[0:32, :]), reads=KRb, pwrites=[db["pay_mla"][l]])

    def lat_na_prefetch(self, l, st):
        s, d = self.s, self.d
        KCT = self.sb(st, "KCT", [128, 4, 256], BF16)
        KCTb = Buf(KCT)
        VCX = self.sb(st, "VCX", [128, 2, 768], BF16)
        VCXb = Buf(VCX)
        BT = [self.sb(st, "BT%d" % i, [128, 2, 7, 128], BF16) for i in range(2)]
        BTb = [Buf(BT[i]) for i in range(2)]
        s.dma("pool", lambda q: q.dma_start(out=KCT[:], in_=d["cnakT"][l].rearrange("(j p) n -> p j n", p=128)), writes=[KCTb])
        s.op("dve", lambda q: q.memset(VCX[:], 0.0), writes=[VCXb])
        for p in range(4):
            s.op("dve", lambda q: q.memset(VCX[:, :, p * 192 + 64:p * 192 + 65], 1.0), pwrites=[VCXb])
        cv = d["cnav"][l].rearrange("(t p) (a e x) -> p t a e x", p=128, e=2, x=64)
        for t in range(2):
            vx = VCX[:, t, :].rearrange("p (a b) -> p a b", b=192)
            s.dma("pool", lambda q: q.dma_start(out=vx[:, :, 0:64], in_=cv[:, t, :, 0, :]), pwrites=[VCXb])
            s.dma("pool", lambda q: q.dma_start(out=vx[:, :, 128:192], in_=cv[:, t, :, 1, :]), pwrites=[VCXb])
        for p in range(2):
            s.dma("pool", lambda q: q.dma_start(out=BT[p][:], in_=d["nab"][l, :, 2 * p:2 * p + 2, :, :]), writes=[BTb[p]])
            s.op("act", lambda q: q.activation(out=BT[p][:], in_=BT[p][:], func=AF.Exp), reads=[BTb[p]], writes=[BTb[p]])
        return dict(KCT=KCT, KCTb=KCTb, VCX=VCX, VCXb=VCXb, BT=BT, BTb=BTb)

    def lat_na(self, l, c, QT, QTb, KT, KTb, V, Vb, nap):
        s, d, db = self.s, self.d, self.db
        KCT, KCTb, VCX, VCXb, BT, BTb = nap["KCT"], nap["KCTb"], nap["VCX"], nap["VCXb"], nap["BT"], nap["BTb"]
        OG, OGb = c["OG"], c["OGb"]
        SELR, SELRb = c["SELR"], c["SELRb"]
        with ExitStack() as N:
            HK = self.sb(N, "HK", [128, 4, 2, 256], BF16)
            HKb = Buf(HK)
            HV = self.sb(N, "HV", [128, 4, 768], BF16)
            HVb = Buf(HV)
            with ExitStack() as N2:
                KC = [self.sb(N2, "KC%d" % i, [128, 4, 512], BF16) for i in range(2)]
                KCb = [Buf(KC[i]) for i in range(2)]
                VC = [self.sb(N2, "VC%d" % i, [128, 4, 768], BF16) for i in range(2)]
                VCb = [Buf(VC[i]) for i in range(2)]
                gk = d["g_k"][l].rearrange("(c j p) n -> p j c n", c=4, j=4)
                for j in range(4):
                    kc, kcb = KC[j % 2], KCb[j % 2]
                    s.dma("sp", lambda q: q.dma_start(out=kc[:], in_=gk[:, j]), reads=[db["g_k"][l]], writes=[kcb])
                    for side in range(2):
                        cols = slice(256, 512) if side == 0 else slice(0, 256)
                        for cc in range(4):
                            sc = SELR[:, side * 4 + cc:side * 4 + cc + 1]
                            if cc == 0:
                                s.op("dve", lambda q: q.tensor_scalar(out=HK[:, j, side, :], in0=kc[:, cc, cols], scalar1=sc,
                                                                      scalar2=0.0, op0=ALU.mult, op1=ALU.add),
                                     reads=[kcb, SELRb], pwrites=[HKb])
                            else:
                                s.op("dve", lambda q: q.scalar_tensor_tensor(out=HK[:, j, side, :], in0=kc[:, cc, cols], scalar=sc,
                                                                             in1=HK[:, j, side, :], op0=ALU.mult, op1=ALU.add),
                                     reads=[kcb, SELRb, HKb], pwrites=[HKb])
                gv = d["g_v"][l].rearrange("(c t p) n -> p t c n", c=4, t=4)
                for ht in range(4):
                    side = ht // 2
                    src_t = (2 + ht) if side == 0 else (ht - 2)
                    vc, vcb = VC[ht % 2], VCb[ht % 2]
                    s.dma("sp", lambda q: q.dma_start(out=vc[:], in_=gv[:, src_t]), reads=[db["g_v"][l]], writes=[vcb])
                    for cc in range(4):
                        sc = SELR[:, side * 4 + cc:side * 4 + cc + 1]
                        if cc == 0:
                            s.op("dve", lambda q: q.tensor_scalar(out=HV[:, ht, :], in0=vc[:, cc, :], scalar1=sc, scalar2=0.0,
                                                                  op0=ALU.mult, op1=ALU.add),
                                 reads=[vcb, SELRb], pwrites=[HVb])
                        else:
                            s.op("dve", lambda q: q.scalar_tensor_tensor(out=HV[:, ht, :], in0=vc[:, cc, :], scalar=sc,
                                                                         in1=HV[:, ht, :], op0=ALU.mult, op1=ALU.add),
                                 reads=[vcb, SELRb, HVb], pwrites=[HVb])
                s.barrier()
            MK = self.sb(N, "MK", [128, 48, 128], BF16)
            MKb = Buf(MK)
            s.dma("sp", lambda q: q.dma_start(out=MK[:], in_=d["namask"][:, :, :]), writes=[MKb])
            all_items = []
            for p in range(4):
                bt, btb = BT[p % 2], BTb[p % 2]
                need_bt = [p >= 2]
                for h in (2 * p, 2 * p + 1):
                    par = h % 2
                    base = 64 * par
                    for grp in range(2):
                        po = self.ps("o")
                        items = []
                        for bi in range(4):
                            b = grp * 4 + bi
                            qs = slice(b * 128, (b + 1) * 128)
                            tiles = []
                            lts = [(b - 2 + jj, jj) for jj in range(5)]
                            if b == 0:
                                lts.append((3, 5))
                            if b == 7:
                                lts.append((4, 5))
                            for (lt, slot) in lts:
                                if 0 <= lt <= 7:
                                    kap = KT[base:base + 64, p, lt * 128:(lt + 1) * 128]
                                    kb_ = KTb[lt // 4]
                                    vt = V[:, lt, :]
                                    vb_ = Vb[lt]
                                elif lt < 0:
                                    kap = HK[base:base + 64, p, 0, (lt + 2) * 128:(lt + 3) * 128]
                                    kb_ = HKb
                                    vt = HV[:, lt + 2, :]
                                    vb_ = HVb
                                else:
                                    kap = HK[base:base + 64, p, 1, (lt - 8) * 128:(lt - 7) * 128]
                                    kb_ = HKb
                                    vt = HV[:, 2 + lt - 8, :]
                                    vb_ = HVb
                                tiles.append((kap, kb_, vt, vb_, lt - b + 3, b * 6 + slot))
                            for t in range(2):
                                tiles.append((KCT[base:base + 64, p, t * 128:(t + 1) * 128], KCTb, VCX[:, t, :], VCXb, None, None))
                            nt = len(tiles)
                            for g0 in range(0, nt, 4):
                                grpt = tiles[g0:g0 + 4]
                                st = {}

                                def front(grpt=grpt, st=st, qs=qs, grp=grp, base=base, p=p, par=par, bt=bt, btb=btb, need_bt=need_bt):
                                    if need_bt[0]:
                                        need_bt[0] = False
                                        s.dma("pool", lambda q: q.dma_start(out=bt[:], in_=d["nab"][l, :, 2 * p:2 * p + 2, :, :]),
                                              writes=[btb])
                                        s.op("act", lambda q: q.activation(out=bt[:], in_=bt[:], func=AF.Exp), reads=[btb], writes=[btb])
                                    ps = self.ps("g")
                                    for i, (kap, kb_, vt, vb_, dj, ms) in enumerate(grpt):
                                        reg = ps.ap[:, i * 128:(i + 1) * 128]
                                        s.op("pe", lambda q: q.matmul(reg, lhsT=kap, rhs=QT[base:base + 64, p, qs], start=True,
                                                                      stop=True),
                                             reads=[kb_, QTb[grp]], pwrites=[ps])
                                    w = len(grpt) * 128
                                    pt = self.next_pt()
                                    s.op("act", lambda q: q.activation(out=pt.ap[:, 0:w], in_=ps.ap[:, 0:w], func=AF.Exp),
                                         reads=[ps], writes=[pt])
                                    i = 0
                                    while i < len(grpt):
                                        dj, ms = grpt[i][4], grpt[i][5]
                                        if dj is None:
                                            i += 1
                                            continue
                                        n = 1
                                        while (i + n < len(grpt) and grpt[i + n][4] is not None
                                               and grpt[i + n][4] == dj + n and grpt[i + n][5] == ms + n):
                                            n += 1
                                        pv3 = pt.ap[:, i * 128:(i + n) * 128].rearrange("p (a b) -> p a b", b=128)
                                        s.op("dve", lambda q: q.tensor_tensor(out=pv3, in0=pv3, in1=bt[:, par, dj:dj + n, :], op=ALU.mult),
                                             reads=[pt, btb], writes=[pt])
                                        s.op("pool", lambda q: q.tensor_tensor(out=pv3, in0=pv3, in1=MK[:, ms:ms + n, :], op=ALU.mult),
                                             reads=[pt, MKb], writes=[pt])
                                        i += n
                                    st["pt"] = pt

                                def back(grpt=grpt, st=st, g0=g0, nt=nt, bi=bi, po=po, par=par, p=p):
                                    pt = st["pt"]
                                    for i, (kap, kb_, vt, vb_, dj, ms) in enumerate(grpt):
                                        n = g0 + i
                                        if par == 0:
                                            out = po.ap[0:65, bi * 128:(bi + 1) * 128]
                                            lhsT = vt[:, p * 192:p * 192 + 65]
                                        else:
                                            out = po.ap[0:128, bi * 128:(bi + 1) * 128]
                                            lhsT = vt[:, p * 192 + 64:p * 192 + 192]
                                        s.op("pe", lambda q: q.matmul(out, lhsT=lhsT, rhs=pt.ap[:, i * 128:(i + 1) * 128],
                                                                      start=(n == 0), stop=(n == nt - 1)),
                                             reads=[vb_, pt], pwrites=[po])
                                items.append([front, back, None])
                        items[-1][2] = (lambda po=po, h=h, grp=grp: self.attn_finish(po, h, OG[0], OGb[0][grp],
                                                                                      slice(grp * 512, (grp + 1) * 512)))
                        all_items.extend(items)
            late = getattr(self, "cc_late", None) or []
            self.cc_late = None
            npos = len(all_items)
            for i, f in enumerate(late):
                pos = (i * npos) // 4
                fr = all_items[pos][0]
                all_items[pos][0] = (lambda fr=fr, f=f: (f(), fr()))
            self.run_pipeline2(all_items, LA=4, DL=2)

    def lat_mla(self, l, c, QL, QLb):
        s, d, db = self.s, self.d, self.db
        OG, OGb = c["OG"], c["OGb"]
        ROPE, ROPEb = c["ROPE"], c["ROPEb"]
        NK = 4352
        NKT = 34
        with ExitStack() as M:
            CKA = self.sb(M, "CKA", [128, NK], BF16)
            CKAb = Buf(CKA)
            KRA = self.sb(M, "KRA", [32, NK], BF16)
            KRAb = Buf(KRA)
            KH = self.sb(M, "KH", [96, NK], BF16)
            KHb = Buf(KH)
            VM = [self.sb(M, "VM%d" % i, [128, NKT, 192], BF16) for i in range(2)]
            VMb = [Buf(VM[i]) for i in range(2)]
            WUQ = self.sb(M, "WUQ", [128, 2, 768], BF16)
            WUQb = Buf(WUQ)
            WUQP = self.sb(M, "WUQP", [128, 2, 768], BF16)
            WUQPb = Buf(WUQP)
            WK = self.sb(M, "WK", [128, 8, 128], BF16)
            WKb = Buf(WK)
            WV = self.sb(M, "WV", [128, 512], BF16)
            WVb = Buf(WV)
            QH = [self.sb(M, "QH%d" % i, [96, 1024], BF16) for i in range(2)]
            QHb = [Buf(QH[i]) for i in range(2)]
            for cc in range(4):
                s.dma("sp", lambda q: q.dma_start(out=CKA[:, cc * 1024:(cc + 1) * 1024], in_=d["g_mla"][l][cc * 160:cc * 160 + 128, :]),
                      reads=[db["g_mla"][l]], pwrites=[CKAb])
                s.dma("sp", lambda q: q.dma_start(out=KRA[0:32, cc * 1024:(cc + 1) * 1024],
                                                  in_=d["g_mla"][l][cc * 160 + 128:cc * 160 + 160, :]),
                      reads=[db["g_mla"][l]], pwrites=[KRAb])
            s.dma("pool", lambda q: q.dma_start(out=CKA[:, 4096:NK], in_=d["cckvT"][l]), pwrites=[CKAb])
            s.dma("pool", lambda q: q.dma_start(out=KRA[0:32, 4096:NK], in_=d["ckrT"][l]), pwrites=[KRAb])
            s.dma("pool", lambda q: q.dma_start(out=WUQ[:], in_=d["w_uq"][l].rearrange("(c p) n -> p c n", p=128)), writes=[WUQb])
            s.dma("pool", lambda q: q.dma_start(out=WUQP[:], in_=d["w_uqp"][l].rearrange("(c p) n -> p c n", p=128)), writes=[WUQPb])
            s.op("dve", lambda q: q.memset(WK[:], 0.0), writes=[WKb])
            wukv = d["w_ukv"][l].rearrange("c (h t x) -> c h t x", t=2, x=64)
            s.dma("pool", lambda q: q.dma_start(out=WK[:, :, 0:64], in_=wukv[:, :, 0, :]), pwrites=[WKb])
            s.dma("pool", lambda q: q.dma_start(out=WV[:].rearrange("p (h x) -> p h x", x=64), in_=wukv[:, :, 1, :]), writes=[WVb])
            for i in range(2):
                s.op("dve", lambda q: q.memset(VM[i][:], 0.0), writes=[VMb[i]])
                s.op("dve", lambda q: q.memset(VM[i][:, :, 64:65], 1.0), pwrites=[VMb[i]])
            KH2 = self.sb(M, "KH2", [96, NK], BF16)
            KHs, KHbs = [KH, KH2], [KHb, Buf(KH2)]

            def gen_v(p):
                vm, vmb = VM[p % 2], VMb[p % 2]
                for k0 in range(0, NKT, 4):
                    nt = min(4, NKT - k0)
                    ps = self.ps("g")
                    for t in range(nt):
                        kt = k0 + t
                        s.op("pe", lambda q: q.matmul(ps.ap[:, t * 128:(t + 1) * 128], lhsT=CKA[:, kt * 128:(kt + 1) * 128],
                                                      rhs=WV[:, p * 128:(p + 1) * 128], start=True, stop=True),
                             reads=[CKAb, WVb], pwrites=[ps])
                    pv = ps.ap[:, 0:nt * 128].rearrange("p (t e x) -> p t e x", e=2, x=64)
                    s.op("dve", lambda q: q.tensor_copy(out=vm[:, k0:k0 + nt, 0:64], in_=pv[:, :, 0, :]), reads=[ps], pwrites=[vmb])
                    s.op("dve", lambda q: q.tensor_copy(out=vm[:, k0:k0 + nt, 128:192], in_=pv[:, :, 1, :]), reads=[ps], pwrites=[vmb])

            def gen_kq(h):
                kh, khb = KHs[h % 2], KHbs[h % 2]
                qh, qhb = QH[h % 2], QHb[h % 2]
                for k0 in range(0, NK, 512):
                    n = min(512, NK - k0)
                    ps = self.ps("g")
                    s.op("pe", lambda q: q.matmul(ps.ap[0:128, 0:n], lhsT=WK[:, h, :], rhs=CKA[:, k0:k0 + n], start=True, stop=False),
                         reads=[WKb, CKAb], pwrites=[ps])
                    s.op("pe", lambda q: q.matmul(ps.ap[0:128, 0:n], lhsT=self.sel32.ap[:, :], rhs=KRA[0:32, k0:k0 + n],
                                                  start=False, stop=True),
                         reads=[self.sel32, KRAb], pwrites=[ps])
                    s.op("dve", lambda q: q.tensor_copy(out=kh[:, k0:k0 + n], in_=ps.ap[0:96, 0:n]), reads=[ps], pwrites=[khb])
                for tb in range(2):
                    ts = slice(tb * 512, (tb + 1) * 512)
                    ps = self.ps("g")
                    ps2 = self.ps("g")
                    for (pp, ww, wwb) in ((ps, WUQ, WUQb), (ps2, WUQP, WUQPb)):
                        for kc in range(2):
                            s.op("pe", lambda q: q.matmul(pp.ap[0:96, :], lhsT=ww[:, kc, h * 96:(h + 1) * 96], rhs=QL[:, kc, ts],
                                                          start=(kc == 0), stop=(kc == 1)),
                                 reads=[wwb, QLb[tb]], pwrites=[pp])
                    s.op("dve", lambda q: q.tensor_copy(out=qh[0:64, ts], in_=ps.ap[0:64, :]), reads=[ps], pwrites=[qhb])
                    t1 = self.next_tmp()
                    t2 = self.next_tmp()
                    s.op("dve", lambda q: q.tensor_tensor(out=t1.ap[64:96, :], in0=ps.ap[64:96, :], in1=ROPE[64:96, 0, ts], op=ALU.mult),
                         reads=[ps, ROPEb], writes=[t1])
                    s.op("dve", lambda q: q.tensor_tensor(out=t2.ap[64:96, :], in0=ps2.ap[64:96, :], in1=ROPE[64:96, 1, ts], op=ALU.mult),
                         reads=[ps2, ROPEb], writes=[t2])
                    s.op("dve", lambda q: q.tensor_tensor(out=qh[64:96, ts], in0=t1.ap[64:96, :], in1=t2.ap[64:96, :], op=ALU.add),
                         reads=[t1, t2], pwrites=[qhb])

            gen_v(0)
            gen_kq(0)
            items = []
            for h in range(8):
                p, par = h // 2, h % 2
                vm, vmb = VM[p % 2], VMb[p % 2]
                kh, khb = KHs[h % 2], KHbs[h % 2]
                qh, qhb = QH[h % 2], QHb[h % 2]
                for tb in range(2):
                    ts = slice(tb * 512, (tb + 1) * 512)
                    po = self.ps("o")
                    for kt in range(NKT):
                        st = {}
                        pre = None
                        if par == 1 and tb == 0 and kt == 0 and p + 1 < 4:
                            pre = (lambda p=p: gen_v(p + 1))
                        if tb == 1 and kt == NKT - 12 and h + 1 < 8:
                            pre = (lambda h=h: gen_kq(h + 1))

                        def front(kt=kt, st=st, ts=ts, kh=kh, khb=khb, qh=qh, qhb=qhb, pre=pre):
                            if pre is not None:
                                pre()
                            ps = self.ps("g")
                            s.op("pe", lambda q: q.matmul(ps.ap[:], lhsT=kh[:, kt * 128:(kt + 1) * 128], rhs=qh[:, ts],
                                                          start=True, stop=True),
                                 reads=[khb, qhb], writes=[ps])
                            pt = self.next_pt()
                            s.op("act", lambda q: q.activation(out=pt.ap[:], in_=ps.ap[:], func=AF.Exp, scale=MLA_SCALE),
                                 reads=[ps], writes=[pt])
                            st["pt"] = pt

                        def back(kt=kt, st=st, po=po, par=par, vm=vm, vmb=vmb):
                            pt = st["pt"]
                            if par == 0:
                                out = po.ap[0:65, :]
                                lhsT = vm[:, kt, 0:65]
                            else:
                                out = po.ap[0:128, :]
                                lhsT = vm[:, kt, 64:192]
                            s.op("pe", lambda q: q.matmul(out, lhsT=lhsT, rhs=pt.ap[:], start=(kt == 0), stop=(kt == NKT - 1)),
                                 reads=[vmb, pt], pwrites=[po])
                        fin = None
                        if kt == NKT - 1:
                            fin = (lambda po=po, h=h, tb=tb, ts=ts: self.attn_finish(po, h, OG[1], OGb[1][tb], ts))
                        items.append((front, back, fin))
            self.run_pipeline2(items, LA=3, DL=2)


    def lat_fourier(self, l, c):
        s, d, db = self.s, self.d, self.db
        OG, OGb = c["OG"], c["OGb"]
        with ExitStack() as Fs:
            DC = [self.sb(Fs, "DC%d" % i, [128, 4, 512], BF16) for i in range(2)]
            DCb = [Buf(DC[i]) for i in range(2)]
            DS = [self.sb(Fs, "DS%d" % i, [128, 4, 512], BF16) for i in range(2)]
            DSb = [Buf(DS[i]) for i in range(2)]
            AB = [self.sb(Fs, "AB%d" % i, [128, 4, 1024], BF16) for i in range(2)]
            ABb = [Buf(AB[i]) for i in range(2)]
            it = 0
            for tb in range(2):
                ts = slice(tb * 512, (tb + 1) * 512)
                acc = [self.PS[4 + g] for g in range(4)]
                for nb in range(8):
                    dc, dcb, ds_, dsb, ab, abb = DC[it % 2], DCb[it % 2], DS[it % 2], DSb[it % 2], AB[it % 2], ABb[it % 2]
                    it += 1
                    rows = slice(nb * 512, (nb + 1) * 512)
                    s.dma("sp", lambda q: q.dma_start(out=dc[:], in_=d["dftc"][rows, ts].rearrange("(i p) n -> p i n", p=128)), writes=[dcb])
                    s.dma("sp", lambda q: q.dma_start(out=ds_[:], in_=d["dftns"][rows, ts].rearrange("(i p) n -> p i n", p=128)), writes=[dsb])
                    gn = "g_ab%d" % (nb % 2)
                    grow = slice((nb // 2) * 512, (nb // 2 + 1) * 512)
                    s.dma("sp", lambda q: q.dma_start(out=ab[:], in_=d[gn][l][grow, :].rearrange("(i p) n -> p i n", p=128)),
                          reads=[db[gn][l]], writes=[abb])
                    for i in range(4):
                        for g in range(4):
                            first = (nb == 0 and i == 0)
                            last = (nb == 7 and i == 3)
                            s.op("pe", lambda q: q.matmul(acc[g].ap[:], lhsT=ab[:, i, g * 256:g * 256 + 128], rhs=dc[:, i, :],
                                                          start=first, stop=False),
                                 reads=[abb, dcb], pwrites=[acc[g]])
                            s.op("pe", lambda q: q.matmul(acc[g].ap[:], lhsT=ab[:, i, g * 256 + 128:g * 256 + 256], rhs=ds_[:, i, :],
                                                          start=False, stop=last),
                                 reads=[abb, dsb], pwrites=[acc[g]])
                for g in range(4):
                    s.op("dve", lambda q: q.tensor_tensor(out=OG[2][:, g, ts], in0=acc[g].ap[:], in1=OG[2][:, g, ts], op=ALU.mult),
                         reads=[acc[g], OGb[2][tb]], pwrites=[OGb[2][tb]])


def _rope_perm():
    P = np.array([i + 8 if (i % 16) < 8 else i - 8 for i in range(32)])
    sgn = np.array([-1.0 if (i % 16) < 8 else 1.0 for i in range(32)], np.float32)
    return P, sgn


def _consts():
    bf = ml_dtypes.bfloat16
    c = {}
    c["identb"] = np.eye(128, dtype=np.float32).astype(bf)
    n = np.arange(128)
    ang = 2 * np.pi * np.outer(n, n) / 128.0
    c["cs128"] = (np.concatenate([np.cos(ang), np.sin(ang)], axis=1) / np.sqrt(128.0)).astype(np.float32).astype(bf)
    n = np.arange(256)
    ang = 2 * np.pi * np.outer(n, n) / 256.0
    c["c256"] = (np.cos(ang) / 16.0).astype(np.float32).astype(bf)
    c["ns256"] = (-np.sin(ang) / 16.0).astype(np.float32).astype(bf)
    sel = np.zeros((32, 128), np.float32)
    sel[np.arange(32), 64 + np.arange(32)] = 1.0
    c["sel32"] = sel.astype(bf)
    return c


def _lat_consts(q):
    bf = ml_dtypes.bfloat16
    c = {}
    P, sgn = _rope_perm()
    t = np.arange(1024 * q, 1024 * q + 1024)
    row = (t // 64).astype(np.float32)
    colp = (t % 64).astype(np.float32)
    half = 16
    inv = (10000.0 ** (-np.arange(0, half, 2, dtype=np.float32) / half)).astype(np.float32)
    ar = row[:, None] * inv[None, :]
    ac = colp[:, None] * inv[None, :]
    ang = np.concatenate([ar, ar, ac, ac], axis=-1)
    cos = np.cos(ang).astype(np.float32)
    sin = (np.sin(ang).astype(np.float32) * sgn[None, :]).astype(np.float32)
    rope = np.zeros((2, 128, 1024), np.float32)
    rope[0, 0:32] = cos.T
    rope[0, 64:96] = cos.T
    rope[1, 0:32] = sin.T
    rope[1, 64:96] = sin.T
    c["ropeT"] = rope
    n = np.arange(4096, dtype=np.float64)
    k = np.arange(1024 * q, 1024 * q + 1024, dtype=np.float64)
    ang = 2 * np.pi * ((np.outer(n, k)) % 4096) / 4096.0
    c["dftc"] = (np.cos(ang) / 64.0).astype(np.float32).astype(bf)
    c["dftns"] = (-np.sin(ang) / 64.0).astype(np.float32).astype(bf)
    kk = np.arange(128)
    kr, kcol = kk // 64, kk % 64
    mask = np.zeros((128, 48, 128), np.float32)
    for b in range(8):
        for slot in range(6):
            if slot < 5:
                lt = b - 2 + slot
            elif b == 0:
                lt = 3
            elif b == 7:
                lt = 4
            else:
                continue
            krow = 16 * q + 2 * lt + kr
            r = 16 * q + 2 * b + kr
            rs = np.clip(r - 4, 0, 56)
            row_ok = ((krow[:, None] >= 0) & (krow[:, None] < 64) &
                      (krow[:, None] >= rs[None, :]) & (krow[:, None] < rs[None, :] + 8))
            cs = np.clip(kcol - 8, 0, 48)
            col_ok = (kcol[:, None] >= cs[None, :]) & (kcol[:, None] < cs[None, :] + 16)
            mask[:, b * 6 + slot, :] = np.where(row_ok & col_ok, 1.0, 0.0)
    c["namask"] = mask.astype(bf)
    sel = np.zeros((128, 8), np.float32)
    if q - 1 >= 0:
        sel[:, q - 1] = 1.0
    if q + 1 <= 3:
        sel[:, 4 + q + 1] = 1.0
    c["selr"] = sel
    return c


_NC_CACHE = {}


def _get_nc(key=("full",)):
    if key not in _NC_CACHE:
        if key[0] == "full":
            kb = KB(run_ctx=True, run_lat=True)
        else:
            kb = KB(**dict(key[1]))
        _NC_CACHE[key] = kb.build()
    return _NC_CACHE[key]


def make_in_maps(inp):
    f = lambda a: np.ascontiguousarray(np.asarray(a, dtype=np.float32))
    x_prompt, x_sample = f(inp["x_prompt"]), f(inp["x_sample"])
    P, sgn = _rope_perm()
    w_in = f(inp["w_in"])
    w_uq = f(inp["w_uq"])
    shared = {}
    shared["w_ada"] = f(inp["w_ada"])
    shared["b_adaT"] = f(f(inp["b_ada"]).reshape(L, 24, 128).transpose(2, 0, 1))
    shared["norm_gT"] = f(f(inp["norm_g"]).reshape(L, 8, 128).transpose(2, 0, 1))
    shared["fin_gT"] = f(f(inp["final_norm_g"]).reshape(8, 128).T)
    shared["qn_gT"] = f(f(inp["q_norm_g"]).reshape(L, 2, 128).transpose(2, 0, 1))
    shared["kvn_gT"] = f(f(inp["kv_norm_g"]).T)
    shared["kvn_bc"] = f(np.broadcast_to(f(inp["kv_norm_g"])[None, :, :], (128, L, 128)))
    shared["w_in"] = w_in
    shared["w_krp"] = f(w_in[:, :, C_KR:C_KR + 32][:, :, P])
    shared["w_uq"] = w_uq
    wq = w_uq.reshape(L, 256, 8, 96).copy()
    wq[:, :, :, 64:96] = wq[:, :, :, 64:96][:, :, :, P]
    shared["w_uqp"] = f(wq.reshape(L, 256, 768))
    shared["w_ukv"] = f(inp["w_ukv"])
    shared["w_o_na"] = f(inp["w_o_na"])
    shared["w_o_mla"] = f(inp["w_o_mla"])
    shared["w_o_fn"] = f(inp["w_o_fourier"])
    shared["w_out"] = f(inp["w_out"])
    shared.update(_consts())
    nb = f(inp["na_bias"])
    kr = np.arange(128) // 64
    kc = np.arange(128) % 64
    dj = np.arange(7)
    dr = 2 * (dj[None, :, None] - 3) + kr[:, None, None] - kr[None, None, :]
    dc = np.clip(kc[:, None, None] - kc[None, None, :] + 15, 0, 30) + 0 * dr
    drc = np.clip(dr + 7, 0, 14)
    nab = nb[:, :, drc, dc]
    shared["nab"] = f(nab.transpose(0, 2, 1, 3, 4))
    cna_k, cna_v = f(inp["cache_na_k"]), f(inp["cache_na_v"])
    cckv, ckr = f(inp["cache_mla_ckv"]), f(inp["cache_mla_krope"])
    cvec, c_ctx = f(inp["c"]), f(inp["c_ctx"])
    maps = []
    latc = [_lat_consts(q) for q in range(4)]
    for core in range(8):
        b, q = core // 4, core % 4
        m = dict(shared)
        m["xc"] = f(x_prompt[2 * core:2 * core + 2].reshape(512, D).T)
        m["xl"] = f(x_sample[b, 1024 * q:1024 * q + 1024].T)
        m["cond"] = f(np.stack([c_ctx, cvec[b]], axis=1))
        m.update(latc[q])
        m["cnakT"] = f(cna_k[b].reshape(L, 256, 512).transpose(0, 2, 1))
        m["cnav"] = f(cna_v[b].reshape(L, 256, 512))
        m["cckvT"] = f(cckv[b].transpose(0, 2, 1))
        m["ckrT"] = f(ckr[b].transpose(0, 2, 1))
        maps.append(m)
    return maps


def assemble(res):
    r = res.results
    y_prompt = np.stack([r[c]["yc"].T.reshape(2, 256, D) for c in range(8)]).reshape(16, 256, D)
    y_sample = np.stack([np.concatenate([r[4 * b + q]["yl"].T for q in range(4)], axis=0) for b in range(2)])

    def cache(name, w):
        return np.concatenate([r[c][name].reshape(L, 2, 256, w).transpose(1, 0, 2, 3) for c in range(8)], axis=0)
    nk = cache("onk", 512).reshape(16, L, 256, 8, 64)
    nv = cache("onv", 512).reshape(16, L, 256, 8, 64)
    nckv = cache("ockv", 128)
    nkr = cache("okr", 32)
    return tuple(np.ascontiguousarray(a.astype(np.float32)) for a in (y_prompt, y_sample, nk, nv, nckv, nkr))


def kernel(**inputs):
    nc = _get_nc()
    in_maps = make_in_maps(inputs)
    res = run_bass_kernel_spmd(nc, in_maps, core_ids=list(range(8)))
    return assemble(res)
```
